# Optimizing a Trainium2 kernel written in Bass

```python
import math
import jax, jax.numpy as jnp
from jax import lax
import numpy as np

D_MODEL = 1024
BATCH = 8
SEQ = 4096
DEPTH = 4

CTX_LEN = 256
GRID_W = 64
HEAD_DIM = 64
A_HEADS = 8
A_KV_HEADS = 2
A_GROUP = A_HEADS // A_KV_HEADS
A_WINDOW = 128
A_BLOCK = 128
B_HEADS = 4
NA_ROWS = 8
NA_COLS = 16
NA_QCOLS = 16
C_HEADS = 4
C_Q_RANK = 256
C_KV_RANK = 128
C_NOPE = 64
C_ROPE = 32
C_V = 64
C_BLOCK = 128
MIX_WIDTH = A_HEADS * HEAD_DIM + B_HEADS * HEAD_DIM + C_HEADS * C_V
IN_WIDTHS = (A_HEADS * HEAD_DIM, A_KV_HEADS * HEAD_DIM, A_KV_HEADS * HEAD_DIM,
             B_HEADS * HEAD_DIM, B_HEADS * HEAD_DIM, B_HEADS * HEAD_DIM,
             C_Q_RANK, C_KV_RANK, C_ROPE)
IN_WIDTH = (A_HEADS + 2 * A_KV_HEADS + 3 * B_HEADS) * HEAD_DIM + C_Q_RANK + C_KV_RANK + C_ROPE
D_FF = 4 * D_MODEL
N_MOD = 6
ROPE_BASE = 10000.0
EPS = 1e-6
NEG_INF = -1e30

kernel_name = 'hybrid_dit_parallel_head_groups'


def rmsnorm(x, g):
    xf = x.astype(jnp.float32)
    y = xf * lax.rsqrt(jnp.mean(jnp.square(xf), axis=-1, keepdims=True) + EPS)
    return (y * g.astype(jnp.float32)).astype(x.dtype)


def modulate(h, shift, scale):
    return h * (1 + scale) + shift


def split_cols(p):
    outs, off = [], 0
    for w in IN_WIDTHS:
        outs.append(p[..., off:off + w])
        off += w
    return outs


def split_heads(t, h):
    return t.reshape(*t.shape[:-1], h, t.shape[-1] // h)


def rope_1d(x, pos):
    half = x.shape[-1] // 2
    freqs = ROPE_BASE ** (-jnp.arange(half, dtype=jnp.float32) / half)
    ang = pos.astype(jnp.float32)[:, None] * freqs
    cos = jnp.cos(ang)[:, None, :].astype(x.dtype)
    sin = jnp.sin(ang)[:, None, :].astype(x.dtype)
    x1, x2 = x[..., :half], x[..., half:]
    return jnp.concatenate([x1 * cos - x2 * sin, x1 * sin + x2 * cos], axis=-1)


def axial_rope(x, row, col):
    n = x.shape[-1] // 2
    return jnp.concatenate([rope_1d(x[..., :n], row), rope_1d(x[..., n:], col)], axis=-1)


def joint_softmax(parts, sink=None):
    parts = [p.astype(jnp.float32) for p in parts]
    m = parts[0].max(axis=-1, keepdims=True)
    for p in parts[1:]:
        m = jnp.maximum(m, p.max(axis=-1, keepdims=True))
    if sink is not None:
        sink = sink.astype(jnp.float32)
        m = jnp.maximum(m, sink)
    exps = [jnp.exp(p - m) for p in parts]
    denom = exps[0].sum(axis=-1, keepdims=True)
    for e in exps[1:]:
        denom = denom + e.sum(axis=-1, keepdims=True)
    if sink is not None:
        denom = denom + jnp.exp(sink - m)
    return [e / denom for e in exps]


def dense_attention(q, k, v, sink=None):
    scale = q.shape[-1] ** -0.5
    s = jnp.einsum('bqhd,bkhd->bhqk', q, k) * scale
    (p,) = joint_softmax([s], None if sink is None else sink[:, None, None])
    return jnp.einsum('bhqk,bkhd->bqhd', p.astype(v.dtype), v)


def windowed_gqa(q, k, v, k_ctx, v_ctx, sink):
    B, S, _, d = q.shape
    nb = S // A_BLOCK
    qb = q.reshape(B, nb, A_BLOCK, A_KV_HEADS, A_GROUP, d)
    pad = ((0, 0), (A_BLOCK, A_BLOCK), (0, 0), (0, 0))

    def band(t):
        tb = jnp.pad(t, pad).reshape(B, nb + 2, A_BLOCK, A_KV_HEADS, d)
        return jnp.concatenate([tb[:, :-2], tb[:, 1:-1], tb[:, 2:]], axis=2)

    kb, vb = band(k), band(v)
    qi = np.arange(A_BLOCK)[:, None]
    ks = np.arange(3 * A_BLOCK)[None, :]
    rel = ks - A_BLOCK - qi
    kpos = (np.arange(nb)[:, None, None] - 1) * A_BLOCK + ks[None]
    mask = (np.abs(rel)[None] <= A_WINDOW) & (kpos >= 0) & (kpos < S)
    scale = d ** -0.5
    s_loc = jnp.einsum('bnqkgd,bnskd->bnkgqs', qb, kb) * scale
    s_loc = jnp.where(mask[None, :, None, None], s_loc.astype(jnp.float32), NEG_INF)
    s_ctx = jnp.einsum('bnqkgd,bckd->bnkgqc', qb, k_ctx) * scale
    p_loc, p_ctx = joint_softmax([s_loc, s_ctx], sink.reshape(A_KV_HEADS, A_GROUP)[:, :, None, None])
    o = (jnp.einsum('bnkgqs,bnskd->bnqkgd', p_loc.astype(v.dtype), vb)
         + jnp.einsum('bnkgqc,bckd->bnqkgd', p_ctx.astype(v.dtype), v_ctx))
    return o.reshape(B, S, A_HEADS * d)


def na_layout(rows):
    kr, kc = min(NA_ROWS, rows), NA_COLS
    qr, qc = math.gcd(rows, NA_ROWS), NA_QCOLS
    krb, kcb = min(qr - 1 + kr, rows), min(qc - 1 + kc, GRID_W)
    nrb, ncb = rows // qr, GRID_W // qc
    q_r = np.arange(nrb)[:, None] * qr + np.arange(qr)[None, :]
    q_c = np.arange(ncb)[:, None] * qc + np.arange(qc)[None, :]
    w_r = np.clip(q_r - kr // 2, 0, rows - kr)
    w_c = np.clip(q_c - kc // 2, 0, GRID_W - kc)
    k_r = np.minimum(w_r[:, 0], rows - krb)[:, None] + np.arange(krb)[None, :]
    k_c = np.minimum(w_c[:, 0], GRID_W - kcb)[:, None] + np.arange(kcb)[None, :]
    qr6 = q_r[:, None, :, None, None, None]
    qc6 = q_c[None, :, None, :, None, None]
    wr6 = w_r[:, None, :, None, None, None]
    wc6 = w_c[None, :, None, :, None, None]
    kr6 = k_r[:, None, None, None, :, None]
    kc6 = k_c[None, :, None, None, None, :]
    shape6 = (nrb, ncb, qr, qc, krb, kcb)

    def flat(a):
        return np.broadcast_to(a, shape6).reshape(nrb, ncb, qr * qc, krb * kcb)

    mask = flat((kr6 >= wr6) & (kr6 < wr6 + kr) & (kc6 >= wc6) & (kc6 < wc6 + kc))
    d_r = flat(np.clip(kr6 - qr6, 1 - NA_ROWS, NA_ROWS - 1) + NA_ROWS - 1)
    d_c = flat(np.clip(kc6 - qc6, 1 - NA_COLS, NA_COLS - 1) + NA_COLS - 1)
    key_tok = (k_r[:, None, :, None] * GRID_W + k_c[None, :, None, :]).reshape(nrb, ncb, krb * kcb)
    return qr, qc, nrb, ncb, mask, d_r, d_c, key_tok


def neighbourhood_attention(q, k, v, k_ctx, v_ctx, rpb):
    B, S, H, d = q.shape
    rows = S // GRID_W
    qr, qc, nrb, ncb, mask, d_r, d_c, key_tok = na_layout(rows)
    qb = q.reshape(B, nrb, qr, ncb, qc, H, d).transpose(0, 1, 3, 2, 4, 5, 6).reshape(B, nrb, ncb, qr * qc, H, d)
    kg = k[:, key_tok]
    vg = v[:, key_tok]
    scale = d ** -0.5
    bias = rpb[:, d_r, d_c].transpose(1, 2, 0, 3, 4)
    s_loc = jnp.einsum('bijqhd,bijkhd->bijhqk', qb, kg) * scale
    s_loc = jnp.where(mask[:, :, None], s_loc.astype(jnp.float32) + bias.astype(jnp.float32), NEG_INF)
    s_ctx = jnp.einsum('bijqhd,bchd->bijhqc', qb, k_ctx) * scale
    p_loc, p_ctx = joint_softmax([s_loc, s_ctx])
    o = (jnp.einsum('bijhqk,bijkhd->bijqhd', p_loc.astype(v.dtype), vg)
         + jnp.einsum('bijhqc,bchd->bijqhd', p_ctx.astype(v.dtype), v_ctx))
    o = o.reshape(B, nrb, ncb, qr, qc, H, d).transpose(0, 1, 3, 2, 4, 5, 6)
    return o.reshape(B, S, H * d)


def mla_q(cq, g, w_uq):
    q = split_heads(rmsnorm(cq, g) @ w_uq, C_HEADS)
    return q[..., :C_NOPE], q[..., C_NOPE:]


def mla_kv(ckv, g, w_ukv):
    kv = split_heads(rmsnorm(ckv, g) @ w_ukv, C_HEADS)
    return kv[..., :C_NOPE], kv[..., C_NOPE:]


def mla_latent(q_nope, q_rope, k_nope, k_rope, v, kn_ctx, kr_ctx, v_ctx):
    B, S, H, _ = q_nope.shape
    kn_all = jnp.concatenate([k_nope, kn_ctx], axis=1)
    kr_all = jnp.concatenate([k_rope, kr_ctx], axis=1)
    v_all = jnp.concatenate([v, v_ctx], axis=1)
    scale = (C_NOPE + C_ROPE) ** -0.5
    nb = S // C_BLOCK

    def block(args):
        qn, qp = args
        s = jnp.einsum('bqhd,bkhd->bhqk', qn, kn_all) + jnp.einsum('bqhr,bkr->bhqk', qp, kr_all)
        (p,) = joint_softmax([s * scale])
        return jnp.einsum('bhqk,bkhd->bqhd', p.astype(v_all.dtype), v_all)

    def to_blocks(t):
        return jnp.moveaxis(t.reshape(B, nb, C_BLOCK, *t.shape[2:]), 1, 0)

    o = lax.map(block, (to_blocks(q_nope), to_blocks(q_rope)))
    return jnp.moveaxis(o, 0, 1).reshape(B, S, H * C_V)


def sq_relu_mlp(h, w1, w2):
    return jnp.square(jax.nn.relu(h @ w1)) @ w2


def setup_inputs(seed: int = 0) -> dict:
    key = jax.random.key(seed)
    ks = jax.random.split(key, 19)

    def nrm(k, shape, s):
        return jax.random.normal(k, shape, jnp.float32) * s

    def gain(k, shape):
        return 1.0 + 0.1 * jax.random.normal(k, shape, jnp.float32)

    return {
        'x': nrm(ks[0], (BATCH, SEQ, D_MODEL), 1.0),
        'c': nrm(ks[1], (BATCH, D_MODEL), 1.0),
        'ctx': nrm(ks[2], (BATCH, CTX_LEN, D_MODEL), 1.0),
        'c_ctx': nrm(ks[3], (D_MODEL,), 1.0),
        'w_ada': nrm(ks[4], (DEPTH, D_MODEL, N_MOD * D_MODEL), 0.5 * D_MODEL ** -0.5),
        'b_ada': nrm(ks[5], (DEPTH, N_MOD * D_MODEL), 0.02),
        'norm1_g': gain(ks[6], (DEPTH, D_MODEL)),
        'norm2_g': gain(ks[7], (DEPTH, D_MODEL)),
        'w_in': nrm(ks[8], (DEPTH, D_MODEL, IN_WIDTH), D_MODEL ** -0.5),
        'attn_sink': nrm(ks[9], (DEPTH, A_HEADS), 0.5),
        'na_rpb': nrm(ks[10], (DEPTH, B_HEADS, 2 * NA_ROWS - 1, 2 * NA_COLS - 1), 0.2),
        'mla_q_norm_g': gain(ks[11], (DEPTH, C_Q_RANK)),
        'mla_w_uq': nrm(ks[12], (DEPTH, C_Q_RANK, C_HEADS * (C_NOPE + C_ROPE)), C_Q_RANK ** -0.5),
        'mla_kv_norm_g': gain(ks[13], (DEPTH, C_KV_RANK)),
        'mla_w_ukv': nrm(ks[14], (DEPTH, C_KV_RANK, C_HEADS * (C_NOPE + C_V)), C_KV_RANK ** -0.5),
        'w_out': nrm(ks[15], (DEPTH, MIX_WIDTH, D_MODEL), MIX_WIDTH ** -0.5),
        'w_mlp_in': nrm(ks[16], (DEPTH, D_MODEL, D_FF), D_MODEL ** -0.5),
        'w_mlp_out': nrm(ks[17], (DEPTH, D_FF, D_MODEL), D_FF ** -0.5),
        'final_norm_g': gain(ks[18], (D_MODEL,)),
    }


def reference(x, c, ctx, c_ctx, w_ada, b_ada, norm1_g, norm2_g, w_in, attn_sink, na_rpb,
              mla_q_norm_g, mla_w_uq, mla_kv_norm_g, mla_w_ukv, w_out, w_mlp_in, w_mlp_out,
              final_norm_g):
    B, S, _ = x.shape
    C = ctx.shape[1]
    tok = jnp.arange(S)
    row, col = tok // GRID_W, tok % GRID_W
    silu_c = jax.nn.silu(c)
    silu_cc = jax.nn.silu(c_ctx)
    for l in range(DEPTH):
        last = l == DEPTH - 1
        mod_x = jnp.split((silu_c @ w_ada[l] + b_ada[l])[:, None, :], N_MOD, axis=-1)
        mod_c = jnp.split(silu_cc @ w_ada[l] + b_ada[l], N_MOD, axis=-1)

        hx = modulate(rmsnorm(x, norm1_g[l]), mod_x[0], mod_x[1])
        hc = modulate(rmsnorm(ctx, norm1_g[l]), mod_c[0], mod_c[1])
        xa_q, xa_k, xa_v, xb_q, xb_k, xb_v, xc_q, xc_kv, xc_kr = split_cols(hx @ w_in[l])
        ca_q, ca_k, ca_v, cb_q, cb_k, cb_v, cc_q, cc_kv, cc_kr = split_cols(hc @ w_in[l])

        ka_c, va_c = split_heads(ca_k, A_KV_HEADS), split_heads(ca_v, A_KV_HEADS)
        kb_c, vb_c = split_heads(cb_k, B_HEADS), split_heads(cb_v, B_HEADS)
        kn_c, vc_c = mla_kv(cc_kv, mla_kv_norm_g[l], mla_w_ukv[l])

        qa = axial_rope(split_heads(xa_q, A_HEADS), row, col)
        ka = axial_rope(split_heads(xa_k, A_KV_HEADS), row, col)
        out_a = windowed_gqa(qa, ka, split_heads(xa_v, A_KV_HEADS), ka_c, va_c, attn_sink[l])
        out_b = neighbourhood_attention(split_heads(xb_q, B_HEADS), split_heads(xb_k, B_HEADS),
                                        split_heads(xb_v, B_HEADS), kb_c, vb_c, na_rpb[l])
        qn, qp = mla_q(xc_q, mla_q_norm_g[l], mla_w_uq[l])
        kn, vc = mla_kv(xc_kv, mla_kv_norm_g[l], mla_w_ukv[l])
        qp = axial_rope(qp, row, col)
        kp = axial_rope(xc_kr[:, :, None, :], row, col)[:, :, 0]
        out_c = mla_latent(qn, qp, kn, kp, vc, kn_c, cc_kr, vc_c)

        x = x + mod_x[2] * (jnp.concatenate([out_a, out_b, out_c], axis=-1) @ w_out[l])

        if not last:
            oa = dense_attention(split_heads(ca_q, A_HEADS), jnp.repeat(ka_c, A_GROUP, axis=2),
                                 jnp.repeat(va_c, A_GROUP, axis=2), attn_sink[l])
            ob = dense_attention(split_heads(cb_q, B_HEADS), kb_c, vb_c)
            qn_c, qp_c = mla_q(cc_q, mla_q_norm_g[l], mla_w_uq[l])
            kp_c = jnp.broadcast_to(cc_kr[:, :, None, :], (B, C, C_HEADS, C_ROPE))
            oc = dense_attention(jnp.concatenate([qn_c, qp_c], axis=-1),
                                 jnp.concatenate([kn_c, kp_c], axis=-1), vc_c)
            mixed_c = jnp.concatenate([oa.reshape(B, C, -1), ob.reshape(B, C, -1),
                                       oc.reshape(B, C, -1)], axis=-1)
            ctx = ctx + mod_c[2] * (mixed_c @ w_out[l])

        x = x + mod_x[5] * sq_relu_mlp(modulate(rmsnorm(x, norm2_g[l]), mod_x[3], mod_x[4]),
                                       w_mlp_in[l], w_mlp_out[l])
        if not last:
            ctx = ctx + mod_c[5] * sq_relu_mlp(modulate(rmsnorm(ctx, norm2_g[l]), mod_c[3], mod_c[4]),
                                               w_mlp_in[l], w_mlp_out[l])
    return rmsnorm(x, final_norm_g)
```

```python
import math
from contextlib import ExitStack
import numpy as np
import concourse.bass as bass
import concourse.mybir as mybir
from concourse.bass_utils import run_bass_kernel_spmd

F32 = mybir.dt.float32
BF16 = mybir.dt.bfloat16
AF = mybir.ActivationFunctionType
ALU = mybir.AluOpType

D = 1024
S_ = 4096
C_ = 256
T_ = S_ + C_
L_ = 4
NCOL = 2624
EPS = 1e-6
NEG = -30000.0
O_QA, O_QAS, O_KA, O_KAS, O_QB, O_KB, O_CQ, O_CKV, O_KR, O_KRS, O_VA = (
    0, 512, 1024, 1152, 1280, 1536, 1792, 2048, 2176, 2208, 2240)
VW = 650


class Buf:
    __slots__ = ("name", "lw", "rs", "psum")

    def __init__(self, name="", psum=False):
        self.name = name
        self.lw = None
        self.rs = {}
        self.psum = psum


class Sched:
    def __init__(self, nc, n_dma_slots=32, same_engine_sync=True):
        self.nc = nc
        self.same = same_engine_sync
        self.eng = {"pe": nc.tensor, "act": nc.scalar, "dve": nc.vector, "pool": nc.gpsimd, "sp": nc.sync}
        self.sem = {k: nc.alloc_semaphore(name=f"sem_{k}") for k in self.eng}
        self.cnt = {k: 0 for k in self.eng}
        self.seen = {k: {} for k in self.eng}
        self.dsem = [nc.alloc_semaphore(name=f"dsem{i}") for i in range(n_dma_slots)]
        self.dcnt = [0] * n_dma_slots
        self.dnext = 0
        self.n_wait = 0
        self.n_inst = 0

    def _semof(self, key):
        return self.dsem[key] if isinstance(key, int) else self.sem[key]

    def _wait(self, e, key, val):
        if key == e and not self.same:
            return
        if self.seen[e].get(key, 0) >= val:
            return
        self.eng[e].wait_ge(self._semof(key), val)
        self.seen[e][key] = val
        self.n_wait += 1

    def _deps(self, e, reads, writes):
        for b in reads:
            if b.lw is not None:
                self._wait(e, *b.lw)
            if b.psum:
                for k, v in b.rs.items():
                    if k != e:
                        self._wait(e, k, v)
        for b in writes:
            if b.lw is not None:
                self._wait(e, *b.lw)
            for k, v in b.rs.items():
                self._wait(e, k, v)

    def op(self, e, fn, reads=(), writes=(), pe_acc=False, inc=True):
        if pe_acc:
            self._deps(e, reads, ())
        else:
            self._deps(e, reads, writes)
        ins = fn(self.eng[e])
        if inc:
            ins.then_inc(self.sem[e], 1)
            self.cnt[e] += 1
            n = self.cnt[e]
        else:
            n = self.cnt[e] + 1
        self.n_inst += 1
        for b in reads:
            b.rs[e] = n
        for b in writes:
            b.lw = (e, n)
            b.rs = {}
        return ins

    def dma(self, q, out, in_, reads=(), writes=(), **kw):
        self._deps(q, reads, writes)
        s = self.dnext
        self.dnext = (self.dnext + 1) % len(self.dsem)
        if self.dcnt[s] > 0:
            self._wait(q, s, 16 * self.dcnt[s])
        ins = self.eng[q].dma_start(out=out, in_=in_, **kw)
        ins.then_inc(self.dsem[s], 16)
        self.dcnt[s] += 1
        v = 16 * self.dcnt[s]
        self.n_inst += 1
        for b in reads:
            b.rs[s] = v
        for b in writes:
            b.lw = (s, v)
            b.rs = {}
        return ins

    def barrier(self):
        for e in self.eng:
            self.finish(e)

    def finish(self, e="sp"):
        for k in self.eng:
            if k != e and self.cnt[k] > 0:
                self._wait(e, k, self.cnt[k])
        for s in range(len(self.dsem)):
            if self.dcnt[s] > 0:
                self._wait(e, s, 16 * self.dcnt[s])


class TB:
    def __init__(self, t, name=""):
        self.t = t
        self.b = Buf(name)


class Ring:
    def __init__(self, items):
        self.items = items
        self.i = 0

    def next(self):
        it = self.items[self.i]
        self.i = (self.i + 1) % len(self.items)
        return it


def _perm_a(n):
    idx = np.arange(n)
    h, d = idx // 64, idx % 64
    j = d % 32
    partner = np.where(j < 16, d + 16, d - 16)
    return h * 64 + partner


def _perm_r(n=32):
    d = np.arange(n)
    jj = d % 16
    return np.where(jj < 8, d + 8, d - 8)


def _rope_tables():
    tok = np.arange(S_)
    row, col = tok // 64, tok % 64

    def tab(ndim_sec, half):
        freqs = (10000.0 ** (-np.arange(half, dtype=np.float32) / half)).astype(np.float32)
        cos = np.zeros((2 * ndim_sec, S_), np.float32)
        sin = np.zeros((2 * ndim_sec, S_), np.float32)
        for d in range(2 * ndim_sec):
            sec, j = d // ndim_sec, d % ndim_sec
            pos = (row if sec == 0 else col).astype(np.float32)
            ang = pos * freqs[j % half]
            cos[d] = np.cos(ang).astype(np.float32)
            sn = np.sin(ang).astype(np.float32)
            sin[d] = -sn if j < half else sn
        return cos, sin

    ca, sa = tab(32, 16)
    cr, sr = tab(16, 8)
    ropeA = np.stack([np.concatenate([ca, ca], 0), np.concatenate([sa, sa], 0)], 1)
    c96 = np.concatenate([np.ones((64, S_), np.float32), cr], 0)
    s96 = np.concatenate([np.zeros((64, S_), np.float32), sr], 0)
    rope96 = np.stack([c96, s96], 1)
    ropeR = np.stack([cr, sr], 1)
    return np.ascontiguousarray(ropeA), np.ascontiguousarray(rope96), np.ascontiguousarray(ropeR)


def _na_layout():
    rows, GW, NR, NCc, NQ = 64, 64, 8, 16, 16
    kr, kc = min(NR, rows), NCc
    qr, qc = math.gcd(rows, NR), NQ
    krb, kcb = min(qr - 1 + kr, rows), min(qc - 1 + kc, GW)
    nrb, ncb = rows // qr, GW // qc
    q_r = np.arange(nrb)[:, None] * qr + np.arange(qr)[None, :]
    q_c = np.arange(ncb)[:, None] * qc + np.arange(qc)[None, :]
    w_r = np.clip(q_r - kr // 2, 0, rows - kr)
    w_c = np.clip(q_c - kc // 2, 0, GW - kc)
    k_r = np.minimum(w_r[:, 0], rows - krb)[:, None] + np.arange(krb)[None, :]
    k_c = np.minimum(w_c[:, 0], GW - kcb)[:, None] + np.arange(kcb)[None, :]
    qr6 = q_r[:, None, :, None, None, None]
    qc6 = q_c[None, :, None, :, None, None]
    wr6 = w_r[:, None, :, None, None, None]
    wc6 = w_c[None, :, None, :, None, None]
    kr6 = k_r[:, None, None, None, :, None]
    kc6 = k_c[None, :, None, None, None, :]
    shape6 = (nrb, ncb, qr, qc, krb, kcb)

    def flat(a):
        return np.broadcast_to(a, shape6).reshape(nrb, ncb, qr * qc, krb * kcb)

    mask = flat((kr6 >= wr6) & (kr6 < wr6 + kr) & (kc6 >= wc6) & (kc6 < wc6 + kc))
    d_r = flat(np.clip(kr6 - qr6, 1 - NR, NR - 1) + NR - 1)
    d_c = flat(np.clip(kc6 - qc6, 1 - NCc, NCc - 1) + NCc - 1)
    return mask, d_r, d_c, k_r[:, 0], k_c[:, 0]


NA_MASK, NA_DR, NA_DC, NA_KR0, NA_KC0 = _na_layout()
ICLS = [0, 1, 1, 1, 1, 1, 1, 2]
JCLS = [0, 1, 1, 2]
IREP = [0, 1, 7]
JREP = [0, 1, 3]
KG_ROWS = [4, 4, 4, 3]


def _rpb_table(na_rpb):
    tab = np.full((L_, 124, 4, 9, 4, 128), NEG, np.float32)
    for ic in range(3):
        for jc in range(3):
            i, j = IREP[ic], JREP[jc]
            m = NA_MASK[i, j]
            dr, dc = NA_DR[i, j], NA_DC[i, j]
            g = na_rpb[:, :, dr, dc]
            g = np.where(m[None, None], g, np.float32(NEG))
            for kg in range(4):
                n = KG_ROWS[kg] * 31
                blk = g[:, :, :, kg * 124: kg * 124 + n]
                tab[:, :n, :, ic * 3 + jc, kg, :] = blk.transpose(0, 3, 1, 2)
    return np.ascontiguousarray(tab.reshape(L_, 124, 4, 9 * 4 * 128))


def _fm(v):
    v = np.asarray(v, np.float32)
    lead = v.shape[:-1]
    c = v.shape[-1] // 128
    v = v.reshape(*lead, c, 128)
    return np.ascontiguousarray(np.moveaxis(v, -1, 0))


def prep_inputs(inp):
    f = lambda a: np.ascontiguousarray(np.asarray(a, np.float32))
    w_in = f(inp["w_in"])
    pa512, pa128, pr = _perm_a(512), _perm_a(128), _perm_r()
    ext = np.concatenate([
        w_in[:, :, 0:512], w_in[:, :, 0:512][:, :, pa512],
        w_in[:, :, 512:640], w_in[:, :, 512:640][:, :, pa128],
        w_in[:, :, 768:1024], w_in[:, :, 1024:1280],
        w_in[:, :, 1536:1792], w_in[:, :, 1792:1920],
        w_in[:, :, 1920:1952], w_in[:, :, 1920:1952][:, :, pr],
        w_in[:, :, 640:768], w_in[:, :, 1280:1536]], axis=2)
    assert ext.shape[2] == NCOL
    w_uq = f(inp["mla_w_uq"])
    sw = np.concatenate([np.concatenate([h * 96 + np.arange(64), h * 96 + 64 + pr]) for h in range(4)])
    w_uq_ext = np.concatenate([w_uq, w_uq[:, :, sw]], axis=2)
    w_ukv = f(inp["mla_w_ukv"]).reshape(L_, 128, 4, 2, 64)
    w_ukv_r = np.ascontiguousarray(w_ukv.transpose(0, 1, 3, 2, 4).reshape(L_, 128, 512))
    ropeA, rope96, ropeR = _rope_tables()
    kq = np.arange(128)
    amask = np.zeros((128, 2, 128), np.float32)
    amask[:, 0, :] = np.where(kq[:, None] >= kq[None, :], 0.0, NEG)
    amask[:, 1, :] = np.where(kq[:, None] <= kq[None, :], 0.0, NEG)
    gains = np.concatenate([_fm(inp["norm1_g"]), _fm(inp["norm2_g"]), _fm(inp["final_norm_g"])[:, None, :]], 1)
    mla_g = np.concatenate([_fm(inp["mla_q_norm_g"]), _fm(inp["mla_kv_norm_g"])], 2)
    shared = {
        "w_ada": f(inp["w_ada"]),
        "b_ada": _fm(inp["b_ada"]),
        "gains": np.ascontiguousarray(gains),
        "w_in_ext": np.ascontiguousarray(ext),
        "w_uq_ext": np.ascontiguousarray(w_uq_ext),
        "w_ukv_r": w_ukv_r,
        "mla_g": np.ascontiguousarray(mla_g),
        "sinkrow": np.ascontiguousarray(np.broadcast_to(f(inp["attn_sink"]).reshape(1, 32), (65, 32))),
        "rpb_tab": _rpb_table(f(inp["na_rpb"])),
        "w_out": f(inp["w_out"]),
        "w_mlp_in": f(inp["w_mlp_in"]),
        "w_mlp_out": f(inp["w_mlp_out"]),
        "ropeA": ropeA, "rope96": rope96, "ropeR": ropeR,
        "amask": amask,
        "ident8": np.ascontiguousarray(8.0 * np.eye(128, dtype=np.float32)),
    }
    x, ctx, c, c_ctx = f(inp["x"]), f(inp["ctx"]), f(inp["c"]), f(inp["c_ctx"])
    per_core = []
    for b in range(x.shape[0]):
        xt = np.ascontiguousarray(np.concatenate([x[b].T, ctx[b].T], axis=1))
        cv = np.ascontiguousarray(np.stack([_fm(c[b]), _fm(c_ctx)], -1))
        d = dict(shared)
        d["xT"] = xt
        d["cvec"] = cv
        per_core.append(d)
    return per_core


def build(n_layers=L_, debug=False, stop_after=None):
    nc = bass.Bass("TRN2", target_bir_lowering=False)
    S = Sched(nc)
    uid = [0]

    def din(name, shape):
        return nc.dram_tensor(name, list(shape), F32, kind="ExternalInput").ap()

    def dscr(name, shape, dt):
        kind = "ExternalOutput" if debug else "Internal"
        return nc.dram_tensor(name, list(shape), dt, kind=kind).ap()

    xT = din("xT", [D, T_])
    cvec = din("cvec", [128, 8, 2])
    w_ada = din("w_ada", [L_, D, 6 * D])
    b_ada = din("b_ada", [128, L_, 48])
    gains = din("gains", [128, 9, 8])
    w_in_ext = din("w_in_ext", [L_, D, NCOL])
    w_uq_ext = din("w_uq_ext", [L_, 256, 768])
    w_ukv_r = din("w_ukv_r", [L_, 128, 512])
    mla_g = din("mla_g", [128, L_, 3])
    sinkrow = din("sinkrow", [65, 32])
    rpb_tab = din("rpb_tab", [L_, 124, 4, 4608])
    w_out = din("w_out", [L_, D, D])
    w_mlp_in = din("w_mlp_in", [L_, D, 4 * D])
    w_mlp_out = din("w_mlp_out", [L_, 4 * D, D])
    ropeA = din("ropeA", [128, 2, S_])
    rope96 = din("rope96", [96, 2, S_])
    ropeR = din("ropeR", [32, 2, S_])
    amask_d = din("amask", [128, 2, 128])
    ident8_d = din("ident8", [128, 128])
    outT = nc.dram_tensor("outT", [D, S_], F32, kind="ExternalOutput").ap()

    xmid = dscr("xmid", [D, T_], F32)
    xres = dscr("xres", [D, T_], F32)
    QA = dscr("QA", [512, T_], BF16)
    KA = dscr("KA", [128, T_], BF16)
    QB = dscr("QB", [256, T_], BF16)
    KB = dscr("KB", [256, T_], BF16)
    QC = dscr("QC", [384, T_], BF16)
    KC = dscr("KC", [384, T_], BF16)
    VALL = dscr("VALL", [T_, VW], BF16)
    MIX = dscr("MIX", [D, T_], BF16)
    NG = 9
    db = {n: [Buf(f"{n}{g}") for g in range(NG)] for n in
          ["xT", "xmid", "xres", "QA", "KA", "QB", "KB", "QC", "KC", "VALL", "MIX"]}

    def gcols(g):
        return (g * 512, 512) if g < 8 else (S_, C_)

    glob = ExitStack()

    def alloc(stack, shape, dt, name=None):
        uid[0] += 1
        t = stack.enter_context(nc.sbuf_tensor(f"{name or 't'}_{uid[0]}", list(shape), dt))
        return TB(t, name or "t")

    def palloc(stack, shape, dt, name=None):
        uid[0] += 1
        t = stack.enter_context(nc.psum_tensor(f"{name or 'p'}_{uid[0]}", [128, 512], F32))
        tb = TB(t, name or "p")
        tb.b.psum = True
        return tb

    def mm(out_ap, lhsT, rhs, first, last, reads, wbuf, inc=None, **kw):
        S.op("pe", lambda e: e.matmul(out_ap, lhsT=lhsT, rhs=rhs, start=first, stop=last, **kw),
             reads=reads, writes=[wbuf], pe_acc=not first, inc=(last if inc is None else inc))

    ones_bf = alloc(glob, [128, 128], BF16, "ones_bf")
    ones_f = alloc(glob, [128, 64], F32, "ones_f")
    ident8 = alloc(glob, [128, 128], BF16, "ident8")
    amask = alloc(glob, [128, 2, 128], BF16, "amask")
    modv = alloc(glob, [128, L_, 6, 8, 2], F32, "modv")
    gain_sb = alloc(glob, [128, 9, 8], F32, "gains")
    mlag_sb = alloc(glob, [128, L_, 3], F32, "mlag")
    esink = alloc(glob, [65, 32], F32, "esink")
    epsc = alloc(glob, [128, 1], F32, "epsc")

    S.op("dve", lambda e: e.memset(ones_bf.t[:], 1.0), writes=[ones_bf.b])
    S.op("dve", lambda e: e.memset(ones_f.t[:], 1.0), writes=[ones_f.b])
    S.op("dve", lambda e: e.memset(epsc.t[:], EPS), writes=[epsc.b])
    S.dma("sp", gain_sb.t[:], gains, writes=[gain_sb.b])
    S.dma("sp", mlag_sb.t[:], mla_g, writes=[mlag_sb.b])
    S.dma("sp", esink.t[:], sinkrow, writes=[esink.b])
    S.op("act", lambda e: e.activation(out=esink.t[:], in_=esink.t[:], func=AF.Exp), reads=[esink.b], writes=[esink.b])

    with ExitStack() as ph:
        stg = alloc(ph, [128, 2, 128], F32, "stg")
        S.dma("sp", stg.t[:, 0, :], ident8_d, writes=[stg.b])
        S.op("dve", lambda e: e.tensor_copy(out=ident8.t[:], in_=stg.t[:, 0, :]), reads=[stg.b], writes=[ident8.b])
        S.dma("sp", stg.t[:], amask_d, reads=[], writes=[stg.b])
        S.op("dve", lambda e: e.tensor_copy(out=amask.t[:], in_=stg.t[:]), reads=[stg.b], writes=[amask.b])
        cv = alloc(ph, [128, 8, 2], F32, "cv")
        sv = alloc(ph, [128, 8, 2], F32, "sv")
        bfm = alloc(ph, [128, L_, 48], F32, "bfm")
        S.dma("sp", cv.t[:], cvec, writes=[cv.b])
        S.dma("sp", bfm.t[:], b_ada, writes=[bfm.b])
        S.op("act", lambda e: e.activation(out=sv.t[:], in_=cv.t[:], func=AF.Silu), reads=[cv.b], writes=[sv.b])
        wst = Ring([alloc(ph, [128, 8, 768], F32, "wst") for _ in range(2)])
        mps = Ring([palloc(ph, [128, 48, 2], F32, "mps") for _ in range(2)])
        mods = alloc(ph, [128, 48, 2], F32, "mods")
        for l in range(n_layers):
            pm = mps.next()
            wa = w_ada[l].rearrange("(kc p) n -> p kc n", p=128)
            for pc in range(8):
                w = wst.next()
                S.dma("sp", w.t[:], wa[:, :, pc * 768:(pc + 1) * 768], writes=[w.b])
                for cc in range(6):
                    ch = pc * 6 + cc
                    for kc in range(8):
                        mm(pm.t[:, 2 * ch:2 * ch + 2], w.t[:, kc, cc * 128:(cc + 1) * 128], sv.t[:, kc, :],
                           kc == 0, kc == 7, [w.b, sv.b], pm.b)
            S.op("dve", lambda e: e.tensor_tensor(
                out=mods.t[:], in0=pm.t[:, 0:96].rearrange("p (c j) -> p c j", j=2), in1=bfm.t[:, l, :].unsqueeze(2).broadcast_to([128, 48, 2]),
                op=ALU.add), reads=[pm.b, bfm.b], writes=[mods.b])
            for (k, src_m, gidx) in ((0, 1, l), (3, 4, 4 + l)):
                S.op("dve", lambda e: e.scalar_tensor_tensor(
                    out=modv.t[:, l, k, :, :], in0=mods.t[:, src_m * 8:(src_m + 1) * 8, :], scalar=1.0,
                    in1=gain_sb.t[:, gidx, :].unsqueeze(2).broadcast_to([128, 8, 2]),
                    op0=ALU.add, op1=ALU.mult), reads=[mods.b, gain_sb.b], writes=[modv.b])
            for (k, src_m) in ((1, 0), (2, 2), (4, 3), (5, 5)):
                S.op("dve", lambda e: e.tensor_copy(out=modv.t[:, l, k, :, :], in_=mods.t[:, src_m * 8:(src_m + 1) * 8, :]),
                     reads=[mods.b], writes=[modv.b])
    S.barrier()
    if stop_after == ("M", 0):
        if debug:
            dbg = nc.dram_tensor("dbg_modv", [128, L_ * 96], F32, kind="ExternalOutput").ap()
            S.dma("sp", dbg, modv.t[:].rearrange("p l k c j -> p (l k c j)"), reads=[modv.b])
        S.finish("sp")
        glob.close()
        return nc, S

    def norm_mod(xg, W, l, kG, kS, j, hT, hbufs, sq, ps_ss, lnv, rstd, tmps):
        S.op("dve", lambda e: e.tensor_tensor(out=sq.t[:, :, :W], in0=xg.t[:, :, :W], in1=xg.t[:, :, :W], op=ALU.mult),
             reads=[xg.b], writes=[sq.b])
        for kc in range(8):
            mm(ps_ss.t[:, :W], ones_bf.t[:], sq.t[:, kc, :W], kc == 0, kc == 7, [sq.b, ones_bf.b], ps_ss.b)
        S.op("act", lambda e: e.activation(out=lnv.t[:, :W], in_=ps_ss.t[:, :W], func=AF.Ln, scale=1.0 / D, bias=epsc.t[:, 0:1]),
             reads=[ps_ss.b, epsc.b], writes=[lnv.b])
        S.op("act", lambda e: e.activation(out=rstd.t[:, :W], in_=lnv.t[:, :W], func=AF.Exp, scale=-0.5),
             reads=[lnv.b], writes=[rstd.b])
        for kc in range(8):
            tm = tmps.next()
            S.op("dve", lambda e: e.scalar_tensor_tensor(
                out=tm.t[:, :W], in0=xg.t[:, kc, :W], scalar=modv.t[:, l, kG, kc, j:j + 1], in1=rstd.t[:, :W],
                op0=ALU.mult, op1=ALU.mult), reads=[xg.b, modv.b, rstd.b], writes=[tm.b])
            S.op("act", lambda e: e.activation(out=hT.t[:, kc, :W], in_=tm.t[:, :W], func=AF.Identity,
                                               bias=modv.t[:, l, kS, kc, j:j + 1], scale=1.0),
                 reads=[tm.b, modv.b], writes=[hbufs[kc]])

    def load_cast(stack_ring, dst_ap_fn, src_ap_fn, n_pieces, dst_buf, engs=("dve", "pool")):
        for i in range(n_pieces):
            st = stack_ring.next()
            src = src_ap_fn(i)
            dst = dst_ap_fn(i)
            shp = list(src.shape)
            view = st.t[:shp[0], :int(np.prod(shp[1:]))]
            if len(shp) == 3:
                view = view.rearrange("p (a b) -> p a b", a=shp[1])
            S.dma("sp", view, src, writes=[st.b])
            eng = engs[i % len(engs)]
            if eng == "act":
                S.op("act", lambda e: e.activation(out=dst, in_=view, func=AF.Copy), reads=[st.b], writes=[dst_buf], pe_acc=False)
            else:
                S.op(eng, lambda e: e.tensor_copy(out=dst, in_=view), reads=[st.b], writes=[dst_buf])

    for l in range(n_layers):
        last = (l == L_ - 1)
        xsrc, xsrc_n = (xT, "xT") if l == 0 else (xres, "xres")
        xs_fm = xsrc.rearrange("(kc p) t -> p kc t", p=128)

        with ExitStack() as ph:
            stg_ring = Ring([alloc(ph, [128, 2624], F32, "stg") for _ in range(2)])
            win = alloc(ph, [128, 8, NCOL], BF16, "win")
            wuq = alloc(ph, [128, 2, 768], BF16, "wuq")
            wukv = alloc(ph, [128, 512], BF16, "wukv")
            wie = w_in_ext[l].rearrange("(kc p) n -> p kc n", p=128)
            load_cast(stg_ring, lambda i: win.t[:, i, :], lambda i: wie[:, i, :], 8, win.b)
            wue = w_uq_ext[l].rearrange("(kc p) n -> p kc n", p=128)
            load_cast(stg_ring, lambda i: wuq.t[:, i, :], lambda i: wue[:, i, :], 2, wuq.b)
            load_cast(stg_ring, lambda i: wukv.t[:], lambda i: w_ukv_r[l], 1, wukv.b)

            xgs = Ring([alloc(ph, [128, 8, 512], F32, "xg") for _ in range(2)])
            hTs = [alloc(ph, [128, 8, 512], BF16, "hT") for _ in range(2)]
            hbs = [[Buf(f"h{i}_{k}") for k in range(8)] for i in range(2)]
            sq = alloc(ph, [128, 8, 512], BF16, "sq")
            lnv = alloc(ph, [128, 512], F32, "lnv")
            rstds = Ring([alloc(ph, [128, 512], F32, "rstd") for _ in range(2)])
            tmps = Ring([alloc(ph, [128, 512], F32, "tmp") for _ in range(3)])
            rtab = Ring([alloc(ph, [128, 6, 512], F32, "rtab") for _ in range(2)])
            outs = Ring([alloc(ph, [128, 512], BF16, "ost") for _ in range(6)])
            t1s = Ring([alloc(ph, [128, 512], F32, "t1") for _ in range(2)])
            t2s = Ring([alloc(ph, [128, 512], F32, "t2") for _ in range(2)])
            cq = alloc(ph, [128, 3, 512], F32, "cq")
            cqsq = alloc(ph, [128, 3, 512], BF16, "cqsq")
            cqn = alloc(ph, [128, 3, 512], BF16, "cqn")
            vsts = Ring([alloc(ph, [128, VW], BF16, "vst") for _ in range(2)])
            ps_ss = palloc(ph, [128, 512], F32, "ps_ss")
            ps_a = Ring([palloc(ph, [128, 512], F32, "ps_a") for _ in range(2)])
            ps_b = Ring([palloc(ph, [128, 512], F32, "ps_b") for _ in range(2)])
            ps_v = palloc(ph, [128, 384], F32, "ps_v")
            ps_v2 = palloc(ph, [128, 256], F32, "ps_v2")
            for v in vsts.items:
                S.op("dve", lambda e: e.memset(v.t[:], 1.0), writes=[v.b])

            def do_norm(g):
                c0, W = gcols(g)
                xg = xgs.next()
                S.dma("sp", xg.t[:, :, :W], xs_fm[:, :, c0:c0 + W], reads=[db[xsrc_n][g]], writes=[xg.b])
                norm_mod(xg, W, l, 0, 1, 0 if g < 8 else 1, hTs[g % 2], hbs[g % 2], sq, ps_ss, lnv, rstds.next(), tmps)

            def evac_copy(ps, M, W, dst_ap, dst_bufs, eng):
                if eng == "act":
                    S.op("act", lambda e: e.activation(out=dst_ap, in_=ps.t[:M, :W], func=AF.Copy), reads=[ps.b], writes=dst_bufs)
                else:
                    S.op(eng, lambda e: e.tensor_copy(out=dst_ap, in_=ps.t[:M, :W]), reads=[ps.b], writes=dst_bufs)

            def proj_fm(g, hT, hb, col, scol, M, rhs_fn, nk, wt, wb, rope_idx, dst_list, rt):
                c0, W = gcols(g)
                pa = ps_a.next()
                for kc in range(nk):
                    mm(pa.t[:M, :W], wt(kc, col, M), rhs_fn(kc, W), kc == 0, kc == nk - 1, [wb] + hb, pa.b)
                o = outs.next()
                if rope_idx is None or g == 8:
                    evac_copy(pa, M, W, o.t[:M, :W], [o.b], "act" if (col // 128) % 2 == 0 else "dve")
                else:
                    pb = ps_b.next()
                    for kc in range(nk):
                        mm(pb.t[:M, :W], wt(kc, scol, M), rhs_fn(kc, W), kc == 0, kc == nk - 1, [wb] + hb, pb.b)
                    t1, t2 = t1s.next(), t2s.next()
                    S.op("dve", lambda e: e.tensor_tensor(out=t1.t[:M, :W], in0=pa.t[:M, :W], in1=rt.t[:M, rope_idx, :W], op=ALU.mult),
                         reads=[pa.b, rt.b], writes=[t1.b])
                    S.op("dve", lambda e: e.tensor_tensor(out=t2.t[:M, :W], in0=pb.t[:M, :W], in1=rt.t[:M, rope_idx + 1, :W], op=ALU.mult),
                         reads=[pb.b, rt.b], writes=[t2.b])
                    S.op("pool", lambda e: e.tensor_tensor(out=o.t[:M, :W], in0=t1.t[:M, :W], in1=t2.t[:M, :W], op=ALU.add),
                         reads=[t1.b, t2.b], writes=[o.b])
                for (dst, dbuf, p0, p1) in dst_list:
                    S.dma("sp", dst[:, c0:c0 + W], o.t[p0:p1, :W], reads=[o.b], writes=[dbuf])

            def do_proj(g):
                c0, W = gcols(g)
                hT, hb = hTs[g % 2], hbs[g % 2]
                rt = None
                if g < 8:
                    rt = rtab.next()
                    S.dma("sp", rt.t[:, 0:2, :], ropeA[:, :, c0:c0 + W], writes=[rt.b])
                    S.dma("sp", rt.t[:96, 2:4, :], rope96[:, :, c0:c0 + W], writes=[rt.b])
                    S.dma("sp", rt.t[:32, 4:6, :], ropeR[:, :, c0:c0 + W], writes=[rt.b])
                wt_in = lambda kc, col, M: win.t[:, kc, col:col + M]
                rhs_h = lambda kc, W_: hT.t[:, kc, :W_]
                for c in range(4):
                    proj_fm(g, hT, hb, O_QA + c * 128, O_QAS + c * 128, 128, rhs_h, 8, wt_in, win.b, 0,
                            [(QA[c * 128:(c + 1) * 128, :], db["QA"][g], 0, 128)], rt)
                proj_fm(g, hT, hb, O_KA, O_KAS, 128, rhs_h, 8, wt_in, win.b, 0, [(KA[:, :], db["KA"][g], 0, 128)], rt)
                for c in range(2):
                    proj_fm(g, hT, hb, O_QB + c * 128, None, 128, rhs_h, 8, wt_in, win.b, None,
                            [(QB[c * 128:(c + 1) * 128, :], db["QB"][g], 0, 128)], rt)
                for c in range(2):
                    proj_fm(g, hT, hb, O_KB + c * 128, None, 128, rhs_h, 8, wt_in, win.b, None,
                            [(KB[c * 128:(c + 1) * 128, :], db["KB"][g], 0, 128)], rt)
                proj_fm(g, hT, hb, O_KR, O_KRS, 32, rhs_h, 8, wt_in, win.b, 4,
                        [(KC[h * 96 + 64:h * 96 + 96, :], db["KC"][g], 0, 32) for h in range(4)], rt)
                for c in range(3):
                    pa = ps_a.next()
                    col = O_CQ + c * 128
                    for kc in range(8):
                        mm(pa.t[:, :W], win.t[:, kc, col:col + 128], hT.t[:, kc, :W], kc == 0, kc == 7, [win.b] + hb, pa.b)
                    S.op("dve", lambda e: e.tensor_copy(out=cq.t[:, c, :W], in_=pa.t[:, :W]), reads=[pa.b], writes=[cq.b])
                    S.op("act", lambda e: e.activation(out=cqsq.t[:, c, :W], in_=cq.t[:, c, :W], func=AF.Square), reads=[cq.b], writes=[cqsq.b])
                for (cs, n, gi0) in (((0, 1), 256, 0), ((2,), 128, 2)):
                    for i, c in enumerate(cs):
                        mm(ps_ss.t[:, :W], ones_bf.t[:], cqsq.t[:, c, :W], i == 0, i == len(cs) - 1, [cqsq.b, ones_bf.b], ps_ss.b)
                    rs = rstds.next()
                    S.op("act", lambda e: e.activation(out=lnv.t[:, :W], in_=ps_ss.t[:, :W], func=AF.Ln, scale=1.0 / n, bias=epsc.t[:, 0:1]),
                         reads=[ps_ss.b, epsc.b], writes=[lnv.b])
                    S.op("act", lambda e: e.activation(out=rs.t[:, :W], in_=lnv.t[:, :W], func=AF.Exp, scale=-0.5),
                         reads=[lnv.b], writes=[rs.b])
                    for c in cs:
                        S.op("dve", lambda e: e.scalar_tensor_tensor(
                            out=cqn.t[:, c, :W], in0=cq.t[:, c, :W], scalar=mlag_sb.t[:, l, c:c + 1], in1=rs.t[:, :W],
                            op0=ALU.mult, op1=ALU.mult), reads=[cq.b, mlag_sb.b, rs.b], writes=[cqn.b])
                wt_uq = lambda kc, col, M: wuq.t[:, kc, col:col + M]
                rhs_cq = lambda kc, W_: cqn.t[:, kc, :W_]
                for h in range(4):
                    proj_fm(g, cqn, [cqn.b], h * 96, 384 + h * 96, 96, rhs_cq, 2, wt_uq, wuq.b, 2,
                            [(QC[h * 96:(h + 1) * 96, :], db["QC"][g], 0, 96)], rt)
                wt_kv = lambda kc, col, M: wukv.t[:, col:col + M]
                rhs_kv = lambda kc, W_: cqn.t[:, 2, :W_]
                for c in range(2):
                    proj_fm(g, cqn, [cqn.b], c * 128, None, 128, rhs_kv, 1, wt_kv, wukv.b, None,
                            [(KC[(2 * c) * 96:(2 * c) * 96 + 64, :], db["KC"][g], 0, 64),
                             (KC[(2 * c + 1) * 96:(2 * c + 1) * 96 + 64, :], db["KC"][g], 64, 128)], rt)
                for tt in range(W // 128):
                    ts = slice(tt * 128, (tt + 1) * 128)
                    for kc in range(8):
                        mm(ps_v.t[:, 0:384], hT.t[:, kc, ts], win.t[:, kc, O_VA:O_VA + 384], kc == 0, kc == 7, [win.b] + hb, ps_v.b)
                    mm(ps_v2.t[:, 0:256], cqn.t[:, 2, ts], wukv.t[:, 256:512], True, True, [wukv.b, cqn.b], ps_v2.b)
                    v = vsts.next()
                    vv = v.t[:].rearrange("p (h c) -> p h c", c=65)
                    S.op("dve", lambda e: e.tensor_copy(out=vv[:, 0:6, 0:64], in_=ps_v.t[:, 0:384].rearrange("p (h c) -> p h c", c=64)),
                         reads=[ps_v.b], writes=[v.b])
                    S.op("act", lambda e: e.activation(out=vv[:, 6:10, 0:64], in_=ps_v2.t[:, 0:256].rearrange("p (h c) -> p h c", c=64), func=AF.Copy),
                         reads=[ps_v2.b], writes=[v.b])
                    S.dma("sp", VALL[c0 + tt * 128:c0 + (tt + 1) * 128, :], v.t[:], reads=[v.b], writes=[db["VALL"][g]])

            sub = stop_after[0] if (stop_after and stop_after[1] == l) else None
            if sub != "P0":
                do_norm(0)
                for g in range(NG):
                    if sub == "P1":
                        break
                    if g + 1 < NG:
                        do_norm(g + 1)
                    do_proj(g)
                    if sub == "P2":
                        break
        S.barrier()
        if stop_after in (("P", l), ("P0", l), ("P1", l), ("P2", l)):
            break

        qgroups = list(range(8)) + ([] if last else [8])
        mixf = MIX.rearrange("(h d) t -> d h t", d=64)

        def normalise(stack_t, O, W, og_ap, og_buf, add_sink=None):
            dsum, rinv, bcs, ps_bc = stack_t
            hv = (lambda a: a.rearrange("p (h t) -> p h t", h=4)) if add_sink is not None else (lambda a: a)
            if add_sink is None:
                S.op("dve", lambda e: e.tensor_copy(out=dsum.t[64:65, :W], in_=O.t[64:65, :W]), reads=[O.b], writes=[dsum.b])
            else:
                S.op("dve", lambda e: e.tensor_tensor(out=hv(dsum.t[64:65, :W]), in0=hv(O.t[64:65, :W]), in1=add_sink, op=ALU.add),
                     reads=[O.b, esrow.b], writes=[dsum.b])
            S.op("dve", lambda e: e.reciprocal(out=rinv.t[64:65, :W], in_=dsum.t[64:65, :W]), reads=[dsum.b], writes=[rinv.b])
            mm(ps_bc.t[:64, :W], ones_f.t[64:65, 0:64], rinv.t[64:65, :W], True, True, [rinv.b, ones_f.b], ps_bc.b)
            S.op("dve", lambda e: e.tensor_copy(out=bcs.t[:64, :W], in_=ps_bc.t[:64, :W]), reads=[ps_bc.b], writes=[bcs.b])
            S.op("dve", lambda e: e.tensor_tensor(out=og_ap, in0=hv(O.t[:64, :W]), in1=hv(bcs.t[:64, :W]), op=ALU.mult),
                 reads=[O.b, bcs.b], writes=[og_buf])

        with ExitStack() as ph:
            KAs = alloc(ph, [64, 2, T_], BF16, "KAs")
            VAs = alloc(ph, [128, 34, 130], BF16, "VAs")
            esrow = alloc(ph, [65, 8, 128], F32, "esrow")
            S.dma("sp", KAs.t[:], KA.rearrange("(h d) t -> d h t", d=64), reads=db["KA"], writes=[KAs.b])
            vall_b = VALL.rearrange("(b p) c -> p b c", p=128)
            for i in range(0, 34, 9):
                S.dma("sp", VAs.t[:, i:min(i + 9, 34), :], vall_b[:, i:min(i + 9, 34), 0:130], reads=db["VALL"], writes=[VAs.b])
            S.op("dve", lambda e: e.memset(esrow.t[:], 0.0), writes=[esrow.b])
            for h in range(8):
                S.op("dve", lambda e: e.tensor_scalar(out=esrow.t[64:65, h, :], in0=esrow.t[64:65, h, :],
                                                      scalar1=esink.t[64:65, l * 8 + h:l * 8 + h + 1], scalar2=None, op0=ALU.add),
                     reads=[esink.b], writes=[esrow.b])
            Qgs = Ring([alloc(ph, [64, 8, 512], BF16, "Qg") for _ in range(2)])
            ogs = Ring([alloc(ph, [64, 8, 512], BF16, "og") for _ in range(2)])
            pts = Ring([alloc(ph, [128, 512], BF16, "pt") for _ in range(4)])
            nst = (alloc(ph, [65, 512], F32, "dsum"), alloc(ph, [65, 512], F32, "rinv"),
                   alloc(ph, [64, 512], F32, "bcs"), palloc(ph, [64, 512], F32, "ps_bc"))
            ps_s = Ring([palloc(ph, [128, 512], F32, "ps_s") for _ in range(3)])
            ps_o = Ring([palloc(ph, [65, 512], F32, "ps_o") for _ in range(2)])
            for g in qgroups:
                c0, W = gcols(g)
                Qg, og = Qgs.next(), ogs.next()
                S.dma("sp", Qg.t[:, :, :W], QA.rearrange("(h d) t -> d h t", d=64)[:, :, c0:c0 + W], reads=[db["QA"][g]], writes=[Qg.b])
                for kv in range(2):
                    for bi in range(W // 128):
                        n = g * 4 + bi
                        if g < 8:
                            kbs = [(kb, m) for kb, m in ((n - 1, 0), (n, None), (n + 1, 1)) if 0 <= kb < 32] + [(32, None), (33, None)]
                        else:
                            kbs = [(32, None), (33, None)]
                        O = ps_o.next()
                        rhs = Qg.t[:, 4 * kv:4 * kv + 4, bi * 128:(bi + 1) * 128]
                        for ki, (kb, m) in enumerate(kbs):
                            ps = ps_s.next()
                            psv = ps.t[:].rearrange("p (h t) -> p h t", h=4)
                            mm(psv, KAs.t[:, kv, kb * 128:(kb + 1) * 128], rhs, True, m is None, [KAs.b, Qg.b], ps.b)
                            if m is not None:
                                mm(psv, ident8.t[:], amask.t[:, m:m + 1, :].broadcast_to([128, 4, 128]), False, True,
                                   [ident8.b, amask.b], ps.b)
                            pt = pts.next()
                            S.op("act", lambda e: e.activation(out=pt.t[:], in_=ps.t[:], func=AF.Exp, scale=0.125),
                                 reads=[ps.b], writes=[pt.b])
                            mm(O.t[:65, :], VAs.t[:, kb, kv * 65:(kv + 1) * 65], pt.t[:], ki == 0, ki == len(kbs) - 1,
                               [VAs.b, pt.b], O.b, inc=True)
                        ogv = og.t[:, 4 * kv:4 * kv + 4, bi * 128:(bi + 1) * 128]
                        normalise(nst, O, 512, ogv, og.b,
                                  add_sink=esrow.t[64:65, 4 * kv:4 * kv + 4, :])
                S.dma("sp", mixf[:, 0:8, c0:c0 + W], og.t[:, :, :W], reads=[og.b], writes=[db["MIX"][g]])
        S.barrier()
        if stop_after == ("TA", l):
            break

        with ExitStack() as ph:
            KBs = alloc(ph, [64, 4, T_], BF16, "KBs")
            VBc = alloc(ph, [128, 2, 260], BF16, "VBc")
            bm = alloc(ph, [124, 4, 4608], BF16, "bm")
            stg_ring = Ring([alloc(ph, [124, 4608], F32, "stgb") for _ in range(2)])
            S.dma("sp", KBs.t[:], KB.rearrange("(h d) t -> d h t", d=64), reads=db["KB"], writes=[KBs.b])
            vall_b = VALL.rearrange("(b p) c -> p b c", p=128)
            S.dma("sp", VBc.t[:], vall_b[:, 32:34, 130:390], reads=db["VALL"], writes=[VBc.b])
            load_cast(stg_ring, lambda i: bm.t[:, i, :], lambda i: rpb_tab[l, :, i, :], 4, bm.b)
            Qgs = Ring([alloc(ph, [64, 4, 512], BF16, "Qg") for _ in range(2)])
            ogs = Ring([alloc(ph, [64, 4, 512], BF16, "og") for _ in range(2)])
            pts = Ring([alloc(ph, [128, 512], BF16, "pt") for _ in range(4)])
            vgs = Ring([alloc(ph, [124, 260], BF16, "vg") for _ in range(8)])
            kgts = Ring([alloc(ph, [64, 4, 124], BF16, "kgt") for _ in range(4)])
            nst = (alloc(ph, [65, 512], F32, "dsum"), alloc(ph, [65, 512], F32, "rinv"),
                   alloc(ph, [64, 512], F32, "bcs"), palloc(ph, [64, 512], F32, "ps_bc"))
            ps_s = Ring([palloc(ph, [128, 512], F32, "ps_s") for _ in range(3)])
            ps_o = [palloc(ph, [65, 512], F32, "ps_o") for _ in range(4)]
            for g in qgroups:
                c0, W = gcols(g)
                Qg, og = Qgs.next(), ogs.next()
                S.dma("sp", Qg.t[:, :, :W], QB.rearrange("(h d) t -> d h t", d=64)[:, :, c0:c0 + W], reads=[db["QB"][g]], writes=[Qg.b])
                for h in range(4):
                    O = ps_o[h]
                    for ki, cb in enumerate((32, 33)):
                        ps = ps_s.next()
                        mm(ps.t[:, :W], KBs.t[:, h, cb * 128:(cb + 1) * 128], Qg.t[:, h, :W], True, True, [KBs.b, Qg.b], ps.b)
                        pt = pts.next()
                        S.op("act", lambda e: e.activation(out=pt.t[:, :W], in_=ps.t[:, :W], func=AF.Exp, scale=0.125),
                             reads=[ps.b], writes=[pt.b])
                        mm(O.t[:65, :W], VBc.t[:, ki, h * 65:(h + 1) * 65], pt.t[:, :W], ki == 0, (g == 8 and ki == 1),
                           [VBc.b, pt.b], O.b, inc=True, skip_group_check=True)
                if g < 8:
                    i = g
                    r0 = int(NA_KR0[i])
                    for j in range(4):
                        cc0 = int(NA_KC0[j])
                        pat = ICLS[i] * 3 + JCLS[j]
                        for kg in range(4):
                            nr = KG_ROWS[kg]
                            M = nr * 31
                            vg = vgs.next()
                            tok0 = (r0 + 4 * kg) * 64 + cc0
                            src = VALL[tok0:tok0 + nr * 64, 130:390].rearrange("(r c) w -> r c w", c=64)[:, 0:31, :]
                            for rr in range(nr):
                                S.dma("sp", vg.t[rr * 31:(rr + 1) * 31, :], src[rr], reads=db["VALL"], writes=[vg.b])
                            kgt = kgts.next()
                            S.op("pool", lambda e: e.tensor_copy(
                                out=kgt.t[:, :, :M].rearrange("p h (r c) -> p h r c", c=31),
                                in_=KBs.t[:, :, tok0:tok0 + nr * 64].rearrange("p h (r c) -> p h r c", c=64)[:, :, :, 0:31]),
                                reads=[KBs.b], writes=[kgt.b])
                            for h in range(4):
                                ps = ps_s.next()
                                kview = kgt.t[:, h, :M]
                                qview = Qg.t[:, h, :].rearrange("p (r c) -> p r c", c=64)[:, :, 16 * j:16 * j + 16]
                                psv = ps.t[:M, 0:128].rearrange("p (r c) -> p r c", c=16)
                                mm(psv, kview, qview, True, False, [kgt.b, Qg.b], ps.b)
                                boff = (pat * 4 + kg) * 128
                                mm(ps.t[:M, 0:128], ident8.t[:M, :M], bm.t[:M, h, boff:boff + 128], False, True, [ident8.b, bm.b], ps.b)
                                pt = pts.next()
                                S.op("act", lambda e: e.activation(out=pt.t[:M, 0:128], in_=ps.t[:M, 0:128], func=AF.Exp, scale=0.125),
                                     reads=[ps.b], writes=[pt.b])
                                O = ps_o[h]
                                ov = O.t[:65, :].rearrange("p (r c) -> p r c", c=64)[:, :, 16 * j:16 * j + 16]
                                ptv = pt.t[:M, 0:128].rearrange("p (r c) -> p r c", c=16)
                                mm(ov, vg.t[:M, h * 65:(h + 1) * 65], ptv, False, (j == 3 and kg == 3), [vg.b, pt.b], O.b,
                                   inc=True, skip_group_check=True)
                for h in range(4):
                    normalise(nst, ps_o[h], W, og.t[:, h, :W], og.b)
                S.dma("sp", mixf[:, 8:12, c0:c0 + W], og.t[:, :, :W], reads=[og.b], writes=[db["MIX"][g]])
        S.barrier()
        if stop_after == ("TB", l):
            break

        with ExitStack() as ph:
            KCs = alloc(ph, [96, 4, T_], BF16, "KCs")
            VCs = alloc(ph, [128, 34, 260], BF16, "VCs")
            S.dma("sp", KCs.t[:], KC.rearrange("(h d) t -> d h t", d=96), reads=db["KC"], writes=[KCs.b])
            vall_b = VALL.rearrange("(b p) c -> p b c", p=128)
            for i in range(0, 34, 9):
                S.dma("sp", VCs.t[:, i:min(i + 9, 34), :], vall_b[:, i:min(i + 9, 34), 390:650], reads=db["VALL"], writes=[VCs.b])
            Qgs = Ring([alloc(ph, [96, 4, 512], BF16, "Qg") for _ in range(2)])
            ogs = Ring([alloc(ph, [64, 4, 512], BF16, "og") for _ in range(2)])
            pts = Ring([alloc(ph, [128, 512], BF16, "pt") for _ in range(4)])
            nst = (alloc(ph, [65, 512], F32, "dsum"), alloc(ph, [65, 512], F32, "rinv"),
                   alloc(ph, [64, 512], F32, "bcs"), palloc(ph, [64, 512], F32, "ps_bc"))
            ps_s = Ring([palloc(ph, [128, 512], F32, "ps_s") for _ in range(4)])
            ps_o = Ring([palloc(ph, [65, 512], F32, "ps_o") for _ in range(2)])
            sc = float(96 ** -0.5)
            for g in qgroups:
                c0, W = gcols(g)
                Qg, og = Qgs.next(), ogs.next()
                S.dma("sp", Qg.t[:, :, :W], QC.rearrange("(h d) t -> d h t", d=96)[:, :, c0:c0 + W], reads=[db["QC"][g]], writes=[Qg.b])
                kbs = list(range(34)) if g < 8 else [32, 33]
                for h in range(4):
                    O = ps_o.next()
                    for ki, kb in enumerate(kbs):
                        ps = ps_s.next()
                        mm(ps.t[:, :W], KCs.t[:, h, kb * 128:(kb + 1) * 128], Qg.t[:, h, :W], True, True, [KCs.b, Qg.b], ps.b)
                        pt = pts.next()
                        S.op("act", lambda e: e.activation(out=pt.t[:, :W], in_=ps.t[:, :W], func=AF.Exp, scale=sc),
                             reads=[ps.b], writes=[pt.b])
                        mm(O.t[:65, :W], VCs.t[:, kb, h * 65:(h + 1) * 65], pt.t[:, :W], ki == 0, ki == len(kbs) - 1, [VCs.b, pt.b], O.b, inc=True)
                    normalise(nst, O, W, og.t[:, h, :W], og.b)
                S.dma("sp", mixf[:, 12:16, c0:c0 + W], og.t[:, :, :W], reads=[og.b], writes=[db["MIX"][g]])
        S.barrier()
        if stop_after == ("TC", l):
            break

        ogroups = list(range(8)) + ([] if last else [8])
        with ExitStack() as ph:
            stg_ring = Ring([alloc(ph, [128, 1024], F32, "stg") for _ in range(2)])
            wo = alloc(ph, [128, 8, D], BF16, "wo")
            wod = w_out[l].rearrange("(kc p) n -> p kc n", p=128)
            load_cast(stg_ring, lambda i: wo.t[:, i, :], lambda i: wod[:, i, :], 8, wo.b)
            mgs = Ring([alloc(ph, [128, 8, 512], BF16, "mixg") for _ in range(2)])
            xgs = Ring([alloc(ph, [128, 8, 512], F32, "xg") for _ in range(2)])
            ps_y = Ring([palloc(ph, [128, 512], F32, "ps_y") for _ in range(4)])
            mix_fm = MIX.rearrange("(kc p) t -> p kc t", p=128)
            xm_fm = xmid.rearrange("(kc p) t -> p kc t", p=128)
            for g in ogroups:
                c0, W = gcols(g)
                j = 0 if g < 8 else 1
                mg, xg = mgs.next(), xgs.next()
                S.dma("sp", mg.t[:, :, :W], mix_fm[:, :, c0:c0 + W], reads=[db["MIX"][g]], writes=[mg.b])
                S.dma("sp", xg.t[:, :, :W], xs_fm[:, :, c0:c0 + W], reads=[db[xsrc_n][g]], writes=[xg.b])
                for c in range(8):
                    py = ps_y.next()
                    for kc in range(8):
                        mm(py.t[:, :W], wo.t[:, kc, c * 128:(c + 1) * 128], mg.t[:, kc, :W], kc == 0, kc == 7, [wo.b, mg.b], py.b)
                    S.op("dve", lambda e: e.scalar_tensor_tensor(
                        out=xg.t[:, c, :W], in0=py.t[:, :W], scalar=modv.t[:, l, 2, c, j:j + 1], in1=xg.t[:, c, :W],
                        op0=ALU.mult, op1=ALU.add), reads=[py.b, modv.b, xg.b], writes=[xg.b])
                S.dma("sp", xm_fm[:, :, c0:c0 + W], xg.t[:, :, :W], reads=[xg.b], writes=[db["xmid"][g]])
        S.barrier()
        if stop_after == ("O1", l):
            break

        with ExitStack() as ph:
            stg_ring = Ring([alloc(ph, [128, 2048], F32, "stg") for _ in range(2)])
            w1 = alloc(ph, [128, 8, 4 * D], BF16, "w1")
            w2 = alloc(ph, [128, 32, D], BF16, "w2")
            w1d = w_mlp_in[l].rearrange("(kc p) n -> p kc n", p=128)
            w2d = w_mlp_out[l].rearrange("(f p) n -> p f n", p=128)
            load_cast(stg_ring, lambda i: w1.t[:, i // 2, (i % 2) * 2048:(i % 2 + 1) * 2048],
                      lambda i: w1d[:, i // 2, (i % 2) * 2048:(i % 2 + 1) * 2048], 16, w1.b)
            load_cast(stg_ring, lambda i: w2.t[:, 2 * i:2 * i + 2, :], lambda i: w2d[:, 2 * i:2 * i + 2, :], 16, w2.b)
            WG = 256
            xgs = Ring([alloc(ph, [128, 8, WG], F32, "xg") for _ in range(2)])
            hT2 = alloc(ph, [128, 8, WG], BF16, "hT2")
            hb2 = [Buf(f"h2_{k}") for k in range(8)]
            sq = alloc(ph, [128, 8, WG], BF16, "sq")
            lnv = alloc(ph, [128, WG], F32, "lnv")
            rstd = alloc(ph, [128, WG], F32, "rstd")
            tmps = Ring([alloc(ph, [128, WG], F32, "tmp") for _ in range(3)])
            rl = Ring([alloc(ph, [128, WG], F32, "rl") for _ in range(3)])
            aT = alloc(ph, [128, 32, WG], BF16, "aT")
            abufs = [Buf(f"a{f}") for f in range(32)]
            ps_ss = palloc(ph, [128, WG], F32, "ps_ss")
            ps_u = Ring([palloc(ph, [128, WG], F32, "ps_u") for _ in range(3)])
            ps_y = Ring([palloc(ph, [128, WG], F32, "ps_y") for _ in range(2)])
            xm_fm = xmid.rearrange("(kc p) t -> p kc t", p=128)
            xr_fm = xres.rearrange("(kc p) t -> p kc t", p=128)
            out_fm = outT.rearrange("(kc p) t -> p kc t", p=128)
            ngr = 16 + (0 if last else 1)
            for gg in range(ngr):
                c0 = gg * WG
                g = gg // 2 if gg < 16 else 8
                j = 0 if gg < 16 else 1
                xg = xgs.next()
                S.dma("sp", xg.t[:], xm_fm[:, :, c0:c0 + WG], reads=[db["xmid"][g]], writes=[xg.b])
                norm_mod(xg, WG, l, 3, 4, j, hT2, hb2, sq, ps_ss, lnv, rstd, tmps)
                for f in range(32):
                    pu = ps_u.next()
                    for kc in range(8):
                        mm(pu.t[:, :WG], w1.t[:, kc, f * 128:(f + 1) * 128], hT2.t[:, kc, :], kc == 0, kc == 7, [w1.b] + hb2, pu.b)
                    r = rl.next()
                    S.op("act", lambda e: e.activation(out=r.t[:], in_=pu.t[:, :WG], func=AF.Relu), reads=[pu.b], writes=[r.b])
                    S.op("pool" if f % 2 else "dve", lambda e: e.tensor_tensor(out=aT.t[:, f, :], in0=r.t[:], in1=r.t[:], op=ALU.mult),
                         reads=[r.b], writes=[abufs[f]])
                for c in range(8):
                    py = ps_y.next()
                    for f in range(32):
                        mm(py.t[:, :WG], w2.t[:, f, c * 128:(c + 1) * 128], aT.t[:, f, :], f == 0, f == 31, [w2.b, abufs[f]], py.b)
                    S.op("dve", lambda e: e.scalar_tensor_tensor(
                        out=xg.t[:, c, :], in0=py.t[:, :WG], scalar=modv.t[:, l, 5, c, j:j + 1], in1=xg.t[:, c, :],
                        op0=ALU.mult, op1=ALU.add), reads=[py.b, modv.b, xg.b], writes=[xg.b])
                if not last:
                    S.dma("sp", xr_fm[:, :, c0:c0 + WG], xg.t[:], reads=[xg.b], writes=[db["xres"][g]])
                else:
                    S.op("dve", lambda e: e.tensor_tensor(out=sq.t[:], in0=xg.t[:], in1=xg.t[:], op=ALU.mult), reads=[xg.b], writes=[sq.b])
                    for kc in range(8):
                        mm(ps_ss.t[:, :WG], ones_bf.t[:], sq.t[:, kc, :], kc == 0, kc == 7, [sq.b, ones_bf.b], ps_ss.b)
                    S.op("act", lambda e: e.activation(out=lnv.t[:], in_=ps_ss.t[:, :WG], func=AF.Ln, scale=1.0 / D, bias=epsc.t[:, 0:1]),
                         reads=[ps_ss.b, epsc.b], writes=[lnv.b])
                    S.op("act", lambda e: e.activation(out=rstd.t[:], in_=lnv.t[:], func=AF.Exp, scale=-0.5), reads=[lnv.b], writes=[rstd.b])
                    for kc in range(8):
                        S.op("dve", lambda e: e.scalar_tensor_tensor(
                            out=xg.t[:, kc, :], in0=xg.t[:, kc, :], scalar=gain_sb.t[:, 8, kc:kc + 1], in1=rstd.t[:],
                            op0=ALU.mult, op1=ALU.mult), reads=[xg.b, gain_sb.b, rstd.b], writes=[xg.b])
                    S.dma("sp", out_fm[:, :, c0:c0 + WG], xg.t[:], reads=[xg.b], writes=[])
        S.barrier()
        if stop_after == ("O2", l):
            break

    S.finish("sp")
    glob.close()
    return nc, S


def kernel(**inputs):
    per_core = prep_inputs(inputs)
    nc, _ = build()
    res = run_bass_kernel_spmd(nc, per_core, core_ids=list(range(8)))
    out = np.stack([np.ascontiguousarray(r["outT"].T) for r in res.results], axis=0)
    return out.astype(np.float32)
```

```python
import math
from contextlib import ExitStack
import numpy as np
import concourse.bass as bass
import concourse.mybir as mybir
from concourse.bass_utils import run_bass_kernel_spmd

F32 = mybir.dt.float32
BF16 = mybir.dt.bfloat16
AF = mybir.ActivationFunctionType
ALU = mybir.AluOpType

D = 1024
S_ = 4096
C_ = 256
T_ = S_ + C_
L_ = 4
NCOL = 2624
EPS = 1e-6
NEG = -30000.0
O_QA, O_QAS, O_KA, O_KAS, O_QB, O_KB, O_CQ, O_CKV, O_KR, O_KRS, O_VA = (
    0, 512, 1024, 1152, 1280, 1536, 1792, 2048, 2176, 2208, 2240)
VW = 650


class Buf:
    __slots__ = ("name", "lw", "rs", "psum")

    def __init__(self, name="", psum=False):
        self.name = name
        self.lw = None
        self.rs = {}
        self.psum = psum


class Sched:
    def __init__(self, nc, n_dma_slots=32, same_engine_sync=True):
        self.nc = nc
        self.same = same_engine_sync
        self.eng = {"pe": nc.tensor, "act": nc.scalar, "dve": nc.vector, "pool": nc.gpsimd, "sp": nc.sync}
        self.sem = {k: nc.alloc_semaphore(name=f"sem_{k}") for k in self.eng}
        self.cnt = {k: 0 for k in self.eng}
        self.seen = {k: {} for k in self.eng}
        self.dsem = [nc.alloc_semaphore(name=f"dsem{i}") for i in range(n_dma_slots)]
        self.dcnt = [0] * n_dma_slots
        self.dnext = 0
        self.n_wait = 0
        self.n_inst = 0
        self.log = {k: [] for k in self.eng}

    def _semof(self, key):
        return self.dsem[key] if isinstance(key, int) else self.sem[key]

    def _wait(self, e, key, val):
        if key == e and not self.same:
            return
        if self.seen[e].get(key, 0) >= val:
            return
        self.eng[e].wait_ge(self._semof(key), val)
        self.log[e].append(("w", key, val))
        self.seen[e][key] = val
        self.n_wait += 1

    def _deps(self, e, reads, writes):
        for b in reads:
            if b.lw is not None:
                self._wait(e, *b.lw)
            if b.psum:
                for k, v in b.rs.items():
                    if k != e:
                        self._wait(e, k, v)
        for b in writes:
            if b.lw is not None:
                self._wait(e, *b.lw)
            for k, v in b.rs.items():
                self._wait(e, k, v)

    def op(self, e, fn, reads=(), writes=(), pe_acc=False, inc=True):
        if pe_acc:
            self._deps(e, reads, ())
        else:
            self._deps(e, reads, writes)
        ins = fn(self.eng[e])
        if inc:
            ins.then_inc(self.sem[e], 1)
            self.log[e].append(("i", e, 1))
            self.cnt[e] += 1
            n = self.cnt[e]
        else:
            n = self.cnt[e] + 1
        self.n_inst += 1
        for b in reads:
            b.rs[e] = n
        for b in writes:
            b.lw = (e, n)
            b.rs = {}
        return ins

    def dma(self, q, out, in_, reads=(), writes=(), **kw):
        self._deps(q, reads, writes)
        s = self.dnext
        self.dnext = (self.dnext + 1) % len(self.dsem)
        if self.dcnt[s] > 0:
            self._wait(q, s, 16 * self.dcnt[s])
        ins = self.eng[q].dma_start(out=out, in_=in_, **kw)
        ins.then_inc(self.dsem[s], 16)
        self.log[q].append(("i", s, 16))
        self.dcnt[s] += 1
        v = 16 * self.dcnt[s]
        self.n_inst += 1
        for b in reads:
            b.rs[s] = v
        for b in writes:
            b.lw = (s, v)
            b.rs = {}
        return ins

    def barrier(self):
        for e in self.eng:
            self.finish(e)

    def finish(self, e="sp"):
        for k in self.eng:
            if k != e and self.cnt[k] > 0:
                self._wait(e, k, self.cnt[k])
        for s in range(len(self.dsem)):
            if self.dcnt[s] > 0:
                self._wait(e, s, 16 * self.dcnt[s])


class Pipe:
    def __init__(self, la, d):
        self.la, self.d = la, d
        self.q = []
        self.later = []
        self.step = 0

    def _fire(self, upto=None):
        while self.later and (upto is None or self.later[0][0] <= upto):
            self.later.pop(0)[1]()

    def _s2(self):
        self.q.pop(0)()
        self.step += 1
        self._fire(self.step)

    def push(self, s1, s2):
        s1()
        self.q.append(s2)
        if len(self.q) > self.la:
            self._s2()

    def defer(self, fn, tag=None):
        self.later.append((self.step + self.d, fn, tag))

    def force(self, tag):
        idx = [i for i, it in enumerate(self.later) if it[2] is tag]
        if idx:
            for _ in range(idx[-1] + 1):
                self.later.pop(0)[1]()

    def flush(self):
        while self.q:
            self._s2()
        self._fire(None)


class TB:
    def __init__(self, t, name=""):
        self.t = t
        self.b = Buf(name)


class Ring:
    def __init__(self, items):
        self.items = items
        self.i = 0

    def next(self):
        it = self.items[self.i]
        self.i = (self.i + 1) % len(self.items)
        return it


def _perm_a(n):
    idx = np.arange(n)
    h, d = idx // 64, idx % 64
    j = d % 32
    partner = np.where(j < 16, d + 16, d - 16)
    return h * 64 + partner


def _perm_r(n=32):
    d = np.arange(n)
    jj = d % 16
    return np.where(jj < 8, d + 8, d - 8)


def _rope_tables():
    tok = np.arange(S_)
    row, col = tok // 64, tok % 64

    def tab(ndim_sec, half):
        freqs = (10000.0 ** (-np.arange(half, dtype=np.float32) / half)).astype(np.float32)
        cos = np.zeros((2 * ndim_sec, S_), np.float32)
        sin = np.zeros((2 * ndim_sec, S_), np.float32)
        for d in range(2 * ndim_sec):
            sec, j = d // ndim_sec, d % ndim_sec
            pos = (row if sec == 0 else col).astype(np.float32)
            ang = pos * freqs[j % half]
            cos[d] = np.cos(ang).astype(np.float32)
            sn = np.sin(ang).astype(np.float32)
            sin[d] = -sn if j < half else sn
        return cos, sin

    ca, sa = tab(32, 16)
    cr, sr = tab(16, 8)
    ropeA = np.stack([np.concatenate([ca, ca], 0), np.concatenate([sa, sa], 0)], 1)
    c96 = np.concatenate([np.ones((64, S_), np.float32), cr], 0)
    s96 = np.concatenate([np.zeros((64, S_), np.float32), sr], 0)
    rope96 = np.stack([c96, s96], 1)
    ropeR = np.stack([cr, sr], 1)
    return np.ascontiguousarray(ropeA), np.ascontiguousarray(rope96), np.ascontiguousarray(ropeR)


def _na_layout():
    rows, GW, NR, NCc, NQ = 64, 64, 8, 16, 16
    kr, kc = min(NR, rows), NCc
    qr, qc = math.gcd(rows, NR), NQ
    krb, kcb = min(qr - 1 + kr, rows), min(qc - 1 + kc, GW)
    nrb, ncb = rows // qr, GW // qc
    q_r = np.arange(nrb)[:, None] * qr + np.arange(qr)[None, :]
    q_c = np.arange(ncb)[:, None] * qc + np.arange(qc)[None, :]
    w_r = np.clip(q_r - kr // 2, 0, rows - kr)
    w_c = np.clip(q_c - kc // 2, 0, GW - kc)
    k_r = np.minimum(w_r[:, 0], rows - krb)[:, None] + np.arange(krb)[None, :]
    k_c = np.minimum(w_c[:, 0], GW - kcb)[:, None] + np.arange(kcb)[None, :]
    qr6 = q_r[:, None, :, None, None, None]
    qc6 = q_c[None, :, None, :, None, None]
    wr6 = w_r[:, None, :, None, None, None]
    wc6 = w_c[None, :, None, :, None, None]
    kr6 = k_r[:, None, None, None, :, None]
    kc6 = k_c[None, :, None, None, None, :]
    shape6 = (nrb, ncb, qr, qc, krb, kcb)

    def flat(a):
        return np.broadcast_to(a, shape6).reshape(nrb, ncb, qr * qc, krb * kcb)

    mask = flat((kr6 >= wr6) & (kr6 < wr6 + kr) & (kc6 >= wc6) & (kc6 < wc6 + kc))
    d_r = flat(np.clip(kr6 - qr6, 1 - NR, NR - 1) + NR - 1)
    d_c = flat(np.clip(kc6 - qc6, 1 - NCc, NCc - 1) + NCc - 1)
    return mask, d_r, d_c, k_r[:, 0], k_c[:, 0]


NA_MASK, NA_DR, NA_DC, NA_KR0, NA_KC0 = _na_layout()
ICLS = [0, 1, 1, 1, 1, 1, 1, 2]
JCLS = [0, 1, 1, 2]
IREP = [0, 1, 7]
JREP = [0, 1, 3]
KG_ROWS = [4, 4, 4, 3]


def _rpb_table(na_rpb):
    tab = np.full((L_, 124, 4, 9, 4, 128), NEG, np.float32)
    for ic in range(3):
        for jc in range(3):
            i, j = IREP[ic], JREP[jc]
            m = NA_MASK[i, j]
            dr, dc = NA_DR[i, j], NA_DC[i, j]
            g = na_rpb[:, :, dr, dc]
            g = np.where(m[None, None], g, np.float32(NEG))
            for kg in range(4):
                n = KG_ROWS[kg] * 31
                blk = g[:, :, :, kg * 124: kg * 124 + n]
                tab[:, :n, :, ic * 3 + jc, kg, :] = blk.transpose(0, 3, 1, 2)
    return np.ascontiguousarray(tab.reshape(L_, 124, 4, 9 * 4 * 128))


def _fm(v):
    v = np.asarray(v, np.float32)
    lead = v.shape[:-1]
    c = v.shape[-1] // 128
    v = v.reshape(*lead, c, 128)
    return np.ascontiguousarray(np.moveaxis(v, -1, 0))


def prep_inputs(inp):
    f = lambda a: np.ascontiguousarray(np.asarray(a, np.float32))
    w_in = f(inp["w_in"])
    pa512, pa128, pr = _perm_a(512), _perm_a(128), _perm_r()
    ext = np.concatenate([
        w_in[:, :, 0:512], w_in[:, :, 0:512][:, :, pa512],
        w_in[:, :, 512:640], w_in[:, :, 512:640][:, :, pa128],
        w_in[:, :, 768:1024], w_in[:, :, 1024:1280],
        w_in[:, :, 1536:1792], w_in[:, :, 1792:1920],
        w_in[:, :, 1920:1952], w_in[:, :, 1920:1952][:, :, pr],
        w_in[:, :, 640:768], w_in[:, :, 1280:1536]], axis=2)
    assert ext.shape[2] == NCOL
    w_uq = f(inp["mla_w_uq"])
    sw = np.concatenate([np.concatenate([h * 96 + np.arange(64), h * 96 + 64 + pr]) for h in range(4)])
    w_uq_ext = np.concatenate([w_uq, w_uq[:, :, sw]], axis=2)
    w_ukv = f(inp["mla_w_ukv"]).reshape(L_, 128, 4, 2, 64)
    w_ukv_r = np.ascontiguousarray(w_ukv.transpose(0, 1, 3, 2, 4).reshape(L_, 128, 512))
    ropeA, rope96, ropeR = _rope_tables()
    kq = np.arange(128)
    amask = np.zeros((128, 2, 128), np.float32)
    amask[:, 0, :] = np.where(kq[:, None] >= kq[None, :], 0.0, NEG)
    amask[:, 1, :] = np.where(kq[:, None] <= kq[None, :], 0.0, NEG)
    gains = np.concatenate([_fm(inp["norm1_g"]), _fm(inp["norm2_g"]), _fm(inp["final_norm_g"])[:, None, :]], 1)
    mla_g = np.concatenate([_fm(inp["mla_q_norm_g"]), _fm(inp["mla_kv_norm_g"])], 2)
    shared = {
        "w_ada": f(inp["w_ada"]),
        "b_ada": _fm(inp["b_ada"]),
        "gains": np.ascontiguousarray(gains),
        "w_in_ext": np.ascontiguousarray(ext),
        "w_uq_ext": np.ascontiguousarray(w_uq_ext),
        "w_ukv_r": w_ukv_r,
        "mla_g": np.ascontiguousarray(mla_g),
        "sinkrow": np.ascontiguousarray(np.broadcast_to(f(inp["attn_sink"]).reshape(1, 32), (65, 32))),
        "rpb_tab": _rpb_table(f(inp["na_rpb"])),
        "w_out": f(inp["w_out"]),
        "w_mlp_in": f(inp["w_mlp_in"]),
        "w_mlp_out": f(inp["w_mlp_out"]),
        "ropeA": ropeA, "rope96": rope96, "ropeR": ropeR,
        "amask": amask,
        "ident8": np.ascontiguousarray(8.0 * np.eye(128, dtype=np.float32)),
    }
    x, ctx, c, c_ctx = f(inp["x"]), f(inp["ctx"]), f(inp["c"]), f(inp["c_ctx"])
    per_core = []
    for b in range(x.shape[0]):
        xt = np.ascontiguousarray(np.concatenate([x[b].T, ctx[b].T], axis=1))
        cv = np.ascontiguousarray(np.stack([_fm(c[b]), _fm(c_ctx)], -1))
        d = dict(shared)
        d["xT"] = xt
        d["cvec"] = cv
        per_core.append(d)
    return per_core


def build(n_layers=L_, debug=False, stop_after=None):
    nc = bass.Bass("TRN2", target_bir_lowering=False)
    S = Sched(nc)
    uid = [0]

    def din(name, shape):
        return nc.dram_tensor(name, list(shape), F32, kind="ExternalInput").ap()

    def dscr(name, shape, dt):
        kind = "ExternalOutput" if debug else "Internal"
        return nc.dram_tensor(name, list(shape), dt, kind=kind).ap()

    xT = din("xT", [D, T_])
    cvec = din("cvec", [128, 8, 2])
    w_ada = din("w_ada", [L_, D, 6 * D])
    b_ada = din("b_ada", [128, L_, 48])
    gains = din("gains", [128, 9, 8])
    w_in_ext = din("w_in_ext", [L_, D, NCOL])
    w_uq_ext = din("w_uq_ext", [L_, 256, 768])
    w_ukv_r = din("w_ukv_r", [L_, 128, 512])
    mla_g = din("mla_g", [128, L_, 3])
    sinkrow = din("sinkrow", [65, 32])
    rpb_tab = din("rpb_tab", [L_, 124, 4, 4608])
    w_out = din("w_out", [L_, D, D])
    w_mlp_in = din("w_mlp_in", [L_, D, 4 * D])
    w_mlp_out = din("w_mlp_out", [L_, 4 * D, D])
    ropeA = din("ropeA", [128, 2, S_])
    rope96 = din("rope96", [96, 2, S_])
    ropeR = din("ropeR", [32, 2, S_])
    amask_d = din("amask", [128, 2, 128])
    ident8_d = din("ident8", [128, 128])
    outT = nc.dram_tensor("outT", [D, S_], F32, kind="ExternalOutput").ap()

    xmid = dscr("xmid", [D, T_], F32)
    xres = dscr("xres", [D, T_], F32)
    QA = dscr("QA", [512, T_], BF16)
    KA = dscr("KA", [128, T_], BF16)
    QB = dscr("QB", [256, T_], BF16)
    KB = dscr("KB", [256, T_], BF16)
    QC = dscr("QC", [384, T_], BF16)
    KC = dscr("KC", [384, T_], BF16)
    VALL = dscr("VALL", [T_, VW], BF16)
    MIX = dscr("MIX", [D, T_], BF16)
    NG = 9
    db = {n: [Buf(f"{n}{g}") for g in range(NG)] for n in
          ["xT", "xmid", "xres", "QA", "KA", "QB", "KB", "QC", "KC", "VALL", "MIX"]}

    def gcols(g):
        return (g * 512, 512) if g < 8 else (S_, C_)

    glob = ExitStack()

    def alloc(stack, shape, dt, name=None):
        uid[0] += 1
        t = stack.enter_context(nc.sbuf_tensor(f"{name or 't'}_{uid[0]}", list(shape), dt))
        return TB(t, name or "t")

    def palloc(stack, shape, dt, name=None):
        uid[0] += 1
        t = stack.enter_context(nc.psum_tensor(f"{name or 'p'}_{uid[0]}", [128, 512], F32))
        tb = TB(t, name or "p")
        tb.b.psum = True
        return tb

    def mm(out_ap, lhsT, rhs, first, last, reads, wbuf, inc=None, **kw):
        S.op("pe", lambda e: e.matmul(out_ap, lhsT=lhsT, rhs=rhs, start=first, stop=last, **kw),
             reads=reads, writes=[wbuf], pe_acc=not first, inc=(last if inc is None else inc))

    ones_bf = alloc(glob, [128, 128], BF16, "ones_bf")
    ones_f = alloc(glob, [128, 64], F32, "ones_f")
    ident8 = alloc(glob, [128, 128], BF16, "ident8")
    amask = alloc(glob, [128, 2, 128], BF16, "amask")
    modv = alloc(glob, [128, L_, 6, 8, 2], F32, "modv")
    gain_sb = alloc(glob, [128, 9, 8], F32, "gains")
    mlag_sb = alloc(glob, [128, L_, 3], F32, "mlag")
    esink = alloc(glob, [65, 32], F32, "esink")
    epsc = alloc(glob, [128, 1], F32, "epsc")

    S.op("dve", lambda e: e.memset(ones_bf.t[:], 1.0), writes=[ones_bf.b])
    S.op("dve", lambda e: e.memset(ones_f.t[:], 1.0), writes=[ones_f.b])
    S.op("dve", lambda e: e.memset(epsc.t[:], EPS), writes=[epsc.b])
    S.dma("sp", gain_sb.t[:], gains, writes=[gain_sb.b])
    S.dma("sp", mlag_sb.t[:], mla_g, writes=[mlag_sb.b])
    S.dma("sp", esink.t[:], sinkrow, writes=[esink.b])
    S.op("act", lambda e: e.activation(out=esink.t[:], in_=esink.t[:], func=AF.Exp), reads=[esink.b], writes=[esink.b])

    with ExitStack() as ph:
        stg = alloc(ph, [128, 2, 128], F32, "stg")
        S.dma("sp", stg.t[:, 0, :], ident8_d, writes=[stg.b])
        S.op("dve", lambda e: e.tensor_copy(out=ident8.t[:], in_=stg.t[:, 0, :]), reads=[stg.b], writes=[ident8.b])
        S.dma("sp", stg.t[:], amask_d, reads=[], writes=[stg.b])
        S.op("dve", lambda e: e.tensor_copy(out=amask.t[:], in_=stg.t[:]), reads=[stg.b], writes=[amask.b])
        cv = alloc(ph, [128, 8, 2], F32, "cv")
        sv = alloc(ph, [128, 8, 2], F32, "sv")
        bfm = alloc(ph, [128, L_, 48], F32, "bfm")
        S.dma("sp", cv.t[:], cvec, writes=[cv.b])
        S.dma("sp", bfm.t[:], b_ada, writes=[bfm.b])
        S.op("act", lambda e: e.activation(out=sv.t[:], in_=cv.t[:], func=AF.Silu), reads=[cv.b], writes=[sv.b])
        wst = Ring([alloc(ph, [128, 8, 768], F32, "wst") for _ in range(2)])
        mps = Ring([palloc(ph, [128, 48, 2], F32, "mps") for _ in range(2)])
        mods = alloc(ph, [128, 48, 2], F32, "mods")
        for l in range(n_layers):
            pm = mps.next()
            wa = w_ada[l].rearrange("(kc p) n -> p kc n", p=128)
            for pc in range(8):
                w = wst.next()
                S.dma("sp", w.t[:], wa[:, :, pc * 768:(pc + 1) * 768], writes=[w.b])
                for cc in range(6):
                    ch = pc * 6 + cc
                    for kc in range(8):
                        mm(pm.t[:, 2 * ch:2 * ch + 2], w.t[:, kc, cc * 128:(cc + 1) * 128], sv.t[:, kc, :],
                           kc == 0, kc == 7, [w.b, sv.b], pm.b)
            S.op("dve", lambda e: e.tensor_tensor(
                out=mods.t[:], in0=pm.t[:, 0:96].rearrange("p (c j) -> p c j", j=2), in1=bfm.t[:, l, :].unsqueeze(2).broadcast_to([128, 48, 2]),
                op=ALU.add), reads=[pm.b, bfm.b], writes=[mods.b])
            for (k, src_m, gidx) in ((0, 1, l), (3, 4, 4 + l)):
                S.op("dve", lambda e: e.scalar_tensor_tensor(
                    out=modv.t[:, l, k, :, :], in0=mods.t[:, src_m * 8:(src_m + 1) * 8, :], scalar=1.0,
                    in1=gain_sb.t[:, gidx, :].unsqueeze(2).broadcast_to([128, 8, 2]),
                    op0=ALU.add, op1=ALU.mult), reads=[mods.b, gain_sb.b], writes=[modv.b])
            for (k, src_m) in ((1, 0), (2, 2), (4, 3), (5, 5)):
                S.op("dve", lambda e: e.tensor_copy(out=modv.t[:, l, k, :, :], in_=mods.t[:, src_m * 8:(src_m + 1) * 8, :]),
                     reads=[mods.b], writes=[modv.b])
    S.barrier()
    if stop_after == ("M", 0):
        if debug:
            dbg = nc.dram_tensor("dbg_modv", [128, L_ * 96], F32, kind="ExternalOutput").ap()
            S.dma("sp", dbg, modv.t[:].rearrange("p l k c j -> p (l k c j)"), reads=[modv.b])
        S.finish("sp")
        glob.close()
        return nc, S

    def norm_mod(xg, W, l, kG, kS, j, hT, hbufs, sq, ps_ss, lnv, rstd, tmps):
        S.op("dve", lambda e: e.tensor_tensor(out=sq.t[:, :, :W], in0=xg.t[:, :, :W], in1=xg.t[:, :, :W], op=ALU.mult),
             reads=[xg.b], writes=[sq.b])
        for kc in range(8):
            mm(ps_ss.t[:, :W], ones_bf.t[:], sq.t[:, kc, :W], kc == 0, kc == 7, [sq.b, ones_bf.b], ps_ss.b)
        S.op("act", lambda e: e.activation(out=lnv.t[:, :W], in_=ps_ss.t[:, :W], func=AF.Ln, scale=1.0 / D, bias=epsc.t[:, 0:1]),
             reads=[ps_ss.b, epsc.b], writes=[lnv.b])
        S.op("act", lambda e: e.activation(out=rstd.t[:, :W], in_=lnv.t[:, :W], func=AF.Exp, scale=-0.5),
             reads=[lnv.b], writes=[rstd.b])
        for kc in range(8):
            tm = tmps.next()
            S.op("dve", lambda e: e.scalar_tensor_tensor(
                out=tm.t[:, :W], in0=xg.t[:, kc, :W], scalar=modv.t[:, l, kG, kc, j:j + 1], in1=rstd.t[:, :W],
                op0=ALU.mult, op1=ALU.mult), reads=[xg.b, modv.b, rstd.b], writes=[tm.b])
            S.op("act", lambda e: e.activation(out=hT.t[:, kc, :W], in_=tm.t[:, :W], func=AF.Identity,
                                               bias=modv.t[:, l, kS, kc, j:j + 1], scale=1.0),
                 reads=[tm.b, modv.b], writes=[hbufs[kc]])

    def load_cast(stack_ring, dst_ap_fn, src_ap_fn, n_pieces, dst_buf, engs=("dve", "pool")):
        for i in range(n_pieces):
            st = stack_ring.next()
            src = src_ap_fn(i)
            dst = dst_ap_fn(i)
            shp = list(src.shape)
            view = st.t[:shp[0], :int(np.prod(shp[1:]))]
            if len(shp) == 3:
                view = view.rearrange("p (a b) -> p a b", a=shp[1])
            S.dma("sp", view, src, writes=[st.b])
            eng = engs[i % len(engs)]
            if eng == "act":
                S.op("act", lambda e: e.activation(out=dst, in_=view, func=AF.Copy), reads=[st.b], writes=[dst_buf], pe_acc=False)
            else:
                S.op(eng, lambda e: e.tensor_copy(out=dst, in_=view), reads=[st.b], writes=[dst_buf])

    for l in range(n_layers):
        last = (l == L_ - 1)
        xsrc, xsrc_n = (xT, "xT") if l == 0 else (xres, "xres")
        xs_fm = xsrc.rearrange("(kc p) t -> p kc t", p=128)

        with ExitStack() as ph:
            stg_ring = Ring([alloc(ph, [128, 2624], F32, "stg") for _ in range(2)])
            win = alloc(ph, [128, 8, NCOL], BF16, "win")
            wuq = alloc(ph, [128, 2, 768], BF16, "wuq")
            wukv = alloc(ph, [128, 512], BF16, "wukv")
            wie = w_in_ext[l].rearrange("(kc p) n -> p kc n", p=128)
            load_cast(stg_ring, lambda i: win.t[:, i, :], lambda i: wie[:, i, :], 8, win.b)
            wue = w_uq_ext[l].rearrange("(kc p) n -> p kc n", p=128)
            load_cast(stg_ring, lambda i: wuq.t[:, i, :], lambda i: wue[:, i, :], 2, wuq.b)
            load_cast(stg_ring, lambda i: wukv.t[:], lambda i: w_ukv_r[l], 1, wukv.b)

            xgs = Ring([alloc(ph, [128, 8, 512], F32, "xg") for _ in range(2)])
            hTs = [alloc(ph, [128, 8, 512], BF16, "hT") for _ in range(2)]
            hbs = [[Buf(f"h{i}_{k}") for k in range(8)] for i in range(2)]
            sq = alloc(ph, [128, 8, 512], BF16, "sq")
            lnv = alloc(ph, [128, 512], F32, "lnv")
            rstds = Ring([alloc(ph, [128, 512], F32, "rstd") for _ in range(2)])
            tmps = Ring([alloc(ph, [128, 512], F32, "tmp") for _ in range(3)])
            rtab = Ring([alloc(ph, [128, 6, 512], F32, "rtab") for _ in range(2)])
            outs = Ring([alloc(ph, [128, 512], BF16, "ost") for _ in range(6)])
            t1s = Ring([alloc(ph, [128, 512], F32, "t1") for _ in range(2)])
            t2s = Ring([alloc(ph, [128, 512], F32, "t2") for _ in range(2)])
            cq = alloc(ph, [128, 3, 512], F32, "cq")
            cqsq = alloc(ph, [128, 3, 512], BF16, "cqsq")
            cqn = alloc(ph, [128, 3, 512], BF16, "cqn")
            vsts = Ring([alloc(ph, [128, VW], BF16, "vst") for _ in range(2)])
            ps_ss = palloc(ph, [128, 512], F32, "ps_ss")
            ps_a = Ring([palloc(ph, [128, 512], F32, "ps_a") for _ in range(2)])
            ps_b = Ring([palloc(ph, [128, 512], F32, "ps_b") for _ in range(2)])
            ps_v = palloc(ph, [128, 384], F32, "ps_v")
            ps_v2 = palloc(ph, [128, 256], F32, "ps_v2")
            for v in vsts.items:
                S.op("dve", lambda e: e.memset(v.t[:], 1.0), writes=[v.b])

            def do_norm(g):
                c0, W = gcols(g)
                xg = xgs.next()
                S.dma("sp", xg.t[:, :, :W], xs_fm[:, :, c0:c0 + W], reads=[db[xsrc_n][g]], writes=[xg.b])
                norm_mod(xg, W, l, 0, 1, 0 if g < 8 else 1, hTs[g % 2], hbs[g % 2], sq, ps_ss, lnv, rstds.next(), tmps)

            def evac_copy(ps, M, W, dst_ap, dst_bufs, eng):
                if eng == "act":
                    S.op("act", lambda e: e.activation(out=dst_ap, in_=ps.t[:M, :W], func=AF.Copy), reads=[ps.b], writes=dst_bufs)
                else:
                    S.op(eng, lambda e: e.tensor_copy(out=dst_ap, in_=ps.t[:M, :W]), reads=[ps.b], writes=dst_bufs)

            def proj_fm(g, hT, hb, col, scol, M, rhs_fn, nk, wt, wb, rope_idx, dst_list, rt):
                c0, W = gcols(g)
                pa = ps_a.next()
                for kc in range(nk):
                    mm(pa.t[:M, :W], wt(kc, col, M), rhs_fn(kc, W), kc == 0, kc == nk - 1, [wb] + hb, pa.b)
                o = outs.next()
                if rope_idx is None or g == 8:
                    evac_copy(pa, M, W, o.t[:M, :W], [o.b], "act" if (col // 128) % 2 == 0 else "dve")
                else:
                    pb = ps_b.next()
                    for kc in range(nk):
                        mm(pb.t[:M, :W], wt(kc, scol, M), rhs_fn(kc, W), kc == 0, kc == nk - 1, [wb] + hb, pb.b)
                    t1, t2 = t1s.next(), t2s.next()
                    S.op("dve", lambda e: e.tensor_tensor(out=t1.t[:M, :W], in0=pa.t[:M, :W], in1=rt.t[:M, rope_idx, :W], op=ALU.mult),
                         reads=[pa.b, rt.b], writes=[t1.b])
                    S.op("dve", lambda e: e.tensor_tensor(out=t2.t[:M, :W], in0=pb.t[:M, :W], in1=rt.t[:M, rope_idx + 1, :W], op=ALU.mult),
                         reads=[pb.b, rt.b], writes=[t2.b])
                    S.op("pool", lambda e: e.tensor_tensor(out=o.t[:M, :W], in0=t1.t[:M, :W], in1=t2.t[:M, :W], op=ALU.add),
                         reads=[t1.b, t2.b], writes=[o.b])
                for (dst, dbuf, p0, p1) in dst_list:
                    S.dma("sp", dst[:, c0:c0 + W], o.t[p0:p1, :W], reads=[o.b], writes=[dbuf])

            def do_proj(g):
                c0, W = gcols(g)
                hT, hb = hTs[g % 2], hbs[g % 2]
                rt = None
                if g < 8:
                    rt = rtab.next()
                    S.dma("sp", rt.t[:, 0:2, :], ropeA[:, :, c0:c0 + W], writes=[rt.b])
                    S.dma("sp", rt.t[:96, 2:4, :], rope96[:, :, c0:c0 + W], writes=[rt.b])
                    S.dma("sp", rt.t[:32, 4:6, :], ropeR[:, :, c0:c0 + W], writes=[rt.b])
                wt_in = lambda kc, col, M: win.t[:, kc, col:col + M]
                rhs_h = lambda kc, W_: hT.t[:, kc, :W_]
                for c in range(4):
                    proj_fm(g, hT, hb, O_QA + c * 128, O_QAS + c * 128, 128, rhs_h, 8, wt_in, win.b, 0,
                            [(QA[c * 128:(c + 1) * 128, :], db["QA"][g], 0, 128)], rt)
                proj_fm(g, hT, hb, O_KA, O_KAS, 128, rhs_h, 8, wt_in, win.b, 0, [(KA[:, :], db["KA"][g], 0, 128)], rt)
                for c in range(2):
                    proj_fm(g, hT, hb, O_QB + c * 128, None, 128, rhs_h, 8, wt_in, win.b, None,
                            [(QB[c * 128:(c + 1) * 128, :], db["QB"][g], 0, 128)], rt)
                for c in range(2):
                    proj_fm(g, hT, hb, O_KB + c * 128, None, 128, rhs_h, 8, wt_in, win.b, None,
                            [(KB[c * 128:(c + 1) * 128, :], db["KB"][g], 0, 128)], rt)
                proj_fm(g, hT, hb, O_KR, O_KRS, 32, rhs_h, 8, wt_in, win.b, 4,
                        [(KC[h * 96 + 64:h * 96 + 96, :], db["KC"][g], 0, 32) for h in range(4)], rt)
                for c in range(3):
                    pa = ps_a.next()
                    col = O_CQ + c * 128
                    for kc in range(8):
                        mm(pa.t[:, :W], win.t[:, kc, col:col + 128], hT.t[:, kc, :W], kc == 0, kc == 7, [win.b] + hb, pa.b)
                    S.op("dve", lambda e: e.tensor_copy(out=cq.t[:, c, :W], in_=pa.t[:, :W]), reads=[pa.b], writes=[cq.b])
                    S.op("act", lambda e: e.activation(out=cqsq.t[:, c, :W], in_=cq.t[:, c, :W], func=AF.Square), reads=[cq.b], writes=[cqsq.b])
                for (cs, n, gi0) in (((0, 1), 256, 0), ((2,), 128, 2)):
                    for i, c in enumerate(cs):
                        mm(ps_ss.t[:, :W], ones_bf.t[:], cqsq.t[:, c, :W], i == 0, i == len(cs) - 1, [cqsq.b, ones_bf.b], ps_ss.b)
                    rs = rstds.next()
                    S.op("act", lambda e: e.activation(out=lnv.t[:, :W], in_=ps_ss.t[:, :W], func=AF.Ln, scale=1.0 / n, bias=epsc.t[:, 0:1]),
                         reads=[ps_ss.b, epsc.b], writes=[lnv.b])
                    S.op("act", lambda e: e.activation(out=rs.t[:, :W], in_=lnv.t[:, :W], func=AF.Exp, scale=-0.5),
                         reads=[lnv.b], writes=[rs.b])
                    for c in cs:
                        S.op("dve", lambda e: e.scalar_tensor_tensor(
                            out=cqn.t[:, c, :W], in0=cq.t[:, c, :W], scalar=mlag_sb.t[:, l, c:c + 1], in1=rs.t[:, :W],
                            op0=ALU.mult, op1=ALU.mult), reads=[cq.b, mlag_sb.b, rs.b], writes=[cqn.b])
                wt_uq = lambda kc, col, M: wuq.t[:, kc, col:col + M]
                rhs_cq = lambda kc, W_: cqn.t[:, kc, :W_]
                for h in range(4):
                    proj_fm(g, cqn, [cqn.b], h * 96, 384 + h * 96, 96, rhs_cq, 2, wt_uq, wuq.b, 2,
                            [(QC[h * 96:(h + 1) * 96, :], db["QC"][g], 0, 96)], rt)
                wt_kv = lambda kc, col, M: wukv.t[:, col:col + M]
                rhs_kv = lambda kc, W_: cqn.t[:, 2, :W_]
                for c in range(2):
                    proj_fm(g, cqn, [cqn.b], c * 128, None, 128, rhs_kv, 1, wt_kv, wukv.b, None,
                            [(KC[(2 * c) * 96:(2 * c) * 96 + 64, :], db["KC"][g], 0, 64),
                             (KC[(2 * c + 1) * 96:(2 * c + 1) * 96 + 64, :], db["KC"][g], 64, 128)], rt)
                for tt in range(W // 128):
                    ts = slice(tt * 128, (tt + 1) * 128)
                    for kc in range(8):
                        mm(ps_v.t[:, 0:384], hT.t[:, kc, ts], win.t[:, kc, O_VA:O_VA + 384], kc == 0, kc == 7, [win.b] + hb, ps_v.b)
                    mm(ps_v2.t[:, 0:256], cqn.t[:, 2, ts], wukv.t[:, 256:512], True, True, [wukv.b, cqn.b], ps_v2.b)
                    v = vsts.next()
                    vv = v.t[:].rearrange("p (h c) -> p h c", c=65)
                    S.op("dve", lambda e: e.tensor_copy(out=vv[:, 0:6, 0:64], in_=ps_v.t[:, 0:384].rearrange("p (h c) -> p h c", c=64)),
                         reads=[ps_v.b], writes=[v.b])
                    S.op("act", lambda e: e.activation(out=vv[:, 6:10, 0:64], in_=ps_v2.t[:, 0:256].rearrange("p (h c) -> p h c", c=64), func=AF.Copy),
                         reads=[ps_v2.b], writes=[v.b])
                    S.dma("sp", VALL[c0 + tt * 128:c0 + (tt + 1) * 128, :], v.t[:], reads=[v.b], writes=[db["VALL"][g]])

            sub = stop_after[0] if (stop_after and stop_after[1] == l) else None
            if sub != "P0":
                do_norm(0)
                for g in range(NG):
                    if sub == "P1":
                        break
                    if g + 1 < NG:
                        do_norm(g + 1)
                    do_proj(g)
                    if sub == "P2":
                        break
        S.barrier()
        if stop_after in (("P", l), ("P0", l), ("P1", l), ("P2", l)):
            break

        qgroups = list(range(8)) + ([] if last else [8])
        mixf = MIX.rearrange("(h d) t -> d h t", d=64)

        def norm1(st, O, W, add_sink=None, use_act=False):
            dsum, rinv, bcs = st
            hv = (lambda a: a.rearrange("p (h t) -> p h t", h=4)) if add_sink is not None else (lambda a: a)
            if add_sink is None:
                S.op("dve", lambda e: e.tensor_copy(out=dsum.t[64:65, :W], in_=O.t[64:65, :W]), reads=[O.b], writes=[dsum.b])
            else:
                S.op("dve", lambda e: e.tensor_tensor(out=hv(dsum.t[64:65, :W]), in0=hv(O.t[64:65, :W]), in1=add_sink, op=ALU.add),
                     reads=[O.b, esrow.b], writes=[dsum.b])
            if use_act:
                S.op("act", lambda e: e.activation(out=dsum.t[64:65, :W], in_=dsum.t[64:65, :W], func=AF.Ln), reads=[dsum.b], writes=[dsum.b])
                S.op("act", lambda e: e.activation(out=rinv.t[64:65, :W], in_=dsum.t[64:65, :W], func=AF.Exp, scale=-1.0), reads=[dsum.b], writes=[rinv.b])
            else:
                S.op("dve", lambda e: e.reciprocal(out=rinv.t[64:65, :W], in_=dsum.t[64:65, :W]), reads=[dsum.b], writes=[rinv.b])

        def norm2(st, ps_bc, O, W, og_ap, og_buf, heads4=False):
            dsum, rinv, bcs = st
            hv = (lambda a: a.rearrange("p (h t) -> p h t", h=4)) if heads4 else (lambda a: a)
            mm(ps_bc.t[:64, :W], ones_f.t[64:65, 0:64], rinv.t[64:65, :W], True, True, [rinv.b, ones_f.b], ps_bc.b)
            S.op("dve", lambda e: e.tensor_copy(out=bcs.t[:64, :W], in_=ps_bc.t[:64, :W]), reads=[ps_bc.b], writes=[bcs.b])
            S.op("dve", lambda e: e.tensor_tensor(out=og_ap, in0=hv(O.t[:64, :W]), in1=hv(bcs.t[:64, :W]), op=ALU.mult),
                 reads=[O.b, bcs.b], writes=[og_buf])

        def nst_ring(ph, n):
            return Ring([(alloc(ph, [65, 512], F32, "dsum"), alloc(ph, [65, 512], F32, "rinv"), alloc(ph, [64, 512], F32, "bcs"))
                         for _ in range(n)])

        with ExitStack() as ph:
            KAs = alloc(ph, [64, 2, T_], BF16, "KAs")
            VAs = alloc(ph, [128, 34, 130], BF16, "VAs")
            esrow = alloc(ph, [65, 8, 128], F32, "esrow")
            S.dma("sp", KAs.t[:], KA.rearrange("(h d) t -> d h t", d=64), reads=db["KA"], writes=[KAs.b])
            vall_b = VALL.rearrange("(b p) c -> p b c", p=128)
            for i in range(0, 34, 9):
                S.dma("sp", VAs.t[:, i:min(i + 9, 34), :], vall_b[:, i:min(i + 9, 34), 0:130], reads=db["VALL"], writes=[VAs.b])
            S.op("dve", lambda e: e.memset(esrow.t[:], 0.0), writes=[esrow.b])
            for h in range(8):
                S.op("dve", lambda e: e.tensor_scalar(out=esrow.t[64:65, h, :], in0=esrow.t[64:65, h, :],
                                                      scalar1=esink.t[64:65, l * 8 + h:l * 8 + h + 1], scalar2=None, op0=ALU.add),
                     reads=[esink.b], writes=[esrow.b])
            Qgs = Ring([alloc(ph, [64, 8, 512], BF16, "Qg") for _ in range(2)])
            ogs = Ring([alloc(ph, [64, 8, 512], BF16, "og") for _ in range(2)])
            pts = Ring([alloc(ph, [128, 512], BF16, "pt") for _ in range(6)])
            nsts = nst_ring(ph, 3)
            ps_bc = palloc(ph, [64, 512], F32, "ps_bc")
            ps_s = Ring([palloc(ph, [128, 512], F32, "ps_s") for _ in range(4)])
            ps_o = Ring([palloc(ph, [65, 512], F32, "ps_o") for _ in range(3)])
            pipe = Pipe(2, 3)
            qaf = QA.rearrange("(h d) t -> d h t", d=64)

            def loadQ(g):
                c0, W = gcols(g)
                Qg = Qgs.next()
                S.dma("sp", Qg.t[:, :, :W], qaf[:, :, c0:c0 + W], reads=[db["QA"][g]], writes=[Qg.b])
                return Qg

            def stepA(Qg, O, kv, bi, kb, m, first, lastk, fin):
                cell = {}
                rhs = Qg.t[:, 4 * kv:4 * kv + 4, bi * 128:(bi + 1) * 128]

                def s1():
                    ps = ps_s.next()
                    psv = ps.t[:].rearrange("p (h t) -> p h t", h=4)
                    mm(psv, KAs.t[:, kv, kb * 128:(kb + 1) * 128], rhs, True, m is None, [KAs.b, Qg.b], ps.b)
                    if m is not None:
                        mm(psv, ident8.t[:], amask.t[:, m:m + 1, :].broadcast_to([128, 4, 128]), False, True,
                           [ident8.b, amask.b], ps.b)
                    pt = pts.next()
                    S.op("act", lambda e: e.activation(out=pt.t[:], in_=ps.t[:], func=AF.Exp, scale=0.125),
                         reads=[ps.b], writes=[pt.b])
                    cell["pt"] = pt

                def s2():
                    pt = cell["pt"]
                    if first:
                        pipe.force(O)
                    mm(O.t[:65, :], VAs.t[:, kb, kv * 65:(kv + 1) * 65], pt.t[:], first, lastk, [VAs.b, pt.b], O.b, inc=True)
                    if lastk:
                        fin()
                return s1, s2

            nxtQ = loadQ(qgroups[0])
            for gi, g in enumerate(qgroups):
                c0, W = gcols(g)
                Qg, og = nxtQ, ogs.next()
                if gi + 1 < len(qgroups):
                    nxtQ = loadQ(qgroups[gi + 1])
                for kv in range(2):
                    for bi in range(W // 128):
                        n = g * 4 + bi
                        if g < 8:
                            kbs = [(kb, m) for kb, m in ((n - 1, 0), (n, None), (n + 1, 1)) if 0 <= kb < 32] + [(32, None), (33, None)]
                        else:
                            kbs = [(32, None), (33, None)]
                        O = ps_o.next()
                        ogv = og.t[:, 4 * kv:4 * kv + 4, bi * 128:(bi + 1) * 128]

                        lastacc = (kv == 1 and bi == W // 128 - 1)

                        def fin(O=O, ogv=ogv, og=og, kv=kv, lastacc=lastacc, c0=c0, W=W, g=g):
                            st = nsts.next()
                            norm1(st, O, 512, add_sink=esrow.t[64:65, 4 * kv:4 * kv + 4, :], use_act=True)
                            pipe.defer(lambda: norm2(st, ps_bc, O, 512, ogv, og.b, heads4=True), tag=O)
                            if lastacc:
                                pipe.defer(lambda: S.dma("sp", mixf[:, 0:8, c0:c0 + W], og.t[:, :, :W], reads=[og.b], writes=[db["MIX"][g]]))
                        for ki, (kb, m) in enumerate(kbs):
                            pipe.push(*stepA(Qg, O, kv, bi, kb, m, ki == 0, ki == len(kbs) - 1, fin))
            pipe.flush()
        S.barrier()
        if stop_after == ("TA", l):
            break

        with ExitStack() as ph:
            KBs = alloc(ph, [64, 4, T_], BF16, "KBs")
            VBc = alloc(ph, [128, 2, 260], BF16, "VBc")
            bm = alloc(ph, [124, 4, 4608], BF16, "bm")
            stg_ring = Ring([alloc(ph, [124, 4608], F32, "stgb") for _ in range(2)])
            S.dma("sp", KBs.t[:], KB.rearrange("(h d) t -> d h t", d=64), reads=db["KB"], writes=[KBs.b])
            vall_b = VALL.rearrange("(b p) c -> p b c", p=128)
            S.dma("sp", VBc.t[:], vall_b[:, 32:34, 130:390], reads=db["VALL"], writes=[VBc.b])
            load_cast(stg_ring, lambda i: bm.t[:, i, :], lambda i: rpb_tab[l, :, i, :], 4, bm.b)
            Qgs = Ring([alloc(ph, [64, 4, 512], BF16, "Qg") for _ in range(2)])
            ogs = Ring([alloc(ph, [64, 4, 512], BF16, "og") for _ in range(2)])
            pts = Ring([alloc(ph, [128, 512], BF16, "pt") for _ in range(6)])
            vgs = Ring([alloc(ph, [124, 260], BF16, "vg") for _ in range(8)])
            kgts = Ring([alloc(ph, [64, 4, 124], BF16, "kgt") for _ in range(4)])
            nsts = nst_ring(ph, 4)
            ps_bc = palloc(ph, [64, 512], F32, "ps_bc")
            ps_s = Ring([palloc(ph, [128, 512], F32, "ps_s") for _ in range(3)])
            ps_o = [palloc(ph, [65, 512], F32, "ps_o") for _ in range(4)]
            pipe = Pipe(2, 2)
            qbf = QB.rearrange("(h d) t -> d h t", d=64)

            def loadQ(g):
                c0, W = gcols(g)
                Qg = Qgs.next()
                S.dma("sp", Qg.t[:, :, :W], qbf[:, :, c0:c0 + W], reads=[db["QB"][g]], writes=[Qg.b])
                return Qg

            def stepBctx(Qg, O, h, ki, cb, W, lastk, fin):
                cell = {}

                def s1():
                    ps = ps_s.next()
                    mm(ps.t[:, :W], KBs.t[:, h, cb * 128:(cb + 1) * 128], Qg.t[:, h, :W], True, True, [KBs.b, Qg.b], ps.b)
                    pt = pts.next()
                    S.op("act", lambda e: e.activation(out=pt.t[:, :W], in_=ps.t[:, :W], func=AF.Exp, scale=0.125),
                         reads=[ps.b], writes=[pt.b])
                    cell["pt"] = pt

                def s2():
                    pt = cell["pt"]
                    if ki == 0:
                        pipe.force(O)
                    mm(O.t[:65, :W], VBc.t[:, ki, h * 65:(h + 1) * 65], pt.t[:, :W], ki == 0, lastk,
                       [VBc.b, pt.b], O.b, inc=True, skip_group_check=True)
                    if lastk:
                        fin()
                return s1, s2

            def stepBloc(Qg, O, h, j, kg, pat, kgt, vg, M, lastk, fin):
                cell = {}

                def s1():
                    ps = ps_s.next()
                    qview = Qg.t[:, h, :].rearrange("p (r c) -> p r c", c=64)[:, :, 16 * j:16 * j + 16]
                    psv = ps.t[:M, 0:128].rearrange("p (r c) -> p r c", c=16)
                    mm(psv, kgt.t[:, h, :M], qview, True, False, [kgt.b, Qg.b], ps.b)
                    boff = (pat * 4 + kg) * 128
                    mm(ps.t[:M, 0:128], ident8.t[:M, :M], bm.t[:M, h, boff:boff + 128], False, True, [ident8.b, bm.b], ps.b)
                    pt = pts.next()
                    S.op("act", lambda e: e.activation(out=pt.t[:M, 0:128], in_=ps.t[:M, 0:128], func=AF.Exp, scale=0.125),
                         reads=[ps.b], writes=[pt.b])
                    cell["pt"] = pt

                def s2():
                    pt = cell["pt"]
                    ov = O.t[:65, :].rearrange("p (r c) -> p r c", c=64)[:, :, 16 * j:16 * j + 16]
                    ptv = pt.t[:M, 0:128].rearrange("p (r c) -> p r c", c=16)
                    mm(ov, vg.t[:M, h * 65:(h + 1) * 65], ptv, False, lastk, [vg.b, pt.b], O.b,
                       inc=True, skip_group_check=True)
                    if lastk:
                        fin()
                return s1, s2

            nxtQ = loadQ(qgroups[0])
            for gi, g in enumerate(qgroups):
                c0, W = gcols(g)
                Qg, og = nxtQ, ogs.next()
                if gi + 1 < len(qgroups):
                    nxtQ = loadQ(qgroups[gi + 1])

                def mkfin(h, og=og, W=W, c0=c0, g=g):
                    def fin():
                        st = nsts.next()
                        O = ps_o[h]
                        norm1(st, O, W)
                        pipe.defer(lambda: norm2(st, ps_bc, O, W, og.t[:, h, :W], og.b), tag=O)
                        if h == 3:
                            pipe.defer(lambda: S.dma("sp", mixf[:, 8:12, c0:c0 + W], og.t[:, :, :W], reads=[og.b], writes=[db["MIX"][g]]))
                    return fin
                for h in range(4):
                    for ki, cb in enumerate((32, 33)):
                        pipe.push(*stepBctx(Qg, ps_o[h], h, ki, cb, W, (g == 8 and ki == 1), mkfin(h)))
                if g < 8:
                    i = g
                    r0 = int(NA_KR0[i])
                    for j in range(4):
                        cc0 = int(NA_KC0[j])
                        pat = ICLS[i] * 3 + JCLS[j]
                        for kg in range(4):
                            nr = KG_ROWS[kg]
                            M = nr * 31
                            vg = vgs.next()
                            tok0 = (r0 + 4 * kg) * 64 + cc0
                            src = VALL[tok0:tok0 + nr * 64, 130:390].rearrange("(r c) w -> r c w", c=64)[:, 0:31, :]
                            for rr in range(nr):
                                S.dma("sp", vg.t[rr * 31:(rr + 1) * 31, :], src[rr], reads=db["VALL"], writes=[vg.b])
                            kgt = kgts.next()
                            S.op("pool", lambda e: e.tensor_copy(
                                out=kgt.t[:, :, :M].rearrange("p h (r c) -> p h r c", c=31),
                                in_=KBs.t[:, :, tok0:tok0 + nr * 64].rearrange("p h (r c) -> p h r c", c=64)[:, :, :, 0:31]),
                                reads=[KBs.b], writes=[kgt.b])
                            for h in range(4):
                                pipe.push(*stepBloc(Qg, ps_o[h], h, j, kg, pat, kgt, vg, M, (j == 3 and kg == 3), mkfin(h)))
            pipe.flush()
        S.barrier()
        if stop_after == ("TB", l):
            break

        with ExitStack() as ph:
            KCs = alloc(ph, [96, 4, T_], BF16, "KCs")
            VCs = alloc(ph, [128, 34, 260], BF16, "VCs")
            S.dma("sp", KCs.t[:], KC.rearrange("(h d) t -> d h t", d=96), reads=db["KC"], writes=[KCs.b])
            vall_b = VALL.rearrange("(b p) c -> p b c", p=128)
            for i in range(0, 34, 9):
                S.dma("sp", VCs.t[:, i:min(i + 9, 34), :], vall_b[:, i:min(i + 9, 34), 390:650], reads=db["VALL"], writes=[VCs.b])
            Qgs = Ring([alloc(ph, [96, 4, 512], BF16, "Qg") for _ in range(2)])
            ogs = Ring([alloc(ph, [64, 4, 512], BF16, "og") for _ in range(2)])
            pts = Ring([alloc(ph, [128, 512], BF16, "pt") for _ in range(6)])
            nsts = nst_ring(ph, 2)
            ps_bc = palloc(ph, [64, 512], F32, "ps_bc")
            ps_s = Ring([palloc(ph, [128, 512], F32, "ps_s") for _ in range(4)])
            ps_o = Ring([palloc(ph, [65, 512], F32, "ps_o") for _ in range(2)])
            sc = float(96 ** -0.5)
            pipe = Pipe(3, 4)
            qcf = QC.rearrange("(h d) t -> d h t", d=96)

            def loadQ(g):
                c0, W = gcols(g)
                Qg = Qgs.next()
                S.dma("sp", Qg.t[:, :, :W], qcf[:, :, c0:c0 + W], reads=[db["QC"][g]], writes=[Qg.b])
                return Qg

            def stepC(Qg, O, h, kb, W, first, lastk, fin):
                cell = {}

                def s1():
                    ps = ps_s.next()
                    mm(ps.t[:, :W], KCs.t[:, h, kb * 128:(kb + 1) * 128], Qg.t[:, h, :W], True, True, [KCs.b, Qg.b], ps.b)
                    pt = pts.next()
                    S.op("act", lambda e: e.activation(out=pt.t[:, :W], in_=ps.t[:, :W], func=AF.Exp, scale=sc),
                         reads=[ps.b], writes=[pt.b])
                    cell["pt"] = pt

                def s2():
                    pt = cell["pt"]
                    if first:
                        pipe.force(O)
                    mm(O.t[:65, :W], VCs.t[:, kb, h * 65:(h + 1) * 65], pt.t[:, :W], first, lastk, [VCs.b, pt.b], O.b, inc=True)
                    if lastk:
                        fin()
                return s1, s2

            nxtQ = loadQ(qgroups[0])
            for gi, g in enumerate(qgroups):
                c0, W = gcols(g)
                Qg, og = nxtQ, ogs.next()
                if gi + 1 < len(qgroups):
                    nxtQ = loadQ(qgroups[gi + 1])
                kbs = list(range(34)) if g < 8 else [32, 33]
                for h in range(4):
                    O = ps_o.next()

                    def fin(O=O, og=og, h=h, W=W, c0=c0, g=g):
                        st = nsts.next()
                        norm1(st, O, W)
                        pipe.defer(lambda: norm2(st, ps_bc, O, W, og.t[:, h, :W], og.b), tag=O)
                        if h == 3:
                            pipe.defer(lambda: S.dma("sp", mixf[:, 12:16, c0:c0 + W], og.t[:, :, :W], reads=[og.b], writes=[db["MIX"][g]]))
                    for ki, kb in enumerate(kbs):
                        pipe.push(*stepC(Qg, O, h, kb, W, ki == 0, ki == len(kbs) - 1, fin))
            pipe.flush()
        S.barrier()
        if stop_after == ("TC", l):
            break

        ogroups = list(range(8)) + ([] if last else [8])
        with ExitStack() as ph:
            stg_ring = Ring([alloc(ph, [128, 1024], F32, "stg") for _ in range(2)])
            wo = alloc(ph, [128, 8, D], BF16, "wo")
            wod = w_out[l].rearrange("(kc p) n -> p kc n", p=128)
            load_cast(stg_ring, lambda i: wo.t[:, i, :], lambda i: wod[:, i, :], 8, wo.b)
            mgs = Ring([alloc(ph, [128, 8, 512], BF16, "mixg") for _ in range(2)])
            xgs = Ring([alloc(ph, [128, 8, 512], F32, "xg") for _ in range(2)])
            ps_y = Ring([palloc(ph, [128, 512], F32, "ps_y") for _ in range(4)])
            mix_fm = MIX.rearrange("(kc p) t -> p kc t", p=128)
            xm_fm = xmid.rearrange("(kc p) t -> p kc t", p=128)
            for g in ogroups:
                c0, W = gcols(g)
                j = 0 if g < 8 else 1
                mg, xg = mgs.next(), xgs.next()
                S.dma("sp", mg.t[:, :, :W], mix_fm[:, :, c0:c0 + W], reads=[db["MIX"][g]], writes=[mg.b])
                S.dma("sp", xg.t[:, :, :W], xs_fm[:, :, c0:c0 + W], reads=[db[xsrc_n][g]], writes=[xg.b])
                for c in range(8):
                    py = ps_y.next()
                    for kc in range(8):
                        mm(py.t[:, :W], wo.t[:, kc, c * 128:(c + 1) * 128], mg.t[:, kc, :W], kc == 0, kc == 7, [wo.b, mg.b], py.b)
                    S.op("dve", lambda e: e.scalar_tensor_tensor(
                        out=xg.t[:, c, :W], in0=py.t[:, :W], scalar=modv.t[:, l, 2, c, j:j + 1], in1=xg.t[:, c, :W],
                        op0=ALU.mult, op1=ALU.add), reads=[py.b, modv.b, xg.b], writes=[xg.b])
                S.dma("sp", xm_fm[:, :, c0:c0 + W], xg.t[:, :, :W], reads=[xg.b], writes=[db["xmid"][g]])
        S.barrier()
        if stop_after == ("O1", l):
            break

        with ExitStack() as ph:
            stg_ring = Ring([alloc(ph, [128, 2048], F32, "stg") for _ in range(2)])
            w1 = alloc(ph, [128, 8, 4 * D], BF16, "w1")
            w2 = alloc(ph, [128, 32, D], BF16, "w2")
            w1d = w_mlp_in[l].rearrange("(kc p) n -> p kc n", p=128)
            w2d = w_mlp_out[l].rearrange("(f p) n -> p f n", p=128)
            load_cast(stg_ring, lambda i: w1.t[:, i // 2, (i % 2) * 2048:(i % 2 + 1) * 2048],
                      lambda i: w1d[:, i // 2, (i % 2) * 2048:(i % 2 + 1) * 2048], 16, w1.b)
            load_cast(stg_ring, lambda i: w2.t[:, 2 * i:2 * i + 2, :], lambda i: w2d[:, 2 * i:2 * i + 2, :], 16, w2.b)
            WG = 256
            xgs = Ring([alloc(ph, [128, 8, WG], F32, "xg") for _ in range(2)])
            hT2 = alloc(ph, [128, 8, WG], BF16, "hT2")
            hb2 = [Buf(f"h2_{k}") for k in range(8)]
            sq = alloc(ph, [128, 8, WG], BF16, "sq")
            lnv = alloc(ph, [128, WG], F32, "lnv")
            rstd = alloc(ph, [128, WG], F32, "rstd")
            tmps = Ring([alloc(ph, [128, WG], F32, "tmp") for _ in range(3)])
            rl = Ring([alloc(ph, [128, WG], F32, "rl") for _ in range(3)])
            aT = alloc(ph, [128, 32, WG], BF16, "aT")
            abufs = [Buf(f"a{f}") for f in range(32)]
            ps_ss = palloc(ph, [128, WG], F32, "ps_ss")
            ps_u = Ring([palloc(ph, [128, WG], F32, "ps_u") for _ in range(3)])
            ps_y = Ring([palloc(ph, [128, WG], F32, "ps_y") for _ in range(2)])
            xm_fm = xmid.rearrange("(kc p) t -> p kc t", p=128)
            xr_fm = xres.rearrange("(kc p) t -> p kc t", p=128)
            out_fm = outT.rearrange("(kc p) t -> p kc t", p=128)
            ngr = 16 + (0 if last else 1)
            for gg in range(ngr):
                c0 = gg * WG
                g = gg // 2 if gg < 16 else 8
                j = 0 if gg < 16 else 1
                xg = xgs.next()
                S.dma("sp", xg.t[:], xm_fm[:, :, c0:c0 + WG], reads=[db["xmid"][g]], writes=[xg.b])
                norm_mod(xg, WG, l, 3, 4, j, hT2, hb2, sq, ps_ss, lnv, rstd, tmps)
                for f in range(32):
                    pu = ps_u.next()
                    for kc in range(8):
                        mm(pu.t[:, :WG], w1.t[:, kc, f * 128:(f + 1) * 128], hT2.t[:, kc, :], kc == 0, kc == 7, [w1.b] + hb2, pu.b)
                    r = rl.next()
                    S.op("act", lambda e: e.activation(out=r.t[:], in_=pu.t[:, :WG], func=AF.Relu), reads=[pu.b], writes=[r.b])
                    S.op("pool" if f % 2 else "dve", lambda e: e.tensor_tensor(out=aT.t[:, f, :], in0=r.t[:], in1=r.t[:], op=ALU.mult),
                         reads=[r.b], writes=[abufs[f]])
                for c in range(8):
                    py = ps_y.next()
                    for f in range(32):
                        mm(py.t[:, :WG], w2.t[:, f, c * 128:(c + 1) * 128], aT.t[:, f, :], f == 0, f == 31, [w2.b, abufs[f]], py.b)
                    S.op("dve", lambda e: e.scalar_tensor_tensor(
                        out=xg.t[:, c, :], in0=py.t[:, :WG], scalar=modv.t[:, l, 5, c, j:j + 1], in1=xg.t[:, c, :],
                        op0=ALU.mult, op1=ALU.add), reads=[py.b, modv.b, xg.b], writes=[xg.b])
                if not last:
                    S.dma("sp", xr_fm[:, :, c0:c0 + WG], xg.t[:], reads=[xg.b], writes=[db["xres"][g]])
                else:
                    S.op("dve", lambda e: e.tensor_tensor(out=sq.t[:], in0=xg.t[:], in1=xg.t[:], op=ALU.mult), reads=[xg.b], writes=[sq.b])
                    for kc in range(8):
                        mm(ps_ss.t[:, :WG], ones_bf.t[:], sq.t[:, kc, :], kc == 0, kc == 7, [sq.b, ones_bf.b], ps_ss.b)
                    S.op("act", lambda e: e.activation(out=lnv.t[:], in_=ps_ss.t[:, :WG], func=AF.Ln, scale=1.0 / D, bias=epsc.t[:, 0:1]),
                         reads=[ps_ss.b, epsc.b], writes=[lnv.b])
                    S.op("act", lambda e: e.activation(out=rstd.t[:], in_=lnv.t[:], func=AF.Exp, scale=-0.5), reads=[lnv.b], writes=[rstd.b])
                    for kc in range(8):
                        S.op("dve", lambda e: e.scalar_tensor_tensor(
                            out=xg.t[:, kc, :], in0=xg.t[:, kc, :], scalar=gain_sb.t[:, 8, kc:kc + 1], in1=rstd.t[:],
                            op0=ALU.mult, op1=ALU.mult), reads=[xg.b, gain_sb.b, rstd.b], writes=[xg.b])
                    S.dma("sp", out_fm[:, :, c0:c0 + WG], xg.t[:], reads=[xg.b], writes=[])
        S.barrier()
        if stop_after == ("O2", l):
            break

    S.finish("sp")
    glob.close()
    return nc, S


def kernel(**inputs):
    per_core = prep_inputs(inputs)
    nc, _ = build()
    res = run_bass_kernel_spmd(nc, per_core, core_ids=list(range(8)))
    out = np.stack([np.ascontiguousarray(r["outT"].T) for r in res.results], axis=0)
    return out.astype(np.float32)
```

```python
import math
from contextlib import ExitStack
import numpy as np
import concourse.bass as bass
import concourse.mybir as mybir
from concourse.bass_utils import run_bass_kernel_spmd

F32 = mybir.dt.float32
BF16 = mybir.dt.bfloat16
AF = mybir.ActivationFunctionType
ALU = mybir.AluOpType

D = 1024
S_ = 4096
C_ = 256
T_ = S_ + C_
L_ = 4
NCOL = 2624
EPS = 1e-6
NEG = -30000.0
O_QA, O_QAS, O_KA, O_KAS, O_QB, O_KB, O_CQ, O_CKV, O_KR, O_KRS, O_VA = (
    0, 512, 1024, 1152, 1280, 1536, 1792, 2048, 2176, 2208, 2240)
VW = 650


class Buf:
    __slots__ = ("name", "lw", "rs", "psum")

    def __init__(self, name="", psum=False):
        self.name = name
        self.lw = None
        self.rs = {}
        self.psum = psum


class Sched:
    def __init__(self, nc, n_dma_slots=32, same_engine_sync=True):
        self.nc = nc
        self.same = same_engine_sync
        self.eng = {"pe": nc.tensor, "act": nc.scalar, "dve": nc.vector, "pool": nc.gpsimd, "sp": nc.sync}
        self.sem = {k: nc.alloc_semaphore(name=f"sem_{k}") for k in self.eng}
        self.cnt = {k: 0 for k in self.eng}
        self.seen = {k: {} for k in self.eng}
        self.dsem = [nc.alloc_semaphore(name=f"dsem{i}") for i in range(n_dma_slots)]
        self.dcnt = [0] * n_dma_slots
        self.dnext = 0
        self.n_wait = 0
        self.n_inst = 0
        self.log = {k: [] for k in self.eng}

    def _semof(self, key):
        return self.dsem[key] if isinstance(key, int) else self.sem[key]

    def _wait(self, e, key, val):
        if key == e and not self.same:
            return
        if self.seen[e].get(key, 0) >= val:
            return
        self.eng[e].wait_ge(self._semof(key), val)
        self.log[e].append(("w", key, val))
        self.seen[e][key] = val
        self.n_wait += 1

    def _deps(self, e, reads, writes):
        for b in reads:
            if b.lw is not None:
                self._wait(e, *b.lw)
            if b.psum:
                for k, v in b.rs.items():
                    if k != e:
                        self._wait(e, k, v)
        for b in writes:
            if b.lw is not None:
                self._wait(e, *b.lw)
            for k, v in b.rs.items():
                self._wait(e, k, v)

    def op(self, e, fn, reads=(), writes=(), pe_acc=False, inc=True):
        if pe_acc:
            self._deps(e, reads, ())
        else:
            self._deps(e, reads, writes)
        ins = fn(self.eng[e])
        if inc:
            ins.then_inc(self.sem[e], 1)
            self.log[e].append(("i", e, 1))
            self.cnt[e] += 1
            n = self.cnt[e]
        else:
            n = self.cnt[e] + 1
        self.n_inst += 1
        for b in reads:
            b.rs[e] = n
        for b in writes:
            b.lw = (e, n)
            b.rs = {}
        return ins

    def dma(self, q, out, in_, reads=(), writes=(), **kw):
        self._deps(q, reads, writes)
        s = self.dnext
        self.dnext = (self.dnext + 1) % len(self.dsem)
        if self.dcnt[s] > 0:
            self._wait(q, s, 16 * self.dcnt[s])
        ins = self.eng[q].dma_start(out=out, in_=in_, **kw)
        ins.then_inc(self.dsem[s], 16)
        self.log[q].append(("i", s, 16))
        self.dcnt[s] += 1
        v = 16 * self.dcnt[s]
        self.n_inst += 1
        for b in reads:
            b.rs[s] = v
        for b in writes:
            b.lw = (s, v)
            b.rs = {}
        return ins

    def barrier(self):
        for e in self.eng:
            self.finish(e)

    def finish(self, e="sp"):
        for k in self.eng:
            if k != e and self.cnt[k] > 0:
                self._wait(e, k, self.cnt[k])
        for s in range(len(self.dsem)):
            if self.dcnt[s] > 0:
                self._wait(e, s, 16 * self.dcnt[s])


class Pipe:
    def __init__(self, la, d):
        self.la, self.d = la, d
        self.q = []
        self.later = []
        self.step = 0

    def _fire(self, upto=None):
        while self.later and (upto is None or self.later[0][0] <= upto):
            self.later.pop(0)[1]()

    def _s2(self):
        self.q.pop(0)()
        self.step += 1
        self._fire(self.step)

    def push(self, s1, s2):
        s1()
        self.q.append(s2)
        if len(self.q) > self.la:
            self._s2()

    def defer(self, fn, tag=None):
        self.later.append((self.step + self.d, fn, tag))

    def force(self, tag):
        idx = [i for i, it in enumerate(self.later) if it[2] is tag]
        if idx:
            for _ in range(idx[-1] + 1):
                self.later.pop(0)[1]()

    def flush(self):
        while self.q:
            self._s2()
        self._fire(None)


class TB:
    def __init__(self, t, name=""):
        self.t = t
        self.b = Buf(name)


class Ring:
    def __init__(self, items):
        self.items = items
        self.i = 0

    def next(self):
        it = self.items[self.i]
        self.i = (self.i + 1) % len(self.items)
        return it


def _perm_a(n):
    idx = np.arange(n)
    h, d = idx // 64, idx % 64
    j = d % 32
    partner = np.where(j < 16, d + 16, d - 16)
    return h * 64 + partner


def _perm_r(n=32):
    d = np.arange(n)
    jj = d % 16
    return np.where(jj < 8, d + 8, d - 8)


def _rope_tables():
    tok = np.arange(S_)
    row, col = tok // 64, tok % 64

    def tab(ndim_sec, half):
        freqs = (10000.0 ** (-np.arange(half, dtype=np.float32) / half)).astype(np.float32)
        cos = np.zeros((2 * ndim_sec, S_), np.float32)
        sin = np.zeros((2 * ndim_sec, S_), np.float32)
        for d in range(2 * ndim_sec):
            sec, j = d // ndim_sec, d % ndim_sec
            pos = (row if sec == 0 else col).astype(np.float32)
            ang = pos * freqs[j % half]
            cos[d] = np.cos(ang).astype(np.float32)
            sn = np.sin(ang).astype(np.float32)
            sin[d] = -sn if j < half else sn
        return cos, sin

    ca, sa = tab(32, 16)
    cr, sr = tab(16, 8)
    ropeA = np.stack([np.concatenate([ca, ca], 0), np.concatenate([sa, sa], 0)], 1)
    c96 = np.concatenate([np.ones((64, S_), np.float32), cr], 0)
    s96 = np.concatenate([np.zeros((64, S_), np.float32), sr], 0)
    rope96 = np.stack([c96, s96], 1)
    ropeR = np.stack([cr, sr], 1)
    return np.ascontiguousarray(ropeA), np.ascontiguousarray(rope96), np.ascontiguousarray(ropeR)


def _na_layout():
    rows, GW, NR, NCc, NQ = 64, 64, 8, 16, 16
    kr, kc = min(NR, rows), NCc
    qr, qc = math.gcd(rows, NR), NQ
    krb, kcb = min(qr - 1 + kr, rows), min(qc - 1 + kc, GW)
    nrb, ncb = rows // qr, GW // qc
    q_r = np.arange(nrb)[:, None] * qr + np.arange(qr)[None, :]
    q_c = np.arange(ncb)[:, None] * qc + np.arange(qc)[None, :]
    w_r = np.clip(q_r - kr // 2, 0, rows - kr)
    w_c = np.clip(q_c - kc // 2, 0, GW - kc)
    k_r = np.minimum(w_r[:, 0], rows - krb)[:, None] + np.arange(krb)[None, :]
    k_c = np.minimum(w_c[:, 0], GW - kcb)[:, None] + np.arange(kcb)[None, :]
    qr6 = q_r[:, None, :, None, None, None]
    qc6 = q_c[None, :, None, :, None, None]
    wr6 = w_r[:, None, :, None, None, None]
    wc6 = w_c[None, :, None, :, None, None]
    kr6 = k_r[:, None, None, None, :, None]
    kc6 = k_c[None, :, None, None, None, :]
    shape6 = (nrb, ncb, qr, qc, krb, kcb)

    def flat(a):
        return np.broadcast_to(a, shape6).reshape(nrb, ncb, qr * qc, krb * kcb)

    mask = flat((kr6 >= wr6) & (kr6 < wr6 + kr) & (kc6 >= wc6) & (kc6 < wc6 + kc))
    d_r = flat(np.clip(kr6 - qr6, 1 - NR, NR - 1) + NR - 1)
    d_c = flat(np.clip(kc6 - qc6, 1 - NCc, NCc - 1) + NCc - 1)
    return mask, d_r, d_c, k_r[:, 0], k_c[:, 0]


NA_MASK, NA_DR, NA_DC, NA_KR0, NA_KC0 = _na_layout()
ICLS = [0, 1, 1, 1, 1, 1, 1, 2]
JCLS = [0, 1, 1, 2]
IREP = [0, 1, 7]
JREP = [0, 1, 3]
KG_ROWS = [4, 4, 4, 3]


def _rpb_table(na_rpb):
    tab = np.full((L_, 128, 4, 9, 4, 128), NEG, np.float32)
    for ic in range(3):
        for jc in range(3):
            i, j = IREP[ic], JREP[jc]
            m = NA_MASK[i, j]
            dr, dc = NA_DR[i, j], NA_DC[i, j]
            g = na_rpb[:, :, dr, dc]
            g = np.where(m[None, None], g, np.float32(NEG))
            kc0 = int(NA_KC0[j])
            c0w = min(kc0, 32)
            for kg in range(4):
                for rr in range(KG_ROWS[kg]):
                    for cc in range(32):
                        col = c0w + cc
                        if kc0 <= col < kc0 + 31:
                            kk = (4 * kg + rr) * 31 + (col - kc0)
                            tab[:, rr * 32 + cc, :, ic * 3 + jc, kg, :] = g[:, :, :, kk]
    return np.ascontiguousarray(tab.reshape(L_, 128, 4, 9 * 4 * 128))


def _fm(v):
    v = np.asarray(v, np.float32)
    lead = v.shape[:-1]
    c = v.shape[-1] // 128
    v = v.reshape(*lead, c, 128)
    return np.ascontiguousarray(np.moveaxis(v, -1, 0))


def prep_inputs(inp):
    f = lambda a: np.ascontiguousarray(np.asarray(a, np.float32))
    w_in = f(inp["w_in"])
    pa512, pa128, pr = _perm_a(512), _perm_a(128), _perm_r()
    ext = np.concatenate([
        w_in[:, :, 0:512], w_in[:, :, 0:512][:, :, pa512],
        w_in[:, :, 512:640], w_in[:, :, 512:640][:, :, pa128],
        w_in[:, :, 768:1024], w_in[:, :, 1024:1280],
        w_in[:, :, 1536:1792], w_in[:, :, 1792:1920],
        w_in[:, :, 1920:1952], w_in[:, :, 1920:1952][:, :, pr],
        w_in[:, :, 640:768], w_in[:, :, 1280:1536]], axis=2)
    assert ext.shape[2] == NCOL
    w_uq = f(inp["mla_w_uq"])
    sw = np.concatenate([np.concatenate([h * 96 + np.arange(64), h * 96 + 64 + pr]) for h in range(4)])
    w_uq_ext = np.concatenate([w_uq, w_uq[:, :, sw]], axis=2)
    w_ukv = f(inp["mla_w_ukv"]).reshape(L_, 128, 4, 2, 64)
    w_ukv_r = np.ascontiguousarray(w_ukv.transpose(0, 1, 3, 2, 4).reshape(L_, 128, 512))
    ropeA, rope96, ropeR = _rope_tables()
    kq = np.arange(128)
    amask = np.zeros((128, 2, 128), np.float32)
    amask[:, 0, :] = np.where(kq[:, None] >= kq[None, :], 0.0, NEG)
    amask[:, 1, :] = np.where(kq[:, None] <= kq[None, :], 0.0, NEG)
    gains = np.concatenate([_fm(inp["norm1_g"]), _fm(inp["norm2_g"]), _fm(inp["final_norm_g"])[:, None, :]], 1)
    mla_g = np.concatenate([_fm(inp["mla_q_norm_g"]), _fm(inp["mla_kv_norm_g"])], 2)
    shared = {
        "w_ada": f(inp["w_ada"]),
        "b_ada": _fm(inp["b_ada"]),
        "gains": np.ascontiguousarray(gains),
        "w_in_ext": np.ascontiguousarray(ext),
        "w_uq_ext": np.ascontiguousarray(w_uq_ext),
        "w_ukv_r": w_ukv_r,
        "mla_g": np.ascontiguousarray(mla_g),
        "sinkrow": np.ascontiguousarray(np.broadcast_to(f(inp["attn_sink"]).reshape(1, 32), (65, 32))),
        "rpb_tab": _rpb_table(f(inp["na_rpb"])),
        "w_out": f(inp["w_out"]),
        "w_mlp_in": f(inp["w_mlp_in"]),
        "w_mlp_out": f(inp["w_mlp_out"]),
        "ropeA": ropeA, "rope96": rope96, "ropeR": ropeR,
        "amask": amask,
        "ident8": np.ascontiguousarray(8.0 * np.eye(128, dtype=np.float32)),
    }
    x, ctx, c, c_ctx = f(inp["x"]), f(inp["ctx"]), f(inp["c"]), f(inp["c_ctx"])
    per_core = []
    for b in range(x.shape[0]):
        xt = np.ascontiguousarray(np.concatenate([x[b].T, ctx[b].T], axis=1))
        cv = np.ascontiguousarray(np.stack([_fm(c[b]), _fm(c_ctx)], -1))
        d = dict(shared)
        d["xT"] = xt
        d["cvec"] = cv
        per_core.append(d)
    return per_core


def build(n_layers=L_, debug=False, stop_after=None):
    nc = bass.Bass("TRN2", target_bir_lowering=False)
    S = Sched(nc)
    uid = [0]

    def din(name, shape):
        return nc.dram_tensor(name, list(shape), F32, kind="ExternalInput").ap()

    def dscr(name, shape, dt):
        kind = "ExternalOutput" if debug else "Internal"
        return nc.dram_tensor(name, list(shape), dt, kind=kind).ap()

    xT = din("xT", [D, T_])
    cvec = din("cvec", [128, 8, 2])
    w_ada = din("w_ada", [L_, D, 6 * D])
    b_ada = din("b_ada", [128, L_, 48])
    gains = din("gains", [128, 9, 8])
    w_in_ext = din("w_in_ext", [L_, D, NCOL])
    w_uq_ext = din("w_uq_ext", [L_, 256, 768])
    w_ukv_r = din("w_ukv_r", [L_, 128, 512])
    mla_g = din("mla_g", [128, L_, 3])
    sinkrow = din("sinkrow", [65, 32])
    rpb_tab = din("rpb_tab", [L_, 128, 4, 4608])
    w_out = din("w_out", [L_, D, D])
    w_mlp_in = din("w_mlp_in", [L_, D, 4 * D])
    w_mlp_out = din("w_mlp_out", [L_, 4 * D, D])
    ropeA = din("ropeA", [128, 2, S_])
    rope96 = din("rope96", [96, 2, S_])
    ropeR = din("ropeR", [32, 2, S_])
    amask_d = din("amask", [128, 2, 128])
    ident8_d = din("ident8", [128, 128])
    outT = nc.dram_tensor("outT", [D, S_], F32, kind="ExternalOutput").ap()

    xmid = dscr("xmid", [D, T_], F32)
    xres = dscr("xres", [D, T_], F32)
    QA = dscr("QA", [512, T_], BF16)
    KA = dscr("KA", [128, T_], BF16)
    QB = dscr("QB", [256, T_], BF16)
    KB = dscr("KB", [256, T_], BF16)
    QC = dscr("QC", [384, T_], BF16)
    KC = dscr("KC", [384, T_], BF16)
    VALL = dscr("VALL", [T_, VW], BF16)
    MIX = dscr("MIX", [D, T_], BF16)
    NG = 9
    db = {n: [Buf(f"{n}{g}") for g in range(NG)] for n in
          ["xT", "xmid", "xres", "QA", "KA", "QB", "KB", "QC", "KC", "VALL", "MIX"]}

    def gcols(g):
        return (g * 512, 512) if g < 8 else (S_, C_)

    glob = ExitStack()

    def alloc(stack, shape, dt, name=None):
        uid[0] += 1
        t = stack.enter_context(nc.sbuf_tensor(f"{name or 't'}_{uid[0]}", list(shape), dt))
        return TB(t, name or "t")

    def palloc(stack, shape, dt, name=None):
        uid[0] += 1
        t = stack.enter_context(nc.psum_tensor(f"{name or 'p'}_{uid[0]}", [128, 512], F32))
        tb = TB(t, name or "p")
        tb.b.psum = True
        return tb

    def mm(out_ap, lhsT, rhs, first, last, reads, wbuf, inc=None, **kw):
        S.op("pe", lambda e: e.matmul(out_ap, lhsT=lhsT, rhs=rhs, start=first, stop=last, **kw),
             reads=reads, writes=[wbuf], pe_acc=not first, inc=(last if inc is None else inc))

    ones_bf = alloc(glob, [128, 128], BF16, "ones_bf")
    ones_f = alloc(glob, [128, 64], F32, "ones_f")
    ident8 = alloc(glob, [128, 128], BF16, "ident8")
    amask = alloc(glob, [128, 2, 128], BF16, "amask")
    modv = alloc(glob, [128, L_, 6, 8, 2], F32, "modv")
    gain_sb = alloc(glob, [128, 9, 8], F32, "gains")
    mlag_sb = alloc(glob, [128, L_, 3], F32, "mlag")
    esink = alloc(glob, [65, 32], F32, "esink")
    epsc = alloc(glob, [128, 1], F32, "epsc")

    S.op("dve", lambda e: e.memset(ones_bf.t[:], 1.0), writes=[ones_bf.b])
    S.op("dve", lambda e: e.memset(ones_f.t[:], 1.0), writes=[ones_f.b])
    S.op("dve", lambda e: e.memset(epsc.t[:], EPS), writes=[epsc.b])
    S.dma("sp", gain_sb.t[:], gains, writes=[gain_sb.b])
    S.dma("sp", mlag_sb.t[:], mla_g, writes=[mlag_sb.b])
    S.dma("sp", esink.t[:], sinkrow, writes=[esink.b])
    S.op("act", lambda e: e.activation(out=esink.t[:], in_=esink.t[:], func=AF.Exp), reads=[esink.b], writes=[esink.b])

    with ExitStack() as ph:
        stg = alloc(ph, [128, 2, 128], F32, "stg")
        S.dma("sp", stg.t[:, 0, :], ident8_d, writes=[stg.b])
        S.op("dve", lambda e: e.tensor_copy(out=ident8.t[:], in_=stg.t[:, 0, :]), reads=[stg.b], writes=[ident8.b])
        S.dma("sp", stg.t[:], amask_d, reads=[], writes=[stg.b])
        S.op("dve", lambda e: e.tensor_copy(out=amask.t[:], in_=stg.t[:]), reads=[stg.b], writes=[amask.b])
        cv = alloc(ph, [128, 8, 2], F32, "cv")
        sv = alloc(ph, [128, 8, 2], F32, "sv")
        bfm = alloc(ph, [128, L_, 48], F32, "bfm")
        S.dma("sp", cv.t[:], cvec, writes=[cv.b])
        S.dma("sp", bfm.t[:], b_ada, writes=[bfm.b])
        S.op("act", lambda e: e.activation(out=sv.t[:], in_=cv.t[:], func=AF.Silu), reads=[cv.b], writes=[sv.b])
        wst = Ring([alloc(ph, [128, 8, 768], F32, "wst") for _ in range(2)])
        mps = Ring([palloc(ph, [128, 48, 2], F32, "mps") for _ in range(2)])
        mods = alloc(ph, [128, 48, 2], F32, "mods")
        for l in range(n_layers):
            pm = mps.next()
            wa = w_ada[l].rearrange("(kc p) n -> p kc n", p=128)
            for pc in range(8):
                w = wst.next()
                S.dma("sp", w.t[:], wa[:, :, pc * 768:(pc + 1) * 768], writes=[w.b])
                for cc in range(6):
                    ch = pc * 6 + cc
                    for kc in range(8):
                        mm(pm.t[:, 2 * ch:2 * ch + 2], w.t[:, kc, cc * 128:(cc + 1) * 128], sv.t[:, kc, :],
                           kc == 0, kc == 7, [w.b, sv.b], pm.b)
            S.op("dve", lambda e: e.tensor_tensor(
                out=mods.t[:], in0=pm.t[:, 0:96].rearrange("p (c j) -> p c j", j=2), in1=bfm.t[:, l, :].unsqueeze(2).broadcast_to([128, 48, 2]),
                op=ALU.add), reads=[pm.b, bfm.b], writes=[mods.b])
            for (k, src_m, gidx) in ((0, 1, l), (3, 4, 4 + l)):
                S.op("dve", lambda e: e.scalar_tensor_tensor(
                    out=modv.t[:, l, k, :, :], in0=mods.t[:, src_m * 8:(src_m + 1) * 8, :], scalar=1.0,
                    in1=gain_sb.t[:, gidx, :].unsqueeze(2).broadcast_to([128, 8, 2]),
                    op0=ALU.add, op1=ALU.mult), reads=[mods.b, gain_sb.b], writes=[modv.b])
            for (k, src_m) in ((1, 0), (2, 2), (4, 3), (5, 5)):
                S.op("dve", lambda e: e.tensor_copy(out=modv.t[:, l, k, :, :], in_=mods.t[:, src_m * 8:(src_m + 1) * 8, :]),
                     reads=[mods.b], writes=[modv.b])
    S.barrier()
    if stop_after == ("M", 0):
        if debug:
            dbg = nc.dram_tensor("dbg_modv", [128, L_ * 96], F32, kind="ExternalOutput").ap()
            S.dma("sp", dbg, modv.t[:].rearrange("p l k c j -> p (l k c j)"), reads=[modv.b])
        S.finish("sp")
        glob.close()
        return nc, S

    def norm_mod(xg, W, l, kG, kS, j, hT, hbufs, sq, ps_ss, lnv, rstd, tmps):
        S.op("dve", lambda e: e.tensor_tensor(out=sq.t[:, :, :W], in0=xg.t[:, :, :W], in1=xg.t[:, :, :W], op=ALU.mult),
             reads=[xg.b], writes=[sq.b])
        for kc in range(8):
            mm(ps_ss.t[:, :W], ones_bf.t[:], sq.t[:, kc, :W], kc == 0, kc == 7, [sq.b, ones_bf.b], ps_ss.b)
        S.op("act", lambda e: e.activation(out=lnv.t[:, :W], in_=ps_ss.t[:, :W], func=AF.Ln, scale=1.0 / D, bias=epsc.t[:, 0:1]),
             reads=[ps_ss.b, epsc.b], writes=[lnv.b])
        S.op("act", lambda e: e.activation(out=rstd.t[:, :W], in_=lnv.t[:, :W], func=AF.Exp, scale=-0.5),
             reads=[lnv.b], writes=[rstd.b])
        for kc in range(8):
            tm = tmps.next()
            S.op("dve", lambda e: e.scalar_tensor_tensor(
                out=tm.t[:, :W], in0=xg.t[:, kc, :W], scalar=modv.t[:, l, kG, kc, j:j + 1], in1=rstd.t[:, :W],
                op0=ALU.mult, op1=ALU.mult), reads=[xg.b, modv.b, rstd.b], writes=[tm.b])
            S.op("act", lambda e: e.activation(out=hT.t[:, kc, :W], in_=tm.t[:, :W], func=AF.Identity,
                                               bias=modv.t[:, l, kS, kc, j:j + 1], scale=1.0),
                 reads=[tm.b, modv.b], writes=[hbufs[kc]])

    def load_cast(stack_ring, dst_ap_fn, src_ap_fn, n_pieces, dst_buf, engs=("dve", "pool")):
        for i in range(n_pieces):
            st = stack_ring.next()
            src = src_ap_fn(i)
            dst = dst_ap_fn(i)
            shp = list(src.shape)
            view = st.t[:shp[0], :int(np.prod(shp[1:]))]
            if len(shp) == 3:
                view = view.rearrange("p (a b) -> p a b", a=shp[1])
            S.dma("sp", view, src, writes=[st.b])
            eng = engs[i % len(engs)]
            if eng == "act":
                S.op("act", lambda e: e.activation(out=dst, in_=view, func=AF.Copy), reads=[st.b], writes=[dst_buf], pe_acc=False)
            else:
                S.op(eng, lambda e: e.tensor_copy(out=dst, in_=view), reads=[st.b], writes=[dst_buf])

    for l in range(n_layers):
        last = (l == L_ - 1)
        xsrc, xsrc_n = (xT, "xT") if l == 0 else (xres, "xres")
        xs_fm = xsrc.rearrange("(kc p) t -> p kc t", p=128)

        with ExitStack() as ph:
            stg_ring = Ring([alloc(ph, [128, 2624], F32, "stg") for _ in range(2)])
            win = alloc(ph, [128, 8, NCOL], BF16, "win")
            wuq = alloc(ph, [128, 2, 768], BF16, "wuq")
            wukv = alloc(ph, [128, 512], BF16, "wukv")
            wie = w_in_ext[l].rearrange("(kc p) n -> p kc n", p=128)
            load_cast(stg_ring, lambda i: win.t[:, i, :], lambda i: wie[:, i, :], 8, win.b)
            wue = w_uq_ext[l].rearrange("(kc p) n -> p kc n", p=128)
            load_cast(stg_ring, lambda i: wuq.t[:, i, :], lambda i: wue[:, i, :], 2, wuq.b)
            load_cast(stg_ring, lambda i: wukv.t[:], lambda i: w_ukv_r[l], 1, wukv.b)

            xgs = Ring([alloc(ph, [128, 8, 512], F32, "xg") for _ in range(2)])
            hTs = [alloc(ph, [128, 8, 512], BF16, "hT") for _ in range(2)]
            hbs = [[Buf(f"h{i}_{k}") for k in range(8)] for i in range(2)]
            sq = alloc(ph, [128, 8, 512], BF16, "sq")
            lnv = alloc(ph, [128, 512], F32, "lnv")
            rstds = Ring([alloc(ph, [128, 512], F32, "rstd") for _ in range(2)])
            tmps = Ring([alloc(ph, [128, 512], F32, "tmp") for _ in range(3)])
            rtab = Ring([alloc(ph, [128, 6, 512], F32, "rtab") for _ in range(2)])
            outs = Ring([alloc(ph, [128, 512], BF16, "ost") for _ in range(6)])
            t1s = Ring([alloc(ph, [128, 512], F32, "t1") for _ in range(2)])
            t2s = Ring([alloc(ph, [128, 512], F32, "t2") for _ in range(2)])
            cq = alloc(ph, [128, 3, 512], F32, "cq")
            cqsq = alloc(ph, [128, 3, 512], BF16, "cqsq")
            cqn = alloc(ph, [128, 3, 512], BF16, "cqn")
            vsts = Ring([alloc(ph, [128, VW], BF16, "vst") for _ in range(2)])
            ps_ss = palloc(ph, [128, 512], F32, "ps_ss")
            ps_a = Ring([palloc(ph, [128, 512], F32, "ps_a") for _ in range(2)])
            ps_b = Ring([palloc(ph, [128, 512], F32, "ps_b") for _ in range(2)])
            ps_v = palloc(ph, [128, 384], F32, "ps_v")
            ps_v2 = palloc(ph, [128, 256], F32, "ps_v2")
            for v in vsts.items:
                S.op("dve", lambda e: e.memset(v.t[:], 1.0), writes=[v.b])

            def do_norm(g):
                c0, W = gcols(g)
                xg = xgs.next()
                S.dma("sp", xg.t[:, :, :W], xs_fm[:, :, c0:c0 + W], reads=[db[xsrc_n][g]], writes=[xg.b])
                norm_mod(xg, W, l, 0, 1, 0 if g < 8 else 1, hTs[g % 2], hbs[g % 2], sq, ps_ss, lnv, rstds.next(), tmps)

            def evac_copy(ps, M, W, dst_ap, dst_bufs, eng):
                if eng == "act":
                    S.op("act", lambda e: e.activation(out=dst_ap, in_=ps.t[:M, :W], func=AF.Copy), reads=[ps.b], writes=dst_bufs)
                else:
                    S.op(eng, lambda e: e.tensor_copy(out=dst_ap, in_=ps.t[:M, :W]), reads=[ps.b], writes=dst_bufs)

            def proj_fm(g, hT, hb, col, scol, M, rhs_fn, nk, wt, wb, rope_idx, dst_list, rt):
                c0, W = gcols(g)
                pa = ps_a.next()
                for kc in range(nk):
                    mm(pa.t[:M, :W], wt(kc, col, M), rhs_fn(kc, W), kc == 0, kc == nk - 1, [wb] + hb, pa.b)
                o = outs.next()
                if rope_idx is None or g == 8:
                    evac_copy(pa, M, W, o.t[:M, :W], [o.b], "act" if (col // 128) % 2 == 0 else "dve")
                else:
                    pb = ps_b.next()
                    for kc in range(nk):
                        mm(pb.t[:M, :W], wt(kc, scol, M), rhs_fn(kc, W), kc == 0, kc == nk - 1, [wb] + hb, pb.b)
                    t1, t2 = t1s.next(), t2s.next()
                    S.op("dve", lambda e: e.tensor_tensor(out=t1.t[:M, :W], in0=pa.t[:M, :W], in1=rt.t[:M, rope_idx, :W], op=ALU.mult),
                         reads=[pa.b, rt.b], writes=[t1.b])
                    S.op("dve", lambda e: e.tensor_tensor(out=t2.t[:M, :W], in0=pb.t[:M, :W], in1=rt.t[:M, rope_idx + 1, :W], op=ALU.mult),
                         reads=[pb.b, rt.b], writes=[t2.b])
                    S.op("pool", lambda e: e.tensor_tensor(out=o.t[:M, :W], in0=t1.t[:M, :W], in1=t2.t[:M, :W], op=ALU.add),
                         reads=[t1.b, t2.b], writes=[o.b])
                for (dst, dbuf, p0, p1) in dst_list:
                    S.dma("sp", dst[:, c0:c0 + W], o.t[p0:p1, :W], reads=[o.b], writes=[dbuf])

            def do_proj(g):
                c0, W = gcols(g)
                hT, hb = hTs[g % 2], hbs[g % 2]
                rt = None
                if g < 8:
                    rt = rtab.next()
                    S.dma("sp", rt.t[:, 0:2, :], ropeA[:, :, c0:c0 + W], writes=[rt.b])
                    S.dma("sp", rt.t[:96, 2:4, :], rope96[:, :, c0:c0 + W], writes=[rt.b])
                    S.dma("sp", rt.t[:32, 4:6, :], ropeR[:, :, c0:c0 + W], writes=[rt.b])
                wt_in = lambda kc, col, M: win.t[:, kc, col:col + M]
                rhs_h = lambda kc, W_: hT.t[:, kc, :W_]
                for c in range(4):
                    proj_fm(g, hT, hb, O_QA + c * 128, O_QAS + c * 128, 128, rhs_h, 8, wt_in, win.b, 0,
                            [(QA[c * 128:(c + 1) * 128, :], db["QA"][g], 0, 128)], rt)
                proj_fm(g, hT, hb, O_KA, O_KAS, 128, rhs_h, 8, wt_in, win.b, 0, [(KA[:, :], db["KA"][g], 0, 128)], rt)
                for c in range(2):
                    proj_fm(g, hT, hb, O_QB + c * 128, None, 128, rhs_h, 8, wt_in, win.b, None,
                            [(QB[c * 128:(c + 1) * 128, :], db["QB"][g], 0, 128)], rt)
                for c in range(2):
                    proj_fm(g, hT, hb, O_KB + c * 128, None, 128, rhs_h, 8, wt_in, win.b, None,
                            [(KB[c * 128:(c + 1) * 128, :], db["KB"][g], 0, 128)], rt)
                proj_fm(g, hT, hb, O_KR, O_KRS, 32, rhs_h, 8, wt_in, win.b, 4,
                        [(KC[h * 96 + 64:h * 96 + 96, :], db["KC"][g], 0, 32) for h in range(4)], rt)
                for c in range(3):
                    pa = ps_a.next()
                    col = O_CQ + c * 128
                    for kc in range(8):
                        mm(pa.t[:, :W], win.t[:, kc, col:col + 128], hT.t[:, kc, :W], kc == 0, kc == 7, [win.b] + hb, pa.b)
                    S.op("dve", lambda e: e.tensor_copy(out=cq.t[:, c, :W], in_=pa.t[:, :W]), reads=[pa.b], writes=[cq.b])
                    S.op("act", lambda e: e.activation(out=cqsq.t[:, c, :W], in_=cq.t[:, c, :W], func=AF.Square), reads=[cq.b], writes=[cqsq.b])
                for (cs, n, gi0) in (((0, 1), 256, 0), ((2,), 128, 2)):
                    for i, c in enumerate(cs):
                        mm(ps_ss.t[:, :W], ones_bf.t[:], cqsq.t[:, c, :W], i == 0, i == len(cs) - 1, [cqsq.b, ones_bf.b], ps_ss.b)
                    rs = rstds.next()
                    S.op("act", lambda e: e.activation(out=lnv.t[:, :W], in_=ps_ss.t[:, :W], func=AF.Ln, scale=1.0 / n, bias=epsc.t[:, 0:1]),
                         reads=[ps_ss.b, epsc.b], writes=[lnv.b])
                    S.op("act", lambda e: e.activation(out=rs.t[:, :W], in_=lnv.t[:, :W], func=AF.Exp, scale=-0.5),
                         reads=[lnv.b], writes=[rs.b])
                    for c in cs:
                        S.op("dve", lambda e: e.scalar_tensor_tensor(
                            out=cqn.t[:, c, :W], in0=cq.t[:, c, :W], scalar=mlag_sb.t[:, l, c:c + 1], in1=rs.t[:, :W],
                            op0=ALU.mult, op1=ALU.mult), reads=[cq.b, mlag_sb.b, rs.b], writes=[cqn.b])
                wt_uq = lambda kc, col, M: wuq.t[:, kc, col:col + M]
                rhs_cq = lambda kc, W_: cqn.t[:, kc, :W_]
                for h in range(4):
                    proj_fm(g, cqn, [cqn.b], h * 96, 384 + h * 96, 96, rhs_cq, 2, wt_uq, wuq.b, 2,
                            [(QC[h * 96:(h + 1) * 96, :], db["QC"][g], 0, 96)], rt)
                wt_kv = lambda kc, col, M: wukv.t[:, col:col + M]
                rhs_kv = lambda kc, W_: cqn.t[:, 2, :W_]
                for c in range(2):
                    proj_fm(g, cqn, [cqn.b], c * 128, None, 128, rhs_kv, 1, wt_kv, wukv.b, None,
                            [(KC[(2 * c) * 96:(2 * c) * 96 + 64, :], db["KC"][g], 0, 64),
                             (KC[(2 * c + 1) * 96:(2 * c + 1) * 96 + 64, :], db["KC"][g], 64, 128)], rt)
                for tt in range(W // 128):
                    ts = slice(tt * 128, (tt + 1) * 128)
                    for kc in range(8):
                        mm(ps_v.t[:, 0:384], hT.t[:, kc, ts], win.t[:, kc, O_VA:O_VA + 384], kc == 0, kc == 7, [win.b] + hb, ps_v.b)
                    mm(ps_v2.t[:, 0:256], cqn.t[:, 2, ts], wukv.t[:, 256:512], True, True, [wukv.b, cqn.b], ps_v2.b)
                    v = vsts.next()
                    vv = v.t[:].rearrange("p (h c) -> p h c", c=65)
                    S.op("dve", lambda e: e.tensor_copy(out=vv[:, 0:6, 0:64], in_=ps_v.t[:, 0:384].rearrange("p (h c) -> p h c", c=64)),
                         reads=[ps_v.b], writes=[v.b])
                    S.op("act", lambda e: e.activation(out=vv[:, 6:10, 0:64], in_=ps_v2.t[:, 0:256].rearrange("p (h c) -> p h c", c=64), func=AF.Copy),
                         reads=[ps_v2.b], writes=[v.b])
                    S.dma("sp", VALL[c0 + tt * 128:c0 + (tt + 1) * 128, :], v.t[:], reads=[v.b], writes=[db["VALL"][g]])

            sub = stop_after[0] if (stop_after and stop_after[1] == l) else None
            if sub != "P0":
                do_norm(0)
                for g in range(NG):
                    if sub == "P1":
                        break
                    if g + 1 < NG:
                        do_norm(g + 1)
                    do_proj(g)
                    if sub == "P2":
                        break
        S.barrier()
        if stop_after in (("P", l), ("P0", l), ("P1", l), ("P2", l)):
            break

        qgroups = list(range(8)) + ([] if last else [8])
        mixf = MIX.rearrange("(h d) t -> d h t", d=64)

        def norm1(st, O, W, add_sink=None, use_act=False):
            dsum, rinv, bcs = st
            hv = (lambda a: a.rearrange("p (h t) -> p h t", h=4)) if add_sink is not None else (lambda a: a)
            if add_sink is None:
                S.op("dve", lambda e: e.tensor_copy(out=dsum.t[64:65, :W], in_=O.t[64:65, :W]), reads=[O.b], writes=[dsum.b])
            else:
                S.op("dve", lambda e: e.tensor_tensor(out=hv(dsum.t[64:65, :W]), in0=hv(O.t[64:65, :W]), in1=add_sink, op=ALU.add),
                     reads=[O.b, esrow.b], writes=[dsum.b])
            if use_act:
                S.op("act", lambda e: e.activation(out=dsum.t[64:65, :W], in_=dsum.t[64:65, :W], func=AF.Ln), reads=[dsum.b], writes=[dsum.b])
                S.op("act", lambda e: e.activation(out=rinv.t[64:65, :W], in_=dsum.t[64:65, :W], func=AF.Exp, scale=-1.0), reads=[dsum.b], writes=[rinv.b])
            else:
                S.op("dve", lambda e: e.reciprocal(out=rinv.t[64:65, :W], in_=dsum.t[64:65, :W]), reads=[dsum.b], writes=[rinv.b])

        def norm2(st, ps_bc, O, W, og_ap, og_buf, heads4=False):
            dsum, rinv, bcs = st
            hv = (lambda a: a.rearrange("p (h t) -> p h t", h=4)) if heads4 else (lambda a: a)
            mm(ps_bc.t[:64, :W], ones_f.t[64:65, 0:64], rinv.t[64:65, :W], True, True, [rinv.b, ones_f.b], ps_bc.b)
            S.op("dve", lambda e: e.tensor_copy(out=bcs.t[:64, :W], in_=ps_bc.t[:64, :W]), reads=[ps_bc.b], writes=[bcs.b])
            S.op("dve", lambda e: e.tensor_tensor(out=og_ap, in0=hv(O.t[:64, :W]), in1=hv(bcs.t[:64, :W]), op=ALU.mult),
                 reads=[O.b, bcs.b], writes=[og_buf])

        def nst_ring(ph, n):
            return Ring([(alloc(ph, [65, 512], F32, "dsum"), alloc(ph, [65, 512], F32, "rinv"), alloc(ph, [64, 512], F32, "bcs"))
                         for _ in range(n)])

        with ExitStack() as ph:
            KAs = alloc(ph, [64, 2, T_], BF16, "KAs")
            VAs = alloc(ph, [128, 34, 130], BF16, "VAs")
            esrow = alloc(ph, [65, 8, 128], F32, "esrow")
            S.dma("sp", KAs.t[:], KA.rearrange("(h d) t -> d h t", d=64), reads=db["KA"], writes=[KAs.b])
            vall_b = VALL.rearrange("(b p) c -> p b c", p=128)
            for i in range(0, 34, 9):
                S.dma("sp", VAs.t[:, i:min(i + 9, 34), :], vall_b[:, i:min(i + 9, 34), 0:130], reads=db["VALL"], writes=[VAs.b])
            S.op("dve", lambda e: e.memset(esrow.t[:], 0.0), writes=[esrow.b])
            for h in range(8):
                S.op("dve", lambda e: e.tensor_scalar(out=esrow.t[64:65, h, :], in0=esrow.t[64:65, h, :],
                                                      scalar1=esink.t[64:65, l * 8 + h:l * 8 + h + 1], scalar2=None, op0=ALU.add),
                     reads=[esink.b], writes=[esrow.b])
            Qgs = Ring([alloc(ph, [64, 8, 512], BF16, "Qg") for _ in range(2)])
            ogs = Ring([alloc(ph, [64, 8, 512], BF16, "og") for _ in range(2)])
            pts = Ring([alloc(ph, [128, 512], BF16, "pt") for _ in range(6)])
            nsts = nst_ring(ph, 3)
            ps_bc = palloc(ph, [64, 512], F32, "ps_bc")
            ps_s = Ring([palloc(ph, [128, 512], F32, "ps_s") for _ in range(4)])
            ps_o = Ring([palloc(ph, [65, 512], F32, "ps_o") for _ in range(3)])
            pipe = Pipe(2, 3)
            qaf = QA.rearrange("(h d) t -> d h t", d=64)

            def loadQ(g):
                c0, W = gcols(g)
                Qg = Qgs.next()
                S.dma("sp", Qg.t[:, :, :W], qaf[:, :, c0:c0 + W], reads=[db["QA"][g]], writes=[Qg.b])
                return Qg

            def stepA(Qg, O, kv, bi, kb, m, first, lastk, fin):
                cell = {}
                rhs = Qg.t[:, 4 * kv:4 * kv + 4, bi * 128:(bi + 1) * 128]

                def s1():
                    ps = ps_s.next()
                    psv = ps.t[:].rearrange("p (h t) -> p h t", h=4)
                    mm(psv, KAs.t[:, kv, kb * 128:(kb + 1) * 128], rhs, True, m is None, [KAs.b, Qg.b], ps.b)
                    if m is not None:
                        mm(psv, ident8.t[:], amask.t[:, m:m + 1, :].broadcast_to([128, 4, 128]), False, True,
                           [ident8.b, amask.b], ps.b)
                    pt = pts.next()
                    S.op("act", lambda e: e.activation(out=pt.t[:], in_=ps.t[:], func=AF.Exp, scale=0.125),
                         reads=[ps.b], writes=[pt.b])
                    cell["pt"] = pt

                def s2():
                    pt = cell["pt"]
                    if first:
                        pipe.force(O)
                    mm(O.t[:65, :], VAs.t[:, kb, kv * 65:(kv + 1) * 65], pt.t[:], first, lastk, [VAs.b, pt.b], O.b, inc=True)
                    if lastk:
                        fin()
                return s1, s2

            nxtQ = loadQ(qgroups[0])
            for gi, g in enumerate(qgroups):
                c0, W = gcols(g)
                Qg, og = nxtQ, ogs.next()
                if gi + 1 < len(qgroups):
                    nxtQ = loadQ(qgroups[gi + 1])
                for kv in range(2):
                    for bi in range(W // 128):
                        n = g * 4 + bi
                        if g < 8:
                            kbs = [(kb, m) for kb, m in ((n - 1, 0), (n, None), (n + 1, 1)) if 0 <= kb < 32] + [(32, None), (33, None)]
                        else:
                            kbs = [(32, None), (33, None)]
                        O = ps_o.next()
                        ogv = og.t[:, 4 * kv:4 * kv + 4, bi * 128:(bi + 1) * 128]

                        lastacc = (kv == 1 and bi == W // 128 - 1)

                        def fin(O=O, ogv=ogv, og=og, kv=kv, lastacc=lastacc, c0=c0, W=W, g=g):
                            st = nsts.next()
                            norm1(st, O, 512, add_sink=esrow.t[64:65, 4 * kv:4 * kv + 4, :], use_act=True)
                            pipe.defer(lambda: norm2(st, ps_bc, O, 512, ogv, og.b, heads4=True), tag=O)
                            if lastacc:
                                pipe.defer(lambda: S.dma("sp", mixf[:, 0:8, c0:c0 + W], og.t[:, :, :W], reads=[og.b], writes=[db["MIX"][g]]))
                        for ki, (kb, m) in enumerate(kbs):
                            pipe.push(*stepA(Qg, O, kv, bi, kb, m, ki == 0, ki == len(kbs) - 1, fin))
            pipe.flush()
        S.barrier()
        if stop_after == ("TA", l):
            break

        with ExitStack() as ph:
            KBs = alloc(ph, [64, 4, T_], BF16, "KBs")
            VBc = alloc(ph, [128, 2, 260], BF16, "VBc")
            bm = alloc(ph, [128, 4, 4608], BF16, "bm")
            stg_ring = Ring([alloc(ph, [128, 4608], F32, "stgb") for _ in range(2)])
            S.dma("sp", KBs.t[:], KB.rearrange("(h d) t -> d h t", d=64), reads=db["KB"], writes=[KBs.b])
            vall_b = VALL.rearrange("(b p) c -> p b c", p=128)
            S.dma("sp", VBc.t[:], vall_b[:, 32:34, 130:390], reads=db["VALL"], writes=[VBc.b])
            load_cast(stg_ring, lambda i: bm.t[:, i, :], lambda i: rpb_tab[l, :, i, :], 4, bm.b)
            Qgs = Ring([alloc(ph, [64, 4, 512], BF16, "Qg") for _ in range(2)])
            ogs = Ring([alloc(ph, [64, 4, 512], BF16, "og") for _ in range(2)])
            pts = Ring([alloc(ph, [128, 512], BF16, "pt") for _ in range(6)])
            vgs = Ring([alloc(ph, [128, 260], BF16, "vg") for _ in range(8)])
            kgts = Ring([alloc(ph, [64, 4, 128], BF16, "kgt") for _ in range(4)])
            nsts = nst_ring(ph, 4)
            ps_bc = palloc(ph, [64, 512], F32, "ps_bc")
            ps_s = Ring([palloc(ph, [128, 512], F32, "ps_s") for _ in range(3)])
            ps_o = [palloc(ph, [65, 512], F32, "ps_o") for _ in range(4)]
            pipe = Pipe(2, 2)
            qbf = QB.rearrange("(h d) t -> d h t", d=64)

            def loadQ(g):
                c0, W = gcols(g)
                Qg = Qgs.next()
                S.dma("sp", Qg.t[:, :, :W], qbf[:, :, c0:c0 + W], reads=[db["QB"][g]], writes=[Qg.b])
                return Qg

            def stepBctx(Qg, O, h, ki, cb, W, lastk, fin):
                cell = {}

                def s1():
                    ps = ps_s.next()
                    mm(ps.t[:, :W], KBs.t[:, h, cb * 128:(cb + 1) * 128], Qg.t[:, h, :W], True, True, [KBs.b, Qg.b], ps.b)
                    pt = pts.next()
                    S.op("act", lambda e: e.activation(out=pt.t[:, :W], in_=ps.t[:, :W], func=AF.Exp, scale=0.125),
                         reads=[ps.b], writes=[pt.b])
                    cell["pt"] = pt

                def s2():
                    pt = cell["pt"]
                    if ki == 0:
                        pipe.force(O)
                    mm(O.t[:65, :W], VBc.t[:, ki, h * 65:(h + 1) * 65], pt.t[:, :W], ki == 0, lastk,
                       [VBc.b, pt.b], O.b, inc=True, skip_group_check=True)
                    if lastk:
                        fin()
                return s1, s2

            def stepBloc(Qg, j, kg, pat, kgt, vg, M, lastk, fins):
                cell = {}

                def s1():
                    ps = ps_s.next()
                    boff = (pat * 4 + kg) * 128
                    mm(ps.t[:M, :].rearrange("p (h t) -> p h t", h=4), ident8.t[:M, :M], bm.t[:M, :, boff:boff + 128],
                       True, False, [ident8.b, bm.b], ps.b)
                    for h in range(4):
                        qview = Qg.t[:, h, :].rearrange("p (r c) -> p r c", c=64)[:, :, 16 * j:16 * j + 16]
                        psv = ps.t[:M, h * 128:(h + 1) * 128].rearrange("p (r c) -> p r c", c=16)
                        mm(psv, kgt.t[:, h, :M], qview, False, h == 3, [kgt.b, Qg.b], ps.b, skip_group_check=True)
                    pt = pts.next()
                    S.op("act", lambda e: e.activation(out=pt.t[:M, :], in_=ps.t[:M, :], func=AF.Exp, scale=0.125),
                         reads=[ps.b], writes=[pt.b])
                    cell["pt"] = pt

                def s2():
                    pt = cell["pt"]
                    for h in range(4):
                        O = ps_o[h]
                        ov = O.t[:65, :].rearrange("p (r c) -> p r c", c=64)[:, :, 16 * j:16 * j + 16]
                        ptv = pt.t[:M, h * 128:(h + 1) * 128].rearrange("p (r c) -> p r c", c=16)
                        mm(ov, vg.t[:M, h * 65:(h + 1) * 65], ptv, False, lastk, [vg.b, pt.b], O.b,
                           inc=(h == 3), skip_group_check=True)
                    if lastk:
                        for f in fins:
                            f()
                return s1, s2

            nxtQ = loadQ(qgroups[0])
            for gi, g in enumerate(qgroups):
                c0, W = gcols(g)
                Qg, og = nxtQ, ogs.next()
                if gi + 1 < len(qgroups):
                    nxtQ = loadQ(qgroups[gi + 1])

                def mkfin(h, og=og, W=W, c0=c0, g=g):
                    def fin():
                        st = nsts.next()
                        O = ps_o[h]
                        norm1(st, O, W)
                        pipe.defer(lambda: norm2(st, ps_bc, O, W, og.t[:, h, :W], og.b), tag=O)
                        if h == 3:
                            pipe.defer(lambda: S.dma("sp", mixf[:, 8:12, c0:c0 + W], og.t[:, :, :W], reads=[og.b], writes=[db["MIX"][g]]))
                    return fin
                for h in range(4):
                    for ki, cb in enumerate((32, 33)):
                        pipe.push(*stepBctx(Qg, ps_o[h], h, ki, cb, W, (g == 8 and ki == 1), mkfin(h)))
                if g < 8:
                    i = g
                    r0 = int(NA_KR0[i])
                    for j in range(4):
                        cc0 = int(NA_KC0[j])
                        pat = ICLS[i] * 3 + JCLS[j]
                        c0w = min(cc0, 32)
                        for kg in range(4):
                            nr = KG_ROWS[kg]
                            M = nr * 32
                            vg = vgs.next()
                            tok0 = (r0 + 4 * kg) * 64 + c0w
                            for rr in range(nr):
                                S.dma("sp", vg.t[rr * 32:(rr + 1) * 32, :], VALL[tok0 + rr * 64:tok0 + rr * 64 + 32, 130:390],
                                      reads=db["VALL"], writes=[vg.b])
                            kgt = kgts.next()
                            S.op("pool", lambda e: e.tensor_copy(
                                out=kgt.t[:, :, :M].rearrange("p h (r c) -> p h r c", c=32),
                                in_=KBs.t[:, :, tok0:tok0 + nr * 64].rearrange("p h (r c) -> p h r c", c=64)[:, :, :, 0:32]),
                                reads=[KBs.b], writes=[kgt.b])
                            pipe.push(*stepBloc(Qg, j, kg, pat, kgt, vg, M, (j == 3 and kg == 3), [mkfin(h) for h in range(4)]))
            pipe.flush()
        S.barrier()
        if stop_after == ("TB", l):
            break

        wsc = ExitStack()
        stgW = Ring([alloc(wsc, [128, 2048], F32, "stgW") for _ in range(2)])
        w1 = alloc(wsc, [128, 8, 4 * D], BF16, "w1")
        w1d = w_mlp_in[l].rearrange("(kc p) n -> p kc n", p=128)
        w2d = w_mlp_out[l].rearrange("(f p) n -> p f n", p=128)
        wpieces = []

        def piece(dst, src, dbuf, eng):
            def f():
                st = stgW.next()
                shp = list(src.shape)
                view = st.t[:, :int(np.prod(shp[1:]))]
                if len(shp) == 3:
                    view = view.rearrange("p (a b) -> p a b", a=shp[1])
                S.dma("sp", view, src, writes=[st.b])
                if eng == "act":
                    S.op("act", lambda e: e.activation(out=dst, in_=view, func=AF.Copy), reads=[st.b], writes=[dbuf])
                else:
                    S.op(eng, lambda e: e.tensor_copy(out=dst, in_=view), reads=[st.b], writes=[dbuf])
            return f

        def emit_pieces(n):
            for _ in range(n):
                if wpieces:
                    wpieces.pop(0)()
        for i in range(16):
            wpieces.append(piece(w1.t[:, i // 2, (i % 2) * 2048:(i % 2 + 1) * 2048],
                                 w1d[:, i // 2, (i % 2) * 2048:(i % 2 + 1) * 2048], w1.b, "pool"))

        with ExitStack() as ph:
            KCs = alloc(ph, [96, 4, T_], BF16, "KCs")
            VCs = alloc(ph, [128, 34, 260], BF16, "VCs")
            S.dma("sp", KCs.t[:], KC.rearrange("(h d) t -> d h t", d=96), reads=db["KC"], writes=[KCs.b])
            vall_b = VALL.rearrange("(b p) c -> p b c", p=128)
            for i in range(0, 34, 9):
                S.dma("sp", VCs.t[:, i:min(i + 9, 34), :], vall_b[:, i:min(i + 9, 34), 390:650], reads=db["VALL"], writes=[VCs.b])
            Qgs = Ring([alloc(ph, [96, 4, 512], BF16, "Qg") for _ in range(2)])
            ogs = Ring([alloc(ph, [64, 4, 512], BF16, "og") for _ in range(2)])
            pts = Ring([alloc(ph, [128, 512], BF16, "pt") for _ in range(6)])
            nsts = nst_ring(ph, 2)
            ps_bc = palloc(ph, [64, 512], F32, "ps_bc")
            ps_s = Ring([palloc(ph, [128, 512], F32, "ps_s") for _ in range(4)])
            ps_o = Ring([palloc(ph, [65, 512], F32, "ps_o") for _ in range(2)])
            sc = float(96 ** -0.5)
            pipe = Pipe(3, 4)
            qcf = QC.rearrange("(h d) t -> d h t", d=96)

            def loadQ(g):
                c0, W = gcols(g)
                Qg = Qgs.next()
                S.dma("sp", Qg.t[:, :, :W], qcf[:, :, c0:c0 + W], reads=[db["QC"][g]], writes=[Qg.b])
                return Qg

            def stepC(Qg, O, h, kb, W, first, lastk, fin):
                cell = {}

                def s1():
                    ps = ps_s.next()
                    mm(ps.t[:, :W], KCs.t[:, h, kb * 128:(kb + 1) * 128], Qg.t[:, h, :W], True, True, [KCs.b, Qg.b], ps.b)
                    pt = pts.next()
                    S.op("act", lambda e: e.activation(out=pt.t[:, :W], in_=ps.t[:, :W], func=AF.Exp, scale=sc),
                         reads=[ps.b], writes=[pt.b])
                    cell["pt"] = pt

                def s2():
                    pt = cell["pt"]
                    if first:
                        pipe.force(O)
                    mm(O.t[:65, :W], VCs.t[:, kb, h * 65:(h + 1) * 65], pt.t[:, :W], first, lastk, [VCs.b, pt.b], O.b, inc=True)
                    if lastk:
                        fin()
                return s1, s2

            nxtQ = loadQ(qgroups[0])
            for gi, g in enumerate(qgroups):
                c0, W = gcols(g)
                Qg, og = nxtQ, ogs.next()
                if gi + 1 < len(qgroups):
                    nxtQ = loadQ(qgroups[gi + 1])
                kbs = list(range(34)) if g < 8 else [32, 33]
                for h in range(4):
                    O = ps_o.next()

                    def fin(O=O, og=og, h=h, W=W, c0=c0, g=g):
                        st = nsts.next()
                        norm1(st, O, W)
                        pipe.defer(lambda: norm2(st, ps_bc, O, W, og.t[:, h, :W], og.b), tag=O)
                        if h == 3:
                            pipe.defer(lambda: S.dma("sp", mixf[:, 12:16, c0:c0 + W], og.t[:, :, :W], reads=[og.b], writes=[db["MIX"][g]]))
                    for ki, kb in enumerate(kbs):
                        pipe.push(*stepC(Qg, O, h, kb, W, ki == 0, ki == len(kbs) - 1, fin))
                    if (gi * 4 + h) % 2 == 1:
                        emit_pieces(1)
            pipe.flush()
            emit_pieces(len(wpieces))
        S.barrier()
        if stop_after == ("TC", l):
            wsc.close()
            break

        w2 = alloc(wsc, [128, 32, D], BF16, "w2")
        for i in range(16):
            wpieces.append(piece(w2.t[:, 2 * i:2 * i + 2, :], w2d[:, 2 * i:2 * i + 2, :], w2.b, "pool" if i % 2 else "act"))
        WG = 256
        ngr = 16 + (0 if last else 1)
        with ExitStack() as ph:
            wo = alloc(ph, [128, 8, D], BF16, "wo")
            wod = w_out[l].rearrange("(kc p) n -> p kc n", p=128)
            for i in range(4):
                piece(wo.t[:, 2 * i:2 * i + 2, :], wod[:, 2 * i:2 * i + 2, :], wo.b, "pool" if i % 2 else "act")()
            mgs = Ring([alloc(ph, [128, 8, WG], BF16, "mixg") for _ in range(2)])
            xgs = Ring([alloc(ph, [128, 8, WG], F32, "xg") for _ in range(2)])
            ps_y = Ring([palloc(ph, [128, 512], F32, "ps_y") for _ in range(4)])
            mix_fm = MIX.rearrange("(kc p) t -> p kc t", p=128)
            xm_fm = xmid.rearrange("(kc p) t -> p kc t", p=128)

            def loadO1(gg):
                c0 = gg * WG
                g = gg // 2 if gg < 16 else 8
                mg, xg = mgs.next(), xgs.next()
                S.dma("sp", mg.t[:], mix_fm[:, :, c0:c0 + WG], reads=[db["MIX"][g]], writes=[mg.b])
                S.dma("sp", xg.t[:], xs_fm[:, :, c0:c0 + WG], reads=[db[xsrc_n][g]], writes=[xg.b])
                return mg, xg
            nxt = loadO1(0)
            for gg in range(ngr):
                c0 = gg * WG
                g = gg // 2 if gg < 16 else 8
                j = 0 if gg < 16 else 1
                mg, xg = nxt
                if gg + 1 < ngr:
                    nxt = loadO1(gg + 1)
                for c in range(8):
                    py = ps_y.next()
                    for kc in range(8):
                        mm(py.t[:, :WG], wo.t[:, kc, c * 128:(c + 1) * 128], mg.t[:, kc, :], kc == 0, kc == 7, [wo.b, mg.b], py.b)
                    S.op("dve", lambda e: e.scalar_tensor_tensor(
                        out=xg.t[:, c, :], in0=py.t[:, :WG], scalar=modv.t[:, l, 2, c, j:j + 1], in1=xg.t[:, c, :],
                        op0=ALU.mult, op1=ALU.add), reads=[py.b, modv.b, xg.b], writes=[xg.b])
                S.dma("sp", xm_fm[:, :, c0:c0 + WG], xg.t[:], reads=[xg.b], writes=[db["xmid"][g]])
                emit_pieces(1)
            emit_pieces(len(wpieces))
        S.barrier()
        if stop_after == ("O1", l):
            wsc.close()
            break

        with ExitStack() as ph:
            xgs = Ring([alloc(ph, [128, 8, WG], F32, "xg") for _ in range(2)])
            hT2 = alloc(ph, [128, 8, WG], BF16, "hT2")
            hb2 = [Buf(f"h2_{k}") for k in range(8)]
            sq = alloc(ph, [128, 8, WG], BF16, "sq")
            lnv = alloc(ph, [128, WG], F32, "lnv")
            rstd = alloc(ph, [128, WG], F32, "rstd")
            tmps = Ring([alloc(ph, [128, WG], F32, "tmp") for _ in range(3)])
            rl = Ring([alloc(ph, [128, WG], F32, "rl") for _ in range(3)])
            aT = alloc(ph, [128, 32, WG], BF16, "aT")
            abufs = [Buf(f"a{f}") for f in range(32)]
            ps_ss = palloc(ph, [128, WG], F32, "ps_ss")
            ps_u = Ring([palloc(ph, [128, WG], F32, "ps_u") for _ in range(3)])
            ps_y = Ring([palloc(ph, [128, WG], F32, "ps_y") for _ in range(2)])
            xm_fm = xmid.rearrange("(kc p) t -> p kc t", p=128)
            xr_fm = xres.rearrange("(kc p) t -> p kc t", p=128)
            out_fm = outT.rearrange("(kc p) t -> p kc t", p=128)
            for gg in range(ngr):
                c0 = gg * WG
                g = gg // 2 if gg < 16 else 8
                j = 0 if gg < 16 else 1
                xg = xgs.next()
                S.dma("sp", xg.t[:], xm_fm[:, :, c0:c0 + WG], reads=[db["xmid"][g]], writes=[xg.b])
                norm_mod(xg, WG, l, 3, 4, j, hT2, hb2, sq, ps_ss, lnv, rstd, tmps)
                for f in range(32):
                    pu = ps_u.next()
                    for kc in range(8):
                        mm(pu.t[:, :WG], w1.t[:, kc, f * 128:(f + 1) * 128], hT2.t[:, kc, :], kc == 0, kc == 7, [w1.b] + hb2, pu.b)
                    r = rl.next()
                    S.op("act", lambda e: e.activation(out=r.t[:], in_=pu.t[:, :WG], func=AF.Relu), reads=[pu.b], writes=[r.b])
                    S.op("pool" if f % 2 else "dve", lambda e: e.tensor_tensor(out=aT.t[:, f, :], in0=r.t[:], in1=r.t[:], op=ALU.mult),
                         reads=[r.b], writes=[abufs[f]])
                for c in range(8):
                    py = ps_y.next()
                    for f in range(32):
                        mm(py.t[:, :WG], w2.t[:, f, c * 128:(c + 1) * 128], aT.t[:, f, :], f == 0, f == 31, [w2.b, abufs[f]], py.b)
                    S.op("dve", lambda e: e.scalar_tensor_tensor(
                        out=xg.t[:, c, :], in0=py.t[:, :WG], scalar=modv.t[:, l, 5, c, j:j + 1], in1=xg.t[:, c, :],
                        op0=ALU.mult, op1=ALU.add), reads=[py.b, modv.b, xg.b], writes=[xg.b])
                if not last:
                    S.dma("sp", xr_fm[:, :, c0:c0 + WG], xg.t[:], reads=[xg.b], writes=[db["xres"][g]])
                else:
                    S.op("dve", lambda e: e.tensor_tensor(out=sq.t[:], in0=xg.t[:], in1=xg.t[:], op=ALU.mult), reads=[xg.b], writes=[sq.b])
                    for kc in range(8):
                        mm(ps_ss.t[:, :WG], ones_bf.t[:], sq.t[:, kc, :], kc == 0, kc == 7, [sq.b, ones_bf.b], ps_ss.b)
                    S.op("act", lambda e: e.activation(out=lnv.t[:], in_=ps_ss.t[:, :WG], func=AF.Ln, scale=1.0 / D, bias=epsc.t[:, 0:1]),
                         reads=[ps_ss.b, epsc.b], writes=[lnv.b])
                    S.op("act", lambda e: e.activation(out=rstd.t[:], in_=lnv.t[:], func=AF.Exp, scale=-0.5), reads=[lnv.b], writes=[rstd.b])
                    for kc in range(8):
                        S.op("dve", lambda e: e.scalar_tensor_tensor(
                            out=xg.t[:, kc, :], in0=xg.t[:, kc, :], scalar=gain_sb.t[:, 8, kc:kc + 1], in1=rstd.t[:],
                            op0=ALU.mult, op1=ALU.mult), reads=[xg.b, gain_sb.b, rstd.b], writes=[xg.b])
                    S.dma("sp", out_fm[:, :, c0:c0 + WG], xg.t[:], reads=[xg.b], writes=[])
        wsc.close()
        S.barrier()
        if stop_after == ("O2", l):
            break

    S.finish("sp")
    glob.close()
    return nc, S


def kernel(**inputs):
    per_core = prep_inputs(inputs)
    nc, _ = build()
    res = run_bass_kernel_spmd(nc, per_core, core_ids=list(range(8)))
    out = np.stack([np.ascontiguousarray(r["outT"].T) for r in res.results], axis=0)
    return out.astype(np.float32)
```

```python
import math
from contextlib import ExitStack
import numpy as np
import concourse.bass as bass
import concourse.mybir as mybir
from concourse.bass_utils import run_bass_kernel_spmd

F32 = mybir.dt.float32
BF16 = mybir.dt.bfloat16
AF = mybir.ActivationFunctionType
ALU = mybir.AluOpType

D = 1024
S_ = 4096
C_ = 256
T_ = S_ + C_
L_ = 4
NCOL = 2624
EPS = 1e-6
NEG = -30000.0
O_QA, O_QAS, O_KA, O_KAS, O_QB, O_KB, O_CQ, O_CKV, O_KR, O_KRS, O_VA = (
    0, 512, 1024, 1152, 1280, 1536, 1792, 2048, 2176, 2208, 2240)
VW = 650


class Buf:
    __slots__ = ("name", "lw", "rs", "plw", "prs", "psum")

    def __init__(self, name="", psum=False):
        self.name = name
        self.lw = {}
        self.rs = {}
        self.plw = {}
        self.prs = {}
        self.psum = psum


class Sched:
    def __init__(self, nc, n_dma_slots=32, same_engine_sync=True):
        self.nc = nc
        self.same = same_engine_sync
        self.eng = {"pe": nc.tensor, "act": nc.scalar, "dve": nc.vector, "pool": nc.gpsimd, "sp": nc.sync}
        self.sem = {k: nc.alloc_semaphore(name=f"sem_{k}") for k in self.eng}
        self.cnt = {k: 0 for k in self.eng}
        self.seen = {k: {} for k in self.eng}
        self.dsem = [nc.alloc_semaphore(name=f"dsem{i}") for i in range(n_dma_slots)]
        self.dcnt = [0] * n_dma_slots
        self.dnext = 0
        self.n_wait = 0
        self.n_inst = 0
        self.log = {k: [] for k in self.eng}

    def _semof(self, key):
        return self.dsem[key] if isinstance(key, int) else self.sem[key]

    def _wait(self, e, key, val):
        if key == e and not self.same:
            return
        if self.seen[e].get(key, 0) >= val:
            return
        self.eng[e].wait_ge(self._semof(key), val)
        self.log[e].append(("w", key, val))
        self.seen[e][key] = val
        self.n_wait += 1

    def _deps(self, e, reads, writes, part=False):
        for b in reads:
            for k, v in b.lw.items():
                self._wait(e, k, v)
            if b.psum:
                for k, v in b.rs.items():
                    if k != e:
                        self._wait(e, k, v)
        for b in writes:
            if part and not b.rs:
                for k, v in b.plw.items():
                    self._wait(e, k, v)
                for k, v in b.prs.items():
                    self._wait(e, k, v)
            else:
                for k, v in b.lw.items():
                    self._wait(e, k, v)
                for k, v in b.rs.items():
                    self._wait(e, k, v)

    def _record(self, key, n, reads, writes, part):
        for b in reads:
            b.rs[key] = max(b.rs.get(key, 0), n)
        for b in writes:
            if b.rs or not part:
                b.plw, b.prs = b.lw, b.rs
                b.lw = {}
            b.lw[key] = max(b.lw.get(key, 0), n)
            b.rs = {}

    def op(self, e, fn, reads=(), writes=(), pe_acc=False, inc=True, part=False):
        if pe_acc:
            self._deps(e, reads, ())
        else:
            self._deps(e, reads, writes, part)
        ins = fn(self.eng[e])
        if inc:
            ins.then_inc(self.sem[e], 1)
            self.log[e].append(("i", e, 1))
            self.cnt[e] += 1
            n = self.cnt[e]
        else:
            n = self.cnt[e] + 1
        self.n_inst += 1
        self._record(e, n, reads, writes, part or pe_acc)
        return ins

    def dma(self, q, out, in_, reads=(), writes=(), part=False, **kw):
        self._deps(q, reads, writes, part)
        s = self.dnext
        self.dnext = (self.dnext + 1) % len(self.dsem)
        if self.dcnt[s] > 0:
            self._wait(q, s, 16 * self.dcnt[s])
        ins = self.eng[q].dma_start(out=out, in_=in_, **kw)
        ins.then_inc(self.dsem[s], 16)
        self.log[q].append(("i", s, 16))
        self.dcnt[s] += 1
        v = 16 * self.dcnt[s]
        self.n_inst += 1
        self._record(s, v, reads, writes, part)
        return ins

    def barrier(self):
        for e in self.eng:
            self.finish(e)

    def finish(self, e="sp"):
        for k in self.eng:
            if k != e and self.cnt[k] > 0:
                self._wait(e, k, self.cnt[k])
        for s in range(len(self.dsem)):
            if self.dcnt[s] > 0:
                self._wait(e, s, 16 * self.dcnt[s])


class Pipe:
    def __init__(self, la, d):
        self.la, self.d = la, d
        self.q = []
        self.later = []
        self.step = 0

    def _fire(self, upto=None):
        while self.later and (upto is None or self.later[0][0] <= upto):
            self.later.pop(0)[1]()

    def _s2(self):
        self.q.pop(0)()
        self.step += 1
        self._fire(self.step)

    def push(self, s1, s2):
        s1()
        self.q.append(s2)
        if len(self.q) > self.la:
            self._s2()

    def defer(self, fn, tag=None):
        self.later.append((self.step + self.d, fn, tag))

    def force(self, tag):
        idx = [i for i, it in enumerate(self.later) if it[2] is tag]
        if idx:
            for _ in range(idx[-1] + 1):
                self.later.pop(0)[1]()

    def flush(self):
        while self.q:
            self._s2()
        self._fire(None)


class TB:
    def __init__(self, t, name=""):
        self.t = t
        self.b = Buf(name)


class Ring:
    def __init__(self, items):
        self.items = items
        self.i = 0

    def next(self):
        it = self.items[self.i]
        self.i = (self.i + 1) % len(self.items)
        return it


def _perm_a(n):
    idx = np.arange(n)
    h, d = idx // 64, idx % 64
    j = d % 32
    partner = np.where(j < 16, d + 16, d - 16)
    return h * 64 + partner


def _perm_r(n=32):
    d = np.arange(n)
    jj = d % 16
    return np.where(jj < 8, d + 8, d - 8)


def _rope_tables():
    tok = np.arange(S_)
    row, col = tok // 64, tok % 64

    def tab(ndim_sec, half):
        freqs = (10000.0 ** (-np.arange(half, dtype=np.float32) / half)).astype(np.float32)
        cos = np.zeros((2 * ndim_sec, S_), np.float32)
        sin = np.zeros((2 * ndim_sec, S_), np.float32)
        for d in range(2 * ndim_sec):
            sec, j = d // ndim_sec, d % ndim_sec
            pos = (row if sec == 0 else col).astype(np.float32)
            ang = pos * freqs[j % half]
            cos[d] = np.cos(ang).astype(np.float32)
            sn = np.sin(ang).astype(np.float32)
            sin[d] = -sn if j < half else sn
        return cos, sin

    ca, sa = tab(32, 16)
    cr, sr = tab(16, 8)
    ropeA = np.stack([np.concatenate([ca, ca], 0), np.concatenate([sa, sa], 0)], 1)
    c96 = np.concatenate([np.ones((64, S_), np.float32), cr], 0)
    s96 = np.concatenate([np.zeros((64, S_), np.float32), sr], 0)
    rope96 = np.stack([c96, s96], 1)
    ropeR = np.stack([cr, sr], 1)
    return np.ascontiguousarray(ropeA), np.ascontiguousarray(rope96), np.ascontiguousarray(ropeR)


def _na_layout():
    rows, GW, NR, NCc, NQ = 64, 64, 8, 16, 16
    kr, kc = min(NR, rows), NCc
    qr, qc = math.gcd(rows, NR), NQ
    krb, kcb = min(qr - 1 + kr, rows), min(qc - 1 + kc, GW)
    nrb, ncb = rows // qr, GW // qc
    q_r = np.arange(nrb)[:, None] * qr + np.arange(qr)[None, :]
    q_c = np.arange(ncb)[:, None] * qc + np.arange(qc)[None, :]
    w_r = np.clip(q_r - kr // 2, 0, rows - kr)
    w_c = np.clip(q_c - kc // 2, 0, GW - kc)
    k_r = np.minimum(w_r[:, 0], rows - krb)[:, None] + np.arange(krb)[None, :]
    k_c = np.minimum(w_c[:, 0], GW - kcb)[:, None] + np.arange(kcb)[None, :]
    qr6 = q_r[:, None, :, None, None, None]
    qc6 = q_c[None, :, None, :, None, None]
    wr6 = w_r[:, None, :, None, None, None]
    wc6 = w_c[None, :, None, :, None, None]
    kr6 = k_r[:, None, None, None, :, None]
    kc6 = k_c[None, :, None, None, None, :]
    shape6 = (nrb, ncb, qr, qc, krb, kcb)

    def flat(a):
        return np.broadcast_to(a, shape6).reshape(nrb, ncb, qr * qc, krb * kcb)

    mask = flat((kr6 >= wr6) & (kr6 < wr6 + kr) & (kc6 >= wc6) & (kc6 < wc6 + kc))
    d_r = flat(np.clip(kr6 - qr6, 1 - NR, NR - 1) + NR - 1)
    d_c = flat(np.clip(kc6 - qc6, 1 - NCc, NCc - 1) + NCc - 1)
    return mask, d_r, d_c, k_r[:, 0], k_c[:, 0]


NA_MASK, NA_DR, NA_DC, NA_KR0, NA_KC0 = _na_layout()
ICLS = [0, 1, 1, 1, 1, 1, 1, 2]
JCLS = [0, 1, 1, 2]
IREP = [0, 1, 7]
JREP = [0, 1, 3]
KG_ROWS = [4, 4, 4, 3]


def _rpb_table(na_rpb):
    tab = np.full((L_, 128, 4, 9, 4, 128), NEG, np.float32)
    for ic in range(3):
        for jc in range(3):
            i, j = IREP[ic], JREP[jc]
            m = NA_MASK[i, j]
            dr, dc = NA_DR[i, j], NA_DC[i, j]
            g = na_rpb[:, :, dr, dc]
            g = np.where(m[None, None], g, np.float32(NEG))
            kc0 = int(NA_KC0[j])
            c0w = min(kc0, 32)
            for kg in range(4):
                for rr in range(KG_ROWS[kg]):
                    for cc in range(32):
                        col = c0w + cc
                        if kc0 <= col < kc0 + 31:
                            kk = (4 * kg + rr) * 31 + (col - kc0)
                            tab[:, rr * 32 + cc, :, ic * 3 + jc, kg, :] = g[:, :, :, kk]
    return np.ascontiguousarray(tab.reshape(L_, 128, 4, 9 * 4 * 128))


def _fm(v):
    v = np.asarray(v, np.float32)
    lead = v.shape[:-1]
    c = v.shape[-1] // 128
    v = v.reshape(*lead, c, 128)
    return np.ascontiguousarray(np.moveaxis(v, -1, 0))


def prep_inputs(inp):
    f = lambda a: np.ascontiguousarray(np.asarray(a, np.float32))
    w_in = f(inp["w_in"])
    pa512, pa128, pr = _perm_a(512), _perm_a(128), _perm_r()
    ext = np.concatenate([
        w_in[:, :, 0:512], w_in[:, :, 0:512][:, :, pa512],
        w_in[:, :, 512:640], w_in[:, :, 512:640][:, :, pa128],
        w_in[:, :, 768:1024], w_in[:, :, 1024:1280],
        w_in[:, :, 1536:1792], w_in[:, :, 1792:1920],
        w_in[:, :, 1920:1952], w_in[:, :, 1920:1952][:, :, pr],
        w_in[:, :, 640:768], w_in[:, :, 1280:1536]], axis=2)
    assert ext.shape[2] == NCOL
    w_uq = f(inp["mla_w_uq"])
    sw = np.concatenate([np.concatenate([h * 96 + np.arange(64), h * 96 + 64 + pr]) for h in range(4)])
    w_uq_ext = np.concatenate([w_uq, w_uq[:, :, sw]], axis=2)
    w_ukv = f(inp["mla_w_ukv"]).reshape(L_, 128, 4, 2, 64)
    w_ukv_r = np.ascontiguousarray(w_ukv.transpose(0, 1, 3, 2, 4).reshape(L_, 128, 512))
    ropeA, rope96, ropeR = _rope_tables()
    kq = np.arange(128)
    amask = np.zeros((128, 2, 128), np.float32)
    amask[:, 0, :] = np.where(kq[:, None] >= kq[None, :], 0.0, NEG)
    amask[:, 1, :] = np.where(kq[:, None] <= kq[None, :], 0.0, NEG)
    gains = np.concatenate([_fm(inp["norm1_g"]), _fm(inp["norm2_g"]), _fm(inp["final_norm_g"])[:, None, :]], 1)
    mla_g = np.concatenate([_fm(inp["mla_q_norm_g"]), _fm(inp["mla_kv_norm_g"])], 2)
    shared = {
        "w_ada": f(inp["w_ada"]),
        "b_ada": _fm(inp["b_ada"]),
        "gains": np.ascontiguousarray(gains),
        "w_in_ext": np.ascontiguousarray(ext),
        "w_uq_ext": np.ascontiguousarray(w_uq_ext),
        "w_ukv_r": w_ukv_r,
        "mla_g": np.ascontiguousarray(mla_g),
        "sinkrow": np.ascontiguousarray(np.broadcast_to(f(inp["attn_sink"]).reshape(1, 32), (65, 32))),
        "rpb_tab": _rpb_table(f(inp["na_rpb"])),
        "w_out": f(inp["w_out"]),
        "w_mlp_in": f(inp["w_mlp_in"]),
        "w_mlp_out": f(inp["w_mlp_out"]),
        "ropeA": ropeA, "rope96": rope96, "ropeR": ropeR,
        "amask": amask,
        "ident8": np.ascontiguousarray(8.0 * np.eye(128, dtype=np.float32)),
    }
    x, ctx, c, c_ctx = f(inp["x"]), f(inp["ctx"]), f(inp["c"]), f(inp["c_ctx"])
    per_core = []
    for b in range(x.shape[0]):
        xt = np.ascontiguousarray(np.concatenate([x[b].T, ctx[b].T], axis=1))
        cv = np.ascontiguousarray(np.stack([_fm(c[b]), _fm(c_ctx)], -1))
        d = dict(shared)
        d["xT"] = xt
        d["cvec"] = cv
        per_core.append(d)
    return per_core


def build(n_layers=L_, debug=False, stop_after=None):
    nc = bass.Bass("TRN2", target_bir_lowering=False)
    S = Sched(nc)
    uid = [0]

    def din(name, shape):
        return nc.dram_tensor(name, list(shape), F32, kind="ExternalInput").ap()

    def dscr(name, shape, dt):
        kind = "ExternalOutput" if debug else "Internal"
        return nc.dram_tensor(name, list(shape), dt, kind=kind).ap()

    xT = din("xT", [D, T_])
    cvec = din("cvec", [128, 8, 2])
    w_ada = din("w_ada", [L_, D, 6 * D])
    b_ada = din("b_ada", [128, L_, 48])
    gains = din("gains", [128, 9, 8])
    w_in_ext = din("w_in_ext", [L_, D, NCOL])
    w_uq_ext = din("w_uq_ext", [L_, 256, 768])
    w_ukv_r = din("w_ukv_r", [L_, 128, 512])
    mla_g = din("mla_g", [128, L_, 3])
    sinkrow = din("sinkrow", [65, 32])
    rpb_tab = din("rpb_tab", [L_, 128, 4, 4608])
    w_out = din("w_out", [L_, D, D])
    w_mlp_in = din("w_mlp_in", [L_, D, 4 * D])
    w_mlp_out = din("w_mlp_out", [L_, 4 * D, D])
    ropeA = din("ropeA", [128, 2, S_])
    rope96 = din("rope96", [96, 2, S_])
    ropeR = din("ropeR", [32, 2, S_])
    amask_d = din("amask", [128, 2, 128])
    ident8_d = din("ident8", [128, 128])
    outT = nc.dram_tensor("outT", [D, S_], F32, kind="ExternalOutput").ap()

    xmid = dscr("xmid", [D, T_], F32)
    xres = dscr("xres", [D, T_], F32)
    QA = dscr("QA", [512, T_], BF16)
    KA = dscr("KA", [128, T_], BF16)
    QB = dscr("QB", [256, T_], BF16)
    KB = dscr("KB", [256, T_], BF16)
    QC = dscr("QC", [384, T_], BF16)
    KC = dscr("KC", [384, T_], BF16)
    VALL = dscr("VALL", [T_, VW], BF16)
    MIX = dscr("MIX", [D, T_], BF16)
    NG = 9
    db = {n: [Buf(f"{n}{g}") for g in range(NG)] for n in
          ["xT", "xmid", "xres", "QA", "KA", "QB", "KB", "QC", "KC", "VALL", "MIX"]}

    def gcols(g):
        return (g * 512, 512) if g < 8 else (S_, C_)

    glob = ExitStack()

    def alloc(stack, shape, dt, name=None):
        uid[0] += 1
        t = stack.enter_context(nc.sbuf_tensor(f"{name or 't'}_{uid[0]}", list(shape), dt))
        return TB(t, name or "t")

    def palloc(stack, shape, dt, name=None):
        uid[0] += 1
        t = stack.enter_context(nc.psum_tensor(f"{name or 'p'}_{uid[0]}", [128, 512], F32))
        tb = TB(t, name or "p")
        tb.b.psum = True
        return tb

    def mm(out_ap, lhsT, rhs, first, last, reads, wbuf, inc=None, **kw):
        S.op("pe", lambda e: e.matmul(out_ap, lhsT=lhsT, rhs=rhs, start=first, stop=last, **kw),
             reads=reads, writes=[wbuf], pe_acc=not first, inc=(last if inc is None else inc))

    ones_bf = alloc(glob, [128, 128], BF16, "ones_bf")
    ones_f = alloc(glob, [128, 64], F32, "ones_f")
    ident8 = alloc(glob, [128, 128], BF16, "ident8")
    amask = alloc(glob, [128, 2, 128], BF16, "amask")
    modv = alloc(glob, [128, L_, 6, 8, 2], F32, "modv")
    gain_sb = alloc(glob, [128, 9, 8], F32, "gains")
    mlag_sb = alloc(glob, [128, L_, 3], F32, "mlag")
    esink = alloc(glob, [65, 32], F32, "esink")
    epsc = alloc(glob, [128, 1], F32, "epsc")

    S.op("dve", lambda e: e.memset(ones_bf.t[:], 1.0), writes=[ones_bf.b])
    S.op("dve", lambda e: e.memset(ones_f.t[:], 1.0), writes=[ones_f.b])
    S.op("dve", lambda e: e.memset(epsc.t[:], EPS), writes=[epsc.b])
    S.dma("sp", gain_sb.t[:], gains, writes=[gain_sb.b])
    S.dma("sp", mlag_sb.t[:], mla_g, writes=[mlag_sb.b])
    S.dma("sp", esink.t[:], sinkrow, writes=[esink.b])
    S.op("act", lambda e: e.activation(out=esink.t[:], in_=esink.t[:], func=AF.Exp), reads=[esink.b], writes=[esink.b])

    with ExitStack() as ph:
        stg = alloc(ph, [128, 2, 128], F32, "stg")
        S.dma("sp", stg.t[:, 0, :], ident8_d, writes=[stg.b])
        S.op("dve", lambda e: e.tensor_copy(out=ident8.t[:], in_=stg.t[:, 0, :]), reads=[stg.b], writes=[ident8.b])
        S.dma("sp", stg.t[:], amask_d, reads=[], writes=[stg.b])
        S.op("dve", lambda e: e.tensor_copy(out=amask.t[:], in_=stg.t[:]), reads=[stg.b], writes=[amask.b])
        cv = alloc(ph, [128, 8, 2], F32, "cv")
        sv = alloc(ph, [128, 8, 2], F32, "sv")
        bfm = alloc(ph, [128, L_, 48], F32, "bfm")
        S.dma("sp", cv.t[:], cvec, writes=[cv.b])
        S.dma("sp", bfm.t[:], b_ada, writes=[bfm.b])
        S.op("act", lambda e: e.activation(out=sv.t[:], in_=cv.t[:], func=AF.Silu), reads=[cv.b], writes=[sv.b])
        wst = Ring([alloc(ph, [128, 8, 768], F32, "wst") for _ in range(2)])
        mps = Ring([palloc(ph, [128, 48, 2], F32, "mps") for _ in range(2)])
        mods = alloc(ph, [128, 48, 2], F32, "mods")
        for l in range(n_layers):
            pm = mps.next()
            wa = w_ada[l].rearrange("(kc p) n -> p kc n", p=128)
            for pc in range(8):
                w = wst.next()
                S.dma("sp", w.t[:], wa[:, :, pc * 768:(pc + 1) * 768], writes=[w.b])
                for cc in range(6):
                    ch = pc * 6 + cc
                    for kc in range(8):
                        mm(pm.t[:, 2 * ch:2 * ch + 2], w.t[:, kc, cc * 128:(cc + 1) * 128], sv.t[:, kc, :],
                           kc == 0, kc == 7, [w.b, sv.b], pm.b)
            S.op("dve", lambda e: e.tensor_tensor(
                out=mods.t[:], in0=pm.t[:, 0:96].rearrange("p (c j) -> p c j", j=2), in1=bfm.t[:, l, :].unsqueeze(2).broadcast_to([128, 48, 2]),
                op=ALU.add), reads=[pm.b, bfm.b], writes=[mods.b])
            for (k, src_m, gidx) in ((0, 1, l), (3, 4, 4 + l)):
                S.op("dve", lambda e: e.scalar_tensor_tensor(
                    out=modv.t[:, l, k, :, :], in0=mods.t[:, src_m * 8:(src_m + 1) * 8, :], scalar=1.0,
                    in1=gain_sb.t[:, gidx, :].unsqueeze(2).broadcast_to([128, 8, 2]),
                    op0=ALU.add, op1=ALU.mult), reads=[mods.b, gain_sb.b], writes=[modv.b], part=True)
            for (k, src_m) in ((1, 0), (2, 2), (4, 3), (5, 5)):
                S.op("dve", lambda e: e.tensor_copy(out=modv.t[:, l, k, :, :], in_=mods.t[:, src_m * 8:(src_m + 1) * 8, :]),
                     reads=[mods.b], writes=[modv.b], part=True)
    S.barrier()
    if stop_after == ("M", 0):
        if debug:
            dbg = nc.dram_tensor("dbg_modv", [128, L_ * 96], F32, kind="ExternalOutput").ap()
            S.dma("sp", dbg, modv.t[:].rearrange("p l k c j -> p (l k c j)"), reads=[modv.b])
        S.finish("sp")
        glob.close()
        return nc, S

    def norm_mod(xg, W, l, kG, kS, j, hT, hbufs, sq, ps_ss, lnv, rstd, tmps):
        S.op("dve", lambda e: e.tensor_tensor(out=sq.t[:, :, :W], in0=xg.t[:, :, :W], in1=xg.t[:, :, :W], op=ALU.mult),
             reads=[xg.b], writes=[sq.b])
        for kc in range(8):
            mm(ps_ss.t[:, :W], ones_bf.t[:], sq.t[:, kc, :W], kc == 0, kc == 7, [sq.b, ones_bf.b], ps_ss.b)
        S.op("act", lambda e: e.activation(out=lnv.t[:, :W], in_=ps_ss.t[:, :W], func=AF.Ln, scale=1.0 / D, bias=epsc.t[:, 0:1]),
             reads=[ps_ss.b, epsc.b], writes=[lnv.b])
        S.op("act", lambda e: e.activation(out=rstd.t[:, :W], in_=lnv.t[:, :W], func=AF.Exp, scale=-0.5),
             reads=[lnv.b], writes=[rstd.b])
        for kc in range(8):
            tm = tmps.next()
            S.op("dve", lambda e: e.scalar_tensor_tensor(
                out=tm.t[:, :W], in0=xg.t[:, kc, :W], scalar=modv.t[:, l, kG, kc, j:j + 1], in1=rstd.t[:, :W],
                op0=ALU.mult, op1=ALU.mult), reads=[xg.b, modv.b, rstd.b], writes=[tm.b])
            S.op("act", lambda e: e.activation(out=hT.t[:, kc, :W], in_=tm.t[:, :W], func=AF.Identity,
                                               bias=modv.t[:, l, kS, kc, j:j + 1], scale=1.0),
                 reads=[tm.b, modv.b], writes=[hbufs[kc]])

    def load_cast(stack_ring, dst_ap_fn, src_ap_fn, n_pieces, dst_buf, engs=("dve", "pool")):
        for i in range(n_pieces):
            st = stack_ring.next()
            src = src_ap_fn(i)
            dst = dst_ap_fn(i)
            shp = list(src.shape)
            view = st.t[:shp[0], :int(np.prod(shp[1:]))]
            if len(shp) == 3:
                view = view.rearrange("p (a b) -> p a b", a=shp[1])
            S.dma("sp", view, src, writes=[st.b])
            eng = engs[i % len(engs)]
            if eng == "act":
                S.op("act", lambda e: e.activation(out=dst, in_=view, func=AF.Copy), reads=[st.b], writes=[dst_buf], part=True)
            else:
                S.op(eng, lambda e: e.tensor_copy(out=dst, in_=view), reads=[st.b], writes=[dst_buf], part=True)

    for l in range(n_layers):
        last = (l == L_ - 1)
        xsrc, xsrc_n = (xT, "xT") if l == 0 else (xres, "xres")
        xs_fm = xsrc.rearrange("(kc p) t -> p kc t", p=128)

        with ExitStack() as ph:
            stg_ring = Ring([alloc(ph, [128, 2624], F32, "stg") for _ in range(2)])
            win = alloc(ph, [128, 8, NCOL], BF16, "win")
            wuq = alloc(ph, [128, 2, 768], BF16, "wuq")
            wukv = alloc(ph, [128, 512], BF16, "wukv")
            wie = w_in_ext[l].rearrange("(kc p) n -> p kc n", p=128)
            load_cast(stg_ring, lambda i: win.t[:, i, :], lambda i: wie[:, i, :], 8, win.b)
            wue = w_uq_ext[l].rearrange("(kc p) n -> p kc n", p=128)
            load_cast(stg_ring, lambda i: wuq.t[:, i, :], lambda i: wue[:, i, :], 2, wuq.b)
            load_cast(stg_ring, lambda i: wukv.t[:], lambda i: w_ukv_r[l], 1, wukv.b)

            xgs = Ring([alloc(ph, [128, 8, 512], F32, "xg") for _ in range(2)])
            hTs = [alloc(ph, [128, 8, 512], BF16, "hT") for _ in range(2)]
            hbs = [[Buf(f"h{i}_{k}") for k in range(8)] for i in range(2)]
            sq = alloc(ph, [128, 8, 512], BF16, "sq")
            lnv = alloc(ph, [128, 512], F32, "lnv")
            rstds = Ring([alloc(ph, [128, 512], F32, "rstd") for _ in range(2)])
            tmps = Ring([alloc(ph, [128, 512], F32, "tmp") for _ in range(3)])
            rtab = Ring([alloc(ph, [128, 6, 512], F32, "rtab") for _ in range(2)])
            outs = Ring([alloc(ph, [128, 512], BF16, "ost") for _ in range(6)])
            t1s = Ring([alloc(ph, [128, 512], F32, "t1") for _ in range(2)])
            t2s = Ring([alloc(ph, [128, 512], F32, "t2") for _ in range(2)])
            cq = alloc(ph, [128, 3, 512], F32, "cq")
            cqsq = alloc(ph, [128, 3, 512], BF16, "cqsq")
            cqn = alloc(ph, [128, 3, 512], BF16, "cqn")
            vsts = Ring([alloc(ph, [128, VW], BF16, "vst") for _ in range(2)])
            ps_ss = palloc(ph, [128, 512], F32, "ps_ss")
            ps_a = Ring([palloc(ph, [128, 512], F32, "ps_a") for _ in range(2)])
            ps_b = Ring([palloc(ph, [128, 512], F32, "ps_b") for _ in range(2)])
            ps_v = palloc(ph, [128, 384], F32, "ps_v")
            ps_v2 = palloc(ph, [128, 256], F32, "ps_v2")
            for v in vsts.items:
                S.op("dve", lambda e: e.memset(v.t[:], 1.0), writes=[v.b])

            def do_norm(g):
                c0, W = gcols(g)
                xg = xgs.next()
                S.dma("sp", xg.t[:, :, :W], xs_fm[:, :, c0:c0 + W], reads=[db[xsrc_n][g]], writes=[xg.b])
                norm_mod(xg, W, l, 0, 1, 0 if g < 8 else 1, hTs[g % 2], hbs[g % 2], sq, ps_ss, lnv, rstds.next(), tmps)

            def evac_copy(ps, M, W, dst_ap, dst_bufs, eng):
                if eng == "act":
                    S.op("act", lambda e: e.activation(out=dst_ap, in_=ps.t[:M, :W], func=AF.Copy), reads=[ps.b], writes=dst_bufs)
                else:
                    S.op(eng, lambda e: e.tensor_copy(out=dst_ap, in_=ps.t[:M, :W]), reads=[ps.b], writes=dst_bufs)

            def proj_fm(g, hT, hb, col, scol, M, rhs_fn, nk, wt, wb, rope_idx, dst_list, rt):
                c0, W = gcols(g)
                pa = ps_a.next()
                for kc in range(nk):
                    mm(pa.t[:M, :W], wt(kc, col, M), rhs_fn(kc, W), kc == 0, kc == nk - 1, [wb] + hb, pa.b)
                o = outs.next()
                if rope_idx is None or g == 8:
                    evac_copy(pa, M, W, o.t[:M, :W], [o.b], "act" if (col // 128) % 2 == 0 else "dve")
                else:
                    pb = ps_b.next()
                    for kc in range(nk):
                        mm(pb.t[:M, :W], wt(kc, scol, M), rhs_fn(kc, W), kc == 0, kc == nk - 1, [wb] + hb, pb.b)
                    t1, t2 = t1s.next(), t2s.next()
                    S.op("dve", lambda e: e.tensor_tensor(out=t1.t[:M, :W], in0=pa.t[:M, :W], in1=rt.t[:M, rope_idx, :W], op=ALU.mult),
                         reads=[pa.b, rt.b], writes=[t1.b])
                    S.op("dve", lambda e: e.tensor_tensor(out=t2.t[:M, :W], in0=pb.t[:M, :W], in1=rt.t[:M, rope_idx + 1, :W], op=ALU.mult),
                         reads=[pb.b, rt.b], writes=[t2.b])
                    S.op("pool", lambda e: e.tensor_tensor(out=o.t[:M, :W], in0=t1.t[:M, :W], in1=t2.t[:M, :W], op=ALU.add),
                         reads=[t1.b, t2.b], writes=[o.b])
                for (dst, dbuf, p0, p1) in dst_list:
                    S.dma("sp", dst[:, c0:c0 + W], o.t[p0:p1, :W], reads=[o.b], writes=[dbuf], part=True)

            def do_proj(g):
                c0, W = gcols(g)
                hT, hb = hTs[g % 2], hbs[g % 2]
                rt = None
                if g < 8:
                    rt = rtab.next()
                    S.dma("sp", rt.t[:, 0:2, :], ropeA[:, :, c0:c0 + W], writes=[rt.b], part=True)
                    S.dma("sp", rt.t[:96, 2:4, :], rope96[:, :, c0:c0 + W], writes=[rt.b], part=True)
                    S.dma("sp", rt.t[:32, 4:6, :], ropeR[:, :, c0:c0 + W], writes=[rt.b], part=True)
                wt_in = lambda kc, col, M: win.t[:, kc, col:col + M]
                rhs_h = lambda kc, W_: hT.t[:, kc, :W_]
                for c in range(4):
                    proj_fm(g, hT, hb, O_QA + c * 128, O_QAS + c * 128, 128, rhs_h, 8, wt_in, win.b, 0,
                            [(QA[c * 128:(c + 1) * 128, :], db["QA"][g], 0, 128)], rt)
                proj_fm(g, hT, hb, O_KA, O_KAS, 128, rhs_h, 8, wt_in, win.b, 0, [(KA[:, :], db["KA"][g], 0, 128)], rt)
                for c in range(2):
                    proj_fm(g, hT, hb, O_QB + c * 128, None, 128, rhs_h, 8, wt_in, win.b, None,
                            [(QB[c * 128:(c + 1) * 128, :], db["QB"][g], 0, 128)], rt)
                for c in range(2):
                    proj_fm(g, hT, hb, O_KB + c * 128, None, 128, rhs_h, 8, wt_in, win.b, None,
                            [(KB[c * 128:(c + 1) * 128, :], db["KB"][g], 0, 128)], rt)
                proj_fm(g, hT, hb, O_KR, O_KRS, 32, rhs_h, 8, wt_in, win.b, 4,
                        [(KC[h * 96 + 64:h * 96 + 96, :], db["KC"][g], 0, 32) for h in range(4)], rt)
                for c in range(3):
                    pa = ps_a.next()
                    col = O_CQ + c * 128
                    for kc in range(8):
                        mm(pa.t[:, :W], win.t[:, kc, col:col + 128], hT.t[:, kc, :W], kc == 0, kc == 7, [win.b] + hb, pa.b)
                    S.op("dve", lambda e: e.tensor_copy(out=cq.t[:, c, :W], in_=pa.t[:, :W]), reads=[pa.b], writes=[cq.b])
                    S.op("act", lambda e: e.activation(out=cqsq.t[:, c, :W], in_=cq.t[:, c, :W], func=AF.Square), reads=[cq.b], writes=[cqsq.b])
                for (cs, n, gi0) in (((0, 1), 256, 0), ((2,), 128, 2)):
                    for i, c in enumerate(cs):
                        mm(ps_ss.t[:, :W], ones_bf.t[:], cqsq.t[:, c, :W], i == 0, i == len(cs) - 1, [cqsq.b, ones_bf.b], ps_ss.b)
                    rs = rstds.next()
                    S.op("act", lambda e: e.activation(out=lnv.t[:, :W], in_=ps_ss.t[:, :W], func=AF.Ln, scale=1.0 / n, bias=epsc.t[:, 0:1]),
                         reads=[ps_ss.b, epsc.b], writes=[lnv.b])
                    S.op("act", lambda e: e.activation(out=rs.t[:, :W], in_=lnv.t[:, :W], func=AF.Exp, scale=-0.5),
                         reads=[lnv.b], writes=[rs.b])
                    for c in cs:
                        S.op("dve", lambda e: e.scalar_tensor_tensor(
                            out=cqn.t[:, c, :W], in0=cq.t[:, c, :W], scalar=mlag_sb.t[:, l, c:c + 1], in1=rs.t[:, :W],
                            op0=ALU.mult, op1=ALU.mult), reads=[cq.b, mlag_sb.b, rs.b], writes=[cqn.b])
                wt_uq = lambda kc, col, M: wuq.t[:, kc, col:col + M]
                rhs_cq = lambda kc, W_: cqn.t[:, kc, :W_]
                for h in range(4):
                    proj_fm(g, cqn, [cqn.b], h * 96, 384 + h * 96, 96, rhs_cq, 2, wt_uq, wuq.b, 2,
                            [(QC[h * 96:(h + 1) * 96, :], db["QC"][g], 0, 96)], rt)
                wt_kv = lambda kc, col, M: wukv.t[:, col:col + M]
                rhs_kv = lambda kc, W_: cqn.t[:, 2, :W_]
                for c in range(2):
                    proj_fm(g, cqn, [cqn.b], c * 128, None, 128, rhs_kv, 1, wt_kv, wukv.b, None,
                            [(KC[(2 * c) * 96:(2 * c) * 96 + 64, :], db["KC"][g], 0, 64),
                             (KC[(2 * c + 1) * 96:(2 * c + 1) * 96 + 64, :], db["KC"][g], 64, 128)], rt)
                for tt in range(W // 128):
                    ts = slice(tt * 128, (tt + 1) * 128)
                    for kc in range(8):
                        mm(ps_v.t[:, 0:384], hT.t[:, kc, ts], win.t[:, kc, O_VA:O_VA + 384], kc == 0, kc == 7, [win.b] + hb, ps_v.b)
                    mm(ps_v2.t[:, 0:256], cqn.t[:, 2, ts], wukv.t[:, 256:512], True, True, [wukv.b, cqn.b], ps_v2.b)
                    v = vsts.next()
                    vv = v.t[:].rearrange("p (h c) -> p h c", c=65)
                    S.op("dve", lambda e: e.tensor_copy(out=vv[:, 0:6, 0:64], in_=ps_v.t[:, 0:384].rearrange("p (h c) -> p h c", c=64)),
                         reads=[ps_v.b], writes=[v.b], part=True)
                    S.op("act", lambda e: e.activation(out=vv[:, 6:10, 0:64], in_=ps_v2.t[:, 0:256].rearrange("p (h c) -> p h c", c=64), func=AF.Copy),
                         reads=[ps_v2.b], writes=[v.b], part=True)
                    S.dma("sp", VALL[c0 + tt * 128:c0 + (tt + 1) * 128, :], v.t[:], reads=[v.b], writes=[db["VALL"][g]], part=True)

            sub = stop_after[0] if (stop_after and stop_after[1] == l) else None
            if sub != "P0":
                do_norm(0)
                for g in range(NG):
                    if sub == "P1":
                        break
                    if g + 1 < NG:
                        do_norm(g + 1)
                    do_proj(g)
                    if sub == "P2":
                        break
        S.barrier()
        if stop_after in (("P", l), ("P0", l), ("P1", l), ("P2", l)):
            break

        qgroups = list(range(8)) + ([] if last else [8])
        mixf = MIX.rearrange("(h d) t -> d h t", d=64)

        def norm1(st, O, W, add_sink=None, use_act=False):
            dsum, rinv, bcs = st
            hv = (lambda a: a.rearrange("p (h t) -> p h t", h=4)) if add_sink is not None else (lambda a: a)
            if add_sink is None:
                S.op("dve", lambda e: e.tensor_copy(out=dsum.t[64:65, :W], in_=O.t[64:65, :W]), reads=[O.b], writes=[dsum.b])
            else:
                S.op("dve", lambda e: e.tensor_tensor(out=hv(dsum.t[64:65, :W]), in0=hv(O.t[64:65, :W]), in1=add_sink, op=ALU.add),
                     reads=[O.b, esrow.b], writes=[dsum.b])
            if use_act:
                S.op("act", lambda e: e.activation(out=dsum.t[64:65, :W], in_=dsum.t[64:65, :W], func=AF.Ln), reads=[dsum.b], writes=[dsum.b])
                S.op("act", lambda e: e.activation(out=rinv.t[64:65, :W], in_=dsum.t[64:65, :W], func=AF.Exp, scale=-1.0), reads=[dsum.b], writes=[rinv.b])
            else:
                S.op("dve", lambda e: e.reciprocal(out=rinv.t[64:65, :W], in_=dsum.t[64:65, :W]), reads=[dsum.b], writes=[rinv.b])

        def norm2(st, ps_bc, O, W, og_ap, og_buf, heads4=False):
            dsum, rinv, bcs = st
            hv = (lambda a: a.rearrange("p (h t) -> p h t", h=4)) if heads4 else (lambda a: a)
            mm(ps_bc.t[:64, :W], ones_f.t[64:65, 0:64], rinv.t[64:65, :W], True, True, [rinv.b, ones_f.b], ps_bc.b)
            S.op("dve", lambda e: e.tensor_copy(out=bcs.t[:64, :W], in_=ps_bc.t[:64, :W]), reads=[ps_bc.b], writes=[bcs.b])
            S.op("dve", lambda e: e.tensor_tensor(out=og_ap, in0=hv(O.t[:64, :W]), in1=hv(bcs.t[:64, :W]), op=ALU.mult),
                 reads=[O.b, bcs.b], writes=[og_buf], part=True)

        def nst_ring(ph, n):
            return Ring([(alloc(ph, [65, 512], F32, "dsum"), alloc(ph, [65, 512], F32, "rinv"), alloc(ph, [64, 512], F32, "bcs"))
                         for _ in range(n)])

        with ExitStack() as ph:
            KAs = alloc(ph, [64, 2, T_], BF16, "KAs")
            VAs = alloc(ph, [128, 34, 130], BF16, "VAs")
            esrow = alloc(ph, [65, 8, 128], F32, "esrow")
            S.dma("sp", KAs.t[:], KA.rearrange("(h d) t -> d h t", d=64), reads=db["KA"], writes=[KAs.b])
            vall_b = VALL.rearrange("(b p) c -> p b c", p=128)
            for i in range(0, 34, 9):
                S.dma("sp", VAs.t[:, i:min(i + 9, 34), :], vall_b[:, i:min(i + 9, 34), 0:130], reads=db["VALL"], writes=[VAs.b], part=True)
            S.op("dve", lambda e: e.memset(esrow.t[:], 0.0), writes=[esrow.b])
            for h in range(8):
                S.op("dve", lambda e: e.tensor_scalar(out=esrow.t[64:65, h, :], in0=esrow.t[64:65, h, :],
                                                      scalar1=esink.t[64:65, l * 8 + h:l * 8 + h + 1], scalar2=None, op0=ALU.add),
                     reads=[esink.b, esrow.b], writes=[esrow.b])
            Qgs = Ring([alloc(ph, [64, 8, 512], BF16, "Qg") for _ in range(2)])
            ogs = Ring([alloc(ph, [64, 8, 512], BF16, "og") for _ in range(2)])
            pts = Ring([alloc(ph, [128, 512], BF16, "pt") for _ in range(6)])
            nsts = nst_ring(ph, 3)
            ps_bc = palloc(ph, [64, 512], F32, "ps_bc")
            ps_s = Ring([palloc(ph, [128, 512], F32, "ps_s") for _ in range(4)])
            ps_o = Ring([palloc(ph, [65, 512], F32, "ps_o") for _ in range(3)])
            pipe = Pipe(2, 3)
            qaf = QA.rearrange("(h d) t -> d h t", d=64)

            def loadQ(g):
                c0, W = gcols(g)
                Qg = Qgs.next()
                S.dma("sp", Qg.t[:, :, :W], qaf[:, :, c0:c0 + W], reads=[db["QA"][g]], writes=[Qg.b])
                return Qg

            def stepA(Qg, O, kv, bi, kb, m, first, lastk, fin):
                cell = {}
                rhs = Qg.t[:, 4 * kv:4 * kv + 4, bi * 128:(bi + 1) * 128]

                def s1():
                    ps = ps_s.next()
                    psv = ps.t[:].rearrange("p (h t) -> p h t", h=4)
                    mm(psv, KAs.t[:, kv, kb * 128:(kb + 1) * 128], rhs, True, m is None, [KAs.b, Qg.b], ps.b)
                    if m is not None:
                        mm(psv, ident8.t[:], amask.t[:, m:m + 1, :].broadcast_to([128, 4, 128]), False, True,
                           [ident8.b, amask.b], ps.b)
                    pt = pts.next()
                    S.op("act", lambda e: e.activation(out=pt.t[:], in_=ps.t[:], func=AF.Exp, scale=0.125),
                         reads=[ps.b], writes=[pt.b])
                    cell["pt"] = pt

                def s2():
                    pt = cell["pt"]
                    if first:
                        pipe.force(O)
                    mm(O.t[:65, :], VAs.t[:, kb, kv * 65:(kv + 1) * 65], pt.t[:], first, lastk, [VAs.b, pt.b], O.b, inc=True)
                    if lastk:
                        fin()
                return s1, s2

            nxtQ = loadQ(qgroups[0])
            for gi, g in enumerate(qgroups):
                c0, W = gcols(g)
                Qg, og = nxtQ, ogs.next()
                if gi + 1 < len(qgroups):
                    nxtQ = loadQ(qgroups[gi + 1])
                for kv in range(2):
                    for bi in range(W // 128):
                        n = g * 4 + bi
                        if g < 8:
                            kbs = [(kb, m) for kb, m in ((n - 1, 0), (n, None), (n + 1, 1)) if 0 <= kb < 32] + [(32, None), (33, None)]
                        else:
                            kbs = [(32, None), (33, None)]
                        O = ps_o.next()
                        ogv = og.t[:, 4 * kv:4 * kv + 4, bi * 128:(bi + 1) * 128]

                        lastacc = (kv == 1 and bi == W // 128 - 1)

                        def fin(O=O, ogv=ogv, og=og, kv=kv, lastacc=lastacc, c0=c0, W=W, g=g):
                            st = nsts.next()
                            norm1(st, O, 512, add_sink=esrow.t[64:65, 4 * kv:4 * kv + 4, :], use_act=True)
                            pipe.defer(lambda: norm2(st, ps_bc, O, 512, ogv, og.b, heads4=True), tag=O)
                            if lastacc:
                                pipe.defer(lambda: S.dma("sp", mixf[:, 0:8, c0:c0 + W], og.t[:, :, :W], reads=[og.b], writes=[db["MIX"][g]], part=True))
                        for ki, (kb, m) in enumerate(kbs):
                            pipe.push(*stepA(Qg, O, kv, bi, kb, m, ki == 0, ki == len(kbs) - 1, fin))
            pipe.flush()
        S.barrier()
        if stop_after == ("TA", l):
            break

        with ExitStack() as ph:
            KBs = alloc(ph, [64, 4, T_], BF16, "KBs")
            VBc = alloc(ph, [128, 2, 260], BF16, "VBc")
            bm = alloc(ph, [128, 4, 4608], BF16, "bm")
            stg_ring = Ring([alloc(ph, [128, 4608], F32, "stgb") for _ in range(2)])
            S.dma("sp", KBs.t[:], KB.rearrange("(h d) t -> d h t", d=64), reads=db["KB"], writes=[KBs.b])
            vall_b = VALL.rearrange("(b p) c -> p b c", p=128)
            S.dma("sp", VBc.t[:], vall_b[:, 32:34, 130:390], reads=db["VALL"], writes=[VBc.b])
            load_cast(stg_ring, lambda i: bm.t[:, i, :], lambda i: rpb_tab[l, :, i, :], 4, bm.b)
            Qgs = Ring([alloc(ph, [64, 4, 512], BF16, "Qg") for _ in range(2)])
            ogs = Ring([alloc(ph, [64, 4, 512], BF16, "og") for _ in range(2)])
            pts = Ring([alloc(ph, [128, 512], BF16, "pt") for _ in range(6)])
            vgs = Ring([alloc(ph, [128, 260], BF16, "vg") for _ in range(8)])
            kgts = Ring([alloc(ph, [64, 4, 128], BF16, "kgt") for _ in range(4)])
            nsts = nst_ring(ph, 4)
            ps_bc = palloc(ph, [64, 512], F32, "ps_bc")
            ps_s = Ring([palloc(ph, [128, 512], F32, "ps_s") for _ in range(3)])
            ps_o = [palloc(ph, [65, 512], F32, "ps_o") for _ in range(4)]
            pipe = Pipe(2, 2)
            qbf = QB.rearrange("(h d) t -> d h t", d=64)

            def loadQ(g):
                c0, W = gcols(g)
                Qg = Qgs.next()
                S.dma("sp", Qg.t[:, :, :W], qbf[:, :, c0:c0 + W], reads=[db["QB"][g]], writes=[Qg.b])
                return Qg

            def stepBctx(Qg, O, h, ki, cb, W, lastk, fin):
                cell = {}

                def s1():
                    ps = ps_s.next()
                    mm(ps.t[:, :W], KBs.t[:, h, cb * 128:(cb + 1) * 128], Qg.t[:, h, :W], True, True, [KBs.b, Qg.b], ps.b)
                    pt = pts.next()
                    S.op("act", lambda e: e.activation(out=pt.t[:, :W], in_=ps.t[:, :W], func=AF.Exp, scale=0.125),
                         reads=[ps.b], writes=[pt.b])
                    cell["pt"] = pt

                def s2():
                    pt = cell["pt"]
                    if ki == 0:
                        pipe.force(O)
                    mm(O.t[:65, :W], VBc.t[:, ki, h * 65:(h + 1) * 65], pt.t[:, :W], ki == 0, lastk,
                       [VBc.b, pt.b], O.b, inc=True, skip_group_check=True)
                    if lastk:
                        fin()
                return s1, s2

            def stepBloc(Qg, j, kg, pat, kgt, vg, M, lastk, fins):
                cell = {}

                def s1():
                    ps = ps_s.next()
                    boff = (pat * 4 + kg) * 128
                    mm(ps.t[:M, :].rearrange("p (h t) -> p h t", h=4), ident8.t[:M, :M], bm.t[:M, :, boff:boff + 128],
                       True, False, [ident8.b, bm.b], ps.b)
                    for h in range(4):
                        qview = Qg.t[:, h, :].rearrange("p (r c) -> p r c", c=64)[:, :, 16 * j:16 * j + 16]
                        psv = ps.t[:M, h * 128:(h + 1) * 128].rearrange("p (r c) -> p r c", c=16)
                        mm(psv, kgt.t[:, h, :M], qview, False, h == 3, [kgt.b, Qg.b], ps.b, skip_group_check=True)
                    pt = pts.next()
                    S.op("act", lambda e: e.activation(out=pt.t[:M, :], in_=ps.t[:M, :], func=AF.Exp, scale=0.125),
                         reads=[ps.b], writes=[pt.b])
                    cell["pt"] = pt

                def s2():
                    pt = cell["pt"]
                    for h in range(4):
                        O = ps_o[h]
                        ov = O.t[:65, :].rearrange("p (r c) -> p r c", c=64)[:, :, 16 * j:16 * j + 16]
                        ptv = pt.t[:M, h * 128:(h + 1) * 128].rearrange("p (r c) -> p r c", c=16)
                        mm(ov, vg.t[:M, h * 65:(h + 1) * 65], ptv, False, lastk, [vg.b, pt.b], O.b,
                           inc=(h == 3), skip_group_check=True)
                    if lastk:
                        for f in fins:
                            f()
                return s1, s2

            nxtQ = loadQ(qgroups[0])
            for gi, g in enumerate(qgroups):
                c0, W = gcols(g)
                Qg, og = nxtQ, ogs.next()
                if gi + 1 < len(qgroups):
                    nxtQ = loadQ(qgroups[gi + 1])

                def mkfin(h, og=og, W=W, c0=c0, g=g):
                    def fin():
                        st = nsts.next()
                        O = ps_o[h]
                        norm1(st, O, W)
                        pipe.defer(lambda: norm2(st, ps_bc, O, W, og.t[:, h, :W], og.b), tag=O)
                        if h == 3:
                            pipe.defer(lambda: S.dma("sp", mixf[:, 8:12, c0:c0 + W], og.t[:, :, :W], reads=[og.b], writes=[db["MIX"][g]], part=True))
                    return fin
                for h in range(4):
                    for ki, cb in enumerate((32, 33)):
                        pipe.push(*stepBctx(Qg, ps_o[h], h, ki, cb, W, (g == 8 and ki == 1), mkfin(h)))
                if g < 8:
                    i = g
                    r0 = int(NA_KR0[i])
                    for j in range(4):
                        cc0 = int(NA_KC0[j])
                        pat = ICLS[i] * 3 + JCLS[j]
                        c0w = min(cc0, 32)
                        for kg in range(4):
                            nr = KG_ROWS[kg]
                            M = nr * 32
                            vg = vgs.next()
                            tok0 = (r0 + 4 * kg) * 64 + c0w
                            for rr in range(nr):
                                S.dma("sp", vg.t[rr * 32:(rr + 1) * 32, :], VALL[tok0 + rr * 64:tok0 + rr * 64 + 32, 130:390],
                                      reads=db["VALL"], writes=[vg.b], part=True)
                            kgt = kgts.next()
                            S.op("pool", lambda e: e.tensor_copy(
                                out=kgt.t[:, :, :M].rearrange("p h (r c) -> p h r c", c=32),
                                in_=KBs.t[:, :, tok0:tok0 + nr * 64].rearrange("p h (r c) -> p h r c", c=64)[:, :, :, 0:32]),
                                reads=[KBs.b], writes=[kgt.b])
                            pipe.push(*stepBloc(Qg, j, kg, pat, kgt, vg, M, (j == 3 and kg == 3), [mkfin(h) for h in range(4)]))
            pipe.flush()
        S.barrier()
        if stop_after == ("TB", l):
            break

        wsc = ExitStack()
        stgW = Ring([alloc(wsc, [128, 2048], F32, "stgW") for _ in range(2)])
        w1 = alloc(wsc, [128, 8, 4 * D], BF16, "w1")
        w1d = w_mlp_in[l].rearrange("(kc p) n -> p kc n", p=128)
        w2d = w_mlp_out[l].rearrange("(f p) n -> p f n", p=128)
        wpieces = []

        def piece(dst, src, dbuf, eng):
            def f():
                st = stgW.next()
                shp = list(src.shape)
                view = st.t[:, :int(np.prod(shp[1:]))]
                if len(shp) == 3:
                    view = view.rearrange("p (a b) -> p a b", a=shp[1])
                S.dma("sp", view, src, writes=[st.b])
                if eng == "act":
                    S.op("act", lambda e: e.activation(out=dst, in_=view, func=AF.Copy), reads=[st.b], writes=[dbuf], part=True)
                else:
                    S.op(eng, lambda e: e.tensor_copy(out=dst, in_=view), reads=[st.b], writes=[dbuf], part=True)
            return f

        def emit_pieces(n):
            for _ in range(n):
                if wpieces:
                    wpieces.pop(0)()
        for i in range(16):
            wpieces.append(piece(w1.t[:, i // 2, (i % 2) * 2048:(i % 2 + 1) * 2048],
                                 w1d[:, i // 2, (i % 2) * 2048:(i % 2 + 1) * 2048], w1.b, "pool"))

        with ExitStack() as ph:
            KCs = alloc(ph, [96, 4, T_], BF16, "KCs")
            VCs = alloc(ph, [128, 34, 260], BF16, "VCs")
            S.dma("sp", KCs.t[:], KC.rearrange("(h d) t -> d h t", d=96), reads=db["KC"], writes=[KCs.b])
            vall_b = VALL.rearrange("(b p) c -> p b c", p=128)
            for i in range(0, 34, 9):
                S.dma("sp", VCs.t[:, i:min(i + 9, 34), :], vall_b[:, i:min(i + 9, 34), 390:650], reads=db["VALL"], writes=[VCs.b], part=True)
            Qgs = Ring([alloc(ph, [96, 4, 512], BF16, "Qg") for _ in range(2)])
            ogs = Ring([alloc(ph, [64, 4, 512], BF16, "og") for _ in range(2)])
            pts = Ring([alloc(ph, [128, 512], BF16, "pt") for _ in range(6)])
            nsts = nst_ring(ph, 2)
            ps_bc = palloc(ph, [64, 512], F32, "ps_bc")
            ps_s = Ring([palloc(ph, [128, 512], F32, "ps_s") for _ in range(4)])
            ps_o = Ring([palloc(ph, [65, 512], F32, "ps_o") for _ in range(2)])
            sc = float(96 ** -0.5)
            pipe = Pipe(3, 4)
            qcf = QC.rearrange("(h d) t -> d h t", d=96)

            def loadQ(g):
                c0, W = gcols(g)
                Qg = Qgs.next()
                S.dma("sp", Qg.t[:, :, :W], qcf[:, :, c0:c0 + W], reads=[db["QC"][g]], writes=[Qg.b])
                return Qg

            def stepC(Qg, O, h, kb, W, first, lastk, fin):
                cell = {}

                def s1():
                    ps = ps_s.next()
                    mm(ps.t[:, :W], KCs.t[:, h, kb * 128:(kb + 1) * 128], Qg.t[:, h, :W], True, True, [KCs.b, Qg.b], ps.b)
                    pt = pts.next()
                    S.op("act", lambda e: e.activation(out=pt.t[:, :W], in_=ps.t[:, :W], func=AF.Exp, scale=sc),
                         reads=[ps.b], writes=[pt.b])
                    cell["pt"] = pt

                def s2():
                    pt = cell["pt"]
                    if first:
                        pipe.force(O)
                    mm(O.t[:65, :W], VCs.t[:, kb, h * 65:(h + 1) * 65], pt.t[:, :W], first, lastk, [VCs.b, pt.b], O.b, inc=True)
                    if lastk:
                        fin()
                return s1, s2

            nxtQ = loadQ(qgroups[0])
            for gi, g in enumerate(qgroups):
                c0, W = gcols(g)
                Qg, og = nxtQ, ogs.next()
                if gi + 1 < len(qgroups):
                    nxtQ = loadQ(qgroups[gi + 1])
                kbs = list(range(34)) if g < 8 else [32, 33]
                for h in range(4):
                    O = ps_o.next()

                    def fin(O=O, og=og, h=h, W=W, c0=c0, g=g):
                        st = nsts.next()
                        norm1(st, O, W)
                        pipe.defer(lambda: norm2(st, ps_bc, O, W, og.t[:, h, :W], og.b), tag=O)
                        if h == 3:
                            pipe.defer(lambda: S.dma("sp", mixf[:, 12:16, c0:c0 + W], og.t[:, :, :W], reads=[og.b], writes=[db["MIX"][g]], part=True))
                    for ki, kb in enumerate(kbs):
                        pipe.push(*stepC(Qg, O, h, kb, W, ki == 0, ki == len(kbs) - 1, fin))
                    if (gi * 4 + h) % 2 == 1:
                        emit_pieces(1)
            pipe.flush()
            emit_pieces(len(wpieces))
        S.barrier()
        if stop_after == ("TC", l):
            wsc.close()
            break

        w2 = alloc(wsc, [128, 32, D], BF16, "w2")
        for i in range(16):
            wpieces.append(piece(w2.t[:, 2 * i:2 * i + 2, :], w2d[:, 2 * i:2 * i + 2, :], w2.b, "pool" if i % 2 else "act"))
        WG = 256
        ngr = 16 + (0 if last else 1)
        with ExitStack() as ph:
            wo = alloc(ph, [128, 8, D], BF16, "wo")
            wod = w_out[l].rearrange("(kc p) n -> p kc n", p=128)
            for i in range(4):
                piece(wo.t[:, 2 * i:2 * i + 2, :], wod[:, 2 * i:2 * i + 2, :], wo.b, "pool" if i % 2 else "act")()
            mgs = Ring([alloc(ph, [128, 8, WG], BF16, "mixg") for _ in range(2)])
            xgs = Ring([alloc(ph, [128, 8, WG], F32, "xg") for _ in range(2)])
            ps_y = Ring([palloc(ph, [128, 512], F32, "ps_y") for _ in range(4)])
            mix_fm = MIX.rearrange("(kc p) t -> p kc t", p=128)
            xm_fm = xmid.rearrange("(kc p) t -> p kc t", p=128)

            def loadO1(gg):
                c0 = gg * WG
                g = gg // 2 if gg < 16 else 8
                mg, xg = mgs.next(), xgs.next()
                S.dma("sp", mg.t[:], mix_fm[:, :, c0:c0 + WG], reads=[db["MIX"][g]], writes=[mg.b])
                S.dma("sp", xg.t[:], xs_fm[:, :, c0:c0 + WG], reads=[db[xsrc_n][g]], writes=[xg.b])
                return mg, xg
            nxt = loadO1(0)
            for gg in range(ngr):
                c0 = gg * WG
                g = gg // 2 if gg < 16 else 8
                j = 0 if gg < 16 else 1
                mg, xg = nxt
                if gg + 1 < ngr:
                    nxt = loadO1(gg + 1)
                for c in range(8):
                    py = ps_y.next()
                    for kc in range(8):
                        mm(py.t[:, :WG], wo.t[:, kc, c * 128:(c + 1) * 128], mg.t[:, kc, :], kc == 0, kc == 7, [wo.b, mg.b], py.b)
                    S.op("dve", lambda e: e.scalar_tensor_tensor(
                        out=xg.t[:, c, :], in0=py.t[:, :WG], scalar=modv.t[:, l, 2, c, j:j + 1], in1=xg.t[:, c, :],
                        op0=ALU.mult, op1=ALU.add), reads=[py.b, modv.b, xg.b], writes=[xg.b])
                S.dma("sp", xm_fm[:, :, c0:c0 + WG], xg.t[:], reads=[xg.b], writes=[db["xmid"][g]])
                emit_pieces(1)
            emit_pieces(len(wpieces))
        S.barrier()
        if stop_after == ("O1", l):
            wsc.close()
            break

        with ExitStack() as ph:
            xgs = Ring([alloc(ph, [128, 8, WG], F32, "xg") for _ in range(2)])
            hT2 = alloc(ph, [128, 8, WG], BF16, "hT2")
            hb2 = [Buf(f"h2_{k}") for k in range(8)]
            sq = alloc(ph, [128, 8, WG], BF16, "sq")
            lnv = alloc(ph, [128, WG], F32, "lnv")
            rstd = alloc(ph, [128, WG], F32, "rstd")
            tmps = Ring([alloc(ph, [128, WG], F32, "tmp") for _ in range(3)])
            rl = Ring([alloc(ph, [128, WG], F32, "rl") for _ in range(3)])
            aT = alloc(ph, [128, 32, WG], BF16, "aT")
            abufs = [Buf(f"a{f}") for f in range(32)]
            ps_ss = palloc(ph, [128, WG], F32, "ps_ss")
            ps_u = Ring([palloc(ph, [128, WG], F32, "ps_u") for _ in range(3)])
            ps_y = Ring([palloc(ph, [128, WG], F32, "ps_y") for _ in range(2)])
            xm_fm = xmid.rearrange("(kc p) t -> p kc t", p=128)
            xr_fm = xres.rearrange("(kc p) t -> p kc t", p=128)
            out_fm = outT.rearrange("(kc p) t -> p kc t", p=128)
            for gg in range(ngr):
                c0 = gg * WG
                g = gg // 2 if gg < 16 else 8
                j = 0 if gg < 16 else 1
                xg = xgs.next()
                S.dma("sp", xg.t[:], xm_fm[:, :, c0:c0 + WG], reads=[db["xmid"][g]], writes=[xg.b])
                norm_mod(xg, WG, l, 3, 4, j, hT2, hb2, sq, ps_ss, lnv, rstd, tmps)
                for f in range(32):
                    pu = ps_u.next()
                    for kc in range(8):
                        mm(pu.t[:, :WG], w1.t[:, kc, f * 128:(f + 1) * 128], hT2.t[:, kc, :], kc == 0, kc == 7, [w1.b] + hb2, pu.b)
                    r = rl.next()
                    S.op("act", lambda e: e.activation(out=r.t[:], in_=pu.t[:, :WG], func=AF.Relu), reads=[pu.b], writes=[r.b])
                    S.op("pool" if f % 2 else "dve", lambda e: e.tensor_tensor(out=aT.t[:, f, :], in0=r.t[:], in1=r.t[:], op=ALU.mult),
                         reads=[r.b], writes=[abufs[f]])
                for c in range(8):
                    py = ps_y.next()
                    for f in range(32):
                        mm(py.t[:, :WG], w2.t[:, f, c * 128:(c + 1) * 128], aT.t[:, f, :], f == 0, f == 31, [w2.b, abufs[f]], py.b)
                    S.op("dve", lambda e: e.scalar_tensor_tensor(
                        out=xg.t[:, c, :], in0=py.t[:, :WG], scalar=modv.t[:, l, 5, c, j:j + 1], in1=xg.t[:, c, :],
                        op0=ALU.mult, op1=ALU.add), reads=[py.b, modv.b, xg.b], writes=[xg.b])
                if not last:
                    S.dma("sp", xr_fm[:, :, c0:c0 + WG], xg.t[:], reads=[xg.b], writes=[db["xres"][g]])
                else:
                    S.op("dve", lambda e: e.tensor_tensor(out=sq.t[:], in0=xg.t[:], in1=xg.t[:], op=ALU.mult), reads=[xg.b], writes=[sq.b])
                    for kc in range(8):
                        mm(ps_ss.t[:, :WG], ones_bf.t[:], sq.t[:, kc, :], kc == 0, kc == 7, [sq.b, ones_bf.b], ps_ss.b)
                    S.op("act", lambda e: e.activation(out=lnv.t[:], in_=ps_ss.t[:, :WG], func=AF.Ln, scale=1.0 / D, bias=epsc.t[:, 0:1]),
                         reads=[ps_ss.b, epsc.b], writes=[lnv.b])
                    S.op("act", lambda e: e.activation(out=rstd.t[:], in_=lnv.t[:], func=AF.Exp, scale=-0.5), reads=[lnv.b], writes=[rstd.b])
                    for kc in range(8):
                        S.op("dve", lambda e: e.scalar_tensor_tensor(
                            out=xg.t[:, kc, :], in0=xg.t[:, kc, :], scalar=gain_sb.t[:, 8, kc:kc + 1], in1=rstd.t[:],
                            op0=ALU.mult, op1=ALU.mult), reads=[xg.b, gain_sb.b, rstd.b], writes=[xg.b])
                    S.dma("sp", out_fm[:, :, c0:c0 + WG], xg.t[:], reads=[xg.b], writes=[])
        wsc.close()
        S.barrier()
        if stop_after == ("O2", l):
            break

    S.finish("sp")
    glob.close()
    return nc, S


def kernel(**inputs):
    per_core = prep_inputs(inputs)
    nc, _ = build()
    res = run_bass_kernel_spmd(nc, per_core, core_ids=list(range(8)))
    out = np.stack([np.ascontiguousarray(r["outT"].T) for r in res.results], axis=0)
    return out.astype(np.float32)
```

```python
import math
from contextlib import ExitStack
import numpy as np
import concourse.bass as bass
import concourse.mybir as mybir
from concourse.bass_utils import run_bass_kernel_spmd

F32 = mybir.dt.float32
BF16 = mybir.dt.bfloat16
AF = mybir.ActivationFunctionType
ALU = mybir.AluOpType

D = 1024
S_ = 4096
C_ = 256
T_ = S_ + C_
L_ = 4
NCOL = 2624
EPS = 1e-6
NEG = -30000.0
O_QA, O_QAS, O_KA, O_KAS, O_QB, O_KB, O_CQ, O_CKV, O_KR, O_KRS, O_VA = (
    0, 512, 1024, 1152, 1280, 1536, 1792, 2048, 2176, 2208, 2240)
VW = 650


class Buf:
    __slots__ = ("name", "lw", "rs", "plw", "prs", "psum")

    def __init__(self, name="", psum=False):
        self.name = name
        self.lw = {}
        self.rs = {}
        self.plw = {}
        self.prs = {}
        self.psum = psum


class Sched:
    def __init__(self, nc, n_dma_slots=32, same_engine_sync=True):
        self.nc = nc
        self.same = same_engine_sync
        self.eng = {"pe": nc.tensor, "act": nc.scalar, "dve": nc.vector, "pool": nc.gpsimd, "sp": nc.sync}
        self.sem = {k: nc.alloc_semaphore(name=f"sem_{k}") for k in self.eng}
        self.cnt = {k: 0 for k in self.eng}
        self.seen = {k: {} for k in self.eng}
        self.dsem = [nc.alloc_semaphore(name=f"dsem{i}") for i in range(n_dma_slots)]
        self.dcnt = [0] * n_dma_slots
        self.dnext = 0
        self.n_wait = 0
        self.n_inst = 0
        self.log = {k: [] for k in self.eng}

    def _semof(self, key):
        return self.dsem[key] if isinstance(key, int) else self.sem[key]

    def _wait(self, e, key, val):
        if key == e and not self.same:
            return
        if self.seen[e].get(key, 0) >= val:
            return
        self.eng[e].wait_ge(self._semof(key), val)
        self.log[e].append(("w", key, val))
        self.seen[e][key] = val
        self.n_wait += 1

    def _deps(self, e, reads, writes, part=False):
        for b in reads:
            for k, v in b.lw.items():
                self._wait(e, k, v)
            if b.psum:
                for k, v in b.rs.items():
                    if k != e:
                        self._wait(e, k, v)
        for b in writes:
            if part and not b.rs:
                for k, v in b.plw.items():
                    self._wait(e, k, v)
                for k, v in b.prs.items():
                    self._wait(e, k, v)
            else:
                for k, v in b.lw.items():
                    self._wait(e, k, v)
                for k, v in b.rs.items():
                    self._wait(e, k, v)

    def _record(self, key, n, reads, writes, part):
        for b in reads:
            b.rs[key] = max(b.rs.get(key, 0), n)
        for b in writes:
            if b.rs or not part:
                b.plw, b.prs = b.lw, b.rs
                b.lw = {}
            b.lw[key] = max(b.lw.get(key, 0), n)
            b.rs = {}

    def op(self, e, fn, reads=(), writes=(), pe_acc=False, inc=True, part=False):
        if pe_acc:
            self._deps(e, reads, ())
        else:
            self._deps(e, reads, writes, part)
        ins = fn(self.eng[e])
        if inc:
            ins.then_inc(self.sem[e], 1)
            self.log[e].append(("i", e, 1))
            self.cnt[e] += 1
            n = self.cnt[e]
        else:
            n = self.cnt[e] + 1
        self.n_inst += 1
        self._record(e, n, reads, writes, part or pe_acc)
        return ins

    def dma(self, q, out, in_, reads=(), writes=(), part=False, **kw):
        self._deps(q, reads, writes, part)
        s = self.dnext
        self.dnext = (self.dnext + 1) % len(self.dsem)
        if self.dcnt[s] > 0:
            self._wait(q, s, 16 * self.dcnt[s])
        ins = self.eng[q].dma_start(out=out, in_=in_, **kw)
        ins.then_inc(self.dsem[s], 16)
        self.log[q].append(("i", s, 16))
        self.dcnt[s] += 1
        v = 16 * self.dcnt[s]
        self.n_inst += 1
        self._record(s, v, reads, writes, part)
        return ins

    def barrier(self):
        for e in self.eng:
            self.finish(e)

    def finish(self, e="sp"):
        for k in self.eng:
            if k != e and self.cnt[k] > 0:
                self._wait(e, k, self.cnt[k])
        for s in range(len(self.dsem)):
            if self.dcnt[s] > 0:
                self._wait(e, s, 16 * self.dcnt[s])


class Pipe:
    def __init__(self, la, d):
        self.la, self.d = la, d
        self.q = []
        self.later = []
        self.step = 0

    def _fire(self, upto=None):
        while self.later and (upto is None or self.later[0][0] <= upto):
            self.later.pop(0)[1]()

    def _s2(self):
        self.q.pop(0)()
        self.step += 1
        self._fire(self.step)

    def push(self, s1, s2):
        s1()
        self.q.append(s2)
        if len(self.q) > self.la:
            self._s2()

    def defer(self, fn, tag=None):
        self.later.append((self.step + self.d, fn, tag))

    def force(self, tag):
        idx = [i for i, it in enumerate(self.later) if it[2] is tag]
        if idx:
            for _ in range(idx[-1] + 1):
                self.later.pop(0)[1]()

    def flush(self):
        while self.q:
            self._s2()
        self._fire(None)


class TB:
    def __init__(self, t, name=""):
        self.t = t
        self.b = Buf(name)


class Ring:
    def __init__(self, items):
        self.items = items
        self.i = 0

    def next(self):
        it = self.items[self.i]
        self.i = (self.i + 1) % len(self.items)
        return it


def _perm_a(n):
    idx = np.arange(n)
    h, d = idx // 64, idx % 64
    j = d % 32
    partner = np.where(j < 16, d + 16, d - 16)
    return h * 64 + partner


def _perm_r(n=32):
    d = np.arange(n)
    jj = d % 16
    return np.where(jj < 8, d + 8, d - 8)


def _rope_tables():
    tok = np.arange(S_)
    row, col = tok // 64, tok % 64

    def tab(ndim_sec, half):
        freqs = (10000.0 ** (-np.arange(half, dtype=np.float32) / half)).astype(np.float32)
        cos = np.zeros((2 * ndim_sec, S_), np.float32)
        sin = np.zeros((2 * ndim_sec, S_), np.float32)
        for d in range(2 * ndim_sec):
            sec, j = d // ndim_sec, d % ndim_sec
            pos = (row if sec == 0 else col).astype(np.float32)
            ang = pos * freqs[j % half]
            cos[d] = np.cos(ang).astype(np.float32)
            sn = np.sin(ang).astype(np.float32)
            sin[d] = -sn if j < half else sn
        return cos, sin

    ca, sa = tab(32, 16)
    cr, sr = tab(16, 8)
    ropeA = np.stack([np.concatenate([ca, ca], 0), np.concatenate([sa, sa], 0)], 1)
    c96 = np.concatenate([np.ones((64, S_), np.float32), cr], 0)
    s96 = np.concatenate([np.zeros((64, S_), np.float32), sr], 0)
    rope96 = np.stack([c96, s96], 1)
    ropeR = np.stack([cr, sr], 1)
    return np.ascontiguousarray(ropeA), np.ascontiguousarray(rope96), np.ascontiguousarray(ropeR)


def _na_layout():
    rows, GW, NR, NCc, NQ = 64, 64, 8, 16, 16
    kr, kc = min(NR, rows), NCc
    qr, qc = math.gcd(rows, NR), NQ
    krb, kcb = min(qr - 1 + kr, rows), min(qc - 1 + kc, GW)
    nrb, ncb = rows // qr, GW // qc
    q_r = np.arange(nrb)[:, None] * qr + np.arange(qr)[None, :]
    q_c = np.arange(ncb)[:, None] * qc + np.arange(qc)[None, :]
    w_r = np.clip(q_r - kr // 2, 0, rows - kr)
    w_c = np.clip(q_c - kc // 2, 0, GW - kc)
    k_r = np.minimum(w_r[:, 0], rows - krb)[:, None] + np.arange(krb)[None, :]
    k_c = np.minimum(w_c[:, 0], GW - kcb)[:, None] + np.arange(kcb)[None, :]
    qr6 = q_r[:, None, :, None, None, None]
    qc6 = q_c[None, :, None, :, None, None]
    wr6 = w_r[:, None, :, None, None, None]
    wc6 = w_c[None, :, None, :, None, None]
    kr6 = k_r[:, None, None, None, :, None]
    kc6 = k_c[None, :, None, None, None, :]
    shape6 = (nrb, ncb, qr, qc, krb, kcb)

    def flat(a):
        return np.broadcast_to(a, shape6).reshape(nrb, ncb, qr * qc, krb * kcb)

    mask = flat((kr6 >= wr6) & (kr6 < wr6 + kr) & (kc6 >= wc6) & (kc6 < wc6 + kc))
    d_r = flat(np.clip(kr6 - qr6, 1 - NR, NR - 1) + NR - 1)
    d_c = flat(np.clip(kc6 - qc6, 1 - NCc, NCc - 1) + NCc - 1)
    return mask, d_r, d_c, k_r[:, 0], k_c[:, 0]


NA_MASK, NA_DR, NA_DC, NA_KR0, NA_KC0 = _na_layout()
ICLS = [0, 1, 1, 1, 1, 1, 1, 2]
JCLS = [0, 1, 1, 2]
IREP = [0, 1, 7]
JREP = [0, 1, 3]
KG_ROWS = [4, 4, 4, 3]


def _rpb_table(na_rpb):
    tab = np.full((L_, 128, 4, 9, 4, 128), NEG, np.float32)
    for ic in range(3):
        for jc in range(3):
            i, j = IREP[ic], JREP[jc]
            m = NA_MASK[i, j]
            dr, dc = NA_DR[i, j], NA_DC[i, j]
            g = na_rpb[:, :, dr, dc]
            g = np.where(m[None, None], g, np.float32(NEG))
            kc0 = int(NA_KC0[j])
            c0w = min(kc0, 32)
            for kg in range(4):
                for rr in range(KG_ROWS[kg]):
                    for cc in range(32):
                        col = c0w + cc
                        if kc0 <= col < kc0 + 31:
                            kk = (4 * kg + rr) * 31 + (col - kc0)
                            tab[:, rr * 32 + cc, :, ic * 3 + jc, kg, :] = g[:, :, :, kk]
    return np.ascontiguousarray(tab.reshape(L_, 128, 4, 9 * 4 * 128))


def _fm(v):
    v = np.asarray(v, np.float32)
    lead = v.shape[:-1]
    c = v.shape[-1] // 128
    v = v.reshape(*lead, c, 128)
    return np.ascontiguousarray(np.moveaxis(v, -1, 0))


def prep_inputs(inp):
    f = lambda a: np.ascontiguousarray(np.asarray(a, np.float32))
    w_in = f(inp["w_in"])
    pa512, pa128, pr = _perm_a(512), _perm_a(128), _perm_r()
    ext = np.concatenate([
        w_in[:, :, 0:512], w_in[:, :, 0:512][:, :, pa512],
        w_in[:, :, 512:640], w_in[:, :, 512:640][:, :, pa128],
        w_in[:, :, 768:1024], w_in[:, :, 1024:1280],
        w_in[:, :, 1536:1792], w_in[:, :, 1792:1920],
        w_in[:, :, 1920:1952], w_in[:, :, 1920:1952][:, :, pr],
        w_in[:, :, 640:768], w_in[:, :, 1280:1536]], axis=2)
    assert ext.shape[2] == NCOL
    w_uq = f(inp["mla_w_uq"])
    sw = np.concatenate([np.concatenate([h * 96 + np.arange(64), h * 96 + 64 + pr]) for h in range(4)])
    w_uq_ext = np.concatenate([w_uq, w_uq[:, :, sw]], axis=2)
    w_ukv = f(inp["mla_w_ukv"]).reshape(L_, 128, 4, 2, 64)
    w_ukv_r = np.ascontiguousarray(w_ukv.transpose(0, 1, 3, 2, 4).reshape(L_, 128, 512))
    ropeA, rope96, ropeR = _rope_tables()
    kq = np.arange(128)
    amask = np.zeros((128, 2, 128), np.float32)
    amask[:, 0, :] = np.where(kq[:, None] >= kq[None, :], 0.0, NEG)
    amask[:, 1, :] = np.where(kq[:, None] <= kq[None, :], 0.0, NEG)
    gains = np.concatenate([_fm(inp["norm1_g"]), _fm(inp["norm2_g"]), _fm(inp["final_norm_g"])[:, None, :]], 1)
    mla_g = np.concatenate([_fm(inp["mla_q_norm_g"]), _fm(inp["mla_kv_norm_g"])], 2)
    shared = {
        "w_ada": f(inp["w_ada"]),
        "b_ada": _fm(inp["b_ada"]),
        "gains": np.ascontiguousarray(gains),
        "w_in_ext": np.ascontiguousarray(ext),
        "w_uq_ext": np.ascontiguousarray(w_uq_ext),
        "w_ukv_r": w_ukv_r,
        "mla_g": np.ascontiguousarray(mla_g),
        "sinkrow": np.ascontiguousarray(np.broadcast_to(f(inp["attn_sink"]).reshape(1, 32), (65, 32))),
        "rpb_tab": _rpb_table(f(inp["na_rpb"])),
        "w_out": f(inp["w_out"]),
        "w_mlp_in": f(inp["w_mlp_in"]),
        "w_mlp_out": f(inp["w_mlp_out"]),
        "ropeA": ropeA, "rope96": rope96, "ropeR": ropeR,
        "amask": amask,
        "ident8": np.ascontiguousarray(8.0 * np.eye(128, dtype=np.float32)),
    }
    x, ctx, c, c_ctx = f(inp["x"]), f(inp["ctx"]), f(inp["c"]), f(inp["c_ctx"])
    per_core = []
    for b in range(x.shape[0]):
        xt = np.ascontiguousarray(np.concatenate([x[b].T, ctx[b].T], axis=1))
        cv = np.ascontiguousarray(np.stack([_fm(c[b]), _fm(c_ctx)], -1))
        d = dict(shared)
        d["xT"] = xt
        d["cvec"] = cv
        per_core.append(d)
    return per_core


def build(n_layers=L_, debug=False, stop_after=None):
    nc = bass.Bass("TRN2", target_bir_lowering=False)
    S = Sched(nc)
    uid = [0]

    def din(name, shape):
        return nc.dram_tensor(name, list(shape), F32, kind="ExternalInput").ap()

    def dscr(name, shape, dt):
        kind = "ExternalOutput" if debug else "Internal"
        return nc.dram_tensor(name, list(shape), dt, kind=kind).ap()

    xT = din("xT", [D, T_])
    cvec = din("cvec", [128, 8, 2])
    w_ada = din("w_ada", [L_, D, 6 * D])
    b_ada = din("b_ada", [128, L_, 48])
    gains = din("gains", [128, 9, 8])
    w_in_ext = din("w_in_ext", [L_, D, NCOL])
    w_uq_ext = din("w_uq_ext", [L_, 256, 768])
    w_ukv_r = din("w_ukv_r", [L_, 128, 512])
    mla_g = din("mla_g", [128, L_, 3])
    sinkrow = din("sinkrow", [65, 32])
    rpb_tab = din("rpb_tab", [L_, 128, 4, 4608])
    w_out = din("w_out", [L_, D, D])
    w_mlp_in = din("w_mlp_in", [L_, D, 4 * D])
    w_mlp_out = din("w_mlp_out", [L_, 4 * D, D])
    ropeA = din("ropeA", [128, 2, S_])
    rope96 = din("rope96", [96, 2, S_])
    ropeR = din("ropeR", [32, 2, S_])
    amask_d = din("amask", [128, 2, 128])
    ident8_d = din("ident8", [128, 128])
    outT = nc.dram_tensor("outT", [D, S_], F32, kind="ExternalOutput").ap()

    xmid = dscr("xmid", [D, T_], F32)
    xres = dscr("xres", [D, T_], F32)
    QA = dscr("QA", [512, T_], BF16)
    KA = dscr("KA", [128, T_], BF16)
    QB = dscr("QB", [256, T_], BF16)
    KB = dscr("KB", [256, T_], BF16)
    QC = dscr("QC", [384, T_], BF16)
    KC = dscr("KC", [384, T_], BF16)
    VALL = dscr("VALL", [T_, VW], BF16)
    MIX = dscr("MIX", [D, T_], BF16)
    NG = 9
    db = {n: [Buf(f"{n}{g}") for g in range(NG)] for n in
          ["xT", "xmid", "xres", "QA", "KA", "QB", "KB", "QC", "KC", "VALL", "MIX"]}

    def gcols(g):
        return (g * 512, 512) if g < 8 else (S_, C_)

    glob = ExitStack()

    def alloc(stack, shape, dt, name=None):
        uid[0] += 1
        t = stack.enter_context(nc.sbuf_tensor(f"{name or 't'}_{uid[0]}", list(shape), dt))
        return TB(t, name or "t")

    def palloc(stack, shape, dt, name=None):
        uid[0] += 1
        t = stack.enter_context(nc.psum_tensor(f"{name or 'p'}_{uid[0]}", [128, 512], F32))
        tb = TB(t, name or "p")
        tb.b.psum = True
        return tb

    def mm(out_ap, lhsT, rhs, first, last, reads, wbuf, inc=None, **kw):
        S.op("pe", lambda e: e.matmul(out_ap, lhsT=lhsT, rhs=rhs, start=first, stop=last, **kw),
             reads=reads, writes=[wbuf], pe_acc=not first, inc=(last if inc is None else inc))

    ones_bf = alloc(glob, [128, 128], BF16, "ones_bf")
    ones_f = alloc(glob, [128, 64], F32, "ones_f")
    ident8 = alloc(glob, [128, 128], BF16, "ident8")
    amask = alloc(glob, [128, 2, 128], BF16, "amask")
    modv = alloc(glob, [128, L_, 6, 8, 2], F32, "modv")
    gain_sb = alloc(glob, [128, 9, 8], F32, "gains")
    mlag_sb = alloc(glob, [128, L_, 3], F32, "mlag")
    esink = alloc(glob, [65, 32], F32, "esink")
    epsc = alloc(glob, [128, 1], F32, "epsc")

    S.op("dve", lambda e: e.memset(ones_bf.t[:], 1.0), writes=[ones_bf.b])
    S.op("dve", lambda e: e.memset(ones_f.t[:], 1.0), writes=[ones_f.b])
    S.op("dve", lambda e: e.memset(epsc.t[:], EPS), writes=[epsc.b])
    S.dma("sp", gain_sb.t[:], gains, writes=[gain_sb.b])
    S.dma("sp", mlag_sb.t[:], mla_g, writes=[mlag_sb.b])
    S.dma("sp", esink.t[:], sinkrow, writes=[esink.b])
    S.op("act", lambda e: e.activation(out=esink.t[:], in_=esink.t[:], func=AF.Exp), reads=[esink.b], writes=[esink.b])

    with ExitStack() as ph:
        stg = alloc(ph, [128, 2, 128], F32, "stg")
        S.dma("sp", stg.t[:, 0, :], ident8_d, writes=[stg.b])
        S.op("dve", lambda e: e.tensor_copy(out=ident8.t[:], in_=stg.t[:, 0, :]), reads=[stg.b], writes=[ident8.b])
        S.dma("sp", stg.t[:], amask_d, reads=[], writes=[stg.b])
        S.op("dve", lambda e: e.tensor_copy(out=amask.t[:], in_=stg.t[:]), reads=[stg.b], writes=[amask.b])
        cv = alloc(ph, [128, 8, 2], F32, "cv")
        sv = alloc(ph, [128, 8, 2], F32, "sv")
        bfm = alloc(ph, [128, L_, 48], F32, "bfm")
        S.dma("sp", cv.t[:], cvec, writes=[cv.b])
        S.dma("sp", bfm.t[:], b_ada, writes=[bfm.b])
        S.op("act", lambda e: e.activation(out=sv.t[:], in_=cv.t[:], func=AF.Silu), reads=[cv.b], writes=[sv.b])
        wst = Ring([alloc(ph, [128, 8, 768], F32, "wst") for _ in range(2)])
        mps = Ring([palloc(ph, [128, 48, 2], F32, "mps") for _ in range(2)])
        mods = alloc(ph, [128, 48, 2], F32, "mods")
        for l in range(n_layers):
            pm = mps.next()
            wa = w_ada[l].rearrange("(kc p) n -> p kc n", p=128)
            for pc in range(8):
                w = wst.next()
                S.dma("sp", w.t[:], wa[:, :, pc * 768:(pc + 1) * 768], writes=[w.b])
                for cc in range(6):
                    ch = pc * 6 + cc
                    for kc in range(8):
                        mm(pm.t[:, 2 * ch:2 * ch + 2], w.t[:, kc, cc * 128:(cc + 1) * 128], sv.t[:, kc, :],
                           kc == 0, kc == 7, [w.b, sv.b], pm.b)
            S.op("dve", lambda e: e.tensor_tensor(
                out=mods.t[:], in0=pm.t[:, 0:96].rearrange("p (c j) -> p c j", j=2), in1=bfm.t[:, l, :].unsqueeze(2).broadcast_to([128, 48, 2]),
                op=ALU.add), reads=[pm.b, bfm.b], writes=[mods.b])
            for (k, src_m, gidx) in ((0, 1, l), (3, 4, 4 + l)):
                S.op("dve", lambda e: e.scalar_tensor_tensor(
                    out=modv.t[:, l, k, :, :], in0=mods.t[:, src_m * 8:(src_m + 1) * 8, :], scalar=1.0,
                    in1=gain_sb.t[:, gidx, :].unsqueeze(2).broadcast_to([128, 8, 2]),
                    op0=ALU.add, op1=ALU.mult), reads=[mods.b, gain_sb.b], writes=[modv.b], part=True)
            for (k, src_m) in ((1, 0), (2, 2), (4, 3), (5, 5)):
                S.op("dve", lambda e: e.tensor_copy(out=modv.t[:, l, k, :, :], in_=mods.t[:, src_m * 8:(src_m + 1) * 8, :]),
                     reads=[mods.b], writes=[modv.b], part=True)
    S.barrier()
    if stop_after == ("M", 0):
        if debug:
            dbg = nc.dram_tensor("dbg_modv", [128, L_ * 96], F32, kind="ExternalOutput").ap()
            S.dma("sp", dbg, modv.t[:].rearrange("p l k c j -> p (l k c j)"), reads=[modv.b])
        S.finish("sp")
        glob.close()
        return nc, S

    def norm_sq(xg, W, sq):
        S.op("dve", lambda e: e.tensor_tensor(out=sq.t[:, :, :W], in0=xg.t[:, :, :W], in1=xg.t[:, :, :W], op=ALU.mult),
             reads=[xg.b], writes=[sq.b])

    def norm_mod(xg, W, l, kG, kS, j, hT, hbufs, sq, ps_ss, lnv, rstd, tmps):
        for kc in range(8):
            mm(ps_ss.t[:, :W], ones_bf.t[:], sq.t[:, kc, :W], kc == 0, kc == 7, [sq.b, ones_bf.b], ps_ss.b)
        S.op("act", lambda e: e.activation(out=lnv.t[:, :W], in_=ps_ss.t[:, :W], func=AF.Ln, scale=1.0 / D, bias=epsc.t[:, 0:1]),
             reads=[ps_ss.b, epsc.b], writes=[lnv.b])
        S.op("act", lambda e: e.activation(out=rstd.t[:, :W], in_=lnv.t[:, :W], func=AF.Exp, scale=-0.5),
             reads=[lnv.b], writes=[rstd.b])
        for kc in range(8):
            tm = tmps.next()
            S.op("dve", lambda e: e.scalar_tensor_tensor(
                out=tm.t[:, :W], in0=xg.t[:, kc, :W], scalar=modv.t[:, l, kG, kc, j:j + 1], in1=rstd.t[:, :W],
                op0=ALU.mult, op1=ALU.mult), reads=[xg.b, modv.b, rstd.b], writes=[tm.b])
            S.op("act", lambda e: e.activation(out=hT.t[:, kc, :W], in_=tm.t[:, :W], func=AF.Identity,
                                               bias=modv.t[:, l, kS, kc, j:j + 1], scale=1.0),
                 reads=[tm.b, modv.b], writes=[hbufs[kc]])

    def load_cast(stack_ring, dst_ap_fn, src_ap_fn, n_pieces, dst_buf, engs=("dve", "pool")):
        for i in range(n_pieces):
            st = stack_ring.next()
            src = src_ap_fn(i)
            dst = dst_ap_fn(i)
            shp = list(src.shape)
            view = st.t[:shp[0], :int(np.prod(shp[1:]))]
            if len(shp) == 3:
                view = view.rearrange("p (a b) -> p a b", a=shp[1])
            S.dma("sp", view, src, writes=[st.b])
            eng = engs[i % len(engs)]
            if eng == "act":
                S.op("act", lambda e: e.activation(out=dst, in_=view, func=AF.Copy), reads=[st.b], writes=[dst_buf], part=True)
            else:
                S.op(eng, lambda e: e.tensor_copy(out=dst, in_=view), reads=[st.b], writes=[dst_buf], part=True)

    for l in range(n_layers):
        last = (l == L_ - 1)
        xsrc, xsrc_n = (xT, "xT") if l == 0 else (xres, "xres")
        xs_fm = xsrc.rearrange("(kc p) t -> p kc t", p=128)

        with ExitStack() as ph:
            xgs = Ring([alloc(ph, [128, 8, 512], F32, "xg") for _ in range(3)])
            stg_ring = Ring([TB(x.t[:].rearrange("p a b -> p (a b)"), "stgx") for x in xgs.items])
            for sgt, x in zip(stg_ring.items, xgs.items):
                sgt.b = x.b
            win = alloc(ph, [128, 8, NCOL], BF16, "win")
            wuq = alloc(ph, [128, 2, 768], BF16, "wuq")
            wukv = alloc(ph, [128, 512], BF16, "wukv")
            wie = w_in_ext[l].rearrange("(kc p) n -> p kc n", p=128)
            load_cast(stg_ring, lambda i: win.t[:, i, :], lambda i: wie[:, i, :], 8, win.b)
            wue = w_uq_ext[l].rearrange("(kc p) n -> p kc n", p=128)
            load_cast(stg_ring, lambda i: wuq.t[:, i, :], lambda i: wue[:, i, :], 2, wuq.b)
            load_cast(stg_ring, lambda i: wukv.t[:], lambda i: w_ukv_r[l], 1, wukv.b)

            hTs = [alloc(ph, [128, 8, 512], BF16, "hT") for _ in range(2)]
            hbs = [[Buf(f"h{i}_{k}") for k in range(8)] for i in range(2)]
            sq = alloc(ph, [128, 8, 512], BF16, "sq")
            lnv = alloc(ph, [128, 512], F32, "lnv")
            rstds = Ring([alloc(ph, [128, 512], F32, "rstd") for _ in range(2)])
            tmps = Ring([alloc(ph, [128, 512], F32, "tmp") for _ in range(3)])
            rtab = Ring([alloc(ph, [128, 6, 512], F32, "rtab") for _ in range(2)])
            outs = Ring([alloc(ph, [128, 512], BF16, "ost") for _ in range(6)])
            t1s = Ring([alloc(ph, [128, 512], F32, "t1") for _ in range(2)])
            t2s = Ring([alloc(ph, [128, 512], F32, "t2") for _ in range(2)])
            cq = alloc(ph, [128, 3, 512], F32, "cq")
            cqsq = alloc(ph, [128, 3, 512], BF16, "cqsq")
            cqn = alloc(ph, [128, 3, 512], BF16, "cqn")
            vsts = Ring([alloc(ph, [128, VW], BF16, "vst") for _ in range(2)])
            ps_ss = palloc(ph, [128, 512], F32, "ps_ss")
            ps_a = Ring([palloc(ph, [128, 512], F32, "ps_a") for _ in range(2)])
            ps_b = Ring([palloc(ph, [128, 512], F32, "ps_b") for _ in range(2)])
            ps_v = palloc(ph, [128, 384], F32, "ps_v")
            ps_v2 = palloc(ph, [128, 256], F32, "ps_v2")
            for v in vsts.items:
                S.op("dve", lambda e: e.memset(v.t[:], 1.0), writes=[v.b])

            xcur = {}

            def loadx(g):
                c0, W = gcols(g)
                xg = xgs.next()
                S.dma("sp", xg.t[:, :, :W], xs_fm[:, :, c0:c0 + W], reads=[db[xsrc_n][g]], writes=[xg.b])
                xcur[g] = xg

            def sqx(g):
                norm_sq(xcur[g], gcols(g)[1], sq)

            def normB(g):
                c0, W = gcols(g)
                norm_mod(xcur[g], W, l, 0, 1, 0 if g < 8 else 1, hTs[g % 2], hbs[g % 2], sq, ps_ss, lnv, rstds.next(), tmps)

            def evac_copy(ps, M, W, dst_ap, dst_bufs, eng):
                if eng == "act":
                    S.op("act", lambda e: e.activation(out=dst_ap, in_=ps.t[:M, :W], func=AF.Copy), reads=[ps.b], writes=dst_bufs)
                else:
                    S.op(eng, lambda e: e.tensor_copy(out=dst_ap, in_=ps.t[:M, :W]), reads=[ps.b], writes=dst_bufs)

            def proj_fm(g, hT, hb, col, scol, M, rhs_fn, nk, wt, wb, rope_idx, dst_list, rt):
                c0, W = gcols(g)
                pa = ps_a.next()
                for kc in range(nk):
                    mm(pa.t[:M, :W], wt(kc, col, M), rhs_fn(kc, W), kc == 0, kc == nk - 1, [wb] + hb, pa.b)
                o = outs.next()
                if rope_idx is None or g == 8:
                    evac_copy(pa, M, W, o.t[:M, :W], [o.b], "act" if (col // 128) % 2 == 0 else "dve")
                else:
                    pb = ps_b.next()
                    for kc in range(nk):
                        mm(pb.t[:M, :W], wt(kc, scol, M), rhs_fn(kc, W), kc == 0, kc == nk - 1, [wb] + hb, pb.b)
                    t1, t2 = t1s.next(), t2s.next()
                    S.op("dve", lambda e: e.tensor_tensor(out=t1.t[:M, :W], in0=pa.t[:M, :W], in1=rt.t[:M, rope_idx, :W], op=ALU.mult),
                         reads=[pa.b, rt.b], writes=[t1.b])
                    S.op("dve", lambda e: e.tensor_tensor(out=t2.t[:M, :W], in0=pb.t[:M, :W], in1=rt.t[:M, rope_idx + 1, :W], op=ALU.mult),
                         reads=[pb.b, rt.b], writes=[t2.b])
                    S.op("pool", lambda e: e.tensor_tensor(out=o.t[:M, :W], in0=t1.t[:M, :W], in1=t2.t[:M, :W], op=ALU.add),
                         reads=[t1.b, t2.b], writes=[o.b])
                for (dst, dbuf, p0, p1) in dst_list:
                    S.dma("sp", dst[:, c0:c0 + W], o.t[p0:p1, :W], reads=[o.b], writes=[dbuf], part=True)

            def do_proj(g, mid=None):
                c0, W = gcols(g)
                hT, hb = hTs[g % 2], hbs[g % 2]
                rt = None
                if g < 8:
                    rt = rtab.next()
                    S.dma("sp", rt.t[:, 0:2, :], ropeA[:, :, c0:c0 + W], writes=[rt.b], part=True)
                    S.dma("sp", rt.t[:96, 2:4, :], rope96[:, :, c0:c0 + W], writes=[rt.b], part=True)
                    S.dma("sp", rt.t[:32, 4:6, :], ropeR[:, :, c0:c0 + W], writes=[rt.b], part=True)
                wt_in = lambda kc, col, M: win.t[:, kc, col:col + M]
                rhs_h = lambda kc, W_: hT.t[:, kc, :W_]
                for c in range(4):
                    proj_fm(g, hT, hb, O_QA + c * 128, O_QAS + c * 128, 128, rhs_h, 8, wt_in, win.b, 0,
                            [(QA[c * 128:(c + 1) * 128, :], db["QA"][g], 0, 128)], rt)
                proj_fm(g, hT, hb, O_KA, O_KAS, 128, rhs_h, 8, wt_in, win.b, 0, [(KA[:, :], db["KA"][g], 0, 128)], rt)
                if mid is not None:
                    mid()
                for c in range(2):
                    proj_fm(g, hT, hb, O_QB + c * 128, None, 128, rhs_h, 8, wt_in, win.b, None,
                            [(QB[c * 128:(c + 1) * 128, :], db["QB"][g], 0, 128)], rt)
                for c in range(2):
                    proj_fm(g, hT, hb, O_KB + c * 128, None, 128, rhs_h, 8, wt_in, win.b, None,
                            [(KB[c * 128:(c + 1) * 128, :], db["KB"][g], 0, 128)], rt)
                proj_fm(g, hT, hb, O_KR, O_KRS, 32, rhs_h, 8, wt_in, win.b, 4,
                        [(KC[h * 96 + 64:h * 96 + 96, :], db["KC"][g], 0, 32) for h in range(4)], rt)
                for c in range(3):
                    pa = ps_a.next()
                    col = O_CQ + c * 128
                    for kc in range(8):
                        mm(pa.t[:, :W], win.t[:, kc, col:col + 128], hT.t[:, kc, :W], kc == 0, kc == 7, [win.b] + hb, pa.b)
                    S.op("dve", lambda e: e.tensor_copy(out=cq.t[:, c, :W], in_=pa.t[:, :W]), reads=[pa.b], writes=[cq.b])
                    S.op("act", lambda e: e.activation(out=cqsq.t[:, c, :W], in_=cq.t[:, c, :W], func=AF.Square), reads=[cq.b], writes=[cqsq.b])
                for (cs, n, gi0) in (((0, 1), 256, 0), ((2,), 128, 2)):
                    for i, c in enumerate(cs):
                        mm(ps_ss.t[:, :W], ones_bf.t[:], cqsq.t[:, c, :W], i == 0, i == len(cs) - 1, [cqsq.b, ones_bf.b], ps_ss.b)
                    rs = rstds.next()
                    S.op("act", lambda e: e.activation(out=lnv.t[:, :W], in_=ps_ss.t[:, :W], func=AF.Ln, scale=1.0 / n, bias=epsc.t[:, 0:1]),
                         reads=[ps_ss.b, epsc.b], writes=[lnv.b])
                    S.op("act", lambda e: e.activation(out=rs.t[:, :W], in_=lnv.t[:, :W], func=AF.Exp, scale=-0.5),
                         reads=[lnv.b], writes=[rs.b])
                    for c in cs:
                        S.op("dve", lambda e: e.scalar_tensor_tensor(
                            out=cqn.t[:, c, :W], in0=cq.t[:, c, :W], scalar=mlag_sb.t[:, l, c:c + 1], in1=rs.t[:, :W],
                            op0=ALU.mult, op1=ALU.mult), reads=[cq.b, mlag_sb.b, rs.b], writes=[cqn.b])
                wt_uq = lambda kc, col, M: wuq.t[:, kc, col:col + M]
                rhs_cq = lambda kc, W_: cqn.t[:, kc, :W_]
                for h in range(4):
                    proj_fm(g, cqn, [cqn.b], h * 96, 384 + h * 96, 96, rhs_cq, 2, wt_uq, wuq.b, 2,
                            [(QC[h * 96:(h + 1) * 96, :], db["QC"][g], 0, 96)], rt)
                wt_kv = lambda kc, col, M: wukv.t[:, col:col + M]
                rhs_kv = lambda kc, W_: cqn.t[:, 2, :W_]
                for c in range(2):
                    proj_fm(g, cqn, [cqn.b], c * 128, None, 128, rhs_kv, 1, wt_kv, wukv.b, None,
                            [(KC[(2 * c) * 96:(2 * c) * 96 + 64, :], db["KC"][g], 0, 64),
                             (KC[(2 * c + 1) * 96:(2 * c + 1) * 96 + 64, :], db["KC"][g], 64, 128)], rt)
                for tt in range(W // 128):
                    ts = slice(tt * 128, (tt + 1) * 128)
                    for kc in range(8):
                        mm(ps_v.t[:, 0:384], hT.t[:, kc, ts], win.t[:, kc, O_VA:O_VA + 384], kc == 0, kc == 7, [win.b] + hb, ps_v.b)
                    mm(ps_v2.t[:, 0:256], cqn.t[:, 2, ts], wukv.t[:, 256:512], True, True, [wukv.b, cqn.b], ps_v2.b)
                    v = vsts.next()
                    vv = v.t[:].rearrange("p (h c) -> p h c", c=65)
                    S.op("dve", lambda e: e.tensor_copy(out=vv[:, 0:6, 0:64], in_=ps_v.t[:, 0:384].rearrange("p (h c) -> p h c", c=64)),
                         reads=[ps_v.b], writes=[v.b], part=True)
                    S.op("act", lambda e: e.activation(out=vv[:, 6:10, 0:64], in_=ps_v2.t[:, 0:256].rearrange("p (h c) -> p h c", c=64), func=AF.Copy),
                         reads=[ps_v2.b], writes=[v.b], part=True)
                    S.dma("sp", VALL[c0 + tt * 128:c0 + (tt + 1) * 128, :], v.t[:], reads=[v.b], writes=[db["VALL"][g]], part=True)

            sub = stop_after[0] if (stop_after and stop_after[1] == l) else None
            if sub != "P0":
                loadx(0)
                loadx(1)
                sqx(0)
                normB(0)
                for g in range(NG):
                    if sub == "P1":
                        break
                    if g + 2 < NG:
                        loadx(g + 2)
                    if g + 1 < NG:
                        sqx(g + 1)
                    do_proj(g, mid=(lambda g=g: normB(g + 1)) if g + 1 < NG else None)
                    if sub == "P2":
                        break
        S.barrier()
        if stop_after in (("P", l), ("P0", l), ("P1", l), ("P2", l)):
            break

        qgroups = list(range(8)) + ([] if last else [8])
        mixf = MIX.rearrange("(h d) t -> d h t", d=64)

        def norm1(st, O, W, add_sink=None, use_act=False):
            dsum, rinv, bcs = st
            hv = (lambda a: a.rearrange("p (h t) -> p h t", h=4)) if add_sink is not None else (lambda a: a)
            if add_sink is None:
                S.op("dve", lambda e: e.tensor_copy(out=dsum.t[64:65, :W], in_=O.t[64:65, :W]), reads=[O.b], writes=[dsum.b])
            else:
                S.op("dve", lambda e: e.tensor_tensor(out=hv(dsum.t[64:65, :W]), in0=hv(O.t[64:65, :W]), in1=add_sink, op=ALU.add),
                     reads=[O.b, esrow.b], writes=[dsum.b])
            if use_act:
                S.op("act", lambda e: e.activation(out=dsum.t[64:65, :W], in_=dsum.t[64:65, :W], func=AF.Ln), reads=[dsum.b], writes=[dsum.b])
                S.op("act", lambda e: e.activation(out=rinv.t[64:65, :W], in_=dsum.t[64:65, :W], func=AF.Exp, scale=-1.0), reads=[dsum.b], writes=[rinv.b])
            else:
                S.op("dve", lambda e: e.reciprocal(out=rinv.t[64:65, :W], in_=dsum.t[64:65, :W]), reads=[dsum.b], writes=[rinv.b])

        def norm2(st, ps_bc, O, W, og_ap, og_buf, heads4=False):
            dsum, rinv, bcs = st
            hv = (lambda a: a.rearrange("p (h t) -> p h t", h=4)) if heads4 else (lambda a: a)
            mm(ps_bc.t[:64, :W], ones_f.t[64:65, 0:64], rinv.t[64:65, :W], True, True, [rinv.b, ones_f.b], ps_bc.b)
            S.op("dve", lambda e: e.tensor_copy(out=bcs.t[:64, :W], in_=ps_bc.t[:64, :W]), reads=[ps_bc.b], writes=[bcs.b])
            S.op("dve", lambda e: e.tensor_tensor(out=og_ap, in0=hv(O.t[:64, :W]), in1=hv(bcs.t[:64, :W]), op=ALU.mult),
                 reads=[O.b, bcs.b], writes=[og_buf], part=True)

        def nst_ring(ph, n):
            return Ring([(alloc(ph, [65, 512], F32, "dsum"), alloc(ph, [65, 512], F32, "rinv"), alloc(ph, [64, 512], F32, "bcs"))
                         for _ in range(n)])

        with ExitStack() as ph:
            KAs = alloc(ph, [64, 2, T_], BF16, "KAs")
            VAs = alloc(ph, [128, 34, 130], BF16, "VAs")
            esrow = alloc(ph, [65, 8, 128], F32, "esrow")
            S.dma("sp", KAs.t[:], KA.rearrange("(h d) t -> d h t", d=64), reads=db["KA"], writes=[KAs.b])
            vall_b = VALL.rearrange("(b p) c -> p b c", p=128)
            for i in range(0, 34, 9):
                S.dma("sp", VAs.t[:, i:min(i + 9, 34), :], vall_b[:, i:min(i + 9, 34), 0:130], reads=db["VALL"], writes=[VAs.b], part=True)
            S.op("dve", lambda e: e.memset(esrow.t[:], 0.0), writes=[esrow.b])
            for h in range(8):
                S.op("dve", lambda e: e.tensor_scalar(out=esrow.t[64:65, h, :], in0=esrow.t[64:65, h, :],
                                                      scalar1=esink.t[64:65, l * 8 + h:l * 8 + h + 1], scalar2=None, op0=ALU.add),
                     reads=[esink.b, esrow.b], writes=[esrow.b])
            Qgs = Ring([alloc(ph, [64, 8, 512], BF16, "Qg") for _ in range(2)])
            ogs = Ring([alloc(ph, [64, 8, 512], BF16, "og") for _ in range(2)])
            pts = Ring([alloc(ph, [128, 512], BF16, "pt") for _ in range(6)])
            nsts = nst_ring(ph, 3)
            ps_bc = palloc(ph, [64, 512], F32, "ps_bc")
            ps_s = Ring([palloc(ph, [128, 512], F32, "ps_s") for _ in range(4)])
            ps_o = Ring([palloc(ph, [65, 512], F32, "ps_o") for _ in range(3)])
            pipe = Pipe(2, 3)
            qaf = QA.rearrange("(h d) t -> d h t", d=64)

            def loadQ(g):
                c0, W = gcols(g)
                Qg = Qgs.next()
                S.dma("sp", Qg.t[:, :, :W], qaf[:, :, c0:c0 + W], reads=[db["QA"][g]], writes=[Qg.b])
                return Qg

            def stepA(Qg, O, kv, bi, kb, m, first, lastk, fin):
                cell = {}
                rhs = Qg.t[:, 4 * kv:4 * kv + 4, bi * 128:(bi + 1) * 128]

                def s1():
                    ps = ps_s.next()
                    psv = ps.t[:].rearrange("p (h t) -> p h t", h=4)
                    mm(psv, KAs.t[:, kv, kb * 128:(kb + 1) * 128], rhs, True, m is None, [KAs.b, Qg.b], ps.b)
                    if m is not None:
                        mm(psv, ident8.t[:], amask.t[:, m:m + 1, :].broadcast_to([128, 4, 128]), False, True,
                           [ident8.b, amask.b], ps.b)
                    pt = pts.next()
                    S.op("act", lambda e: e.activation(out=pt.t[:], in_=ps.t[:], func=AF.Exp, scale=0.125),
                         reads=[ps.b], writes=[pt.b])
                    cell["pt"] = pt

                def s2():
                    pt = cell["pt"]
                    if first:
                        pipe.force(O)
                    mm(O.t[:65, :], VAs.t[:, kb, kv * 65:(kv + 1) * 65], pt.t[:], first, lastk, [VAs.b, pt.b], O.b, inc=True)
                    if lastk:
                        fin()
                return s1, s2

            nxtQ = loadQ(qgroups[0])
            for gi, g in enumerate(qgroups):
                c0, W = gcols(g)
                Qg, og = nxtQ, ogs.next()
                if gi + 1 < len(qgroups):
                    nxtQ = loadQ(qgroups[gi + 1])
                for kv in range(2):
                    for bi in range(W // 128):
                        n = g * 4 + bi
                        if g < 8:
                            kbs = [(kb, m) for kb, m in ((n - 1, 0), (n, None), (n + 1, 1)) if 0 <= kb < 32] + [(32, None), (33, None)]
                        else:
                            kbs = [(32, None), (33, None)]
                        O = ps_o.next()
                        ogv = og.t[:, 4 * kv:4 * kv + 4, bi * 128:(bi + 1) * 128]

                        lastacc = (kv == 1 and bi == W // 128 - 1)

                        def fin(O=O, ogv=ogv, og=og, kv=kv, lastacc=lastacc, c0=c0, W=W, g=g):
                            st = nsts.next()
                            norm1(st, O, 512, add_sink=esrow.t[64:65, 4 * kv:4 * kv + 4, :], use_act=True)
                            pipe.defer(lambda: norm2(st, ps_bc, O, 512, ogv, og.b, heads4=True), tag=O)
                            if lastacc:
                                pipe.defer(lambda: S.dma("sp", mixf[:, 0:8, c0:c0 + W], og.t[:, :, :W], reads=[og.b], writes=[db["MIX"][g]], part=True))
                        for ki, (kb, m) in enumerate(kbs):
                            pipe.push(*stepA(Qg, O, kv, bi, kb, m, ki == 0, ki == len(kbs) - 1, fin))
            pipe.flush()
        S.barrier()
        if stop_after == ("TA", l):
            break

        with ExitStack() as ph:
            KBs = alloc(ph, [64, 4, T_], BF16, "KBs")
            VBc = alloc(ph, [128, 2, 260], BF16, "VBc")
            bm = alloc(ph, [128, 4, 4608], BF16, "bm")
            stg_ring = Ring([alloc(ph, [128, 4608], F32, "stgb") for _ in range(2)])
            S.dma("sp", KBs.t[:], KB.rearrange("(h d) t -> d h t", d=64), reads=db["KB"], writes=[KBs.b])
            vall_b = VALL.rearrange("(b p) c -> p b c", p=128)
            S.dma("sp", VBc.t[:], vall_b[:, 32:34, 130:390], reads=db["VALL"], writes=[VBc.b])
            load_cast(stg_ring, lambda i: bm.t[:, i, :], lambda i: rpb_tab[l, :, i, :], 4, bm.b)
            Qgs = Ring([alloc(ph, [64, 4, 512], BF16, "Qg") for _ in range(2)])
            ogs = Ring([alloc(ph, [64, 4, 512], BF16, "og") for _ in range(2)])
            pts = Ring([alloc(ph, [128, 512], BF16, "pt") for _ in range(6)])
            vgs = Ring([alloc(ph, [128, 260], BF16, "vg") for _ in range(8)])
            kgts = Ring([alloc(ph, [64, 4, 128], BF16, "kgt") for _ in range(4)])
            nsts = nst_ring(ph, 4)
            ps_bc = palloc(ph, [64, 512], F32, "ps_bc")
            ps_s = Ring([palloc(ph, [128, 512], F32, "ps_s") for _ in range(3)])
            ps_o = [palloc(ph, [65, 512], F32, "ps_o") for _ in range(4)]
            pipe = Pipe(2, 2)
            qbf = QB.rearrange("(h d) t -> d h t", d=64)

            def loadQ(g):
                c0, W = gcols(g)
                Qg = Qgs.next()
                S.dma("sp", Qg.t[:, :, :W], qbf[:, :, c0:c0 + W], reads=[db["QB"][g]], writes=[Qg.b])
                return Qg

            def stepBctx(Qg, O, h, ki, cb, W, lastk, fin):
                cell = {}

                def s1():
                    ps = ps_s.next()
                    mm(ps.t[:, :W], KBs.t[:, h, cb * 128:(cb + 1) * 128], Qg.t[:, h, :W], True, True, [KBs.b, Qg.b], ps.b)
                    pt = pts.next()
                    S.op("act", lambda e: e.activation(out=pt.t[:, :W], in_=ps.t[:, :W], func=AF.Exp, scale=0.125),
                         reads=[ps.b], writes=[pt.b])
                    cell["pt"] = pt

                def s2():
                    pt = cell["pt"]
                    if ki == 0:
                        pipe.force(O)
                    mm(O.t[:65, :W], VBc.t[:, ki, h * 65:(h + 1) * 65], pt.t[:, :W], ki == 0, lastk,
                       [VBc.b, pt.b], O.b, inc=True, skip_group_check=True)
                    if lastk:
                        fin()
                return s1, s2

            def stepBloc(Qg, j, kg, pat, kgt, vg, M, lastk, fins):
                cell = {}

                def s1():
                    ps = ps_s.next()
                    boff = (pat * 4 + kg) * 128
                    mm(ps.t[:M, :].rearrange("p (h t) -> p h t", h=4), ident8.t[:M, :M], bm.t[:M, :, boff:boff + 128],
                       True, False, [ident8.b, bm.b], ps.b)
                    for h in range(4):
                        qview = Qg.t[:, h, :].rearrange("p (r c) -> p r c", c=64)[:, :, 16 * j:16 * j + 16]
                        psv = ps.t[:M, h * 128:(h + 1) * 128].rearrange("p (r c) -> p r c", c=16)
                        mm(psv, kgt.t[:, h, :M], qview, False, h == 3, [kgt.b, Qg.b], ps.b, skip_group_check=True)
                    pt = pts.next()
                    S.op("act", lambda e: e.activation(out=pt.t[:M, :], in_=ps.t[:M, :], func=AF.Exp, scale=0.125),
                         reads=[ps.b], writes=[pt.b])
                    cell["pt"] = pt

                def s2():
                    pt = cell["pt"]
                    for h in range(4):
                        O = ps_o[h]
                        ov = O.t[:65, :].rearrange("p (r c) -> p r c", c=64)[:, :, 16 * j:16 * j + 16]
                        ptv = pt.t[:M, h * 128:(h + 1) * 128].rearrange("p (r c) -> p r c", c=16)
                        mm(ov, vg.t[:M, h * 65:(h + 1) * 65], ptv, False, lastk, [vg.b, pt.b], O.b,
                           inc=(h == 3), skip_group_check=True)
                    if lastk:
                        for f in fins:
                            f()
                return s1, s2

            nxtQ = loadQ(qgroups[0])
            for gi, g in enumerate(qgroups):
                c0, W = gcols(g)
                Qg, og = nxtQ, ogs.next()
                if gi + 1 < len(qgroups):
                    nxtQ = loadQ(qgroups[gi + 1])

                def mkfin(h, og=og, W=W, c0=c0, g=g):
                    def fin():
                        st = nsts.next()
                        O = ps_o[h]
                        norm1(st, O, W)
                        pipe.defer(lambda: norm2(st, ps_bc, O, W, og.t[:, h, :W], og.b), tag=O)
                        if h == 3:
                            pipe.defer(lambda: S.dma("sp", mixf[:, 8:12, c0:c0 + W], og.t[:, :, :W], reads=[og.b], writes=[db["MIX"][g]], part=True))
                    return fin
                for h in range(4):
                    for ki, cb in enumerate((32, 33)):
                        pipe.push(*stepBctx(Qg, ps_o[h], h, ki, cb, W, (g == 8 and ki == 1), mkfin(h)))
                if g < 8:
                    i = g
                    r0 = int(NA_KR0[i])
                    for j in range(4):
                        cc0 = int(NA_KC0[j])
                        pat = ICLS[i] * 3 + JCLS[j]
                        c0w = min(cc0, 32)
                        for kg in range(4):
                            nr = KG_ROWS[kg]
                            M = nr * 32
                            vg = vgs.next()
                            tok0 = (r0 + 4 * kg) * 64 + c0w
                            for rr in range(nr):
                                S.dma("sp", vg.t[rr * 32:(rr + 1) * 32, :], VALL[tok0 + rr * 64:tok0 + rr * 64 + 32, 130:390],
                                      reads=db["VALL"], writes=[vg.b], part=True)
                            kgt = kgts.next()
                            S.op("pool", lambda e: e.tensor_copy(
                                out=kgt.t[:, :, :M].rearrange("p h (r c) -> p h r c", c=32),
                                in_=KBs.t[:, :, tok0:tok0 + nr * 64].rearrange("p h (r c) -> p h r c", c=64)[:, :, :, 0:32]),
                                reads=[KBs.b], writes=[kgt.b])
                            pipe.push(*stepBloc(Qg, j, kg, pat, kgt, vg, M, (j == 3 and kg == 3), [mkfin(h) for h in range(4)]))
            pipe.flush()
        S.barrier()
        if stop_after == ("TB", l):
            break

        wsc = ExitStack()
        stgW = Ring([alloc(wsc, [128, 2048], F32, "stgW") for _ in range(2)])
        w1 = alloc(wsc, [128, 8, 4 * D], BF16, "w1")
        w1d = w_mlp_in[l].rearrange("(kc p) n -> p kc n", p=128)
        w2d = w_mlp_out[l].rearrange("(f p) n -> p f n", p=128)
        wpieces = []

        def piece(dst, src, dbuf, eng):
            def f():
                st = stgW.next()
                shp = list(src.shape)
                view = st.t[:, :int(np.prod(shp[1:]))]
                if len(shp) == 3:
                    view = view.rearrange("p (a b) -> p a b", a=shp[1])
                S.dma("sp", view, src, writes=[st.b])
                if eng == "act":
                    S.op("act", lambda e: e.activation(out=dst, in_=view, func=AF.Copy), reads=[st.b], writes=[dbuf], part=True)
                else:
                    S.op(eng, lambda e: e.tensor_copy(out=dst, in_=view), reads=[st.b], writes=[dbuf], part=True)
            return f

        def emit_pieces(n):
            for _ in range(n):
                if wpieces:
                    wpieces.pop(0)()
        for i in range(16):
            wpieces.append(piece(w1.t[:, i // 2, (i % 2) * 2048:(i % 2 + 1) * 2048],
                                 w1d[:, i // 2, (i % 2) * 2048:(i % 2 + 1) * 2048], w1.b, "pool"))

        with ExitStack() as ph:
            KCs = alloc(ph, [96, 4, T_], BF16, "KCs")
            VCs = alloc(ph, [128, 34, 260], BF16, "VCs")
            S.dma("sp", KCs.t[:], KC.rearrange("(h d) t -> d h t", d=96), reads=db["KC"], writes=[KCs.b])
            vall_b = VALL.rearrange("(b p) c -> p b c", p=128)
            for i in range(0, 34, 9):
                S.dma("sp", VCs.t[:, i:min(i + 9, 34), :], vall_b[:, i:min(i + 9, 34), 390:650], reads=db["VALL"], writes=[VCs.b], part=True)
            Qgs = Ring([alloc(ph, [96, 4, 512], BF16, "Qg") for _ in range(2)])
            ogs = Ring([alloc(ph, [64, 4, 512], BF16, "og") for _ in range(2)])
            pts = Ring([alloc(ph, [128, 512], BF16, "pt") for _ in range(6)])
            nsts = nst_ring(ph, 2)
            ps_bc = palloc(ph, [64, 512], F32, "ps_bc")
            ps_s = Ring([palloc(ph, [128, 512], F32, "ps_s") for _ in range(4)])
            ps_o = Ring([palloc(ph, [65, 512], F32, "ps_o") for _ in range(2)])
            sc = float(96 ** -0.5)
            pipe = Pipe(3, 4)
            qcf = QC.rearrange("(h d) t -> d h t", d=96)

            def loadQ(g):
                c0, W = gcols(g)
                Qg = Qgs.next()
                S.dma("sp", Qg.t[:, :, :W], qcf[:, :, c0:c0 + W], reads=[db["QC"][g]], writes=[Qg.b])
                return Qg

            def stepC(Qg, O, h, kb, W, first, lastk, fin):
                cell = {}

                def s1():
                    ps = ps_s.next()
                    mm(ps.t[:, :W], KCs.t[:, h, kb * 128:(kb + 1) * 128], Qg.t[:, h, :W], True, True, [KCs.b, Qg.b], ps.b)
                    pt = pts.next()
                    S.op("act", lambda e: e.activation(out=pt.t[:, :W], in_=ps.t[:, :W], func=AF.Exp, scale=sc),
                         reads=[ps.b], writes=[pt.b])
                    cell["pt"] = pt

                def s2():
                    pt = cell["pt"]
                    if first:
                        pipe.force(O)
                    mm(O.t[:65, :W], VCs.t[:, kb, h * 65:(h + 1) * 65], pt.t[:, :W], first, lastk, [VCs.b, pt.b], O.b, inc=True)
                    if lastk:
                        fin()
                return s1, s2

            nxtQ = loadQ(qgroups[0])
            for gi, g in enumerate(qgroups):
                c0, W = gcols(g)
                Qg, og = nxtQ, ogs.next()
                if gi + 1 < len(qgroups):
                    nxtQ = loadQ(qgroups[gi + 1])
                kbs = list(range(34)) if g < 8 else [32, 33]
                for h in range(4):
                    O = ps_o.next()

                    def fin(O=O, og=og, h=h, W=W, c0=c0, g=g):
                        st = nsts.next()
                        norm1(st, O, W)
                        pipe.defer(lambda: norm2(st, ps_bc, O, W, og.t[:, h, :W], og.b), tag=O)
                        if h == 3:
                            pipe.defer(lambda: S.dma("sp", mixf[:, 12:16, c0:c0 + W], og.t[:, :, :W], reads=[og.b], writes=[db["MIX"][g]], part=True))
                    for ki, kb in enumerate(kbs):
                        pipe.push(*stepC(Qg, O, h, kb, W, ki == 0, ki == len(kbs) - 1, fin))
                    if (gi * 4 + h) % 2 == 1:
                        emit_pieces(1)
            pipe.flush()
            emit_pieces(len(wpieces))
        S.barrier()
        if stop_after == ("TC", l):
            wsc.close()
            break

        w2 = alloc(wsc, [128, 32, D], BF16, "w2")
        for i in range(16):
            wpieces.append(piece(w2.t[:, 2 * i:2 * i + 2, :], w2d[:, 2 * i:2 * i + 2, :], w2.b, "pool" if i % 2 else "act"))
        WG = 256
        ngr = 16 + (0 if last else 1)
        with ExitStack() as ph:
            wo = alloc(ph, [128, 8, D], BF16, "wo")
            wod = w_out[l].rearrange("(kc p) n -> p kc n", p=128)
            for i in range(4):
                piece(wo.t[:, 2 * i:2 * i + 2, :], wod[:, 2 * i:2 * i + 2, :], wo.b, "pool" if i % 2 else "act")()
            mgs = Ring([alloc(ph, [128, 8, WG], BF16, "mixg") for _ in range(2)])
            xgs = Ring([alloc(ph, [128, 8, WG], F32, "xg") for _ in range(2)])
            ps_y = Ring([palloc(ph, [128, 512], F32, "ps_y") for _ in range(4)])
            mix_fm = MIX.rearrange("(kc p) t -> p kc t", p=128)
            xm_fm = xmid.rearrange("(kc p) t -> p kc t", p=128)

            def loadO1(gg):
                c0 = gg * WG
                g = gg // 2 if gg < 16 else 8
                mg, xg = mgs.next(), xgs.next()
                S.dma("sp", mg.t[:], mix_fm[:, :, c0:c0 + WG], reads=[db["MIX"][g]], writes=[mg.b])
                S.dma("sp", xg.t[:], xs_fm[:, :, c0:c0 + WG], reads=[db[xsrc_n][g]], writes=[xg.b])
                return mg, xg
            nxt = loadO1(0)
            for gg in range(ngr):
                c0 = gg * WG
                g = gg // 2 if gg < 16 else 8
                j = 0 if gg < 16 else 1
                mg, xg = nxt
                if gg + 1 < ngr:
                    nxt = loadO1(gg + 1)
                for c in range(8):
                    py = ps_y.next()
                    for kc in range(8):
                        mm(py.t[:, :WG], wo.t[:, kc, c * 128:(c + 1) * 128], mg.t[:, kc, :], kc == 0, kc == 7, [wo.b, mg.b], py.b)
                    S.op("dve", lambda e: e.scalar_tensor_tensor(
                        out=xg.t[:, c, :], in0=py.t[:, :WG], scalar=modv.t[:, l, 2, c, j:j + 1], in1=xg.t[:, c, :],
                        op0=ALU.mult, op1=ALU.add), reads=[py.b, modv.b, xg.b], writes=[xg.b])
                S.dma("sp", xm_fm[:, :, c0:c0 + WG], xg.t[:], reads=[xg.b], writes=[db["xmid"][g]])
                emit_pieces(1)
            emit_pieces(len(wpieces))
        S.barrier()
        if stop_after == ("O1", l):
            wsc.close()
            break

        with ExitStack() as ph:
            xgs_items = [alloc(ph, [128, 8, WG], F32, "xg")]
            for st_ in stgW.items:
                xv = TB(st_.t[:].rearrange("p (a b) -> p a b", a=8), "xgW")
                xv.b = st_.b
                xgs_items.append(xv)
            xgs = Ring(xgs_items)
            hT2s = [alloc(ph, [128, 8, WG], BF16, "hT2") for _ in range(2)]
            hb2s = [[Buf(f"h2_{i}_{k}") for k in range(8)] for i in range(2)]
            sq = alloc(ph, [128, 8, WG], BF16, "sq")
            lnv = alloc(ph, [128, WG], F32, "lnv")
            rstd = alloc(ph, [128, WG], F32, "rstd")
            tmps = Ring([alloc(ph, [128, WG], F32, "tmp") for _ in range(3)])
            rl = Ring([alloc(ph, [128, WG], F32, "rl") for _ in range(3)])
            aT = alloc(ph, [128, 32, WG], BF16, "aT")
            abufs = [Buf(f"a{f}") for f in range(32)]
            ps_ss = palloc(ph, [128, WG], F32, "ps_ss")
            ps_u = Ring([palloc(ph, [128, WG], F32, "ps_u") for _ in range(3)])
            ps_y = Ring([palloc(ph, [128, WG], F32, "ps_y") for _ in range(2)])
            xm_fm = xmid.rearrange("(kc p) t -> p kc t", p=128)
            xr_fm = xres.rearrange("(kc p) t -> p kc t", p=128)
            out_fm = outT.rearrange("(kc p) t -> p kc t", p=128)
            xcur2 = {}

            def loadx2(gg):
                xg = xgs.next()
                g_ = gg // 2 if gg < 16 else 8
                S.dma("sp", xg.t[:], xm_fm[:, :, gg * WG:(gg + 1) * WG], reads=[db["xmid"][g_]], writes=[xg.b])
                xcur2[gg] = xg

            def normB2(gg):
                norm_mod(xcur2[gg], WG, l, 3, 4, 0 if gg < 16 else 1, hT2s[gg % 2], hb2s[gg % 2], sq, ps_ss, lnv, rstd, tmps)
            loadx2(0)
            if ngr > 1:
                loadx2(1)
            norm_sq(xcur2[0], WG, sq)
            normB2(0)
            for gg in range(ngr):
                c0 = gg * WG
                g = gg // 2 if gg < 16 else 8
                j = 0 if gg < 16 else 1
                xg = xcur2[gg]
                hT2, hb2 = hT2s[gg % 2], hb2s[gg % 2]
                if gg + 2 < ngr:
                    loadx2(gg + 2)
                if gg + 1 < ngr:
                    norm_sq(xcur2[gg + 1], WG, sq)
                for f in range(32):
                    pu = ps_u.next()
                    for kc in range(8):
                        mm(pu.t[:, :WG], w1.t[:, kc, f * 128:(f + 1) * 128], hT2.t[:, kc, :], kc == 0, kc == 7, [w1.b] + hb2, pu.b)
                    r = rl.next()
                    S.op("act", lambda e: e.activation(out=r.t[:], in_=pu.t[:, :WG], func=AF.Relu), reads=[pu.b], writes=[r.b])
                    S.op("pool" if f % 2 else "dve", lambda e: e.tensor_tensor(out=aT.t[:, f, :], in0=r.t[:], in1=r.t[:], op=ALU.mult),
                         reads=[r.b], writes=[abufs[f]])
                if gg + 1 < ngr:
                    normB2(gg + 1)
                for c in range(8):
                    py = ps_y.next()
                    for f in range(32):
                        mm(py.t[:, :WG], w2.t[:, f, c * 128:(c + 1) * 128], aT.t[:, f, :], f == 0, f == 31, [w2.b, abufs[f]], py.b)
                    S.op("dve", lambda e: e.scalar_tensor_tensor(
                        out=xg.t[:, c, :], in0=py.t[:, :WG], scalar=modv.t[:, l, 5, c, j:j + 1], in1=xg.t[:, c, :],
                        op0=ALU.mult, op1=ALU.add), reads=[py.b, modv.b, xg.b], writes=[xg.b])
                if not last:
                    S.dma("sp", xr_fm[:, :, c0:c0 + WG], xg.t[:], reads=[xg.b], writes=[db["xres"][g]])
                else:
                    S.op("dve", lambda e: e.tensor_tensor(out=sq.t[:], in0=xg.t[:], in1=xg.t[:], op=ALU.mult), reads=[xg.b], writes=[sq.b])
                    for kc in range(8):
                        mm(ps_ss.t[:, :WG], ones_bf.t[:], sq.t[:, kc, :], kc == 0, kc == 7, [sq.b, ones_bf.b], ps_ss.b)
                    S.op("act", lambda e: e.activation(out=lnv.t[:], in_=ps_ss.t[:, :WG], func=AF.Ln, scale=1.0 / D, bias=epsc.t[:, 0:1]),
                         reads=[ps_ss.b, epsc.b], writes=[lnv.b])
                    S.op("act", lambda e: e.activation(out=rstd.t[:], in_=lnv.t[:], func=AF.Exp, scale=-0.5), reads=[lnv.b], writes=[rstd.b])
                    for kc in range(8):
                        S.op("dve", lambda e: e.scalar_tensor_tensor(
                            out=xg.t[:, kc, :], in0=xg.t[:, kc, :], scalar=gain_sb.t[:, 8, kc:kc + 1], in1=rstd.t[:],
                            op0=ALU.mult, op1=ALU.mult), reads=[xg.b, gain_sb.b, rstd.b], writes=[xg.b])
                    S.dma("sp", out_fm[:, :, c0:c0 + WG], xg.t[:], reads=[xg.b], writes=[])
        wsc.close()
        S.barrier()
        if stop_after == ("O2", l):
            break

    S.finish("sp")
    glob.close()
    return nc, S


def kernel(**inputs):
    per_core = prep_inputs(inputs)
    nc, _ = build()
    res = run_bass_kernel_spmd(nc, per_core, core_ids=list(range(8)))
    out = np.stack([np.ascontiguousarray(r["outT"].T) for r in res.results], axis=0)
    return out.astype(np.float32)
```

```python
import math
from contextlib import ExitStack
import numpy as np
import concourse.bass as bass
import concourse.mybir as mybir
from concourse.bass_utils import run_bass_kernel_spmd

F32 = mybir.dt.float32
BF16 = mybir.dt.bfloat16
AF = mybir.ActivationFunctionType
ALU = mybir.AluOpType

D = 1024
S_ = 4096
C_ = 256
T_ = S_ + C_
L_ = 4
NCOL = 2624
EPS = 1e-6
NEG = -30000.0
O_QA, O_QAS, O_KA, O_KAS, O_QB, O_KB, O_CQ, O_CKV, O_KR, O_KRS, O_VA = (
    0, 512, 1024, 1152, 1280, 1536, 1792, 2048, 2176, 2208, 2240)
VW = 650


class Buf:
    __slots__ = ("name", "lw", "rs", "plw", "prs", "psum")

    def __init__(self, name="", psum=False):
        self.name = name
        self.lw = {}
        self.rs = {}
        self.plw = {}
        self.prs = {}
        self.psum = psum


class Sched:
    def __init__(self, nc, n_dma_slots=32, same_engine_sync=True):
        self.nc = nc
        self.same = same_engine_sync
        self.eng = {"pe": nc.tensor, "act": nc.scalar, "dve": nc.vector, "pool": nc.gpsimd, "sp": nc.sync}
        self.sem = {k: nc.alloc_semaphore(name=f"sem_{k}") for k in self.eng}
        self.cnt = {k: 0 for k in self.eng}
        self.seen = {k: {} for k in self.eng}
        self.dsem = [nc.alloc_semaphore(name=f"dsem{i}") for i in range(n_dma_slots)]
        self.dcnt = [0] * n_dma_slots
        self.dnext = 0
        self.n_wait = 0
        self.n_inst = 0
        self.log = {k: [] for k in self.eng}

    def _semof(self, key):
        return self.dsem[key] if isinstance(key, int) else self.sem[key]

    def _wait(self, e, key, val):
        if key == e and not self.same:
            return
        if self.seen[e].get(key, 0) >= val:
            return
        self.eng[e].wait_ge(self._semof(key), val)
        self.log[e].append(("w", key, val))
        self.seen[e][key] = val
        self.n_wait += 1

    def _deps(self, e, reads, writes, part=False):
        for b in reads:
            for k, v in b.lw.items():
                self._wait(e, k, v)
            if b.psum:
                for k, v in b.rs.items():
                    if k != e:
                        self._wait(e, k, v)
        for b in writes:
            if part and not b.rs:
                for k, v in b.plw.items():
                    self._wait(e, k, v)
                for k, v in b.prs.items():
                    self._wait(e, k, v)
            else:
                for k, v in b.lw.items():
                    self._wait(e, k, v)
                for k, v in b.rs.items():
                    self._wait(e, k, v)

    def _record(self, key, n, reads, writes, part):
        for b in reads:
            b.rs[key] = max(b.rs.get(key, 0), n)
        for b in writes:
            if b.rs or not part:
                b.plw, b.prs = b.lw, b.rs
                b.lw = {}
            b.lw[key] = max(b.lw.get(key, 0), n)
            b.rs = {}

    def op(self, e, fn, reads=(), writes=(), pe_acc=False, inc=True, part=False):
        if pe_acc:
            self._deps(e, reads, ())
        else:
            self._deps(e, reads, writes, part)
        ins = fn(self.eng[e])
        if inc:
            ins.then_inc(self.sem[e], 1)
            self.log[e].append(("i", e, 1))
            self.cnt[e] += 1
            n = self.cnt[e]
        else:
            n = self.cnt[e] + 1
        self.n_inst += 1
        self._record(e, n, reads, writes, part or pe_acc)
        return ins

    def dma(self, q, out, in_, reads=(), writes=(), part=False, **kw):
        self._deps(q, reads, writes, part)
        s = self.dnext
        self.dnext = (self.dnext + 1) % len(self.dsem)
        if self.dcnt[s] > 0:
            self._wait(q, s, 16 * self.dcnt[s])
        ins = self.eng[q].dma_start(out=out, in_=in_, **kw)
        ins.then_inc(self.dsem[s], 16)
        self.log[q].append(("i", s, 16))
        self.dcnt[s] += 1
        v = 16 * self.dcnt[s]
        self.n_inst += 1
        self._record(s, v, reads, writes, part)
        return ins

    def barrier(self):
        for e in self.eng:
            self.finish(e)

    def finish(self, e="sp"):
        for k in self.eng:
            if k != e and self.cnt[k] > 0:
                self._wait(e, k, self.cnt[k])
        for s in range(len(self.dsem)):
            if self.dcnt[s] > 0:
                self._wait(e, s, 16 * self.dcnt[s])


class Pipe:
    def __init__(self, la, d):
        self.la, self.d = la, d
        self.q = []
        self.later = []
        self.step = 0

    def _fire(self, upto=None):
        while self.later and (upto is None or self.later[0][0] <= upto):
            self.later.pop(0)[1]()

    def _s2(self):
        self.q.pop(0)()
        self.step += 1
        self._fire(self.step)

    def push(self, s1, s2):
        s1()
        self.q.append(s2)
        if len(self.q) > self.la:
            self._s2()

    def defer(self, fn, tag=None):
        self.later.append((self.step + self.d, fn, tag))

    def force(self, tag):
        idx = [i for i, it in enumerate(self.later) if it[2] is tag]
        if idx:
            for _ in range(idx[-1] + 1):
                self.later.pop(0)[1]()

    def flush(self):
        while self.q:
            self._s2()
        self._fire(None)


class TB:
    def __init__(self, t, name=""):
        self.t = t
        self.b = Buf(name)


class Ring:
    def __init__(self, items):
        self.items = items
        self.i = 0

    def next(self):
        it = self.items[self.i]
        self.i = (self.i + 1) % len(self.items)
        return it


def _perm_a(n):
    idx = np.arange(n)
    h, d = idx // 64, idx % 64
    j = d % 32
    partner = np.where(j < 16, d + 16, d - 16)
    return h * 64 + partner


def _perm_r(n=32):
    d = np.arange(n)
    jj = d % 16
    return np.where(jj < 8, d + 8, d - 8)


def _rope_tables():
    tok = np.arange(S_)
    row, col = tok // 64, tok % 64

    def tab(ndim_sec, half):
        freqs = (10000.0 ** (-np.arange(half, dtype=np.float32) / half)).astype(np.float32)
        cos = np.zeros((2 * ndim_sec, S_), np.float32)
        sin = np.zeros((2 * ndim_sec, S_), np.float32)
        for d in range(2 * ndim_sec):
            sec, j = d // ndim_sec, d % ndim_sec
            pos = (row if sec == 0 else col).astype(np.float32)
            ang = pos * freqs[j % half]
            cos[d] = np.cos(ang).astype(np.float32)
            sn = np.sin(ang).astype(np.float32)
            sin[d] = -sn if j < half else sn
        return cos, sin

    ca, sa = tab(32, 16)
    cr, sr = tab(16, 8)
    ropeA = np.stack([np.concatenate([ca, ca], 0), np.concatenate([sa, sa], 0)], 1)
    c96 = np.concatenate([np.ones((64, S_), np.float32), cr], 0)
    s96 = np.concatenate([np.zeros((64, S_), np.float32), sr], 0)
    rope96 = np.stack([c96, s96], 1)
    ropeR = np.stack([cr, sr], 1)
    return np.ascontiguousarray(ropeA), np.ascontiguousarray(rope96), np.ascontiguousarray(ropeR)


def _na_layout():
    rows, GW, NR, NCc, NQ = 64, 64, 8, 16, 16
    kr, kc = min(NR, rows), NCc
    qr, qc = math.gcd(rows, NR), NQ
    krb, kcb = min(qr - 1 + kr, rows), min(qc - 1 + kc, GW)
    nrb, ncb = rows // qr, GW // qc
    q_r = np.arange(nrb)[:, None] * qr + np.arange(qr)[None, :]
    q_c = np.arange(ncb)[:, None] * qc + np.arange(qc)[None, :]
    w_r = np.clip(q_r - kr // 2, 0, rows - kr)
    w_c = np.clip(q_c - kc // 2, 0, GW - kc)
    k_r = np.minimum(w_r[:, 0], rows - krb)[:, None] + np.arange(krb)[None, :]
    k_c = np.minimum(w_c[:, 0], GW - kcb)[:, None] + np.arange(kcb)[None, :]
    qr6 = q_r[:, None, :, None, None, None]
    qc6 = q_c[None, :, None, :, None, None]
    wr6 = w_r[:, None, :, None, None, None]
    wc6 = w_c[None, :, None, :, None, None]
    kr6 = k_r[:, None, None, None, :, None]
    kc6 = k_c[None, :, None, None, None, :]
    shape6 = (nrb, ncb, qr, qc, krb, kcb)

    def flat(a):
        return np.broadcast_to(a, shape6).reshape(nrb, ncb, qr * qc, krb * kcb)

    mask = flat((kr6 >= wr6) & (kr6 < wr6 + kr) & (kc6 >= wc6) & (kc6 < wc6 + kc))
    d_r = flat(np.clip(kr6 - qr6, 1 - NR, NR - 1) + NR - 1)
    d_c = flat(np.clip(kc6 - qc6, 1 - NCc, NCc - 1) + NCc - 1)
    return mask, d_r, d_c, k_r[:, 0], k_c[:, 0]


NA_MASK, NA_DR, NA_DC, NA_KR0, NA_KC0 = _na_layout()
ICLS = [0, 1, 1, 1, 1, 1, 1, 2]
JCLS = [0, 1, 1, 2]
IREP = [0, 1, 7]
JREP = [0, 1, 3]
KG_ROWS = [4, 4, 4, 3]


def _rpb_table(na_rpb):
    tab = np.full((L_, 128, 4, 9, 4, 128), NEG, np.float32)
    for ic in range(3):
        for jc in range(3):
            i, j = IREP[ic], JREP[jc]
            m = NA_MASK[i, j]
            dr, dc = NA_DR[i, j], NA_DC[i, j]
            g = na_rpb[:, :, dr, dc]
            g = np.where(m[None, None], g, np.float32(NEG))
            kc0 = int(NA_KC0[j])
            c0w = min(kc0, 32)
            for kg in range(4):
                for rr in range(KG_ROWS[kg]):
                    for cc in range(32):
                        col = c0w + cc
                        if kc0 <= col < kc0 + 31:
                            kk = (4 * kg + rr) * 31 + (col - kc0)
                            tab[:, rr * 32 + cc, :, ic * 3 + jc, kg, :] = g[:, :, :, kk]
    return np.ascontiguousarray(tab.reshape(L_, 128, 4, 9 * 4 * 128))


def _fm(v):
    v = np.asarray(v, np.float32)
    lead = v.shape[:-1]
    c = v.shape[-1] // 128
    v = v.reshape(*lead, c, 128)
    return np.ascontiguousarray(np.moveaxis(v, -1, 0))


def prep_inputs(inp):
    f = lambda a: np.ascontiguousarray(np.asarray(a, np.float32))
    w_in = f(inp["w_in"])
    pa512, pa128, pr = _perm_a(512), _perm_a(128), _perm_r()
    ext = np.concatenate([
        w_in[:, :, 0:512], w_in[:, :, 0:512][:, :, pa512],
        w_in[:, :, 512:640], w_in[:, :, 512:640][:, :, pa128],
        w_in[:, :, 768:1024], w_in[:, :, 1024:1280],
        w_in[:, :, 1536:1792], w_in[:, :, 1792:1920],
        w_in[:, :, 1920:1952], w_in[:, :, 1920:1952][:, :, pr],
        w_in[:, :, 640:768], w_in[:, :, 1280:1536]], axis=2)
    assert ext.shape[2] == NCOL
    w_uq = f(inp["mla_w_uq"])
    sw = np.concatenate([np.concatenate([h * 96 + np.arange(64), h * 96 + 64 + pr]) for h in range(4)])
    w_uq_ext = np.concatenate([w_uq, w_uq[:, :, sw]], axis=2)
    w_ukv = f(inp["mla_w_ukv"]).reshape(L_, 128, 4, 2, 64)
    w_ukv_r = np.ascontiguousarray(w_ukv.transpose(0, 1, 3, 2, 4).reshape(L_, 128, 512))
    ropeA, rope96, ropeR = _rope_tables()
    kq = np.arange(128)
    amask = np.zeros((128, 2, 128), np.float32)
    amask[:, 0, :] = np.where(kq[:, None] >= kq[None, :], 0.0, NEG)
    amask[:, 1, :] = np.where(kq[:, None] <= kq[None, :], 0.0, NEG)
    gains = np.concatenate([_fm(inp["norm1_g"]), _fm(inp["norm2_g"]), _fm(inp["final_norm_g"])[:, None, :]], 1)
    mla_g = np.concatenate([_fm(inp["mla_q_norm_g"]), _fm(inp["mla_kv_norm_g"])], 2)
    shared = {
        "w_ada": f(inp["w_ada"]),
        "b_ada": _fm(inp["b_ada"]),
        "gains": np.ascontiguousarray(gains),
        "w_in_ext": np.ascontiguousarray(ext),
        "w_uq_ext": np.ascontiguousarray(w_uq_ext),
        "w_ukv_r": w_ukv_r,
        "mla_g": np.ascontiguousarray(mla_g),
        "sinkrow": np.ascontiguousarray(np.broadcast_to(f(inp["attn_sink"]).reshape(1, 32), (65, 32))),
        "rpb_tab": _rpb_table(f(inp["na_rpb"])),
        "w_out": f(inp["w_out"]),
        "w_mlp_in": f(inp["w_mlp_in"]),
        "w_mlp_out": f(inp["w_mlp_out"]),
        "ropeA": ropeA, "rope96": rope96, "ropeR": ropeR,
        "amask": amask,
        "ident8": np.ascontiguousarray(8.0 * np.eye(128, dtype=np.float32)),
    }
    x, ctx, c, c_ctx = f(inp["x"]), f(inp["ctx"]), f(inp["c"]), f(inp["c_ctx"])
    per_core = []
    for b in range(x.shape[0]):
        xt = np.ascontiguousarray(np.concatenate([x[b].T, ctx[b].T], axis=1))
        cv = np.ascontiguousarray(np.stack([_fm(c[b]), _fm(c_ctx)], -1))
        d = dict(shared)
        d["xT"] = xt
        d["cvec"] = cv
        per_core.append(d)
    return per_core


def build(n_layers=L_, debug=False, stop_after=None):
    nc = bass.Bass("TRN2", target_bir_lowering=False)
    S = Sched(nc)
    uid = [0]

    def din(name, shape):
        return nc.dram_tensor(name, list(shape), F32, kind="ExternalInput").ap()

    def dscr(name, shape, dt):
        kind = "ExternalOutput" if debug else "Internal"
        return nc.dram_tensor(name, list(shape), dt, kind=kind).ap()

    xT = din("xT", [D, T_])
    cvec = din("cvec", [128, 8, 2])
    w_ada = din("w_ada", [L_, D, 6 * D])
    b_ada = din("b_ada", [128, L_, 48])
    gains = din("gains", [128, 9, 8])
    w_in_ext = din("w_in_ext", [L_, D, NCOL])
    w_uq_ext = din("w_uq_ext", [L_, 256, 768])
    w_ukv_r = din("w_ukv_r", [L_, 128, 512])
    mla_g = din("mla_g", [128, L_, 3])
    sinkrow = din("sinkrow", [65, 32])
    rpb_tab = din("rpb_tab", [L_, 128, 4, 4608])
    w_out = din("w_out", [L_, D, D])
    w_mlp_in = din("w_mlp_in", [L_, D, 4 * D])
    w_mlp_out = din("w_mlp_out", [L_, 4 * D, D])
    ropeA = din("ropeA", [128, 2, S_])
    rope96 = din("rope96", [96, 2, S_])
    ropeR = din("ropeR", [32, 2, S_])
    amask_d = din("amask", [128, 2, 128])
    ident8_d = din("ident8", [128, 128])
    outT = nc.dram_tensor("outT", [D, S_], F32, kind="ExternalOutput").ap()

    xmid = dscr("xmid", [D, T_], F32)
    xres = dscr("xres", [D, T_], F32)
    QA = dscr("QA", [512, T_], BF16)
    KA = dscr("KA", [128, T_], BF16)
    QB = dscr("QB", [256, T_], BF16)
    KB = dscr("KB", [256, T_], BF16)
    QC = dscr("QC", [384, T_], BF16)
    KC = dscr("KC", [384, T_], BF16)
    VALL = dscr("VALL", [T_, VW], BF16)
    MIX = dscr("MIX", [D, T_], BF16)
    NG = 9
    db = {n: [Buf(f"{n}{g}") for g in range(NG)] for n in
          ["xT", "xmid", "xres", "QA", "KA", "QB", "KB", "QC", "KC", "VALL", "MIX"]}

    def gcols(g):
        return (g * 512, 512) if g < 8 else (S_, C_)

    glob = ExitStack()

    def alloc(stack, shape, dt, name=None):
        uid[0] += 1
        t = stack.enter_context(nc.sbuf_tensor(f"{name or 't'}_{uid[0]}", list(shape), dt))
        return TB(t, name or "t")

    def palloc(stack, shape, dt, name=None):
        uid[0] += 1
        t = stack.enter_context(nc.psum_tensor(f"{name or 'p'}_{uid[0]}", [128, 512], F32))
        tb = TB(t, name or "p")
        tb.b.psum = True
        return tb

    def mm(out_ap, lhsT, rhs, first, last, reads, wbuf, inc=None, **kw):
        S.op("pe", lambda e: e.matmul(out_ap, lhsT=lhsT, rhs=rhs, start=first, stop=last, **kw),
             reads=reads, writes=[wbuf], pe_acc=not first, inc=(last if inc is None else inc))

    ones_bf = alloc(glob, [128, 128], BF16, "ones_bf")
    ones_f = alloc(glob, [128, 64], F32, "ones_f")
    ident8 = alloc(glob, [128, 128], BF16, "ident8")
    amask = alloc(glob, [128, 2, 128], BF16, "amask")
    modv = alloc(glob, [128, L_, 6, 8, 2], F32, "modv")
    gain_sb = alloc(glob, [128, 9, 8], F32, "gains")
    mlag_sb = alloc(glob, [128, L_, 3], F32, "mlag")
    esink = alloc(glob, [65, 32], F32, "esink")
    epsc = alloc(glob, [128, 1], F32, "epsc")

    S.op("dve", lambda e: e.memset(ones_bf.t[:], 1.0), writes=[ones_bf.b])
    S.op("dve", lambda e: e.memset(ones_f.t[:], 1.0), writes=[ones_f.b])
    S.op("dve", lambda e: e.memset(epsc.t[:], EPS), writes=[epsc.b])
    S.dma("sp", gain_sb.t[:], gains, writes=[gain_sb.b])
    S.dma("sp", mlag_sb.t[:], mla_g, writes=[mlag_sb.b])
    S.dma("sp", esink.t[:], sinkrow, writes=[esink.b])
    S.op("act", lambda e: e.activation(out=esink.t[:], in_=esink.t[:], func=AF.Exp), reads=[esink.b], writes=[esink.b])

    with ExitStack() as ph:
        stg = alloc(ph, [128, 2, 128], F32, "stg")
        S.dma("sp", stg.t[:, 0, :], ident8_d, writes=[stg.b])
        S.op("dve", lambda e: e.tensor_copy(out=ident8.t[:], in_=stg.t[:, 0, :]), reads=[stg.b], writes=[ident8.b])
        S.dma("sp", stg.t[:], amask_d, reads=[], writes=[stg.b])
        S.op("dve", lambda e: e.tensor_copy(out=amask.t[:], in_=stg.t[:]), reads=[stg.b], writes=[amask.b])
        cv = alloc(ph, [128, 8, 2], F32, "cv")
        sv = alloc(ph, [128, 8, 2], F32, "sv")
        bfm = alloc(ph, [128, L_, 48], F32, "bfm")
        S.dma("sp", cv.t[:], cvec, writes=[cv.b])
        S.dma("sp", bfm.t[:], b_ada, writes=[bfm.b])
        S.op("act", lambda e: e.activation(out=sv.t[:], in_=cv.t[:], func=AF.Silu), reads=[cv.b], writes=[sv.b])
        wst = Ring([alloc(ph, [128, 8, 1024], F32, "wst") for _ in range(2)])
        mps = Ring([palloc(ph, [128, 48, 2], F32, "mps") for _ in range(2)])
        rps = Ring([palloc(ph, [2, 512], F32, "rps") for _ in range(3)])
        mods = alloc(ph, [128, 48, 2], F32, "mods")
        modrow = Ring([alloc(ph, [2, 6 * D], F32, "modrow") for _ in range(2)])
        id2 = alloc(ph, [2, 2], F32, "id2")
        S.dma("sp", id2.t[:], ident8_d[0:2, 0:2], writes=[id2.b])
        S.op("dve", lambda e: e.tensor_scalar(out=id2.t[:], in0=id2.t[:], scalar1=0.125, scalar2=None, op0=ALU.mult),
             reads=[id2.b], writes=[id2.b])
        for l in range(n_layers):
            pm = mps.next()
            mr = modrow.next()
            wa = w_ada[l].rearrange("(kc p) n -> p kc n", p=128)
            for pc in range(6):
                w = wst.next()
                S.dma("sp", w.t[:], wa[:, :, pc * 1024:(pc + 1) * 1024], writes=[w.b])
                for cc in range(2):
                    rp = rps.next()
                    for kc in range(8):
                        mm(rp.t[0:2, :], sv.t[:, kc, :], w.t[:, kc, cc * 512:(cc + 1) * 512], kc == 0, kc == 7, [w.b, sv.b], rp.b)
                    col = pc * 1024 + cc * 512
                    S.op("act", lambda e: e.activation(out=mr.t[:, col:col + 512], in_=rp.t[0:2, :], func=AF.Copy),
                         reads=[rp.b], writes=[mr.b], part=True)
            for ch in range(48):
                S.op("pe", lambda e: e.transpose(pm.t[:, 2 * ch:2 * ch + 2], mr.t[:, ch * 128:(ch + 1) * 128], id2.t[:]),
                     reads=[mr.b, id2.b], writes=[pm.b], pe_acc=(ch > 0), inc=(ch == 47))
            S.op("dve", lambda e: e.tensor_tensor(
                out=mods.t[:], in0=pm.t[:, 0:96].rearrange("p (c j) -> p c j", j=2), in1=bfm.t[:, l, :].unsqueeze(2).broadcast_to([128, 48, 2]),
                op=ALU.add), reads=[pm.b, bfm.b], writes=[mods.b])
            for (k, src_m, gidx) in ((0, 1, l), (3, 4, 4 + l)):
                S.op("dve", lambda e: e.scalar_tensor_tensor(
                    out=modv.t[:, l, k, :, :], in0=mods.t[:, src_m * 8:(src_m + 1) * 8, :], scalar=1.0,
                    in1=gain_sb.t[:, gidx, :].unsqueeze(2).broadcast_to([128, 8, 2]),
                    op0=ALU.add, op1=ALU.mult), reads=[mods.b, gain_sb.b], writes=[modv.b], part=True)
            for (k, src_m) in ((1, 0), (2, 2), (4, 3), (5, 5)):
                S.op("dve", lambda e: e.tensor_copy(out=modv.t[:, l, k, :, :], in_=mods.t[:, src_m * 8:(src_m + 1) * 8, :]),
                     reads=[mods.b], writes=[modv.b], part=True)
    S.barrier()
    if stop_after == ("M", 0):
        if debug:
            dbg = nc.dram_tensor("dbg_modv", [128, L_ * 96], F32, kind="ExternalOutput").ap()
            S.dma("sp", dbg, modv.t[:].rearrange("p l k c j -> p (l k c j)"), reads=[modv.b])
        S.finish("sp")
        glob.close()
        return nc, S

    def norm_sq(xg, W, sq):
        S.op("dve", lambda e: e.tensor_tensor(out=sq.t[:, :, :W], in0=xg.t[:, :, :W], in1=xg.t[:, :, :W], op=ALU.mult),
             reads=[xg.b], writes=[sq.b])

    def norm_mod(xg, W, l, kG, kS, j, hT, hbufs, sq, ps_ss, lnv, rstd, tmps):
        for kc in range(8):
            mm(ps_ss.t[:, :W], ones_bf.t[:], sq.t[:, kc, :W], kc == 0, kc == 7, [sq.b, ones_bf.b], ps_ss.b)
        S.op("act", lambda e: e.activation(out=lnv.t[:, :W], in_=ps_ss.t[:, :W], func=AF.Ln, scale=1.0 / D, bias=epsc.t[:, 0:1]),
             reads=[ps_ss.b, epsc.b], writes=[lnv.b])
        S.op("act", lambda e: e.activation(out=rstd.t[:, :W], in_=lnv.t[:, :W], func=AF.Exp, scale=-0.5),
             reads=[lnv.b], writes=[rstd.b])
        for kc in range(8):
            tm = tmps.next()
            S.op("dve", lambda e: e.scalar_tensor_tensor(
                out=tm.t[:, :W], in0=xg.t[:, kc, :W], scalar=modv.t[:, l, kG, kc, j:j + 1], in1=rstd.t[:, :W],
                op0=ALU.mult, op1=ALU.mult), reads=[xg.b, modv.b, rstd.b], writes=[tm.b])
            S.op("act", lambda e: e.activation(out=hT.t[:, kc, :W], in_=tm.t[:, :W], func=AF.Identity,
                                               bias=modv.t[:, l, kS, kc, j:j + 1], scale=1.0),
                 reads=[tm.b, modv.b], writes=[hbufs[kc]])

    def load_cast(stack_ring, dst_ap_fn, src_ap_fn, n_pieces, dst_buf, engs=("dve", "pool")):
        for i in range(n_pieces):
            st = stack_ring.next()
            src = src_ap_fn(i)
            dst = dst_ap_fn(i)
            shp = list(src.shape)
            view = st.t[:shp[0], :int(np.prod(shp[1:]))]
            if len(shp) == 3:
                view = view.rearrange("p (a b) -> p a b", a=shp[1])
            S.dma("sp", view, src, writes=[st.b])
            eng = engs[i % len(engs)]
            if eng == "act":
                S.op("act", lambda e: e.activation(out=dst, in_=view, func=AF.Copy), reads=[st.b], writes=[dst_buf], part=True)
            else:
                S.op(eng, lambda e: e.tensor_copy(out=dst, in_=view), reads=[st.b], writes=[dst_buf], part=True)

    for l in range(n_layers):
        last = (l == L_ - 1)
        xsrc, xsrc_n = (xT, "xT") if l == 0 else (xres, "xres")
        xs_fm = xsrc.rearrange("(kc p) t -> p kc t", p=128)

        with ExitStack() as ph:
            xgs = Ring([alloc(ph, [128, 8, 512], F32, "xg") for _ in range(3)])
            stg_ring = Ring([TB(x.t[:].rearrange("p a b -> p (a b)"), "stgx") for x in xgs.items])
            for sgt, x in zip(stg_ring.items, xgs.items):
                sgt.b = x.b
            win = alloc(ph, [128, 8, NCOL], BF16, "win")
            wuq = alloc(ph, [128, 2, 768], BF16, "wuq")
            wukv = alloc(ph, [128, 512], BF16, "wukv")
            wie = w_in_ext[l].rearrange("(kc p) n -> p kc n", p=128)
            load_cast(stg_ring, lambda i: win.t[:, i, :], lambda i: wie[:, i, :], 8, win.b)
            wue = w_uq_ext[l].rearrange("(kc p) n -> p kc n", p=128)
            load_cast(stg_ring, lambda i: wuq.t[:, i, :], lambda i: wue[:, i, :], 2, wuq.b)
            load_cast(stg_ring, lambda i: wukv.t[:], lambda i: w_ukv_r[l], 1, wukv.b)

            hTs = [alloc(ph, [128, 8, 512], BF16, "hT") for _ in range(2)]
            hbs = [[Buf(f"h{i}_{k}") for k in range(8)] for i in range(2)]
            sq = alloc(ph, [128, 8, 512], BF16, "sq")
            lnv = alloc(ph, [128, 512], F32, "lnv")
            rstds = Ring([alloc(ph, [128, 512], F32, "rstd") for _ in range(2)])
            tmps = Ring([alloc(ph, [128, 512], F32, "tmp") for _ in range(3)])
            rtab = Ring([alloc(ph, [128, 6, 512], F32, "rtab") for _ in range(2)])
            outs = Ring([alloc(ph, [128, 512], BF16, "ost") for _ in range(6)])
            t1s = Ring([alloc(ph, [128, 512], F32, "t1") for _ in range(2)])
            t2s = Ring([alloc(ph, [128, 512], F32, "t2") for _ in range(2)])
            cq = alloc(ph, [128, 3, 512], F32, "cq")
            cqsq = alloc(ph, [128, 3, 512], BF16, "cqsq")
            cqn = alloc(ph, [128, 3, 512], BF16, "cqn")
            vsts = Ring([alloc(ph, [128, VW], BF16, "vst") for _ in range(2)])
            ps_ss = palloc(ph, [128, 512], F32, "ps_ss")
            ps_a = Ring([palloc(ph, [128, 512], F32, "ps_a") for _ in range(2)])
            ps_b = Ring([palloc(ph, [128, 512], F32, "ps_b") for _ in range(2)])
            ps_v = palloc(ph, [128, 384], F32, "ps_v")
            ps_v2 = palloc(ph, [128, 256], F32, "ps_v2")
            for v in vsts.items:
                S.op("dve", lambda e: e.memset(v.t[:], 1.0), writes=[v.b])

            xcur = {}

            def loadx(g):
                c0, W = gcols(g)
                xg = xgs.next()
                S.dma("sp", xg.t[:, :, :W], xs_fm[:, :, c0:c0 + W], reads=[db[xsrc_n][g]], writes=[xg.b])
                xcur[g] = xg

            def sqx(g):
                norm_sq(xcur[g], gcols(g)[1], sq)

            def normB(g):
                c0, W = gcols(g)
                norm_mod(xcur[g], W, l, 0, 1, 0 if g < 8 else 1, hTs[g % 2], hbs[g % 2], sq, ps_ss, lnv, rstds.next(), tmps)

            def evac_copy(ps, M, W, dst_ap, dst_bufs, eng):
                if eng == "act":
                    S.op("act", lambda e: e.activation(out=dst_ap, in_=ps.t[:M, :W], func=AF.Copy), reads=[ps.b], writes=dst_bufs)
                else:
                    S.op(eng, lambda e: e.tensor_copy(out=dst_ap, in_=ps.t[:M, :W]), reads=[ps.b], writes=dst_bufs)

            def proj_fm(g, hT, hb, col, scol, M, rhs_fn, nk, wt, wb, rope_idx, dst_list, rt):
                c0, W = gcols(g)
                pa = ps_a.next()
                for kc in range(nk):
                    mm(pa.t[:M, :W], wt(kc, col, M), rhs_fn(kc, W), kc == 0, kc == nk - 1, [wb] + hb, pa.b)
                o = outs.next()
                if rope_idx is None or g == 8:
                    evac_copy(pa, M, W, o.t[:M, :W], [o.b], "act" if (col // 128) % 2 == 0 else "dve")
                else:
                    pb = ps_b.next()
                    for kc in range(nk):
                        mm(pb.t[:M, :W], wt(kc, scol, M), rhs_fn(kc, W), kc == 0, kc == nk - 1, [wb] + hb, pb.b)
                    t1, t2 = t1s.next(), t2s.next()
                    S.op("dve", lambda e: e.tensor_tensor(out=t1.t[:M, :W], in0=pa.t[:M, :W], in1=rt.t[:M, rope_idx, :W], op=ALU.mult),
                         reads=[pa.b, rt.b], writes=[t1.b])
                    S.op("dve", lambda e: e.tensor_tensor(out=t2.t[:M, :W], in0=pb.t[:M, :W], in1=rt.t[:M, rope_idx + 1, :W], op=ALU.mult),
                         reads=[pb.b, rt.b], writes=[t2.b])
                    S.op("pool", lambda e: e.tensor_tensor(out=o.t[:M, :W], in0=t1.t[:M, :W], in1=t2.t[:M, :W], op=ALU.add),
                         reads=[t1.b, t2.b], writes=[o.b])
                for (dst, dbuf, p0, p1) in dst_list:
                    S.dma("sp", dst[:, c0:c0 + W], o.t[p0:p1, :W], reads=[o.b], writes=[dbuf], part=True)

            def do_proj(g, mid=None):
                c0, W = gcols(g)
                hT, hb = hTs[g % 2], hbs[g % 2]
                rt = None
                if g < 8:
                    rt = rtab.next()
                    S.dma("sp", rt.t[:, 0:2, :], ropeA[:, :, c0:c0 + W], writes=[rt.b], part=True)
                    S.dma("sp", rt.t[:96, 2:4, :], rope96[:, :, c0:c0 + W], writes=[rt.b], part=True)
                    S.dma("sp", rt.t[:32, 4:6, :], ropeR[:, :, c0:c0 + W], writes=[rt.b], part=True)
                wt_in = lambda kc, col, M: win.t[:, kc, col:col + M]
                rhs_h = lambda kc, W_: hT.t[:, kc, :W_]
                for c in range(4):
                    proj_fm(g, hT, hb, O_QA + c * 128, O_QAS + c * 128, 128, rhs_h, 8, wt_in, win.b, 0,
                            [(QA[c * 128:(c + 1) * 128, :], db["QA"][g], 0, 128)], rt)
                proj_fm(g, hT, hb, O_KA, O_KAS, 128, rhs_h, 8, wt_in, win.b, 0, [(KA[:, :], db["KA"][g], 0, 128)], rt)
                if mid is not None:
                    mid()
                for c in range(2):
                    proj_fm(g, hT, hb, O_QB + c * 128, None, 128, rhs_h, 8, wt_in, win.b, None,
                            [(QB[c * 128:(c + 1) * 128, :], db["QB"][g], 0, 128)], rt)
                for c in range(2):
                    proj_fm(g, hT, hb, O_KB + c * 128, None, 128, rhs_h, 8, wt_in, win.b, None,
                            [(KB[c * 128:(c + 1) * 128, :], db["KB"][g], 0, 128)], rt)
                proj_fm(g, hT, hb, O_KR, O_KRS, 32, rhs_h, 8, wt_in, win.b, 4,
                        [(KC[h * 96 + 64:h * 96 + 96, :], db["KC"][g], 0, 32) for h in range(4)], rt)
                for c in range(3):
                    pa = ps_a.next()
                    col = O_CQ + c * 128
                    for kc in range(8):
                        mm(pa.t[:, :W], win.t[:, kc, col:col + 128], hT.t[:, kc, :W], kc == 0, kc == 7, [win.b] + hb, pa.b)
                    S.op("dve", lambda e: e.tensor_copy(out=cq.t[:, c, :W], in_=pa.t[:, :W]), reads=[pa.b], writes=[cq.b])
                    S.op("act", lambda e: e.activation(out=cqsq.t[:, c, :W], in_=cq.t[:, c, :W], func=AF.Square), reads=[cq.b], writes=[cqsq.b])
                for (cs, n, gi0) in (((0, 1), 256, 0), ((2,), 128, 2)):
                    for i, c in enumerate(cs):
                        mm(ps_ss.t[:, :W], ones_bf.t[:], cqsq.t[:, c, :W], i == 0, i == len(cs) - 1, [cqsq.b, ones_bf.b], ps_ss.b)
                    rs = rstds.next()
                    S.op("act", lambda e: e.activation(out=lnv.t[:, :W], in_=ps_ss.t[:, :W], func=AF.Ln, scale=1.0 / n, bias=epsc.t[:, 0:1]),
                         reads=[ps_ss.b, epsc.b], writes=[lnv.b])
                    S.op("act", lambda e: e.activation(out=rs.t[:, :W], in_=lnv.t[:, :W], func=AF.Exp, scale=-0.5),
                         reads=[lnv.b], writes=[rs.b])
                    for c in cs:
                        S.op("dve", lambda e: e.scalar_tensor_tensor(
                            out=cqn.t[:, c, :W], in0=cq.t[:, c, :W], scalar=mlag_sb.t[:, l, c:c + 1], in1=rs.t[:, :W],
                            op0=ALU.mult, op1=ALU.mult), reads=[cq.b, mlag_sb.b, rs.b], writes=[cqn.b])
                wt_uq = lambda kc, col, M: wuq.t[:, kc, col:col + M]
                rhs_cq = lambda kc, W_: cqn.t[:, kc, :W_]
                for h in range(4):
                    proj_fm(g, cqn, [cqn.b], h * 96, 384 + h * 96, 96, rhs_cq, 2, wt_uq, wuq.b, 2,
                            [(QC[h * 96:(h + 1) * 96, :], db["QC"][g], 0, 96)], rt)
                wt_kv = lambda kc, col, M: wukv.t[:, col:col + M]
                rhs_kv = lambda kc, W_: cqn.t[:, 2, :W_]
                for c in range(2):
                    proj_fm(g, cqn, [cqn.b], c * 128, None, 128, rhs_kv, 1, wt_kv, wukv.b, None,
                            [(KC[(2 * c) * 96:(2 * c) * 96 + 64, :], db["KC"][g], 0, 64),
                             (KC[(2 * c + 1) * 96:(2 * c + 1) * 96 + 64, :], db["KC"][g], 64, 128)], rt)
                for tt in range(W // 128):
                    ts = slice(tt * 128, (tt + 1) * 128)
                    for kc in range(8):
                        mm(ps_v.t[:, 0:384], hT.t[:, kc, ts], win.t[:, kc, O_VA:O_VA + 384], kc == 0, kc == 7, [win.b] + hb, ps_v.b)
                    mm(ps_v2.t[:, 0:256], cqn.t[:, 2, ts], wukv.t[:, 256:512], True, True, [wukv.b, cqn.b], ps_v2.b)
                    v = vsts.next()
                    vv = v.t[:].rearrange("p (h c) -> p h c", c=65)
                    S.op("dve", lambda e: e.tensor_copy(out=vv[:, 0:6, 0:64], in_=ps_v.t[:, 0:384].rearrange("p (h c) -> p h c", c=64)),
                         reads=[ps_v.b], writes=[v.b], part=True)
                    S.op("act", lambda e: e.activation(out=vv[:, 6:10, 0:64], in_=ps_v2.t[:, 0:256].rearrange("p (h c) -> p h c", c=64), func=AF.Copy),
                         reads=[ps_v2.b], writes=[v.b], part=True)
                    S.dma("sp", VALL[c0 + tt * 128:c0 + (tt + 1) * 128, :], v.t[:], reads=[v.b], writes=[db["VALL"][g]], part=True)

            sub = stop_after[0] if (stop_after and stop_after[1] == l) else None
            if sub != "P0":
                loadx(0)
                loadx(1)
                sqx(0)
                normB(0)
                for g in range(NG):
                    if sub == "P1":
                        break
                    if g + 2 < NG:
                        loadx(g + 2)
                    if g + 1 < NG:
                        sqx(g + 1)
                    do_proj(g, mid=(lambda g=g: normB(g + 1)) if g + 1 < NG else None)
                    if sub == "P2":
                        break
        S.barrier()
        if stop_after in (("P", l), ("P0", l), ("P1", l), ("P2", l)):
            break

        qgroups = list(range(8)) + ([] if last else [8])
        mixf = MIX.rearrange("(h d) t -> d h t", d=64)

        def norm1(st, O, W, add_sink=None, use_act=False):
            dsum, rinv, bcs = st
            hv = (lambda a: a.rearrange("p (h t) -> p h t", h=4)) if add_sink is not None else (lambda a: a)
            if add_sink is None:
                S.op("dve", lambda e: e.tensor_copy(out=dsum.t[64:65, :W], in_=O.t[64:65, :W]), reads=[O.b], writes=[dsum.b])
            else:
                S.op("dve", lambda e: e.tensor_tensor(out=hv(dsum.t[64:65, :W]), in0=hv(O.t[64:65, :W]), in1=add_sink, op=ALU.add),
                     reads=[O.b, esrow.b], writes=[dsum.b])
            if use_act:
                S.op("act", lambda e: e.activation(out=dsum.t[64:65, :W], in_=dsum.t[64:65, :W], func=AF.Ln), reads=[dsum.b], writes=[dsum.b])
                S.op("act", lambda e: e.activation(out=rinv.t[64:65, :W], in_=dsum.t[64:65, :W], func=AF.Exp, scale=-1.0), reads=[dsum.b], writes=[rinv.b])
            else:
                S.op("dve", lambda e: e.reciprocal(out=rinv.t[64:65, :W], in_=dsum.t[64:65, :W]), reads=[dsum.b], writes=[rinv.b])

        def norm2(st, ps_bc, O, W, og_ap, og_buf, heads4=False):
            dsum, rinv, bcs = st
            hv = (lambda a: a.rearrange("p (h t) -> p h t", h=4)) if heads4 else (lambda a: a)
            mm(ps_bc.t[:64, :W], ones_f.t[64:65, 0:64], rinv.t[64:65, :W], True, True, [rinv.b, ones_f.b], ps_bc.b)
            S.op("dve", lambda e: e.tensor_copy(out=bcs.t[:64, :W], in_=ps_bc.t[:64, :W]), reads=[ps_bc.b], writes=[bcs.b])
            S.op("dve", lambda e: e.tensor_tensor(out=og_ap, in0=hv(O.t[:64, :W]), in1=hv(bcs.t[:64, :W]), op=ALU.mult),
                 reads=[O.b, bcs.b], writes=[og_buf], part=True)

        def nst_ring(ph, n):
            return Ring([(alloc(ph, [65, 512], F32, "dsum"), alloc(ph, [65, 512], F32, "rinv"), alloc(ph, [64, 512], F32, "bcs"))
                         for _ in range(n)])

        with ExitStack() as ph:
            KAs = alloc(ph, [64, 2, T_], BF16, "KAs")
            VAs = alloc(ph, [128, 34, 130], BF16, "VAs")
            esrow = alloc(ph, [65, 8, 128], F32, "esrow")
            S.dma("sp", KAs.t[:], KA.rearrange("(h d) t -> d h t", d=64), reads=db["KA"], writes=[KAs.b])
            vall_b = VALL.rearrange("(b p) c -> p b c", p=128)
            for i in range(0, 34, 9):
                S.dma("sp", VAs.t[:, i:min(i + 9, 34), :], vall_b[:, i:min(i + 9, 34), 0:130], reads=db["VALL"], writes=[VAs.b], part=True)
            S.op("dve", lambda e: e.memset(esrow.t[:], 0.0), writes=[esrow.b])
            for h in range(8):
                S.op("dve", lambda e: e.tensor_scalar(out=esrow.t[64:65, h, :], in0=esrow.t[64:65, h, :],
                                                      scalar1=esink.t[64:65, l * 8 + h:l * 8 + h + 1], scalar2=None, op0=ALU.add),
                     reads=[esink.b, esrow.b], writes=[esrow.b])
            Qgs = Ring([alloc(ph, [64, 8, 512], BF16, "Qg") for _ in range(2)])
            ogs = Ring([alloc(ph, [64, 8, 512], BF16, "og") for _ in range(2)])
            pts = Ring([alloc(ph, [128, 512], BF16, "pt") for _ in range(6)])
            nsts = nst_ring(ph, 3)
            ps_bc = palloc(ph, [64, 512], F32, "ps_bc")
            ps_s = Ring([palloc(ph, [128, 512], F32, "ps_s") for _ in range(4)])
            ps_o = Ring([palloc(ph, [65, 512], F32, "ps_o") for _ in range(3)])
            pipe = Pipe(2, 3)
            qaf = QA.rearrange("(h d) t -> d h t", d=64)

            def loadQ(g):
                c0, W = gcols(g)
                Qg = Qgs.next()
                S.dma("sp", Qg.t[:, :, :W], qaf[:, :, c0:c0 + W], reads=[db["QA"][g]], writes=[Qg.b])
                return Qg

            def stepA(Qg, O, kv, bi, kb, m, first, lastk, fin):
                cell = {}
                rhs = Qg.t[:, 4 * kv:4 * kv + 4, bi * 128:(bi + 1) * 128]

                def s1():
                    ps = ps_s.next()
                    psv = ps.t[:].rearrange("p (h t) -> p h t", h=4)
                    mm(psv, KAs.t[:, kv, kb * 128:(kb + 1) * 128], rhs, True, m is None, [KAs.b, Qg.b], ps.b)
                    if m is not None:
                        mm(psv, ident8.t[:], amask.t[:, m:m + 1, :].broadcast_to([128, 4, 128]), False, True,
                           [ident8.b, amask.b], ps.b)
                    pt = pts.next()
                    S.op("act", lambda e: e.activation(out=pt.t[:], in_=ps.t[:], func=AF.Exp, scale=0.125),
                         reads=[ps.b], writes=[pt.b])
                    cell["pt"] = pt

                def s2():
                    pt = cell["pt"]
                    if first:
                        pipe.force(O)
                    mm(O.t[:65, :], VAs.t[:, kb, kv * 65:(kv + 1) * 65], pt.t[:], first, lastk, [VAs.b, pt.b], O.b, inc=True)
                    if lastk:
                        fin()
                return s1, s2

            nxtQ = loadQ(qgroups[0])
            for gi, g in enumerate(qgroups):
                c0, W = gcols(g)
                Qg, og = nxtQ, ogs.next()
                if gi + 1 < len(qgroups):
                    nxtQ = loadQ(qgroups[gi + 1])
                for kv in range(2):
                    for bi in range(W // 128):
                        n = g * 4 + bi
                        if g < 8:
                            kbs = [(kb, m) for kb, m in ((n - 1, 0), (n, None), (n + 1, 1)) if 0 <= kb < 32] + [(32, None), (33, None)]
                        else:
                            kbs = [(32, None), (33, None)]
                        O = ps_o.next()
                        ogv = og.t[:, 4 * kv:4 * kv + 4, bi * 128:(bi + 1) * 128]

                        lastacc = (kv == 1 and bi == W // 128 - 1)

                        def fin(O=O, ogv=ogv, og=og, kv=kv, lastacc=lastacc, c0=c0, W=W, g=g):
                            st = nsts.next()
                            norm1(st, O, 512, add_sink=esrow.t[64:65, 4 * kv:4 * kv + 4, :], use_act=True)
                            pipe.defer(lambda: norm2(st, ps_bc, O, 512, ogv, og.b, heads4=True), tag=O)
                            if lastacc:
                                pipe.defer(lambda: S.dma("sp", mixf[:, 0:8, c0:c0 + W], og.t[:, :, :W], reads=[og.b], writes=[db["MIX"][g]], part=True))
                        for ki, (kb, m) in enumerate(kbs):
                            pipe.push(*stepA(Qg, O, kv, bi, kb, m, ki == 0, ki == len(kbs) - 1, fin))
            pipe.flush()
        S.barrier()
        if stop_after == ("TA", l):
            break

        with ExitStack() as ph:
            KBs = alloc(ph, [64, 4, T_], BF16, "KBs")
            VBc = alloc(ph, [128, 2, 260], BF16, "VBc")
            bm = alloc(ph, [128, 4, 4608], BF16, "bm")
            stg_ring = Ring([alloc(ph, [128, 4608], F32, "stgb") for _ in range(2)])
            S.dma("sp", KBs.t[:], KB.rearrange("(h d) t -> d h t", d=64), reads=db["KB"], writes=[KBs.b])
            vall_b = VALL.rearrange("(b p) c -> p b c", p=128)
            S.dma("sp", VBc.t[:], vall_b[:, 32:34, 130:390], reads=db["VALL"], writes=[VBc.b])
            load_cast(stg_ring, lambda i: bm.t[:, i, :], lambda i: rpb_tab[l, :, i, :], 4, bm.b)
            Qgs = Ring([alloc(ph, [64, 4, 512], BF16, "Qg") for _ in range(2)])
            ogs = Ring([alloc(ph, [64, 4, 512], BF16, "og") for _ in range(2)])
            pts = Ring([alloc(ph, [128, 512], BF16, "pt") for _ in range(6)])
            vgs = Ring([alloc(ph, [128, 260], BF16, "vg") for _ in range(8)])
            kgts = Ring([alloc(ph, [64, 4, 128], BF16, "kgt") for _ in range(4)])
            nsts = nst_ring(ph, 4)
            ps_bc = palloc(ph, [64, 512], F32, "ps_bc")
            ps_s = Ring([palloc(ph, [128, 512], F32, "ps_s") for _ in range(3)])
            ps_o = [palloc(ph, [65, 512], F32, "ps_o") for _ in range(4)]
            pipe = Pipe(2, 2)
            qbf = QB.rearrange("(h d) t -> d h t", d=64)

            def loadQ(g):
                c0, W = gcols(g)
                Qg = Qgs.next()
                S.dma("sp", Qg.t[:, :, :W], qbf[:, :, c0:c0 + W], reads=[db["QB"][g]], writes=[Qg.b])
                return Qg

            def stepBctx(Qg, O, h, ki, cb, W, lastk, fin):
                cell = {}

                def s1():
                    ps = ps_s.next()
                    mm(ps.t[:, :W], KBs.t[:, h, cb * 128:(cb + 1) * 128], Qg.t[:, h, :W], True, True, [KBs.b, Qg.b], ps.b)
                    pt = pts.next()
                    S.op("act", lambda e: e.activation(out=pt.t[:, :W], in_=ps.t[:, :W], func=AF.Exp, scale=0.125),
                         reads=[ps.b], writes=[pt.b])
                    cell["pt"] = pt

                def s2():
                    pt = cell["pt"]
                    if ki == 0:
                        pipe.force(O)
                    mm(O.t[:65, :W], VBc.t[:, ki, h * 65:(h + 1) * 65], pt.t[:, :W], ki == 0, lastk,
                       [VBc.b, pt.b], O.b, inc=True, skip_group_check=True)
                    if lastk:
                        fin()
                return s1, s2

            def stepBloc(Qg, j, kg, pat, kgt, vg, M, lastk, fins):
                cell = {}

                def s1():
                    ps = ps_s.next()
                    boff = (pat * 4 + kg) * 128
                    mm(ps.t[:M, :].rearrange("p (h t) -> p h t", h=4), ident8.t[:M, :M], bm.t[:M, :, boff:boff + 128],
                       True, False, [ident8.b, bm.b], ps.b)
                    for h in range(4):
                        qview = Qg.t[:, h, :].rearrange("p (r c) -> p r c", c=64)[:, :, 16 * j:16 * j + 16]
                        psv = ps.t[:M, h * 128:(h + 1) * 128].rearrange("p (r c) -> p r c", c=16)
                        mm(psv, kgt.t[:, h, :M], qview, False, h == 3, [kgt.b, Qg.b], ps.b, skip_group_check=True)
                    pt = pts.next()
                    S.op("act", lambda e: e.activation(out=pt.t[:M, :], in_=ps.t[:M, :], func=AF.Exp, scale=0.125),
                         reads=[ps.b], writes=[pt.b])
                    cell["pt"] = pt

                def s2():
                    pt = cell["pt"]
                    for h in range(4):
                        O = ps_o[h]
                        ov = O.t[:65, :].rearrange("p (r c) -> p r c", c=64)[:, :, 16 * j:16 * j + 16]
                        ptv = pt.t[:M, h * 128:(h + 1) * 128].rearrange("p (r c) -> p r c", c=16)
                        mm(ov, vg.t[:M, h * 65:(h + 1) * 65], ptv, False, lastk, [vg.b, pt.b], O.b,
                           inc=(h == 3), skip_group_check=True)
                    if lastk:
                        for f in fins:
                            f()
                return s1, s2

            nxtQ = loadQ(qgroups[0])
            for gi, g in enumerate(qgroups):
                c0, W = gcols(g)
                Qg, og = nxtQ, ogs.next()
                if gi + 1 < len(qgroups):
                    nxtQ = loadQ(qgroups[gi + 1])

                def mkfin(h, og=og, W=W, c0=c0, g=g):
                    def fin():
                        st = nsts.next()
                        O = ps_o[h]
                        norm1(st, O, W)
                        pipe.defer(lambda: norm2(st, ps_bc, O, W, og.t[:, h, :W], og.b), tag=O)
                        if h == 3:
                            pipe.defer(lambda: S.dma("sp", mixf[:, 8:12, c0:c0 + W], og.t[:, :, :W], reads=[og.b], writes=[db["MIX"][g]], part=True))
                    return fin
                for h in range(4):
                    for ki, cb in enumerate((32, 33)):
                        pipe.push(*stepBctx(Qg, ps_o[h], h, ki, cb, W, (g == 8 and ki == 1), mkfin(h)))
                if g < 8:
                    i = g
                    r0 = int(NA_KR0[i])
                    for j in range(4):
                        cc0 = int(NA_KC0[j])
                        pat = ICLS[i] * 3 + JCLS[j]
                        c0w = min(cc0, 32)
                        for kg in range(4):
                            nr = KG_ROWS[kg]
                            M = nr * 32
                            vg = vgs.next()
                            tok0 = (r0 + 4 * kg) * 64 + c0w
                            for rr in range(nr):
                                S.dma("sp", vg.t[rr * 32:(rr + 1) * 32, :], VALL[tok0 + rr * 64:tok0 + rr * 64 + 32, 130:390],
                                      reads=db["VALL"], writes=[vg.b], part=True)
                            kgt = kgts.next()
                            S.op("pool", lambda e: e.tensor_copy(
                                out=kgt.t[:, :, :M].rearrange("p h (r c) -> p h r c", c=32),
                                in_=KBs.t[:, :, tok0:tok0 + nr * 64].rearrange("p h (r c) -> p h r c", c=64)[:, :, :, 0:32]),
                                reads=[KBs.b], writes=[kgt.b])
                            pipe.push(*stepBloc(Qg, j, kg, pat, kgt, vg, M, (j == 3 and kg == 3), [mkfin(h) for h in range(4)]))
            pipe.flush()
        S.barrier()
        if stop_after == ("TB", l):
            break

        wsc = ExitStack()
        stgW = Ring([alloc(wsc, [128, 2048], F32, "stgW") for _ in range(2)])
        w1 = alloc(wsc, [128, 8, 4 * D], BF16, "w1")
        w1d = w_mlp_in[l].rearrange("(kc p) n -> p kc n", p=128)
        w2d = w_mlp_out[l].rearrange("(f p) n -> p f n", p=128)
        wpieces = []

        def piece(dst, src, dbuf, eng):
            def f():
                st = stgW.next()
                shp = list(src.shape)
                view = st.t[:, :int(np.prod(shp[1:]))]
                if len(shp) == 3:
                    view = view.rearrange("p (a b) -> p a b", a=shp[1])
                S.dma("sp", view, src, writes=[st.b])
                if eng == "act":
                    S.op("act", lambda e: e.activation(out=dst, in_=view, func=AF.Copy), reads=[st.b], writes=[dbuf], part=True)
                else:
                    S.op(eng, lambda e: e.tensor_copy(out=dst, in_=view), reads=[st.b], writes=[dbuf], part=True)
            return f

        def emit_pieces(n):
            for _ in range(n):
                if wpieces:
                    wpieces.pop(0)()
        for i in range(16):
            wpieces.append(piece(w1.t[:, i // 2, (i % 2) * 2048:(i % 2 + 1) * 2048],
                                 w1d[:, i // 2, (i % 2) * 2048:(i % 2 + 1) * 2048], w1.b, "pool"))

        with ExitStack() as ph:
            KCs = alloc(ph, [96, 4, T_], BF16, "KCs")
            VCs = alloc(ph, [128, 34, 260], BF16, "VCs")
            S.dma("sp", KCs.t[:], KC.rearrange("(h d) t -> d h t", d=96), reads=db["KC"], writes=[KCs.b])
            vall_b = VALL.rearrange("(b p) c -> p b c", p=128)
            for i in range(0, 34, 9):
                S.dma("sp", VCs.t[:, i:min(i + 9, 34), :], vall_b[:, i:min(i + 9, 34), 390:650], reads=db["VALL"], writes=[VCs.b], part=True)
            Qgs = Ring([alloc(ph, [96, 4, 512], BF16, "Qg") for _ in range(2)])
            ogs = Ring([alloc(ph, [64, 4, 512], BF16, "og") for _ in range(2)])
            pts = Ring([alloc(ph, [128, 512], BF16, "pt") for _ in range(6)])
            nsts = nst_ring(ph, 2)
            ps_bc = palloc(ph, [64, 512], F32, "ps_bc")
            ps_s = Ring([palloc(ph, [128, 512], F32, "ps_s") for _ in range(4)])
            ps_o = Ring([palloc(ph, [65, 512], F32, "ps_o") for _ in range(2)])
            sc = float(96 ** -0.5)
            pipe = Pipe(3, 4)
            qcf = QC.rearrange("(h d) t -> d h t", d=96)

            def loadQ(g):
                c0, W = gcols(g)
                Qg = Qgs.next()
                S.dma("sp", Qg.t[:, :, :W], qcf[:, :, c0:c0 + W], reads=[db["QC"][g]], writes=[Qg.b])
                return Qg

            def stepC(Qg, O, h, kb, W, first, lastk, fin):
                cell = {}

                def s1():
                    ps = ps_s.next()
                    mm(ps.t[:, :W], KCs.t[:, h, kb * 128:(kb + 1) * 128], Qg.t[:, h, :W], True, True, [KCs.b, Qg.b], ps.b)
                    pt = pts.next()
                    S.op("act", lambda e: e.activation(out=pt.t[:, :W], in_=ps.t[:, :W], func=AF.Exp, scale=sc),
                         reads=[ps.b], writes=[pt.b])
                    cell["pt"] = pt

                def s2():
                    pt = cell["pt"]
                    if first:
                        pipe.force(O)
                    mm(O.t[:65, :W], VCs.t[:, kb, h * 65:(h + 1) * 65], pt.t[:, :W], first, lastk, [VCs.b, pt.b], O.b, inc=True)
                    if lastk:
                        fin()
                return s1, s2

            nxtQ = loadQ(qgroups[0])
            for gi, g in enumerate(qgroups):
                c0, W = gcols(g)
                Qg, og = nxtQ, ogs.next()
                if gi + 1 < len(qgroups):
                    nxtQ = loadQ(qgroups[gi + 1])
                kbs = list(range(34)) if g < 8 else [32, 33]
                for h in range(4):
                    O = ps_o.next()

                    def fin(O=O, og=og, h=h, W=W, c0=c0, g=g):
                        st = nsts.next()
                        norm1(st, O, W)
                        pipe.defer(lambda: norm2(st, ps_bc, O, W, og.t[:, h, :W], og.b), tag=O)
                        if h == 3:
                            pipe.defer(lambda: S.dma("sp", mixf[:, 12:16, c0:c0 + W], og.t[:, :, :W], reads=[og.b], writes=[db["MIX"][g]], part=True))
                    for ki, kb in enumerate(kbs):
                        pipe.push(*stepC(Qg, O, h, kb, W, ki == 0, ki == len(kbs) - 1, fin))
                    if (gi * 4 + h) % 2 == 1:
                        emit_pieces(1)
            pipe.flush()
            emit_pieces(len(wpieces))
        S.barrier()
        if stop_after == ("TC", l):
            wsc.close()
            break

        w2 = alloc(wsc, [128, 32, D], BF16, "w2")
        for i in range(16):
            wpieces.append(piece(w2.t[:, 2 * i:2 * i + 2, :], w2d[:, 2 * i:2 * i + 2, :], w2.b, "pool" if i % 2 else "act"))
        WG = 256
        ngr = 16 + (0 if last else 1)
        with ExitStack() as ph:
            wo = alloc(ph, [128, 8, D], BF16, "wo")
            wod = w_out[l].rearrange("(kc p) n -> p kc n", p=128)
            for i in range(4):
                piece(wo.t[:, 2 * i:2 * i + 2, :], wod[:, 2 * i:2 * i + 2, :], wo.b, "pool" if i % 2 else "act")()
            mgs = Ring([alloc(ph, [128, 8, WG], BF16, "mixg") for _ in range(2)])
            xgs = Ring([alloc(ph, [128, 8, WG], F32, "xg") for _ in range(2)])
            ps_y = Ring([palloc(ph, [128, 512], F32, "ps_y") for _ in range(4)])
            mix_fm = MIX.rearrange("(kc p) t -> p kc t", p=128)
            xm_fm = xmid.rearrange("(kc p) t -> p kc t", p=128)

            def loadO1(gg):
                c0 = gg * WG
                g = gg // 2 if gg < 16 else 8
                mg, xg = mgs.next(), xgs.next()
                S.dma("sp", mg.t[:], mix_fm[:, :, c0:c0 + WG], reads=[db["MIX"][g]], writes=[mg.b])
                S.dma("sp", xg.t[:], xs_fm[:, :, c0:c0 + WG], reads=[db[xsrc_n][g]], writes=[xg.b])
                return mg, xg
            nxt = loadO1(0)
            for gg in range(ngr):
                c0 = gg * WG
                g = gg // 2 if gg < 16 else 8
                j = 0 if gg < 16 else 1
                mg, xg = nxt
                if gg + 1 < ngr:
                    nxt = loadO1(gg + 1)
                for c in range(8):
                    py = ps_y.next()
                    for kc in range(8):
                        mm(py.t[:, :WG], wo.t[:, kc, c * 128:(c + 1) * 128], mg.t[:, kc, :], kc == 0, kc == 7, [wo.b, mg.b], py.b)
                    S.op("dve", lambda e: e.scalar_tensor_tensor(
                        out=xg.t[:, c, :], in0=py.t[:, :WG], scalar=modv.t[:, l, 2, c, j:j + 1], in1=xg.t[:, c, :],
                        op0=ALU.mult, op1=ALU.add), reads=[py.b, modv.b, xg.b], writes=[xg.b])
                S.dma("sp", xm_fm[:, :, c0:c0 + WG], xg.t[:], reads=[xg.b], writes=[db["xmid"][g]])
                emit_pieces(1)
            emit_pieces(len(wpieces))
        S.barrier()
        if stop_after == ("O1", l):
            wsc.close()
            break

        with ExitStack() as ph:
            xgs_items = [alloc(ph, [128, 8, WG], F32, "xg")]
            for st_ in stgW.items:
                xv = TB(st_.t[:].rearrange("p (a b) -> p a b", a=8), "xgW")
                xv.b = st_.b
                xgs_items.append(xv)
            xgs = Ring(xgs_items)
            hT2s = [alloc(ph, [128, 8, WG], BF16, "hT2") for _ in range(2)]
            hb2s = [[Buf(f"h2_{i}_{k}") for k in range(8)] for i in range(2)]
            sq = alloc(ph, [128, 8, WG], BF16, "sq")
            lnv = alloc(ph, [128, WG], F32, "lnv")
            rstd = alloc(ph, [128, WG], F32, "rstd")
            tmps = Ring([alloc(ph, [128, WG], F32, "tmp") for _ in range(3)])
            rl = Ring([alloc(ph, [128, WG], F32, "rl") for _ in range(3)])
            aT = alloc(ph, [128, 32, WG], BF16, "aT")
            abufs = [Buf(f"a{f}") for f in range(32)]
            ps_ss = palloc(ph, [128, WG], F32, "ps_ss")
            ps_u = Ring([palloc(ph, [128, WG], F32, "ps_u") for _ in range(3)])
            ps_y = Ring([palloc(ph, [128, WG], F32, "ps_y") for _ in range(2)])
            xm_fm = xmid.rearrange("(kc p) t -> p kc t", p=128)
            xr_fm = xres.rearrange("(kc p) t -> p kc t", p=128)
            out_fm = outT.rearrange("(kc p) t -> p kc t", p=128)
            xcur2 = {}

            def loadx2(gg):
                xg = xgs.next()
                g_ = gg // 2 if gg < 16 else 8
                S.dma("sp", xg.t[:], xm_fm[:, :, gg * WG:(gg + 1) * WG], reads=[db["xmid"][g_]], writes=[xg.b])
                xcur2[gg] = xg

            def normB2(gg):
                norm_mod(xcur2[gg], WG, l, 3, 4, 0 if gg < 16 else 1, hT2s[gg % 2], hb2s[gg % 2], sq, ps_ss, lnv, rstd, tmps)
            loadx2(0)
            if ngr > 1:
                loadx2(1)
            norm_sq(xcur2[0], WG, sq)
            normB2(0)
            for gg in range(ngr):
                c0 = gg * WG
                g = gg // 2 if gg < 16 else 8
                j = 0 if gg < 16 else 1
                xg = xcur2[gg]
                hT2, hb2 = hT2s[gg % 2], hb2s[gg % 2]
                if gg + 2 < ngr:
                    loadx2(gg + 2)
                if gg + 1 < ngr:
                    norm_sq(xcur2[gg + 1], WG, sq)
                for f in range(32):
                    pu = ps_u.next()
                    for kc in range(8):
                        mm(pu.t[:, :WG], w1.t[:, kc, f * 128:(f + 1) * 128], hT2.t[:, kc, :], kc == 0, kc == 7, [w1.b] + hb2, pu.b)
                    r = rl.next()
                    S.op("act", lambda e: e.activation(out=r.t[:], in_=pu.t[:, :WG], func=AF.Relu), reads=[pu.b], writes=[r.b])
                    S.op("pool" if f % 2 else "dve", lambda e: e.tensor_tensor(out=aT.t[:, f, :], in0=r.t[:], in1=r.t[:], op=ALU.mult),
                         reads=[r.b], writes=[abufs[f]])
                if gg + 1 < ngr:
                    normB2(gg + 1)
                for c in range(8):
                    py = ps_y.next()
                    for f in range(32):
                        mm(py.t[:, :WG], w2.t[:, f, c * 128:(c + 1) * 128], aT.t[:, f, :], f == 0, f == 31, [w2.b, abufs[f]], py.b)
                    S.op("dve", lambda e: e.scalar_tensor_tensor(
                        out=xg.t[:, c, :], in0=py.t[:, :WG], scalar=modv.t[:, l, 5, c, j:j + 1], in1=xg.t[:, c, :],
                        op0=ALU.mult, op1=ALU.add), reads=[py.b, modv.b, xg.b], writes=[xg.b])
                if not last:
                    S.dma("sp", xr_fm[:, :, c0:c0 + WG], xg.t[:], reads=[xg.b], writes=[db["xres"][g]])
                else:
                    S.op("dve", lambda e: e.tensor_tensor(out=sq.t[:], in0=xg.t[:], in1=xg.t[:], op=ALU.mult), reads=[xg.b], writes=[sq.b])
                    for kc in range(8):
                        mm(ps_ss.t[:, :WG], ones_bf.t[:], sq.t[:, kc, :], kc == 0, kc == 7, [sq.b, ones_bf.b], ps_ss.b)
                    S.op("act", lambda e: e.activation(out=lnv.t[:], in_=ps_ss.t[:, :WG], func=AF.Ln, scale=1.0 / D, bias=epsc.t[:, 0:1]),
                         reads=[ps_ss.b, epsc.b], writes=[lnv.b])
                    S.op("act", lambda e: e.activation(out=rstd.t[:], in_=lnv.t[:], func=AF.Exp, scale=-0.5), reads=[lnv.b], writes=[rstd.b])
                    for kc in range(8):
                        S.op("dve", lambda e: e.scalar_tensor_tensor(
                            out=xg.t[:, kc, :], in0=xg.t[:, kc, :], scalar=gain_sb.t[:, 8, kc:kc + 1], in1=rstd.t[:],
                            op0=ALU.mult, op1=ALU.mult), reads=[xg.b, gain_sb.b, rstd.b], writes=[xg.b])
                    S.dma("sp", out_fm[:, :, c0:c0 + WG], xg.t[:], reads=[xg.b], writes=[])
        wsc.close()
        S.barrier()
        if stop_after == ("O2", l):
            break

    S.finish("sp")
    glob.close()
    return nc, S


def kernel(**inputs):
    per_core = prep_inputs(inputs)
    nc, _ = build()
    res = run_bass_kernel_spmd(nc, per_core, core_ids=list(range(8)))
    out = np.stack([np.ascontiguousarray(r["outT"].T) for r in res.results], axis=0)
    return out.astype(np.float32)
```

```python
import math
from contextlib import ExitStack
import numpy as np
import concourse.bass as bass
import concourse.mybir as mybir
from concourse.bass_utils import run_bass_kernel_spmd

F32 = mybir.dt.float32
BF16 = mybir.dt.bfloat16
AF = mybir.ActivationFunctionType
ALU = mybir.AluOpType

D = 1024
S_ = 4096
C_ = 256
T_ = S_ + C_
L_ = 4
NCOL = 2624
EPS = 1e-6
NEG = -30000.0
O_QA, O_QAS, O_KA, O_KAS, O_QB, O_KB, O_CQ, O_CKV, O_KR, O_KRS, O_VA = (
    0, 512, 1024, 1152, 1280, 1536, 1792, 2048, 2176, 2208, 2240)
VW = 650


class Buf:
    __slots__ = ("name", "lw", "rs", "plw", "prs", "psum")

    def __init__(self, name="", psum=False):
        self.name = name
        self.lw = {}
        self.rs = {}
        self.plw = {}
        self.prs = {}
        self.psum = psum


class Sched:
    def __init__(self, nc, n_dma_slots=32, same_engine_sync=True):
        self.nc = nc
        self.same = same_engine_sync
        self.eng = {"pe": nc.tensor, "act": nc.scalar, "dve": nc.vector, "pool": nc.gpsimd, "sp": nc.sync}
        self.sem = {k: nc.alloc_semaphore(name=f"sem_{k}") for k in self.eng}
        self.cnt = {k: 0 for k in self.eng}
        self.seen = {k: {} for k in self.eng}
        self.dsem = [nc.alloc_semaphore(name=f"dsem{i}") for i in range(n_dma_slots)]
        self.dcnt = [0] * n_dma_slots
        self.dnext = 0
        self.n_wait = 0
        self.n_inst = 0
        self.log = {k: [] for k in self.eng}

    def _semof(self, key):
        return self.dsem[key] if isinstance(key, int) else self.sem[key]

    def _wait(self, e, key, val):
        if key == e and not self.same:
            return
        if self.seen[e].get(key, 0) >= val:
            return
        self.eng[e].wait_ge(self._semof(key), val)
        self.log[e].append(("w", key, val))
        self.seen[e][key] = val
        self.n_wait += 1

    def _deps(self, e, reads, writes, part=False):
        for b in reads:
            for k, v in b.lw.items():
                self._wait(e, k, v)
            if b.psum:
                for k, v in b.rs.items():
                    if k != e:
                        self._wait(e, k, v)
        for b in writes:
            if part and not b.rs:
                for k, v in b.plw.items():
                    self._wait(e, k, v)
                for k, v in b.prs.items():
                    self._wait(e, k, v)
            else:
                for k, v in b.lw.items():
                    self._wait(e, k, v)
                for k, v in b.rs.items():
                    self._wait(e, k, v)

    def _record(self, key, n, reads, writes, part):
        for b in reads:
            b.rs[key] = max(b.rs.get(key, 0), n)
        for b in writes:
            if b.rs or not part:
                b.plw, b.prs = b.lw, b.rs
                b.lw = {}
            b.lw[key] = max(b.lw.get(key, 0), n)
            b.rs = {}

    def op(self, e, fn, reads=(), writes=(), pe_acc=False, inc=True, part=False):
        if pe_acc:
            self._deps(e, reads, ())
        else:
            self._deps(e, reads, writes, part)
        ins = fn(self.eng[e])
        if inc:
            ins.then_inc(self.sem[e], 1)
            self.log[e].append(("i", e, 1))
            self.cnt[e] += 1
            n = self.cnt[e]
        else:
            n = self.cnt[e] + 1
        self.n_inst += 1
        self._record(e, n, reads, writes, part or pe_acc)
        return ins

    def dma(self, q, out, in_, reads=(), writes=(), part=False, **kw):
        self._deps(q, reads, writes, part)
        s = self.dnext
        self.dnext = (self.dnext + 1) % len(self.dsem)
        if self.dcnt[s] > 0:
            self._wait(q, s, 16 * self.dcnt[s])
        ins = self.eng[q].dma_start(out=out, in_=in_, **kw)
        ins.then_inc(self.dsem[s], 16)
        self.log[q].append(("i", s, 16))
        self.dcnt[s] += 1
        v = 16 * self.dcnt[s]
        self.n_inst += 1
        self._record(s, v, reads, writes, part)
        return ins

    def barrier(self):
        for e in self.eng:
            self.finish(e)

    def finish(self, e="sp"):
        for k in self.eng:
            if k != e and self.cnt[k] > 0:
                self._wait(e, k, self.cnt[k])
        for s in range(len(self.dsem)):
            if self.dcnt[s] > 0:
                self._wait(e, s, 16 * self.dcnt[s])


class Pipe:
    def __init__(self, la, d):
        self.la, self.d = la, d
        self.q = []
        self.later = []
        self.step = 0

    def _fire(self, upto=None):
        while self.later and (upto is None or self.later[0][0] <= upto):
            self.later.pop(0)[1]()

    def _s2(self):
        self.q.pop(0)()
        self.step += 1
        self._fire(self.step)

    def push(self, s1, s2):
        s1()
        self.q.append(s2)
        if len(self.q) > self.la:
            self._s2()

    def defer(self, fn, tag=None):
        self.later.append((self.step + self.d, fn, tag))

    def force(self, tag):
        idx = [i for i, it in enumerate(self.later) if it[2] is tag]
        if idx:
            for _ in range(idx[-1] + 1):
                self.later.pop(0)[1]()

    def flush(self):
        while self.q:
            self._s2()
        self._fire(None)


class TB:
    def __init__(self, t, name=""):
        self.t = t
        self.b = Buf(name)


class Ring:
    def __init__(self, items):
        self.items = items
        self.i = 0

    def next(self):
        it = self.items[self.i]
        self.i = (self.i + 1) % len(self.items)
        return it


def _perm_a(n):
    idx = np.arange(n)
    h, d = idx // 64, idx % 64
    j = d % 32
    partner = np.where(j < 16, d + 16, d - 16)
    return h * 64 + partner


def _perm_r(n=32):
    d = np.arange(n)
    jj = d % 16
    return np.where(jj < 8, d + 8, d - 8)


def _rope_tables():
    tok = np.arange(S_)
    row, col = tok // 64, tok % 64

    def tab(ndim_sec, half):
        freqs = (10000.0 ** (-np.arange(half, dtype=np.float32) / half)).astype(np.float32)
        cos = np.zeros((2 * ndim_sec, S_), np.float32)
        sin = np.zeros((2 * ndim_sec, S_), np.float32)
        for d in range(2 * ndim_sec):
            sec, j = d // ndim_sec, d % ndim_sec
            pos = (row if sec == 0 else col).astype(np.float32)
            ang = pos * freqs[j % half]
            cos[d] = np.cos(ang).astype(np.float32)
            sn = np.sin(ang).astype(np.float32)
            sin[d] = -sn if j < half else sn
        return cos, sin

    ca, sa = tab(32, 16)
    cr, sr = tab(16, 8)
    ropeA = np.stack([np.concatenate([ca, ca], 0), np.concatenate([sa, sa], 0)], 1)
    c96 = np.concatenate([np.ones((64, S_), np.float32), cr], 0)
    s96 = np.concatenate([np.zeros((64, S_), np.float32), sr], 0)
    rope96 = np.stack([c96, s96], 1)
    ropeR = np.stack([cr, sr], 1)
    return np.ascontiguousarray(ropeA), np.ascontiguousarray(rope96), np.ascontiguousarray(ropeR)


def _na_layout():
    rows, GW, NR, NCc, NQ = 64, 64, 8, 16, 16
    kr, kc = min(NR, rows), NCc
    qr, qc = math.gcd(rows, NR), NQ
    krb, kcb = min(qr - 1 + kr, rows), min(qc - 1 + kc, GW)
    nrb, ncb = rows // qr, GW // qc
    q_r = np.arange(nrb)[:, None] * qr + np.arange(qr)[None, :]
    q_c = np.arange(ncb)[:, None] * qc + np.arange(qc)[None, :]
    w_r = np.clip(q_r - kr // 2, 0, rows - kr)
    w_c = np.clip(q_c - kc // 2, 0, GW - kc)
    k_r = np.minimum(w_r[:, 0], rows - krb)[:, None] + np.arange(krb)[None, :]
    k_c = np.minimum(w_c[:, 0], GW - kcb)[:, None] + np.arange(kcb)[None, :]
    qr6 = q_r[:, None, :, None, None, None]
    qc6 = q_c[None, :, None, :, None, None]
    wr6 = w_r[:, None, :, None, None, None]
    wc6 = w_c[None, :, None, :, None, None]
    kr6 = k_r[:, None, None, None, :, None]
    kc6 = k_c[None, :, None, None, None, :]
    shape6 = (nrb, ncb, qr, qc, krb, kcb)

    def flat(a):
        return np.broadcast_to(a, shape6).reshape(nrb, ncb, qr * qc, krb * kcb)

    mask = flat((kr6 >= wr6) & (kr6 < wr6 + kr) & (kc6 >= wc6) & (kc6 < wc6 + kc))
    d_r = flat(np.clip(kr6 - qr6, 1 - NR, NR - 1) + NR - 1)
    d_c = flat(np.clip(kc6 - qc6, 1 - NCc, NCc - 1) + NCc - 1)
    return mask, d_r, d_c, k_r[:, 0], k_c[:, 0]


NA_MASK, NA_DR, NA_DC, NA_KR0, NA_KC0 = _na_layout()
ICLS = [0, 1, 1, 1, 1, 1, 1, 2]
JCLS = [0, 1, 1, 2]
IREP = [0, 1, 7]
JREP = [0, 1, 3]
KG_ROWS = [4, 4, 4, 3]


def _rpb_table(na_rpb):
    tab = np.full((L_, 128, 4, 9, 4, 128), NEG, np.float32)
    for ic in range(3):
        for jc in range(3):
            i, j = IREP[ic], JREP[jc]
            m = NA_MASK[i, j]
            dr, dc = NA_DR[i, j], NA_DC[i, j]
            g = na_rpb[:, :, dr, dc]
            g = np.where(m[None, None], g, np.float32(NEG))
            kc0 = int(NA_KC0[j])
            c0w = min(kc0, 32)
            for kg in range(4):
                for rr in range(KG_ROWS[kg]):
                    for cc in range(32):
                        col = c0w + cc
                        if kc0 <= col < kc0 + 31:
                            kk = (4 * kg + rr) * 31 + (col - kc0)
                            tab[:, rr * 32 + cc, :, ic * 3 + jc, kg, :] = g[:, :, :, kk]
    return np.ascontiguousarray(tab.reshape(L_, 128, 4, 9 * 4 * 128))


def _fm(v):
    v = np.asarray(v, np.float32)
    lead = v.shape[:-1]
    c = v.shape[-1] // 128
    v = v.reshape(*lead, c, 128)
    return np.ascontiguousarray(np.moveaxis(v, -1, 0))


def prep_inputs(inp):
    f = lambda a: np.ascontiguousarray(np.asarray(a, np.float32))
    w_in = f(inp["w_in"])
    pa512, pa128, pr = _perm_a(512), _perm_a(128), _perm_r()
    ext = np.concatenate([
        w_in[:, :, 0:512], w_in[:, :, 0:512][:, :, pa512],
        w_in[:, :, 512:640], w_in[:, :, 512:640][:, :, pa128],
        w_in[:, :, 768:1024], w_in[:, :, 1024:1280],
        w_in[:, :, 1536:1792], w_in[:, :, 1792:1920],
        w_in[:, :, 1920:1952], w_in[:, :, 1920:1952][:, :, pr],
        w_in[:, :, 640:768], w_in[:, :, 1280:1536]], axis=2)
    assert ext.shape[2] == NCOL
    w_uq = f(inp["mla_w_uq"])
    sw = np.concatenate([np.concatenate([h * 96 + np.arange(64), h * 96 + 64 + pr]) for h in range(4)])
    w_uq_ext = np.concatenate([w_uq, w_uq[:, :, sw]], axis=2)
    w_ukv = f(inp["mla_w_ukv"]).reshape(L_, 128, 4, 2, 64)
    w_ukv_r = np.ascontiguousarray(w_ukv.transpose(0, 1, 3, 2, 4).reshape(L_, 128, 512))
    ropeA, rope96, ropeR = _rope_tables()
    kq = np.arange(128)
    amask = np.zeros((128, 2, 128), np.float32)
    amask[:, 0, :] = np.where(kq[:, None] >= kq[None, :], 0.0, NEG)
    amask[:, 1, :] = np.where(kq[:, None] <= kq[None, :], 0.0, NEG)
    gains = np.concatenate([_fm(inp["norm1_g"]), _fm(inp["norm2_g"]), _fm(inp["final_norm_g"])[:, None, :]], 1)
    mla_g = np.concatenate([_fm(inp["mla_q_norm_g"]), _fm(inp["mla_kv_norm_g"])], 2)
    shared = {
        "w_ada": f(inp["w_ada"]),
        "b_ada": _fm(inp["b_ada"]),
        "gains": np.ascontiguousarray(gains),
        "w_in_ext": np.ascontiguousarray(ext),
        "w_uq_ext": np.ascontiguousarray(w_uq_ext),
        "w_ukv_r": w_ukv_r,
        "mla_g": np.ascontiguousarray(mla_g),
        "sinkrow": np.ascontiguousarray(np.broadcast_to(f(inp["attn_sink"]).reshape(1, 32), (65, 32))),
        "rpb_tab": _rpb_table(f(inp["na_rpb"])),
        "w_out": f(inp["w_out"]),
        "w_mlp_in": f(inp["w_mlp_in"]),
        "w_mlp_out": f(inp["w_mlp_out"]),
        "ropeA": ropeA, "rope96": rope96, "ropeR": ropeR,
        "amask": amask,
        "ident8": np.ascontiguousarray(8.0 * np.eye(128, dtype=np.float32)),
    }
    x, ctx, c, c_ctx = f(inp["x"]), f(inp["ctx"]), f(inp["c"]), f(inp["c_ctx"])
    per_core = []
    for b in range(x.shape[0]):
        xt = np.ascontiguousarray(np.concatenate([x[b].T, ctx[b].T], axis=1))
        cv = np.ascontiguousarray(np.stack([_fm(c[b]), _fm(c_ctx)], -1))
        d = dict(shared)
        d["xT"] = xt
        d["cvec"] = cv
        per_core.append(d)
    return per_core


def build(n_layers=L_, debug=False, stop_after=None):
    nc = bass.Bass("TRN2", target_bir_lowering=False)
    S = Sched(nc)
    uid = [0]

    def din(name, shape):
        return nc.dram_tensor(name, list(shape), F32, kind="ExternalInput").ap()

    def dscr(name, shape, dt):
        kind = "ExternalOutput" if debug else "Internal"
        return nc.dram_tensor(name, list(shape), dt, kind=kind).ap()

    xT = din("xT", [D, T_])
    cvec = din("cvec", [128, 8, 2])
    w_ada = din("w_ada", [L_, D, 6 * D])
    b_ada = din("b_ada", [128, L_, 48])
    gains = din("gains", [128, 9, 8])
    w_in_ext = din("w_in_ext", [L_, D, NCOL])
    w_uq_ext = din("w_uq_ext", [L_, 256, 768])
    w_ukv_r = din("w_ukv_r", [L_, 128, 512])
    mla_g = din("mla_g", [128, L_, 3])
    sinkrow = din("sinkrow", [65, 32])
    rpb_tab = din("rpb_tab", [L_, 128, 4, 4608])
    w_out = din("w_out", [L_, D, D])
    w_mlp_in = din("w_mlp_in", [L_, D, 4 * D])
    w_mlp_out = din("w_mlp_out", [L_, 4 * D, D])
    ropeA = din("ropeA", [128, 2, S_])
    rope96 = din("rope96", [96, 2, S_])
    ropeR = din("ropeR", [32, 2, S_])
    amask_d = din("amask", [128, 2, 128])
    ident8_d = din("ident8", [128, 128])
    outT = nc.dram_tensor("outT", [D, S_], F32, kind="ExternalOutput").ap()

    xmid = dscr("xmid", [D, T_], F32)
    xres = dscr("xres", [D, T_], F32)
    QA = dscr("QA", [512, T_], BF16)
    KA = dscr("KA", [128, T_], BF16)
    QB = dscr("QB", [256, T_], BF16)
    KB = dscr("KB", [256, T_], BF16)
    QC = dscr("QC", [384, T_], BF16)
    KC = dscr("KC", [384, T_], BF16)
    VALL = dscr("VALL", [T_, VW], BF16)
    MIX = dscr("MIX", [D, T_], BF16)
    NG = 9
    db = {n: [Buf(f"{n}{g}") for g in range(NG)] for n in
          ["xT", "xmid", "xres", "QA", "KA", "QB", "KB", "QC", "KC", "VALL", "MIX"]}

    def gcols(g):
        return (g * 512, 512) if g < 8 else (S_, C_)

    glob = ExitStack()

    def alloc(stack, shape, dt, name=None):
        uid[0] += 1
        t = stack.enter_context(nc.sbuf_tensor(f"{name or 't'}_{uid[0]}", list(shape), dt))
        return TB(t, name or "t")

    def palloc(stack, shape, dt, name=None):
        uid[0] += 1
        t = stack.enter_context(nc.psum_tensor(f"{name or 'p'}_{uid[0]}", [128, 512], F32))
        tb = TB(t, name or "p")
        tb.b.psum = True
        return tb

    def mm(out_ap, lhsT, rhs, first, last, reads, wbuf, inc=None, **kw):
        S.op("pe", lambda e: e.matmul(out_ap, lhsT=lhsT, rhs=rhs, start=first, stop=last, **kw),
             reads=reads, writes=[wbuf], pe_acc=not first, inc=(last if inc is None else inc))

    ones_bf = alloc(glob, [128, 128], BF16, "ones_bf")
    ones_f = alloc(glob, [128, 64], F32, "ones_f")
    ident8 = alloc(glob, [128, 128], BF16, "ident8")
    amask = alloc(glob, [128, 2, 128], BF16, "amask")
    modv = alloc(glob, [128, L_, 6, 8, 2], F32, "modv")
    gain_sb = alloc(glob, [128, 9, 8], F32, "gains")
    mlag_sb = alloc(glob, [128, L_, 3], F32, "mlag")
    esink = alloc(glob, [65, 32], F32, "esink")
    epsc = alloc(glob, [128, 1], F32, "epsc")

    S.op("dve", lambda e: e.memset(ones_bf.t[:], 1.0), writes=[ones_bf.b])
    S.op("dve", lambda e: e.memset(ones_f.t[:], 1.0), writes=[ones_f.b])
    S.op("dve", lambda e: e.memset(epsc.t[:], EPS), writes=[epsc.b])
    S.dma("sp", gain_sb.t[:], gains, writes=[gain_sb.b])
    S.dma("sp", mlag_sb.t[:], mla_g, writes=[mlag_sb.b])
    S.dma("sp", esink.t[:], sinkrow, writes=[esink.b])
    S.op("act", lambda e: e.activation(out=esink.t[:], in_=esink.t[:], func=AF.Exp), reads=[esink.b], writes=[esink.b])

    with ExitStack() as ph:
        stg = alloc(ph, [128, 2, 128], F32, "stg")
        S.dma("sp", stg.t[:, 0, :], ident8_d, writes=[stg.b])
        S.op("dve", lambda e: e.tensor_copy(out=ident8.t[:], in_=stg.t[:, 0, :]), reads=[stg.b], writes=[ident8.b])
        S.dma("sp", stg.t[:], amask_d, reads=[], writes=[stg.b])
        S.op("dve", lambda e: e.tensor_copy(out=amask.t[:], in_=stg.t[:]), reads=[stg.b], writes=[amask.b])
        cv = alloc(ph, [128, 8, 2], F32, "cv")
        sv = alloc(ph, [128, 8, 2], F32, "sv")
        bfm = alloc(ph, [128, L_, 48], F32, "bfm")
        S.dma("sp", cv.t[:], cvec, writes=[cv.b])
        S.dma("sp", bfm.t[:], b_ada, writes=[bfm.b])
        S.op("act", lambda e: e.activation(out=sv.t[:], in_=cv.t[:], func=AF.Silu), reads=[cv.b], writes=[sv.b])
        wst = Ring([alloc(ph, [128, 8, 1024], F32, "wst") for _ in range(2)])
        mps = Ring([palloc(ph, [128, 48, 2], F32, "mps") for _ in range(2)])
        rps = Ring([palloc(ph, [2, 512], F32, "rps") for _ in range(3)])
        mods = alloc(ph, [128, 48, 2], F32, "mods")
        modrow = Ring([alloc(ph, [2, 6 * D], F32, "modrow") for _ in range(2)])
        id2 = alloc(ph, [2, 2], F32, "id2")
        S.dma("sp", id2.t[:], ident8_d[0:2, 0:2], writes=[id2.b])
        S.op("dve", lambda e: e.tensor_scalar(out=id2.t[:], in0=id2.t[:], scalar1=0.125, scalar2=None, op0=ALU.mult),
             reads=[id2.b], writes=[id2.b])
        for l in range(n_layers):
            pm = mps.next()
            mr = modrow.next()
            wa = w_ada[l].rearrange("(kc p) n -> p kc n", p=128)
            for pc in range(6):
                w = wst.next()
                S.dma("sp", w.t[:], wa[:, :, pc * 1024:(pc + 1) * 1024], writes=[w.b])
                for cc in range(2):
                    rp = rps.next()
                    for kc in range(8):
                        mm(rp.t[0:2, :], sv.t[:, kc, :], w.t[:, kc, cc * 512:(cc + 1) * 512], kc == 0, kc == 7, [w.b, sv.b], rp.b)
                    col = pc * 1024 + cc * 512
                    S.op("act", lambda e: e.activation(out=mr.t[:, col:col + 512], in_=rp.t[0:2, :], func=AF.Copy),
                         reads=[rp.b], writes=[mr.b], part=True)
            for ch in range(48):
                S.op("pe", lambda e: e.transpose(pm.t[:, 2 * ch:2 * ch + 2], mr.t[:, ch * 128:(ch + 1) * 128], id2.t[:]),
                     reads=[mr.b, id2.b], writes=[pm.b], pe_acc=(ch > 0), inc=(ch == 47))
            S.op("dve", lambda e: e.tensor_tensor(
                out=mods.t[:], in0=pm.t[:, 0:96].rearrange("p (c j) -> p c j", j=2), in1=bfm.t[:, l, :].unsqueeze(2).broadcast_to([128, 48, 2]),
                op=ALU.add), reads=[pm.b, bfm.b], writes=[mods.b])
            for (k, src_m, gidx) in ((0, 1, l), (3, 4, 4 + l)):
                S.op("dve", lambda e: e.scalar_tensor_tensor(
                    out=modv.t[:, l, k, :, :], in0=mods.t[:, src_m * 8:(src_m + 1) * 8, :], scalar=1.0,
                    in1=gain_sb.t[:, gidx, :].unsqueeze(2).broadcast_to([128, 8, 2]),
                    op0=ALU.add, op1=ALU.mult), reads=[mods.b, gain_sb.b], writes=[modv.b], part=True)
            for (k, src_m) in ((1, 0), (2, 2), (4, 3), (5, 5)):
                S.op("dve", lambda e: e.tensor_copy(out=modv.t[:, l, k, :, :], in_=mods.t[:, src_m * 8:(src_m + 1) * 8, :]),
                     reads=[mods.b], writes=[modv.b], part=True)
    S.barrier()
    if stop_after == ("M", 0):
        if debug:
            dbg = nc.dram_tensor("dbg_modv", [128, L_ * 96], F32, kind="ExternalOutput").ap()
            S.dma("sp", dbg, modv.t[:].rearrange("p l k c j -> p (l k c j)"), reads=[modv.b])
        S.finish("sp")
        glob.close()
        return nc, S

    def norm_sq(xg, W, sq):
        S.op("dve", lambda e: e.tensor_tensor(out=sq.t[:, :, :W], in0=xg.t[:, :, :W], in1=xg.t[:, :, :W], op=ALU.mult),
             reads=[xg.b], writes=[sq.b])

    def norm_mod(xg, W, l, kG, kS, j, hT, hbufs, sq, ps_ss, lnv, rstd, tmps):
        for kc in range(8):
            mm(ps_ss.t[:, :W], ones_bf.t[:], sq.t[:, kc, :W], kc == 0, kc == 7, [sq.b, ones_bf.b], ps_ss.b)
        S.op("act", lambda e: e.activation(out=lnv.t[:, :W], in_=ps_ss.t[:, :W], func=AF.Ln, scale=1.0 / D, bias=epsc.t[:, 0:1]),
             reads=[ps_ss.b, epsc.b], writes=[lnv.b])
        S.op("act", lambda e: e.activation(out=rstd.t[:, :W], in_=lnv.t[:, :W], func=AF.Exp, scale=-0.5),
             reads=[lnv.b], writes=[rstd.b])
        for kc in range(8):
            tm = tmps.next()
            S.op("dve", lambda e: e.scalar_tensor_tensor(
                out=tm.t[:, :W], in0=xg.t[:, kc, :W], scalar=modv.t[:, l, kG, kc, j:j + 1], in1=rstd.t[:, :W],
                op0=ALU.mult, op1=ALU.mult), reads=[xg.b, modv.b, rstd.b], writes=[tm.b])
            S.op("act", lambda e: e.activation(out=hT.t[:, kc, :W], in_=tm.t[:, :W], func=AF.Identity,
                                               bias=modv.t[:, l, kS, kc, j:j + 1], scale=1.0),
                 reads=[tm.b, modv.b], writes=[hbufs[kc]])

    def load_cast(stack_ring, dst_ap_fn, src_ap_fn, n_pieces, dst_buf, engs=("dve", "pool")):
        for i in range(n_pieces):
            st = stack_ring.next()
            src = src_ap_fn(i)
            dst = dst_ap_fn(i)
            shp = list(src.shape)
            view = st.t[:shp[0], :int(np.prod(shp[1:]))]
            if len(shp) == 3:
                view = view.rearrange("p (a b) -> p a b", a=shp[1])
            S.dma("sp", view, src, writes=[st.b])
            eng = engs[i % len(engs)]
            if eng == "act":
                S.op("act", lambda e: e.activation(out=dst, in_=view, func=AF.Copy), reads=[st.b], writes=[dst_buf], part=True)
            else:
                S.op(eng, lambda e: e.tensor_copy(out=dst, in_=view), reads=[st.b], writes=[dst_buf], part=True)

    for l in range(n_layers):
        last = (l == L_ - 1)
        xsrc, xsrc_n = (xT, "xT") if l == 0 else (xres, "xres")
        xs_fm = xsrc.rearrange("(kc p) t -> p kc t", p=128)

        with ExitStack() as ph:
            xgs = Ring([alloc(ph, [128, 8, 512], F32, "xg") for _ in range(3)])
            stg_ring = Ring([TB(x.t[:].rearrange("p a b -> p (a b)"), "stgx") for x in xgs.items])
            for sgt, x in zip(stg_ring.items, xgs.items):
                sgt.b = x.b
            win = alloc(ph, [128, 8, NCOL], BF16, "win")
            wuq = alloc(ph, [128, 2, 768], BF16, "wuq")
            wukv = alloc(ph, [128, 512], BF16, "wukv")
            wie = w_in_ext[l].rearrange("(kc p) n -> p kc n", p=128)
            load_cast(stg_ring, lambda i: win.t[:, i, :], lambda i: wie[:, i, :], 8, win.b)
            wue = w_uq_ext[l].rearrange("(kc p) n -> p kc n", p=128)
            load_cast(stg_ring, lambda i: wuq.t[:, i, :], lambda i: wue[:, i, :], 2, wuq.b)
            load_cast(stg_ring, lambda i: wukv.t[:], lambda i: w_ukv_r[l], 1, wukv.b)

            hTs = [alloc(ph, [128, 8, 512], BF16, "hT") for _ in range(2)]
            hbs = [[Buf(f"h{i}_{k}") for k in range(8)] for i in range(2)]
            sq = alloc(ph, [128, 8, 512], BF16, "sq")
            lnv = alloc(ph, [128, 512], F32, "lnv")
            rstds = Ring([alloc(ph, [128, 512], F32, "rstd") for _ in range(2)])
            tmps = Ring([alloc(ph, [128, 512], F32, "tmp") for _ in range(3)])
            rtab = Ring([alloc(ph, [128, 6, 512], F32, "rtab") for _ in range(2)])
            outs = Ring([alloc(ph, [128, 512], BF16, "ost") for _ in range(6)])
            t1s = Ring([alloc(ph, [128, 512], F32, "t1") for _ in range(2)])
            t2s = Ring([alloc(ph, [128, 512], F32, "t2") for _ in range(2)])
            cq = alloc(ph, [128, 3, 512], F32, "cq")
            cqsq = alloc(ph, [128, 3, 512], BF16, "cqsq")
            cqn = alloc(ph, [128, 3, 512], BF16, "cqn")
            vsts = Ring([alloc(ph, [128, VW], BF16, "vst") for _ in range(4)])
            ps_ss = palloc(ph, [128, 512], F32, "ps_ss")
            ps_a = Ring([palloc(ph, [128, 512], F32, "ps_a") for _ in range(2)])
            ps_b = Ring([palloc(ph, [128, 512], F32, "ps_b") for _ in range(2)])
            ps_vr = Ring([palloc(ph, [128, 384], F32, "ps_v") for _ in range(2)])
            ps_v2 = palloc(ph, [128, 256], F32, "ps_v2")
            for v in vsts.items:
                S.op("dve", lambda e: e.memset(v.t[:], 1.0), writes=[v.b])

            xcur = {}

            def loadx(g):
                c0, W = gcols(g)
                xg = xgs.next()
                S.dma("sp", xg.t[:, :, :W], xs_fm[:, :, c0:c0 + W], reads=[db[xsrc_n][g]], writes=[xg.b])
                xcur[g] = xg

            def sqx(g):
                norm_sq(xcur[g], gcols(g)[1], sq)

            def normB(g):
                c0, W = gcols(g)
                norm_mod(xcur[g], W, l, 0, 1, 0 if g < 8 else 1, hTs[g % 2], hbs[g % 2], sq, ps_ss, lnv, rstds.next(), tmps)

            def evac_copy(ps, M, W, dst_ap, dst_bufs, eng):
                if eng == "act":
                    S.op("act", lambda e: e.activation(out=dst_ap, in_=ps.t[:M, :W], func=AF.Copy), reads=[ps.b], writes=dst_bufs)
                else:
                    S.op(eng, lambda e: e.tensor_copy(out=dst_ap, in_=ps.t[:M, :W]), reads=[ps.b], writes=dst_bufs)

            def proj_fm(g, hT, hb, col, scol, M, rhs_fn, nk, wt, wb, rope_idx, dst_list, rt):
                c0, W = gcols(g)
                pa = ps_a.next()
                for kc in range(nk):
                    mm(pa.t[:M, :W], wt(kc, col, M), rhs_fn(kc, W), kc == 0, kc == nk - 1, [wb] + hb, pa.b)
                o = outs.next()
                if rope_idx is None or g == 8:
                    evac_copy(pa, M, W, o.t[:M, :W], [o.b], "act" if (col // 128) % 2 == 0 else "dve")
                else:
                    pb = ps_b.next()
                    for kc in range(nk):
                        mm(pb.t[:M, :W], wt(kc, scol, M), rhs_fn(kc, W), kc == 0, kc == nk - 1, [wb] + hb, pb.b)
                    t1, t2 = t1s.next(), t2s.next()
                    S.op("dve", lambda e: e.tensor_tensor(out=t1.t[:M, :W], in0=pa.t[:M, :W], in1=rt.t[:M, rope_idx, :W], op=ALU.mult),
                         reads=[pa.b, rt.b], writes=[t1.b])
                    S.op("dve", lambda e: e.tensor_tensor(out=t2.t[:M, :W], in0=pb.t[:M, :W], in1=rt.t[:M, rope_idx + 1, :W], op=ALU.mult),
                         reads=[pb.b, rt.b], writes=[t2.b])
                    S.op("pool", lambda e: e.tensor_tensor(out=o.t[:M, :W], in0=t1.t[:M, :W], in1=t2.t[:M, :W], op=ALU.add),
                         reads=[t1.b, t2.b], writes=[o.b])
                for (dst, dbuf, p0, p1) in dst_list:
                    S.dma("sp", dst[:, c0:c0 + W], o.t[p0:p1, :W], reads=[o.b], writes=[dbuf], part=True)

            rtcur = {}

            def loadrt(g):
                if g >= 8:
                    return
                c0, W = gcols(g)
                rt = rtab.next()
                S.dma("sp", rt.t[:, 0:2, :], ropeA[:, :, c0:c0 + W], writes=[rt.b], part=True)
                S.dma("sp", rt.t[:96, 2:4, :], rope96[:, :, c0:c0 + W], writes=[rt.b], part=True)
                S.dma("sp", rt.t[:32, 4:6, :], ropeR[:, :, c0:c0 + W], writes=[rt.b], part=True)
                rtcur[g] = rt

            def do_proj(g, mid=None):
                c0, W = gcols(g)
                hT, hb = hTs[g % 2], hbs[g % 2]
                rt = rtcur.get(g)
                wt_in = lambda kc, col, M: win.t[:, kc, col:col + M]
                rhs_h = lambda kc, W_: hT.t[:, kc, :W_]
                for c in range(4):
                    proj_fm(g, hT, hb, O_QA + c * 128, O_QAS + c * 128, 128, rhs_h, 8, wt_in, win.b, 0,
                            [(QA[c * 128:(c + 1) * 128, :], db["QA"][g], 0, 128)], rt)
                proj_fm(g, hT, hb, O_KA, O_KAS, 128, rhs_h, 8, wt_in, win.b, 0, [(KA[:, :], db["KA"][g], 0, 128)], rt)
                if mid is not None:
                    mid()
                for c in range(2):
                    proj_fm(g, hT, hb, O_QB + c * 128, None, 128, rhs_h, 8, wt_in, win.b, None,
                            [(QB[c * 128:(c + 1) * 128, :], db["QB"][g], 0, 128)], rt)
                for c in range(2):
                    proj_fm(g, hT, hb, O_KB + c * 128, None, 128, rhs_h, 8, wt_in, win.b, None,
                            [(KB[c * 128:(c + 1) * 128, :], db["KB"][g], 0, 128)], rt)
                proj_fm(g, hT, hb, O_KR, O_KRS, 32, rhs_h, 8, wt_in, win.b, 4,
                        [(KC[h * 96 + 64:h * 96 + 96, :], db["KC"][g], 0, 32) for h in range(4)], rt)
                for c in range(3):
                    pa = ps_a.next()
                    col = O_CQ + c * 128
                    for kc in range(8):
                        mm(pa.t[:, :W], win.t[:, kc, col:col + 128], hT.t[:, kc, :W], kc == 0, kc == 7, [win.b] + hb, pa.b)
                    S.op("dve", lambda e: e.tensor_copy(out=cq.t[:, c, :W], in_=pa.t[:, :W]), reads=[pa.b], writes=[cq.b])
                    S.op("act", lambda e: e.activation(out=cqsq.t[:, c, :W], in_=cq.t[:, c, :W], func=AF.Square), reads=[cq.b], writes=[cqsq.b])
                for (cs, n, gi0) in (((0, 1), 256, 0), ((2,), 128, 2)):
                    for i, c in enumerate(cs):
                        mm(ps_ss.t[:, :W], ones_bf.t[:], cqsq.t[:, c, :W], i == 0, i == len(cs) - 1, [cqsq.b, ones_bf.b], ps_ss.b)
                    rs = rstds.next()
                    S.op("act", lambda e: e.activation(out=lnv.t[:, :W], in_=ps_ss.t[:, :W], func=AF.Ln, scale=1.0 / n, bias=epsc.t[:, 0:1]),
                         reads=[ps_ss.b, epsc.b], writes=[lnv.b])
                    S.op("act", lambda e: e.activation(out=rs.t[:, :W], in_=lnv.t[:, :W], func=AF.Exp, scale=-0.5),
                         reads=[lnv.b], writes=[rs.b])
                    for c in cs:
                        S.op("dve", lambda e: e.scalar_tensor_tensor(
                            out=cqn.t[:, c, :W], in0=cq.t[:, c, :W], scalar=mlag_sb.t[:, l, c:c + 1], in1=rs.t[:, :W],
                            op0=ALU.mult, op1=ALU.mult), reads=[cq.b, mlag_sb.b, rs.b], writes=[cqn.b])
                vtiles = []
                for tt in range(W // 128):
                    ts = slice(tt * 128, (tt + 1) * 128)
                    ps_v = ps_vr.next()
                    for kc in range(8):
                        mm(ps_v.t[:, 0:384], hT.t[:, kc, ts], win.t[:, kc, O_VA:O_VA + 384], kc == 0, kc == 7, [win.b] + hb, ps_v.b)
                    v = vsts.next()
                    vv = v.t[:].rearrange("p (h c) -> p h c", c=65)
                    S.op("dve", lambda e: e.tensor_copy(out=vv[:, 0:6, 0:64], in_=ps_v.t[:, 0:384].rearrange("p (h c) -> p h c", c=64)),
                         reads=[ps_v.b], writes=[v.b], part=True)
                    vtiles.append(v)
                wt_uq = lambda kc, col, M: wuq.t[:, kc, col:col + M]
                rhs_cq = lambda kc, W_: cqn.t[:, kc, :W_]
                for h in range(4):
                    proj_fm(g, cqn, [cqn.b], h * 96, 384 + h * 96, 96, rhs_cq, 2, wt_uq, wuq.b, 2,
                            [(QC[h * 96:(h + 1) * 96, :], db["QC"][g], 0, 96)], rt)
                wt_kv = lambda kc, col, M: wukv.t[:, col:col + M]
                rhs_kv = lambda kc, W_: cqn.t[:, 2, :W_]
                for c in range(2):
                    proj_fm(g, cqn, [cqn.b], c * 128, None, 128, rhs_kv, 1, wt_kv, wukv.b, None,
                            [(KC[(2 * c) * 96:(2 * c) * 96 + 64, :], db["KC"][g], 0, 64),
                             (KC[(2 * c + 1) * 96:(2 * c + 1) * 96 + 64, :], db["KC"][g], 64, 128)], rt)
                for tt in range(W // 128):
                    ts = slice(tt * 128, (tt + 1) * 128)
                    mm(ps_v2.t[:, 0:256], cqn.t[:, 2, ts], wukv.t[:, 256:512], True, True, [wukv.b, cqn.b], ps_v2.b)
                    v = vtiles[tt]
                    vv = v.t[:].rearrange("p (h c) -> p h c", c=65)
                    S.op("act", lambda e: e.activation(out=vv[:, 6:10, 0:64], in_=ps_v2.t[:, 0:256].rearrange("p (h c) -> p h c", c=64), func=AF.Copy),
                         reads=[ps_v2.b], writes=[v.b], part=True)
                    S.dma("sp", VALL[c0 + tt * 128:c0 + (tt + 1) * 128, :], v.t[:], reads=[v.b], writes=[db["VALL"][g]], part=True)

            sub = stop_after[0] if (stop_after and stop_after[1] == l) else None
            if sub != "P0":
                loadx(0)
                loadrt(0)
                loadx(1)
                sqx(0)
                normB(0)
                for g in range(NG):
                    if sub == "P1":
                        break
                    if g + 1 < NG:
                        loadrt(g + 1)
                    if g + 2 < NG:
                        loadx(g + 2)
                    if g + 1 < NG:
                        sqx(g + 1)
                    do_proj(g, mid=(lambda g=g: normB(g + 1)) if g + 1 < NG else None)
                    if sub == "P2":
                        break
        S.barrier()
        if stop_after in (("P", l), ("P0", l), ("P1", l), ("P2", l)):
            break

        qgroups = list(range(8)) + ([] if last else [8])
        mixf = MIX.rearrange("(h d) t -> d h t", d=64)

        def norm1(st, O, W, add_sink=None, use_act=False):
            dsum, rinv, bcs = st
            hv = (lambda a: a.rearrange("p (h t) -> p h t", h=4)) if add_sink is not None else (lambda a: a)
            if add_sink is None:
                S.op("dve", lambda e: e.tensor_copy(out=dsum.t[64:65, :W], in_=O.t[64:65, :W]), reads=[O.b], writes=[dsum.b])
            else:
                S.op("dve", lambda e: e.tensor_tensor(out=hv(dsum.t[64:65, :W]), in0=hv(O.t[64:65, :W]), in1=add_sink, op=ALU.add),
                     reads=[O.b, esrow.b], writes=[dsum.b])
            if use_act:
                S.op("act", lambda e: e.activation(out=dsum.t[64:65, :W], in_=dsum.t[64:65, :W], func=AF.Ln), reads=[dsum.b], writes=[dsum.b])
                S.op("act", lambda e: e.activation(out=rinv.t[64:65, :W], in_=dsum.t[64:65, :W], func=AF.Exp, scale=-1.0), reads=[dsum.b], writes=[rinv.b])
            else:
                S.op("dve", lambda e: e.reciprocal(out=rinv.t[64:65, :W], in_=dsum.t[64:65, :W]), reads=[dsum.b], writes=[rinv.b])

        def norm2(st, ps_bc, O, W, og_ap, og_buf, heads4=False):
            dsum, rinv, bcs = st
            hv = (lambda a: a.rearrange("p (h t) -> p h t", h=4)) if heads4 else (lambda a: a)
            mm(ps_bc.t[:64, :W], ones_f.t[64:65, 0:64], rinv.t[64:65, :W], True, True, [rinv.b, ones_f.b], ps_bc.b)
            S.op("dve", lambda e: e.tensor_copy(out=bcs.t[:64, :W], in_=ps_bc.t[:64, :W]), reads=[ps_bc.b], writes=[bcs.b])
            S.op("dve", lambda e: e.tensor_tensor(out=og_ap, in0=hv(O.t[:64, :W]), in1=hv(bcs.t[:64, :W]), op=ALU.mult),
                 reads=[O.b, bcs.b], writes=[og_buf], part=True)

        def nst_ring(ph, n):
            return Ring([(alloc(ph, [65, 512], F32, "dsum"), alloc(ph, [65, 512], F32, "rinv"), alloc(ph, [64, 512], F32, "bcs"))
                         for _ in range(n)])

        with ExitStack() as ph:
            KAs = alloc(ph, [64, 2, T_], BF16, "KAs")
            VAs = alloc(ph, [128, 34, 130], BF16, "VAs")
            esrow = alloc(ph, [65, 8, 128], F32, "esrow")
            S.dma("sp", KAs.t[:], KA.rearrange("(h d) t -> d h t", d=64), reads=db["KA"], writes=[KAs.b])
            vall_b = VALL.rearrange("(b p) c -> p b c", p=128)
            for i in range(0, 34, 9):
                S.dma("sp", VAs.t[:, i:min(i + 9, 34), :], vall_b[:, i:min(i + 9, 34), 0:130], reads=db["VALL"], writes=[VAs.b], part=True)
            S.op("dve", lambda e: e.memset(esrow.t[:], 0.0), writes=[esrow.b])
            for h in range(8):
                S.op("dve", lambda e: e.tensor_scalar(out=esrow.t[64:65, h, :], in0=esrow.t[64:65, h, :],
                                                      scalar1=esink.t[64:65, l * 8 + h:l * 8 + h + 1], scalar2=None, op0=ALU.add),
                     reads=[esink.b, esrow.b], writes=[esrow.b])
            Qgs = Ring([alloc(ph, [64, 8, 512], BF16, "Qg") for _ in range(2)])
            ogs = Ring([alloc(ph, [64, 8, 512], BF16, "og") for _ in range(2)])
            pts = Ring([alloc(ph, [128, 512], BF16, "pt") for _ in range(6)])
            nsts = nst_ring(ph, 3)
            ps_bc = palloc(ph, [64, 512], F32, "ps_bc")
            ps_s = Ring([palloc(ph, [128, 512], F32, "ps_s") for _ in range(4)])
            ps_o = Ring([palloc(ph, [65, 512], F32, "ps_o") for _ in range(3)])
            pipe = Pipe(2, 3)
            qaf = QA.rearrange("(h d) t -> d h t", d=64)

            def loadQ(g):
                c0, W = gcols(g)
                Qg = Qgs.next()
                S.dma("sp", Qg.t[:, :, :W], qaf[:, :, c0:c0 + W], reads=[db["QA"][g]], writes=[Qg.b])
                return Qg

            def stepA(Qg, O, kv, bi, kb, m, first, lastk, fin):
                cell = {}
                rhs = Qg.t[:, 4 * kv:4 * kv + 4, bi * 128:(bi + 1) * 128]

                def s1():
                    ps = ps_s.next()
                    psv = ps.t[:].rearrange("p (h t) -> p h t", h=4)
                    mm(psv, KAs.t[:, kv, kb * 128:(kb + 1) * 128], rhs, True, m is None, [KAs.b, Qg.b], ps.b)
                    if m is not None:
                        mm(psv, ident8.t[:], amask.t[:, m:m + 1, :].broadcast_to([128, 4, 128]), False, True,
                           [ident8.b, amask.b], ps.b)
                    pt = pts.next()
                    S.op("act", lambda e: e.activation(out=pt.t[:], in_=ps.t[:], func=AF.Exp, scale=0.125),
                         reads=[ps.b], writes=[pt.b])
                    cell["pt"] = pt

                def s2():
                    pt = cell["pt"]
                    if first:
                        pipe.force(O)
                    mm(O.t[:65, :], VAs.t[:, kb, kv * 65:(kv + 1) * 65], pt.t[:], first, lastk, [VAs.b, pt.b], O.b, inc=True)
                    if lastk:
                        fin()
                return s1, s2

            nxtQ = loadQ(qgroups[0])
            for gi, g in enumerate(qgroups):
                c0, W = gcols(g)
                Qg, og = nxtQ, ogs.next()
                if gi + 1 < len(qgroups):
                    nxtQ = loadQ(qgroups[gi + 1])
                for kv in range(2):
                    for bi in range(W // 128):
                        n = g * 4 + bi
                        if g < 8:
                            kbs = [(kb, m) for kb, m in ((n - 1, 0), (n, None), (n + 1, 1)) if 0 <= kb < 32] + [(32, None), (33, None)]
                        else:
                            kbs = [(32, None), (33, None)]
                        O = ps_o.next()
                        ogv = og.t[:, 4 * kv:4 * kv + 4, bi * 128:(bi + 1) * 128]

                        lastacc = (kv == 1 and bi == W // 128 - 1)

                        def fin(O=O, ogv=ogv, og=og, kv=kv, lastacc=lastacc, c0=c0, W=W, g=g):
                            st = nsts.next()
                            norm1(st, O, 512, add_sink=esrow.t[64:65, 4 * kv:4 * kv + 4, :], use_act=True)
                            pipe.defer(lambda: norm2(st, ps_bc, O, 512, ogv, og.b, heads4=True), tag=O)
                            if lastacc:
                                pipe.defer(lambda: S.dma("sp", mixf[:, 0:8, c0:c0 + W], og.t[:, :, :W], reads=[og.b], writes=[db["MIX"][g]], part=True))
                        for ki, (kb, m) in enumerate(kbs):
                            pipe.push(*stepA(Qg, O, kv, bi, kb, m, ki == 0, ki == len(kbs) - 1, fin))
            pipe.flush()
        S.barrier()
        if stop_after == ("TA", l):
            break

        with ExitStack() as ph:
            KBs = alloc(ph, [64, 4, T_], BF16, "KBs")
            VBc = alloc(ph, [128, 2, 260], BF16, "VBc")
            bm = alloc(ph, [128, 4, 4608], BF16, "bm")
            stg_ring = Ring([alloc(ph, [128, 4608], F32, "stgb") for _ in range(2)])
            S.dma("sp", KBs.t[:], KB.rearrange("(h d) t -> d h t", d=64), reads=db["KB"], writes=[KBs.b])
            vall_b = VALL.rearrange("(b p) c -> p b c", p=128)
            S.dma("sp", VBc.t[:], vall_b[:, 32:34, 130:390], reads=db["VALL"], writes=[VBc.b])
            load_cast(stg_ring, lambda i: bm.t[:, i, :], lambda i: rpb_tab[l, :, i, :], 4, bm.b)
            Qgs = Ring([alloc(ph, [64, 4, 512], BF16, "Qg") for _ in range(2)])
            ogs = Ring([alloc(ph, [64, 4, 512], BF16, "og") for _ in range(2)])
            pts = Ring([alloc(ph, [128, 512], BF16, "pt") for _ in range(6)])
            vgs = Ring([alloc(ph, [128, 260], BF16, "vg") for _ in range(8)])
            kgts = Ring([alloc(ph, [64, 4, 128], BF16, "kgt") for _ in range(4)])
            nsts = nst_ring(ph, 4)
            ps_bc = palloc(ph, [64, 512], F32, "ps_bc")
            ps_s = Ring([palloc(ph, [128, 512], F32, "ps_s") for _ in range(3)])
            ps_o = [palloc(ph, [65, 512], F32, "ps_o") for _ in range(4)]
            pipe = Pipe(2, 2)
            qbf = QB.rearrange("(h d) t -> d h t", d=64)

            def loadQ(g):
                c0, W = gcols(g)
                Qg = Qgs.next()
                S.dma("sp", Qg.t[:, :, :W], qbf[:, :, c0:c0 + W], reads=[db["QB"][g]], writes=[Qg.b])
                return Qg

            def stepBctx(Qg, O, h, ki, cb, W, lastk, fin):
                cell = {}

                def s1():
                    ps = ps_s.next()
                    mm(ps.t[:, :W], KBs.t[:, h, cb * 128:(cb + 1) * 128], Qg.t[:, h, :W], True, True, [KBs.b, Qg.b], ps.b)
                    pt = pts.next()
                    S.op("act", lambda e: e.activation(out=pt.t[:, :W], in_=ps.t[:, :W], func=AF.Exp, scale=0.125),
                         reads=[ps.b], writes=[pt.b])
                    cell["pt"] = pt

                def s2():
                    pt = cell["pt"]
                    if ki == 0:
                        pipe.force(O)
                    mm(O.t[:65, :W], VBc.t[:, ki, h * 65:(h + 1) * 65], pt.t[:, :W], ki == 0, lastk,
                       [VBc.b, pt.b], O.b, inc=True, skip_group_check=True)
                    if lastk:
                        fin()
                return s1, s2

            def stepBloc(Qg, j, kg, pat, kgt, vg, M, lastk, fins):
                cell = {}

                def s1():
                    ps = ps_s.next()
                    boff = (pat * 4 + kg) * 128
                    mm(ps.t[:M, :].rearrange("p (h t) -> p h t", h=4), ident8.t[:M, :M], bm.t[:M, :, boff:boff + 128],
                       True, False, [ident8.b, bm.b], ps.b)
                    for h in range(4):
                        qview = Qg.t[:, h, :].rearrange("p (r c) -> p r c", c=64)[:, :, 16 * j:16 * j + 16]
                        psv = ps.t[:M, h * 128:(h + 1) * 128].rearrange("p (r c) -> p r c", c=16)
                        mm(psv, kgt.t[:, h, :M], qview, False, h == 3, [kgt.b, Qg.b], ps.b, skip_group_check=True)
                    pt = pts.next()
                    S.op("act", lambda e: e.activation(out=pt.t[:M, :], in_=ps.t[:M, :], func=AF.Exp, scale=0.125),
                         reads=[ps.b], writes=[pt.b])
                    cell["pt"] = pt

                def s2():
                    pt = cell["pt"]
                    for h in range(4):
                        O = ps_o[h]
                        ov = O.t[:65, :].rearrange("p (r c) -> p r c", c=64)[:, :, 16 * j:16 * j + 16]
                        ptv = pt.t[:M, h * 128:(h + 1) * 128].rearrange("p (r c) -> p r c", c=16)
                        mm(ov, vg.t[:M, h * 65:(h + 1) * 65], ptv, False, lastk, [vg.b, pt.b], O.b,
                           inc=(h == 3), skip_group_check=True)
                    if lastk:
                        for f in fins:
                            f()
                return s1, s2

            nxtQ = loadQ(qgroups[0])
            for gi, g in enumerate(qgroups):
                c0, W = gcols(g)
                Qg, og = nxtQ, ogs.next()
                if gi + 1 < len(qgroups):
                    nxtQ = loadQ(qgroups[gi + 1])

                def mkfin(h, og=og, W=W, c0=c0, g=g):
                    def fin():
                        st = nsts.next()
                        O = ps_o[h]
                        norm1(st, O, W)
                        pipe.defer(lambda: norm2(st, ps_bc, O, W, og.t[:, h, :W], og.b), tag=O)
                        if h == 3:
                            pipe.defer(lambda: S.dma("sp", mixf[:, 8:12, c0:c0 + W], og.t[:, :, :W], reads=[og.b], writes=[db["MIX"][g]], part=True))
                    return fin
                for h in range(4):
                    for ki, cb in enumerate((32, 33)):
                        pipe.push(*stepBctx(Qg, ps_o[h], h, ki, cb, W, (g == 8 and ki == 1), mkfin(h)))
                if g < 8:
                    i = g
                    r0 = int(NA_KR0[i])
                    for j in range(4):
                        cc0 = int(NA_KC0[j])
                        pat = ICLS[i] * 3 + JCLS[j]
                        c0w = min(cc0, 32)
                        for kg in range(4):
                            nr = KG_ROWS[kg]
                            M = nr * 32
                            vg = vgs.next()
                            tok0 = (r0 + 4 * kg) * 64 + c0w
                            for rr in range(nr):
                                S.dma("sp", vg.t[rr * 32:(rr + 1) * 32, :], VALL[tok0 + rr * 64:tok0 + rr * 64 + 32, 130:390],
                                      reads=db["VALL"], writes=[vg.b], part=True)
                            kgt = kgts.next()
                            S.op("pool", lambda e: e.tensor_copy(
                                out=kgt.t[:, :, :M].rearrange("p h (r c) -> p h r c", c=32),
                                in_=KBs.t[:, :, tok0:tok0 + nr * 64].rearrange("p h (r c) -> p h r c", c=64)[:, :, :, 0:32]),
                                reads=[KBs.b], writes=[kgt.b])
                            pipe.push(*stepBloc(Qg, j, kg, pat, kgt, vg, M, (j == 3 and kg == 3), [mkfin(h) for h in range(4)]))
            pipe.flush()
        S.barrier()
        if stop_after == ("TB", l):
            break

        wsc = ExitStack()
        stgW = Ring([alloc(wsc, [128, 2048], F32, "stgW") for _ in range(2)])
        w1 = alloc(wsc, [128, 8, 4 * D], BF16, "w1")
        w1d = w_mlp_in[l].rearrange("(kc p) n -> p kc n", p=128)
        w2d = w_mlp_out[l].rearrange("(f p) n -> p f n", p=128)
        wpieces = []

        def piece(dst, src, dbuf, eng):
            def f():
                st = stgW.next()
                shp = list(src.shape)
                view = st.t[:, :int(np.prod(shp[1:]))]
                if len(shp) == 3:
                    view = view.rearrange("p (a b) -> p a b", a=shp[1])
                S.dma("sp", view, src, writes=[st.b])
                if eng == "act":
                    S.op("act", lambda e: e.activation(out=dst, in_=view, func=AF.Copy), reads=[st.b], writes=[dbuf], part=True)
                else:
                    S.op(eng, lambda e: e.tensor_copy(out=dst, in_=view), reads=[st.b], writes=[dbuf], part=True)
            return f

        def emit_pieces(n):
            for _ in range(n):
                if wpieces:
                    wpieces.pop(0)()
        for i in range(16):
            wpieces.append(piece(w1.t[:, i // 2, (i % 2) * 2048:(i % 2 + 1) * 2048],
                                 w1d[:, i // 2, (i % 2) * 2048:(i % 2 + 1) * 2048], w1.b, "pool"))

        with ExitStack() as ph:
            KCs = alloc(ph, [96, 4, T_], BF16, "KCs")
            VCs = alloc(ph, [128, 34, 260], BF16, "VCs")
            S.dma("sp", KCs.t[:], KC.rearrange("(h d) t -> d h t", d=96), reads=db["KC"], writes=[KCs.b])
            vall_b = VALL.rearrange("(b p) c -> p b c", p=128)
            for i in range(0, 34, 9):
                S.dma("sp", VCs.t[:, i:min(i + 9, 34), :], vall_b[:, i:min(i + 9, 34), 390:650], reads=db["VALL"], writes=[VCs.b], part=True)
            Qgs = Ring([alloc(ph, [96, 4, 512], BF16, "Qg") for _ in range(2)])
            ogs = Ring([alloc(ph, [64, 4, 512], BF16, "og") for _ in range(2)])
            pts = Ring([alloc(ph, [128, 512], BF16, "pt") for _ in range(6)])
            nsts = nst_ring(ph, 2)
            ps_bc = palloc(ph, [64, 512], F32, "ps_bc")
            ps_s = Ring([palloc(ph, [128, 512], F32, "ps_s") for _ in range(4)])
            ps_o = Ring([palloc(ph, [65, 512], F32, "ps_o") for _ in range(2)])
            sc = float(96 ** -0.5)
            pipe = Pipe(3, 4)
            qcf = QC.rearrange("(h d) t -> d h t", d=96)

            def loadQ(g):
                c0, W = gcols(g)
                Qg = Qgs.next()
                S.dma("sp", Qg.t[:, :, :W], qcf[:, :, c0:c0 + W], reads=[db["QC"][g]], writes=[Qg.b])
                return Qg

            def stepC(Qg, O, h, kb, W, first, lastk, fin):
                cell = {}

                def s1():
                    ps = ps_s.next()
                    mm(ps.t[:, :W], KCs.t[:, h, kb * 128:(kb + 1) * 128], Qg.t[:, h, :W], True, True, [KCs.b, Qg.b], ps.b)
                    pt = pts.next()
                    S.op("act", lambda e: e.activation(out=pt.t[:, :W], in_=ps.t[:, :W], func=AF.Exp, scale=sc),
                         reads=[ps.b], writes=[pt.b])
                    cell["pt"] = pt

                def s2():
                    pt = cell["pt"]
                    if first:
                        pipe.force(O)
                    mm(O.t[:65, :W], VCs.t[:, kb, h * 65:(h + 1) * 65], pt.t[:, :W], first, lastk, [VCs.b, pt.b], O.b, inc=True)
                    if lastk:
                        fin()
                return s1, s2

            nxtQ = loadQ(qgroups[0])
            for gi, g in enumerate(qgroups):
                c0, W = gcols(g)
                Qg, og = nxtQ, ogs.next()
                if gi + 1 < len(qgroups):
                    nxtQ = loadQ(qgroups[gi + 1])
                kbs = list(range(34)) if g < 8 else [32, 33]
                for h in range(4):
                    O = ps_o.next()

                    def fin(O=O, og=og, h=h, W=W, c0=c0, g=g):
                        st = nsts.next()
                        norm1(st, O, W)
                        pipe.defer(lambda: norm2(st, ps_bc, O, W, og.t[:, h, :W], og.b), tag=O)
                        if h == 3:
                            pipe.defer(lambda: S.dma("sp", mixf[:, 12:16, c0:c0 + W], og.t[:, :, :W], reads=[og.b], writes=[db["MIX"][g]], part=True))
                    for ki, kb in enumerate(kbs):
                        pipe.push(*stepC(Qg, O, h, kb, W, ki == 0, ki == len(kbs) - 1, fin))
                    if (gi * 4 + h) % 2 == 1:
                        emit_pieces(1)
            pipe.flush()
            emit_pieces(len(wpieces))
        S.barrier()
        if stop_after == ("TC", l):
            wsc.close()
            break

        w2 = alloc(wsc, [128, 32, D], BF16, "w2")
        for i in range(16):
            wpieces.append(piece(w2.t[:, 2 * i:2 * i + 2, :], w2d[:, 2 * i:2 * i + 2, :], w2.b, "pool" if i % 2 else "act"))
        WG = 256
        ngr = 16 + (0 if last else 1)
        with ExitStack() as ph:
            wo = alloc(ph, [128, 8, D], BF16, "wo")
            wod = w_out[l].rearrange("(kc p) n -> p kc n", p=128)
            for i in range(4):
                piece(wo.t[:, 2 * i:2 * i + 2, :], wod[:, 2 * i:2 * i + 2, :], wo.b, "pool" if i % 2 else "act")()
            mgs = Ring([alloc(ph, [128, 8, WG], BF16, "mixg") for _ in range(2)])
            xgs = Ring([alloc(ph, [128, 8, WG], F32, "xg") for _ in range(2)])
            ps_y = Ring([palloc(ph, [128, 512], F32, "ps_y") for _ in range(4)])
            mix_fm = MIX.rearrange("(kc p) t -> p kc t", p=128)
            xm_fm = xmid.rearrange("(kc p) t -> p kc t", p=128)

            def loadO1(gg):
                c0 = gg * WG
                g = gg // 2 if gg < 16 else 8
                mg, xg = mgs.next(), xgs.next()
                S.dma("sp", mg.t[:], mix_fm[:, :, c0:c0 + WG], reads=[db["MIX"][g]], writes=[mg.b])
                S.dma("sp", xg.t[:], xs_fm[:, :, c0:c0 + WG], reads=[db[xsrc_n][g]], writes=[xg.b])
                return mg, xg
            nxt = loadO1(0)
            for gg in range(ngr):
                c0 = gg * WG
                g = gg // 2 if gg < 16 else 8
                j = 0 if gg < 16 else 1
                mg, xg = nxt
                if gg + 1 < ngr:
                    nxt = loadO1(gg + 1)
                for c in range(8):
                    py = ps_y.next()
                    for kc in range(8):
                        mm(py.t[:, :WG], wo.t[:, kc, c * 128:(c + 1) * 128], mg.t[:, kc, :], kc == 0, kc == 7, [wo.b, mg.b], py.b)
                    S.op("dve", lambda e: e.scalar_tensor_tensor(
                        out=xg.t[:, c, :], in0=py.t[:, :WG], scalar=modv.t[:, l, 2, c, j:j + 1], in1=xg.t[:, c, :],
                        op0=ALU.mult, op1=ALU.add), reads=[py.b, modv.b, xg.b], writes=[xg.b])
                S.dma("sp", xm_fm[:, :, c0:c0 + WG], xg.t[:], reads=[xg.b], writes=[db["xmid"][g]])
                emit_pieces(1)
            emit_pieces(len(wpieces))
        S.barrier()
        if stop_after == ("O1", l):
            wsc.close()
            break

        with ExitStack() as ph:
            xgs_items = [alloc(ph, [128, 8, WG], F32, "xg")]
            for st_ in stgW.items:
                xv = TB(st_.t[:].rearrange("p (a b) -> p a b", a=8), "xgW")
                xv.b = st_.b
                xgs_items.append(xv)
            xgs = Ring(xgs_items)
            hT2s = [alloc(ph, [128, 8, WG], BF16, "hT2") for _ in range(2)]
            hb2s = [[Buf(f"h2_{i}_{k}") for k in range(8)] for i in range(2)]
            sq = alloc(ph, [128, 8, WG], BF16, "sq")
            lnv = alloc(ph, [128, WG], F32, "lnv")
            rstd = alloc(ph, [128, WG], F32, "rstd")
            tmps = Ring([alloc(ph, [128, WG], F32, "tmp") for _ in range(3)])
            rl = Ring([alloc(ph, [128, WG], F32, "rl") for _ in range(3)])
            aT = alloc(ph, [128, 32, WG], BF16, "aT")
            abufs = [Buf(f"a{f}") for f in range(32)]
            ps_ss = palloc(ph, [128, WG], F32, "ps_ss")
            ps_u = Ring([palloc(ph, [128, WG], F32, "ps_u") for _ in range(3)])
            ps_y = Ring([palloc(ph, [128, WG], F32, "ps_y") for _ in range(2)])
            xm_fm = xmid.rearrange("(kc p) t -> p kc t", p=128)
            xr_fm = xres.rearrange("(kc p) t -> p kc t", p=128)
            out_fm = outT.rearrange("(kc p) t -> p kc t", p=128)
            xcur2 = {}

            def loadx2(gg):
                xg = xgs.next()
                g_ = gg // 2 if gg < 16 else 8
                S.dma("sp", xg.t[:], xm_fm[:, :, gg * WG:(gg + 1) * WG], reads=[db["xmid"][g_]], writes=[xg.b])
                xcur2[gg] = xg

            def normB2(gg):
                norm_mod(xcur2[gg], WG, l, 3, 4, 0 if gg < 16 else 1, hT2s[gg % 2], hb2s[gg % 2], sq, ps_ss, lnv, rstd, tmps)
            loadx2(0)
            if ngr > 1:
                loadx2(1)
            norm_sq(xcur2[0], WG, sq)
            normB2(0)
            for gg in range(ngr):
                c0 = gg * WG
                g = gg // 2 if gg < 16 else 8
                j = 0 if gg < 16 else 1
                xg = xcur2[gg]
                hT2, hb2 = hT2s[gg % 2], hb2s[gg % 2]
                if gg + 2 < ngr:
                    loadx2(gg + 2)
                if gg + 1 < ngr:
                    norm_sq(xcur2[gg + 1], WG, sq)
                for f in range(32):
                    pu = ps_u.next()
                    for kc in range(8):
                        mm(pu.t[:, :WG], w1.t[:, kc, f * 128:(f + 1) * 128], hT2.t[:, kc, :], kc == 0, kc == 7, [w1.b] + hb2, pu.b)
                    r = rl.next()
                    S.op("act", lambda e: e.activation(out=r.t[:], in_=pu.t[:, :WG], func=AF.Relu), reads=[pu.b], writes=[r.b])
                    S.op("pool" if f % 2 else "dve", lambda e: e.tensor_tensor(out=aT.t[:, f, :], in0=r.t[:], in1=r.t[:], op=ALU.mult),
                         reads=[r.b], writes=[abufs[f]])
                if gg + 1 < ngr:
                    normB2(gg + 1)
                for c in range(8):
                    py = ps_y.next()
                    for f in range(32):
                        mm(py.t[:, :WG], w2.t[:, f, c * 128:(c + 1) * 128], aT.t[:, f, :], f == 0, f == 31, [w2.b, abufs[f]], py.b)
                    S.op("dve", lambda e: e.scalar_tensor_tensor(
                        out=xg.t[:, c, :], in0=py.t[:, :WG], scalar=modv.t[:, l, 5, c, j:j + 1], in1=xg.t[:, c, :],
                        op0=ALU.mult, op1=ALU.add), reads=[py.b, modv.b, xg.b], writes=[xg.b])
                if not last:
                    S.dma("sp", xr_fm[:, :, c0:c0 + WG], xg.t[:], reads=[xg.b], writes=[db["xres"][g]])
                else:
                    S.op("dve", lambda e: e.tensor_tensor(out=sq.t[:], in0=xg.t[:], in1=xg.t[:], op=ALU.mult), reads=[xg.b], writes=[sq.b])
                    for kc in range(8):
                        mm(ps_ss.t[:, :WG], ones_bf.t[:], sq.t[:, kc, :], kc == 0, kc == 7, [sq.b, ones_bf.b], ps_ss.b)
                    S.op("act", lambda e: e.activation(out=lnv.t[:], in_=ps_ss.t[:, :WG], func=AF.Ln, scale=1.0 / D, bias=epsc.t[:, 0:1]),
                         reads=[ps_ss.b, epsc.b], writes=[lnv.b])
                    S.op("act", lambda e: e.activation(out=rstd.t[:], in_=lnv.t[:], func=AF.Exp, scale=-0.5), reads=[lnv.b], writes=[rstd.b])
                    for kc in range(8):
                        S.op("dve", lambda e: e.scalar_tensor_tensor(
                            out=xg.t[:, kc, :], in0=xg.t[:, kc, :], scalar=gain_sb.t[:, 8, kc:kc + 1], in1=rstd.t[:],
                            op0=ALU.mult, op1=ALU.mult), reads=[xg.b, gain_sb.b, rstd.b], writes=[xg.b])
                    S.dma("sp", out_fm[:, :, c0:c0 + WG], xg.t[:], reads=[xg.b], writes=[])
        wsc.close()
        S.barrier()
        if stop_after == ("O2", l):
            break

    S.finish("sp")
    glob.close()
    return nc, S


def kernel(**inputs):
    per_core = prep_inputs(inputs)
    nc, _ = build()
    res = run_bass_kernel_spmd(nc, per_core, core_ids=list(range(8)))
    out = np.stack([np.ascontiguousarray(r["outT"].T) for r in res.results], axis=0)
    return out.astype(np.float32)
```
